# Optimizing a Trainium2 kernel written in Bass

```python
import math
import jax, jax.numpy as jnp
from jax import lax
import numpy as np

D_MODEL = 1024
BATCH = 32
SEQ = 2048
DEPTH = 1

D_SSM = D_MODEL // 2
SSM_GROUP = 16
N_SSM_GROUPS = D_SSM // SSM_GROUP
SSM_STATE = 64
D_CONV = D_MODEL // 2
CONV_WIDTH = 3
D_FF = 4 * D_MODEL
N_BRANCH = 2
N_MOD = 6
RMS_EPS = 1e-6
DT_MIN = 1e-3
DT_MAX = 1e-1
IN_COLS = D_SSM + 3 * D_CONV + N_BRANCH * D_MODEL

kernel_name = "hybrid_s5_shortconv_gated_adaln_block"


def rmsnorm(x, g):
    xf = x.astype(jnp.float32)
    y = xf * lax.rsqrt(jnp.mean(xf * xf, axis=-1, keepdims=True) + RMS_EPS)
    return (y * g.astype(jnp.float32)).astype(x.dtype)


def modulate(h, shift, scale):
    return h * (1 + scale[:, None, :]) + shift[:, None, :]


def s5_mimo(u, lam_re, lam_im, log_dt, b_re, b_im, c_re, c_im, d_skip):
    f32 = jnp.float32
    n_len = u.shape[1]
    u32 = u.astype(f32)
    lam = lax.complex(lam_re.astype(f32), lam_im.astype(f32))
    dt = jnp.exp(log_dt.astype(f32))[:, None]
    lam_bar = jnp.exp(lam * dt)
    b = lax.complex(b_re.astype(f32), b_im.astype(f32))
    b_bar = ((lam_bar - 1) / lam)[..., None] * b
    bu = jnp.einsum('blgh,gph->blgp', u32.astype(jnp.complex64), b_bar)
    a = jnp.broadcast_to(lam_bar[None, None], (1, n_len) + lam_bar.shape)

    def combine(e1, e2):
        a1, s1 = e1
        a2, s2 = e2
        return a1 * a2, a2 * s1 + s2

    _, states = lax.associative_scan(combine, (a, bu), axis=1)
    c = lax.complex(c_re.astype(f32), c_im.astype(f32))
    y = jnp.real(jnp.einsum('blgp,ghp->blgh', states, c))
    y = y + d_skip.astype(f32) * u32
    return y.astype(u.dtype)


def short_gated_conv(cx, cb, cc, conv_w):
    v = cc * cx
    w = conv_w.astype(v.dtype)[:, None, :]
    y = lax.conv_general_dilated(
        v, w, window_strides=(1,), padding=[(CONV_WIDTH - 1, 0)],
        dimension_numbers=('NWC', 'WIO', 'NWC'), feature_group_count=D_CONV)
    return cb * y


def setup_inputs(seed: int = 0) -> dict:
    key = jax.random.key(seed)
    ks = jax.random.split(key, 24)
    f32 = jnp.float32
    G, H, P = N_SSM_GROUPS, SSM_GROUP, SSM_STATE

    def nrm(k, shape, scale):
        return jax.random.normal(k, shape, f32) * scale

    x = jax.random.normal(ks[0], (BATCH, SEQ, D_MODEL), f32)
    c = jax.random.normal(ks[1], (BATCH, D_MODEL), f32)
    norm1_g = 1.0 + nrm(ks[2], (DEPTH, D_MODEL), 0.02)
    norm2_g = 1.0 + nrm(ks[3], (DEPTH, D_MODEL), 0.02)
    w_ada = nrm(ks[4], (DEPTH, D_MODEL, N_MOD * D_MODEL), 0.5 * D_MODEL ** -0.5)
    b_ada = nrm(ks[5], (DEPTH, N_MOD * D_MODEL), 0.01)
    w_in = nrm(ks[6], (DEPTH, D_MODEL, IN_COLS), D_MODEL ** -0.5)
    lam_re = -0.5 + nrm(ks[7], (DEPTH, G, P), 0.02)
    lam_im = math.pi * jnp.arange(P, dtype=f32)[None, None, :] + nrm(ks[8], (DEPTH, G, P), 0.02)
    log_dt = jax.random.uniform(ks[9], (DEPTH, G), f32, math.log(DT_MIN), math.log(DT_MAX))
    b_re = nrm(ks[10], (DEPTH, G, P, H), (2 * H) ** -0.5)
    b_im = nrm(ks[11], (DEPTH, G, P, H), (2 * H) ** -0.5)
    c_re = nrm(ks[12], (DEPTH, G, H, P), P ** -0.5)
    c_im = nrm(ks[13], (DEPTH, G, H, P), P ** -0.5)
    d_skip = nrm(ks[14], (DEPTH, D_SSM), 1.0)
    w_glu = nrm(ks[15], (DEPTH, D_SSM, D_SSM), D_SSM ** -0.5)
    b_glu = nrm(ks[16], (DEPTH, D_SSM), 0.01)
    conv_w = nrm(ks[17], (DEPTH, CONV_WIDTH, D_CONV), CONV_WIDTH ** -0.5)
    w_proj_ssm = nrm(ks[18], (DEPTH, D_SSM, D_MODEL), D_SSM ** -0.5)
    w_proj_conv = nrm(ks[19], (DEPTH, D_CONV, D_MODEL), D_CONV ** -0.5)
    w_out = nrm(ks[20], (DEPTH, D_MODEL, D_MODEL), D_MODEL ** -0.5)
    w_ff1 = nrm(ks[21], (DEPTH, D_MODEL, D_FF), D_MODEL ** -0.5)
    w_ff2 = nrm(ks[22], (DEPTH, D_FF, D_MODEL), D_FF ** -0.5)
    final_g = 1.0 + nrm(ks[23], (D_MODEL,), 0.02)
    return {"x": x, "c": c, "norm1_g": norm1_g, "norm2_g": norm2_g,
            "w_ada": w_ada, "b_ada": b_ada, "w_in": w_in,
            "lam_re": lam_re, "lam_im": lam_im, "log_dt": log_dt,
            "b_re": b_re, "b_im": b_im, "c_re": c_re, "c_im": c_im,
            "d_skip": d_skip, "w_glu": w_glu, "b_glu": b_glu, "conv_w": conv_w,
            "w_proj_ssm": w_proj_ssm, "w_proj_conv": w_proj_conv, "w_out": w_out,
            "w_ff1": w_ff1, "w_ff2": w_ff2, "final_g": final_g}


def reference(x, c, norm1_g, norm2_g, w_ada, b_ada, w_in, lam_re, lam_im, log_dt,
              b_re, b_im, c_re, c_im, d_skip, w_glu, b_glu, conv_w,
              w_proj_ssm, w_proj_conv, w_out, w_ff1, w_ff2, final_g):
    n_b, n_len, _ = x.shape
    split_at = [D_SSM, D_SSM + D_CONV, D_SSM + 2 * D_CONV, D_SSM + 3 * D_CONV,
                D_SSM + 3 * D_CONV + D_MODEL]
    c_act = jax.nn.silu(c)
    for l in range(DEPTH):
        mod = c_act @ w_ada[l] + b_ada[l]
        sh1, sc1, g1, sh2, sc2, g2 = jnp.split(mod, N_MOD, axis=-1)

        h = modulate(rmsnorm(x, norm1_g[l]), sh1, sc1)
        p = h @ w_in[l]
        u_s, cb, cc, cx, gate_s, gate_c = jnp.split(p, split_at, axis=-1)

        u_g = u_s.reshape(n_b, n_len, N_SSM_GROUPS, SSM_GROUP)
        y_s = s5_mimo(u_g, lam_re[l], lam_im[l], log_dt[l], b_re[l], b_im[l],
                      c_re[l], c_im[l], d_skip[l].reshape(N_SSM_GROUPS, SSM_GROUP))
        y_s = jax.nn.gelu(y_s.reshape(n_b, n_len, D_SSM))
        y_s = y_s * jax.nn.sigmoid(y_s @ w_glu[l] + b_glu[l])

        y_c = short_gated_conv(cx, cb, cc, conv_w[l])

        merged = (jax.nn.sigmoid(gate_s) * (y_s @ w_proj_ssm[l])
                  + jax.nn.sigmoid(gate_c) * (y_c @ w_proj_conv[l]))
        x = x + g1[:, None, :] * (merged @ w_out[l])

        h2 = modulate(rmsnorm(x, norm2_g[l]), sh2, sc2)
        f = jnp.square(jax.nn.relu(h2 @ w_ff1[l])) @ w_ff2[l]
        x = x + g2[:, None, :] * f
    return rmsnorm(x, final_g)
```

```python
import contextlib
import math
import numpy as np
import concourse.bass as bass
import concourse.mybir as mybir
from concourse.bass_utils import run_bass_kernel_spmd

F32 = mybir.dt.float32
BF16 = mybir.dt.bfloat16
I32 = mybir.dt.int32
AF = mybir.ActivationFunctionType
ALU = mybir.AluOpType

NCORES = 8
D = 1024
SEQ = 2048
NB = 4
NTOK = NB * SEQ
BLK = 512
NBLK = NTOK // BLK
EPS = 1e-6
TWO_PI = 2.0 * math.pi
EPOCH = 24000
DEBUG = False


class Prog:
    ENGS = ("pe", "act", "dve", "pool", "sp")
    COMPUTE = ("pe", "act", "dve", "pool")

    def __init__(self, nc):
        self.nc = nc
        self.ops = {e: [] for e in self.ENGS}
        self.count = {e: 0 for e in self.COMPUTE}
        self.dma_count = {}
        self.last_write = {}
        self.readers = {}
        self.waited = {e: {} for e in self.ENGS}

    def _deps(self, eng, reads, writes, skip_key=None):
        writes = list(writes) + [t for t in reads if t.startswith("psb") and t not in writes]
        deps = []
        for t in reads:
            lw = self.last_write.get(t)
            if lw is not None:
                deps.append(lw)
        for t in writes:
            lw = self.last_write.get(t)
            if lw is not None:
                deps.append(lw)
            deps.extend(self.readers.get(t, ()))
        out = {}
        for key, val in deps:
            if key == eng and eng == "pe":
                continue
            if key == skip_key:
                continue
            if self.waited[eng].get(key, 0) >= val:
                continue
            if out.get(key, 0) < val:
                out[key] = val
        for key, val in out.items():
            self.waited[eng][key] = val
        return list(out.items())

    def _commit(self, sig, reads, writes):
        writes = list(writes) + [t for t in reads if t.startswith("psb") and t not in writes]
        for t in writes:
            self.last_write[t] = sig
            self.readers[t] = []
        for t in reads:
            if t not in writes:
                self.readers.setdefault(t, []).append(sig)

    frozen = False

    def op(self, eng, fn, reads=(), writes=()):
        if self.frozen:
            return
        waits = self._deps(eng, reads, writes)
        self.count[eng] += 1
        sig = (eng, self.count[eng])
        self._commit(sig, reads, writes)
        self.ops[eng].append((fn, waits, sig, 1))

    def dma(self, eng, sem, fn, reads=(), writes=(), final=None):
        if self.frozen:
            return
        waits = self._deps(eng, reads, writes, skip_key=(sem if final is not None else None))
        self.dma_count[sem] = self.dma_count.get(sem, 0) + 16
        sig = (sem, self.dma_count[sem] if final is None else final)
        self._commit(sig, reads, writes)
        self.ops[eng].append((fn, waits, (sem, self.dma_count[sem]), 16))

    def barrier(self):
        if self.frozen:
            return
        sigs = [(e, c) for e, c in self.count.items() if c > 0]
        sigs += [(s, c) for s, c in self.dma_count.items()]
        for e in self.ENGS:
            waits = []
            for key, val in sigs:
                if key == e and e == "pe":
                    continue
                if self.waited[e].get(key, 0) >= val:
                    continue
                self.waited[e][key] = val
                waits.append((key, val))
            if waits:
                self.ops[e].append((None, waits, None, 0))

    def emit(self):
        nc = self.nc
        with contextlib.ExitStack() as st:
            sems = {}

            def get(key, val):
                if key in self.COMPUTE:
                    k = (val - 1) // EPOCH
                    loc = val - k * EPOCH
                    name = f"s_{key}_{k}"
                else:
                    name, loc = f"d_{key}", val
                if name not in sems:
                    sems[name] = st.enter_context(nc.semaphore(name))
                return sems[name], loc

            for e in self.ENGS:
                for fn, waits, sig, inc in self.ops[e]:
                    for key, val in waits:
                        get(key, val)
                    if sig is not None:
                        get(*sig)
            block = st.enter_context(nc.Block())

            def run(e):
                def body(engine):
                    for fn, waits, sig, inc in self.ops[e]:
                        for key, val in waits:
                            s, loc = get(key, val)
                            engine.wait_ge(s, loc)
                        if fn is None:
                            continue
                        ins = fn(engine)
                        s, _ = get(*sig)
                        ins.then_inc(s, inc)
                return body

            block.tensor(run("pe"))
            block.scalar(run("act"))
            block.vector(run("dve"))
            block.gpsimd(run("pool"))
            block.sync(run("sp"))


def _consts():
    r = np.arange(128)
    j4 = r // 32
    g2_c = (r // 16) % 2
    g2_s = r // 64
    maskY = (g2_s[:, None] == g2_c[None, :]).astype(np.float32)
    maskX = np.ascontiguousarray(maskY.T)
    causal = (j4[None, :] >= j4[:, None]).astype(np.float32)
    kY = np.broadcast_to((j4 + 1).astype(np.float32)[None, :], (128, 128)).copy()
    kX = (3 - j4).astype(np.float32).reshape(128, 1)
    iota = np.broadcast_to((np.arange(130) - 1).astype(np.float32)[None, :], (128, 130)).copy()
    ident = np.eye(128, dtype=np.float32)
    ones = np.ones((128, 128), np.float32)
    return dict(maskY=maskY, maskX=maskX, causal=causal, kY=kY, kX=kX, iota=iota,
                ident=ident, ones=ones)


def _shared_inputs(inp):
    f = np.float32
    A = lambda a: np.ascontiguousarray(a, dtype=f)
    d = {}
    d["w_ada"] = A(inp["w_ada"][0])
    d["b_adaT"] = A(inp["b_ada"][0].reshape(48, 128).T)
    d["n1T"] = A(inp["norm1_g"][0].reshape(8, 128).T)
    d["n2T"] = A(inp["norm2_g"][0].reshape(8, 128).T)
    d["fg_row"] = A(np.broadcast_to(inp["final_g"][None, :], (128, 1024)))
    d["w_in"] = A(inp["w_in"][0])
    d["w_glu"] = A(inp["w_glu"][0])
    d["b_gluT"] = A(inp["b_glu"][0].reshape(4, 128).T)
    d["conv_wT"] = A(inp["conv_w"][0].reshape(3, 4, 128).transpose(2, 1, 0).reshape(128, 12))
    dsk = inp["d_skip"][0].reshape(16, 2, 16).transpose(1, 2, 0).reshape(32, 16)
    d["dX"] = A(np.tile(dsk, (4, 1)))
    d["w_ps"] = A(inp["w_proj_ssm"][0])
    d["w_pc"] = A(inp["w_proj_conv"][0])
    d["w_out"] = A(inp["w_out"][0])
    d["w_ff1"] = A(inp["w_ff1"][0])
    d["w_ff2"] = A(inp["w_ff2"][0])
    lre = inp["lam_re"][0].reshape(16, 2, 64)
    lim = inp["lam_im"][0].reshape(16, 2, 64)
    ldt = inp["log_dt"][0].reshape(16, 2)
    d["lamreY"] = A(lre.transpose(1, 2, 0).reshape(128, 16))
    d["lamimY"] = A(lim.transpose(1, 2, 0).reshape(128, 16))
    d["logdtY"] = A(np.repeat(ldt.T[:, None, :], 64, axis=1).reshape(128, 16))

    def expY(a_p_gp_g2_h):
        t = a_p_gp_g2_h[None, :, :, None, :, :]
        t = np.broadcast_to(t, (2, 64, 16, 4, 2, 16))
        return A(t.reshape(128, 16 * 128))
    d["cYre"] = expY(inp["c_re"][0].reshape(16, 2, 16, 64).transpose(3, 0, 1, 2))
    d["cYim"] = expY(inp["c_im"][0].reshape(16, 2, 16, 64).transpose(3, 0, 1, 2))
    d["bYre"] = expY(inp["b_re"][0].reshape(16, 2, 64, 16).transpose(2, 0, 1, 3))
    d["bYim"] = expY(inp["b_im"][0].reshape(16, 2, 64, 16).transpose(2, 0, 1, 3))
    d["lamreX"] = A(np.broadcast_to(inp["lam_re"][0].reshape(1, 2048), (128, 2048)))
    d["lamimX"] = A(np.broadcast_to(inp["lam_im"][0].reshape(1, 2048), (128, 2048)))
    d["logdtX"] = A(np.broadcast_to(np.repeat(ldt, 64, axis=1).reshape(1, 2048), (128, 2048)))

    def expX(a_h_gp_g2_p):
        t = a_h_gp_g2_p[None, None, :, :, :, :]
        t = np.broadcast_to(t, (4, 2, 16, 16, 2, 64))
        return A(t.reshape(128, 2048))
    d["bXre"] = expX(inp["b_re"][0].reshape(16, 2, 64, 16).transpose(3, 0, 1, 2))
    d["bXim"] = expX(inp["b_im"][0].reshape(16, 2, 64, 16).transpose(3, 0, 1, 2))
    d.update(_consts())
    return d


IN_SHAPES = dict(
    x=[NTOK, D], cT=[128, 32], w_ada=[1024, 6144], b_adaT=[128, 48], n1T=[128, 8], n2T=[128, 8],
    fg_row=[128, 1024], w_in=[1024, 4096], w_glu=[512, 512], b_gluT=[128, 4], conv_wT=[128, 12],
    dX=[128, 16], w_ps=[512, 1024], w_pc=[512, 1024], w_out=[1024, 1024], w_ff1=[1024, 4096],
    w_ff2=[4096, 1024], lamreY=[128, 16], lamimY=[128, 16], logdtY=[128, 16],
    cYre=[128, 2048], cYim=[128, 2048], bYre=[128, 2048], bYim=[128, 2048],
    lamreX=[128, 2048], lamimX=[128, 2048], logdtX=[128, 2048], bXre=[128, 2048], bXim=[128, 2048],
    maskY=[128, 128], maskX=[128, 128], causal=[128, 128], kY=[128, 128], kX=[128, 1],
    iota=[128, 130], ident=[128, 128], ones=[128, 128],
)


def build_program(nblk_a=NBLK, nblk_b=NBLK, taps=(), stop_at=None):
    nc = bass.Bass("TRN2", target_bir_lowering=False)
    dr = {k: nc.dram_tensor(k, s, F32, kind="ExternalInput").ap() for k, s in IN_SHAPES.items()}
    out_d = nc.dram_tensor("out", [NTOK, D], F32, kind="ExternalOutput").ap()
    x1_d = nc.dram_tensor("x1s", [NTOK, D], F32, kind="Internal").ap()
    tap_d = {}
    for name, shape in taps:
        tap_d[name] = nc.dram_tensor("tap_" + name, shape, F32, kind="ExternalOutput").ap()

    with contextlib.ExitStack() as st:
        ARENA = 52800
        arena = st.enter_context(nc.sbuf_tensor("arena", [128, ARENA], F32))
        psb = [st.enter_context(nc.psum_tensor(f"psb{i}", [128, 512], F32)) for i in range(8)]
        P = Prog(nc)
        pos = [0]
        uniq = [0]

        def alloc(n, dtype=F32):
            nf = n if dtype == F32 else (n + 1) // 2
            assert pos[0] + nf <= ARENA, f"SBUF arena overflow {pos[0]}+{nf}"
            v = arena[:, pos[0]:pos[0] + nf]
            pos[0] += nf
            if dtype != F32:
                v = v.bitcast(dtype)
            return v

        def r3(ap, a):
            return ap.rearrange("p (a b) -> p a b", a=a)

        def pbf(i):
            return psb[i][:, :].bitcast(BF16)

        def dve(fn, *a, reads, writes, **kw):
            P.op("dve", lambda e: getattr(e, fn)(*a, **kw), reads=reads, writes=writes)

        def act(out, in_, func, reads, writes, **kw):
            P.op("act", lambda e: e.activation(out=out, in_=in_, func=func, **kw), reads=reads, writes=writes)

        def mmgroup(items, reads, writes):
            def fn(e):
                ins = None
                for (o, l, r, s0, s1, tp) in items:
                    if tp is None:
                        ins = e.matmul(o, l, r, start=s0, stop=s1)
                    else:
                        ins = e.matmul(o, l, r, start=s0, stop=s1, tile_position=tp)
                return ins
            P.op("pe", fn, reads=reads, writes=writes)

        def dma(eng, sem, out, in_, reads=(), writes=(), final=None):
            P.dma(eng, sem, lambda e: e.dma_start(out=out, in_=in_), reads=reads, writes=writes, final=final)

        def load_group(eng, sem, items):
            base = P.dma_count.get(sem, 0)
            final = base + 16 * len(items)
            for (o, i, tag) in items:
                P.dma(eng, sem, lambda e, o=o, i=i: e.dma_start(out=o, in_=i), writes=[tag], final=final)

        def checkpoint(name):
            if stop_at == name:
                P.barrier()
                P.frozen = True

        def tap(name, src, tag):
            if name in tap_d:
                if src.dtype == BF16:
                    src = src.bitcast(F32)
                uniq[0] += 1
                P.dma("sp", f"tap{uniq[0]}", lambda e, src=src, name=name: e.dma_start(out=tap_d[name], in_=src),
                      reads=[tag], writes=["tapout_" + name])

        ident32 = alloc(128); ones32 = alloc(128)
        ident16 = alloc(128, BF16)
        cT = alloc(32); b_adaT = alloc(48); n1T = alloc(8); n2T = alloc(8)
        b_gluT = alloc(4); conv_wT = alloc(12); dX = alloc(16)
        modT = alloc(192)
        gs1T = alloc(32); gs2T = alloc(32)
        bias_in = alloc(128)
        bias_ff1 = alloc(128)
        sgc = alloc(32); scb = alloc(32, BF16)
        sh1b = alloc(32, BF16); sh2b = alloc(32, BF16)
        persistB_end = pos[0]
        rT = alloc(16)
        cosT = alloc(16 * 130); sinT = alloc(16 * 130)
        carry_re = alloc(16); carry_im = alloc(16)
        vcarry = alloc(8)
        ssA = alloc(4); rstdA = alloc(4)
        Wssm = alloc(16 * 5 * 128, BF16)
        Wssm4 = Wssm.rearrange("p (g w m) -> p g w m", g=16, w=5)
        modT3 = r3(modT, 48); gs1T3 = r3(gs1T, 8); gs2T3 = r3(gs2T, 8)
        bias_in3 = r3(bias_in, 32); bias_ff13 = r3(bias_ff1, 32)
        cosT3 = r3(cosT, 16); sinT3 = r3(sinT, 16)
        persist_end = pos[0]

        load_group("sp", "ld_small", [
            (ident32, dr["ident"], "ident32"), (ones32, dr["ones"], "ones32"),
            (cT, dr["cT"], "cT"), (b_adaT, dr["b_adaT"], "b_adaT"), (n1T, dr["n1T"], "n1T"),
            (n2T, dr["n2T"], "n2T"), (b_gluT, dr["b_gluT"], "b_gluT"),
            (conv_wT, dr["conv_wT"], "conv_wT"), (dX, dr["dX"], "dX")])
        dve("tensor_copy", ident16, ident32, reads=["ident32"], writes=["ident16"])

        act(sgc, cT, AF.Sigmoid, reads=["cT"], writes=["sgc"])
        dve("tensor_tensor", scb, cT, sgc, ALU.mult, reads=["cT", "sgc"], writes=["scb"])
        scb3 = r3(scb, 8)
        scf = alloc(32)
        dve("tensor_tensor", scf, cT, sgc, ALU.mult, reads=["cT", "sgc"], writes=["scf"])
        scf3 = r3(scf, 8)
        wada_ring = [alloc(8 * 128) for _ in range(2)]
        for j in range(48):
            slot = j % 2
            wt = r3(wada_ring[slot], 8)
            src = dr["w_ada"][:, j * 128:(j + 1) * 128].rearrange("(k p) n -> p k n", p=128)
            dma("act", f"wada{slot}", wt, src, writes=[f"wada{slot}"])
            mmgroup([(psb[0][:, j * 4:(j + 1) * 4], wt[:, k, :], scf3[:, k, :], k == 0, k == 7, None)
                     for k in range(8)], reads=[f"wada{slot}", "scf"], writes=["psb0"])
        dve("tensor_tensor", modT3, r3(psb[0][:, 0:192], 48),
            b_adaT.unsqueeze(2).to_broadcast([128, 48, 4]), ALU.add,
            reads=["psb0", "b_adaT"], writes=["modT"])
        dve("scalar_tensor_tensor", gs1T3, modT3[:, 8:16, :], 1.0, n1T.unsqueeze(2).to_broadcast([128, 8, 4]),
            ALU.add, ALU.mult, reads=["modT", "n1T"], writes=["gs1T"])
        dve("scalar_tensor_tensor", gs2T3, modT3[:, 32:40, :], 1.0, n2T.unsqueeze(2).to_broadcast([128, 8, 4]),
            ALU.add, ALU.mult, reads=["modT", "n2T"], writes=["gs2T"])
        dve("tensor_copy", r3(sh1b, 8), modT3[:, 0:8, :], reads=["modT"], writes=["sh1b"])
        dve("tensor_copy", r3(sh2b, 8), modT3[:, 24:32, :], reads=["modT"], writes=["sh2b"])
        tap("modT", modT, "modT")
        checkpoint("mod")

        BIG = 2048
        TMPN = 16 * 130
        T_i = alloc(TMPN).bitcast(I32); T_f = alloc(TMPN); T_red = alloc(TMPN)
        T_c1 = alloc(TMPN); T_c2 = alloc(TMPN)
        setup_scratch = pos[0]

        def big():
            return alloc(BIG)

        def b3(ap):
            return r3(ap, 16)

        def range_reduce(dst, src, shift, n3=None, tagd=None, tags=None):
            n = dst.shape[-1]
            ti = T_i[:, 0:n]; tf = T_f[:, 0:n]
            dve("tensor_scalar", tf, src, 1.0 / TWO_PI, shift / TWO_PI, ALU.mult, ALU.add,
                reads=[tags], writes=["rr_f"])
            dve("tensor_copy", ti, tf, reads=["rr_f"], writes=["rr_i"])
            dve("tensor_copy", tf, ti, reads=["rr_i"], writes=["rr_f"])
            dve("scalar_tensor_tensor", tf, tf, -TWO_PI, src, ALU.mult, ALU.add,
                reads=["rr_f", tags], writes=["rr_f"])
            dve("tensor_scalar", dst, tf, shift, None, ALU.add, reads=["rr_f"], writes=[tagd])
            dve("tensor_scalar", dst, dst, -math.pi, math.pi, ALU.max, ALU.min, reads=[tagd], writes=[tagd])

        def sincos(sin_dst, cos_dst, ang, tag_ang, tag_s, tag_c, n):
            red = T_red[:, 0:n]
            range_reduce(red, ang, 0.0, tagd="red", tags=tag_ang)
            act(sin_dst, red, AF.Sin, reads=["red"], writes=[tag_s])
            range_reduce(red, ang, math.pi / 2, tagd="red", tags=tag_ang)
            act(cos_dst, red, AF.Sin, reads=["red"], writes=[tag_c])

        def cmul(ore, oim, are, aim, bre, bim, tags_a, tags_b, tag_o, n, conj_b=False):
            t1 = T_c1[:, 0:n]; t2 = T_c2[:, 0:n]
            shp = list(ore.shape)

            def v(ap):
                return ap if len(shp) == 2 else ap.rearrange("p (a b) -> p a b", a=shp[1])
            dve("tensor_tensor", v(t1), are, bre, ALU.mult, reads=tags_a + tags_b, writes=["cm1"])
            dve("tensor_tensor", v(t2), aim, bim, ALU.mult, reads=tags_a + tags_b, writes=["cm2"])
            dve("tensor_tensor", ore, v(t1), v(t2), ALU.add if conj_b else ALU.subtract,
                reads=["cm1", "cm2"], writes=[tag_o + "re"])
            dve("tensor_tensor", v(t1), are, bim, ALU.mult, reads=tags_a + tags_b, writes=["cm1"])
            dve("tensor_tensor", v(t2), aim, bre, ALU.mult, reads=tags_a + tags_b, writes=["cm2"])
            dve("tensor_tensor", oim, v(t2), v(t1), ALU.subtract if conj_b else ALU.add,
                reads=["cm1", "cm2"], writes=[tag_o + "im"])

        lamreY = alloc(16); lamimY = alloc(16); logdtY = alloc(16)
        cYre = big(); cYim = big(); bYre = big(); bYim = big()
        maskY = alloc(128); causal = alloc(128); kY = alloc(128); iota = alloc(130)
        load_group("sp", "ld_ssmY", [
            (lamreY, dr["lamreY"], "lamreY"), (lamimY, dr["lamimY"], "lamimY"), (logdtY, dr["logdtY"], "logdtY"),
            (cYre, dr["cYre"], "cYre"), (cYim, dr["cYim"], "cYim"), (bYre, dr["bYre"], "bYre"),
            (bYim, dr["bYim"], "bYim"), (maskY, dr["maskY"], "maskY"), (causal, dr["causal"], "causal"),
            (kY, dr["kY"], "kY"), (iota, dr["iota"], "iota")])
        dtY = alloc(16); lrd = alloc(16); lid = alloc(16)
        act(dtY, logdtY, AF.Exp, reads=["logdtY"], writes=["dtY"])
        dve("tensor_tensor", lrd, lamreY, dtY, ALU.mult, reads=["lamreY", "dtY"], writes=["lrd"])
        dve("tensor_tensor", lid, lamimY, dtY, ALU.mult, reads=["lamimY", "dtY"], writes=["lid"])
        z = alloc(16); acc = alloc(16)
        dve("tensor_scalar", z, lrd, 4.0, None, ALU.mult, reads=["lrd"], writes=["z"])
        dve("tensor_scalar", acc, z, 1.0 / 5040.0, 1.0 / 720.0, ALU.mult, ALU.add, reads=["z"], writes=["acc"])
        for coef in (1.0 / 120.0, 1.0 / 24.0, 1.0 / 6.0, 0.5, 1.0, 1.0):
            dve("tensor_tensor", acc, acc, z, ALU.mult, reads=["acc", "z"], writes=["acc"])
            dve("tensor_scalar", acc, acc, coef, None, ALU.add, reads=["acc"], writes=["acc"])
        dve("tensor_copy", rT, acc, reads=["acc"], writes=["rT"])
        m1 = alloc(16); s1 = alloc(16); c1 = alloc(16)
        act(m1, lrd, AF.Exp, reads=["lrd"], writes=["m1"])
        sincos(s1, c1, lid, "lid", "s1", "c1", 16)
        a1re = alloc(16); a1im = alloc(16)
        dve("tensor_tensor", a1re, m1, c1, ALU.mult, reads=["m1", "c1"], writes=["a1re"])
        dve("tensor_tensor", a1im, m1, s1, ALU.mult, reads=["m1", "s1"], writes=["a1im"])
        dve("tensor_scalar", a1re, a1re, -1.0, None, ALU.add, reads=["a1re"], writes=["a1re"])
        den = alloc(16); t16 = alloc(16); qre = alloc(16); qim = alloc(16)
        dve("tensor_tensor", den, lamreY, lamreY, ALU.mult, reads=["lamreY"], writes=["den"])
        dve("tensor_tensor", t16, lamimY, lamimY, ALU.mult, reads=["lamimY"], writes=["t16"])
        dve("tensor_tensor", den, den, t16, ALU.add, reads=["den", "t16"], writes=["den"])
        dve("reciprocal", den, den, reads=["den"], writes=["den"])
        cmul(qre, qim, a1re, a1im, lamreY, lamimY, ["a1re", "a1im"], ["lamreY", "lamimY"], "q", 16, conj_b=True)
        dve("tensor_tensor", qre, qre, den, ALU.mult, reads=["qre", "den"], writes=["qre"])
        dve("tensor_tensor", qim, qim, den, ALU.mult, reads=["qim", "den"], writes=["qim"])
        bbYre = big(); bbYim = big()
        bc16 = lambda ap: ap.unsqueeze(2).to_broadcast([128, 16, 128])
        cmul(b3(bbYre), b3(bbYim), bc16(qre), bc16(qim), b3(bYre), b3(bYim),
             ["qre", "qim"], ["bYre", "bYim"], "bbY", BIG)
        argm = big(); ang = big()
        kYb = kY.unsqueeze(1).to_broadcast([128, 16, 128])
        dve("tensor_tensor", b3(argm), bc16(lrd), kYb, ALU.mult, reads=["lrd", "kY"], writes=["argm"])
        dve("tensor_tensor", b3(ang), bc16(lid), kYb, ALU.mult, reads=["lid", "kY"], writes=["ang"])
        sinA = big(); cosA = big()
        sincos(sinA, cosA, ang, "ang", "sinA", "cosA", BIG)
        mag = big()
        act(mag, argm, AF.Exp, reads=["argm"], writes=["mag"])
        Are = big(); Aim = big()
        dve("tensor_tensor", Are, mag, cosA, ALU.mult, reads=["mag", "cosA"], writes=["Are"])
        dve("tensor_tensor", Aim, mag, sinA, ALU.mult, reads=["mag", "sinA"], writes=["Aim"])
        Rre = bYre; Rim = bYim
        cmul(Rre, Rim, cYre, cYim, Are, Aim, ["cYre", "cYim", "bbYre", "bbYim"], ["Are", "Aim"], "R", BIG)
        mYb = maskY.unsqueeze(1).to_broadcast([128, 16, 128])
        dve("tensor_tensor", b3(Rre), b3(Rre), mYb, ALU.mult, reads=["Rre", "maskY"], writes=["Rre"])
        dve("scalar_tensor_tensor", b3(Rim), b3(Rim), -1.0, mYb, ALU.mult, ALU.mult,
            reads=["Rim", "maskY"], writes=["Rim"])
        dve("tensor_copy", Wssm4[:, :, 2, :], b3(Rre), reads=["Rre"], writes=["W2re"])
        dve("tensor_copy", Wssm4[:, :, 3, :], b3(Rim), reads=["Rim"], writes=["W2im"])
        act(mag, argm, AF.Exp, reads=["argm", "Are", "Aim"], writes=["mag"], scale=-1.0)
        dve("tensor_tensor", Are, mag, cosA, ALU.mult, reads=["mag", "cosA", "Rre", "Rim"], writes=["Are"])
        dve("scalar_tensor_tensor", Aim, mag, -1.0, sinA, ALU.mult, ALU.mult,
            reads=["mag", "sinA", "Rre", "Rim"], writes=["Aim"])
        Lre = cYre; Lim = cYim
        cmul(Lre, Lim, Are, Aim, bbYre, bbYim, ["Are", "Aim", "Rre", "Rim"], ["bbYre", "bbYim"], "L", BIG)
        dve("tensor_tensor", b3(Lre), b3(Lre), mYb, ALU.mult, reads=["Lre", "maskY"], writes=["Lre"])
        dve("tensor_tensor", b3(Lim), b3(Lim), mYb, ALU.mult, reads=["Lim", "maskY"], writes=["Lim"])
        kt = alloc(128)
        for gp in range(16):
            bank = psb[1 + gp % 2]
            mmgroup([(bank[:, 0:128], b3(Lre)[:, gp, :], b3(Rre)[:, gp, :], True, False, None),
                     (bank[:, 0:128], b3(Lim)[:, gp, :], b3(Rim)[:, gp, :], False, True, None)],
                    reads=["Lre", "Lim", "Rre", "Rim"], writes=[f"psb{1 + gp % 2}"])
            dve("tensor_tensor", kt, bank[:, 0:128], causal, ALU.mult,
                reads=[f"psb{1 + gp % 2}", "causal"], writes=["kt"])
            dve("scalar_tensor_tensor", Wssm4[:, gp, 4, :], ident32, dX[:, gp:gp + 1], kt, ALU.mult, ALU.add,
                reads=["ident32", "dX", "kt"], writes=["W1"])
        th4 = alloc(16); th4r = alloc(16)
        dve("tensor_scalar", th4, lid, 4.0, None, ALU.mult, reads=["lid"], writes=["th4"])
        range_reduce(th4r, th4, 0.0, tagd="th4r", tags="th4")
        angT = alloc(16 * 130)
        dve("tensor_tensor", r3(angT, 16), th4r.unsqueeze(2).to_broadcast([128, 16, 130]),
            iota.unsqueeze(1).to_broadcast([128, 16, 130]), ALU.mult, reads=["th4r", "iota"], writes=["angT"])
        def sincos_tab(dst, shift, tagd):
            red = T_red; tmp_i = T_i; tf = T_f
            dve("tensor_scalar", tf, angT, 1.0 / TWO_PI, shift / TWO_PI, ALU.mult, ALU.add,
                reads=["angT"], writes=["rr_f"])
            dve("tensor_copy", tmp_i, tf, reads=["rr_f"], writes=["rr_i"])
            dve("tensor_copy", tf, tmp_i, reads=["rr_i"], writes=["rr_f"])
            dve("scalar_tensor_tensor", tf, tf, -TWO_PI, angT, ALU.mult, ALU.add,
                reads=["rr_f", "angT"], writes=["rr_f"])
            dve("tensor_scalar", red, tf, shift, None, ALU.add, reads=["rr_f"], writes=["red"])
            dve("tensor_scalar", red, red, -math.pi, math.pi, ALU.max, ALU.min, reads=["red"], writes=["red"])
            act(dst, red, AF.Sin, reads=["red"], writes=[tagd])
        sincos_tab(sinT, 0.0, "sinT")
        sincos_tab(cosT, math.pi / 2, "cosT")
        tap("W2re", Rre, "Rre"); tap("cosT", cosT, "cosT"); tap("sinT", sinT, "sinT"); tap("rT", rT, "rT")
        checkpoint("ssmY")
        P.barrier()
        pos[0] = setup_scratch

        lamreX = big(); lamimX = big(); logdtX = big(); bXre = big(); bXim = big()
        maskX = alloc(128); kX = alloc(1)
        load_group("sp", "ld_ssmX", [
            (lamreX, dr["lamreX"], "lamreX"), (lamimX, dr["lamimX"], "lamimX"), (logdtX, dr["logdtX"], "logdtX"),
            (bXre, dr["bXre"], "bXre"), (bXim, dr["bXim"], "bXim"), (maskX, dr["maskX"], "maskX"),
            (kX, dr["kX"], "kX")])
        dtX = big(); lrdX = big(); lidX = big()
        act(dtX, logdtX, AF.Exp, reads=["logdtX"], writes=["dtX"])
        dve("tensor_tensor", lrdX, lamreX, dtX, ALU.mult, reads=["lamreX", "dtX"], writes=["lrdX"])
        dve("tensor_tensor", lidX, lamimX, dtX, ALU.mult, reads=["lamimX", "dtX"], writes=["lidX"])
        m1X = dtX
        act(m1X, lrdX, AF.Exp, reads=["lrdX", "lidX"], writes=["m1X"])
        s1X = big(); c1X = big()
        sincos(s1X, c1X, lidX, "lidX", "s1X", "c1X", BIG)
        a1reX = big(); a1imX = big()
        dve("tensor_tensor", a1reX, m1X, c1X, ALU.mult, reads=["m1X", "c1X"], writes=["a1reX"])
        dve("tensor_tensor", a1imX, m1X, s1X, ALU.mult, reads=["m1X", "s1X"], writes=["a1imX"])
        dve("tensor_scalar", a1reX, a1reX, -1.0, None, ALU.add, reads=["a1reX"], writes=["a1reX"])
        denX = s1X; tX = c1X
        dve("tensor_tensor", denX, lamreX, lamreX, ALU.mult, reads=["lamreX", "a1imX", "a1reX"], writes=["denX"])
        dve("tensor_tensor", tX, lamimX, lamimX, ALU.mult, reads=["lamimX", "a1imX", "a1reX"], writes=["tX"])
        dve("tensor_tensor", denX, denX, tX, ALU.add, reads=["denX", "tX"], writes=["denX"])
        dve("reciprocal", denX, denX, reads=["denX"], writes=["denX"])
        qreX = big(); qimX = big()
        cmul(qreX, qimX, a1reX, a1imX, lamreX, lamimX, ["a1reX", "a1imX"], ["lamreX", "lamimX"], "qX", BIG, conj_b=True)
        dve("tensor_tensor", qreX, qreX, denX, ALU.mult, reads=["qXre", "denX"], writes=["qXre"])
        dve("tensor_tensor", qimX, qimX, denX, ALU.mult, reads=["qXim", "denX"], writes=["qXim"])
        bbXre = a1reX; bbXim = a1imX
        cmul(bbXre, bbXim, qreX, qimX, bXre, bXim, ["qXre", "qXim", "denX"], ["bXre", "bXim"], "bbX", BIG)
        angX = qreX
        dve("tensor_scalar", angX, lidX, kX[:, 0:1], None, ALU.mult, reads=["lidX", "kX", "bbXre", "bbXim"], writes=["angX"])
        sinX = bXre; cosX = bXim
        sincos(sinX, cosX, angX, "angX", "sinX", "cosX", BIG)
        magX = qimX
        act(magX, lrdX, AF.Exp, reads=["lrdX", "bbXre", "bbXim"], writes=["magX"], scale=kX[:, 0:1])
        AXre = lamreX; AXim = lamimX
        dve("tensor_tensor", AXre, magX, cosX, ALU.mult, reads=["magX", "cosX", "denX", "qXre"], writes=["AXre"])
        dve("tensor_tensor", AXim, magX, sinX, ALU.mult, reads=["magX", "sinX", "denX", "qXre"], writes=["AXim"])
        W3re = lrdX; W3im = lidX
        cmul(W3re, W3im, AXre, AXim, bbXre, bbXim, ["AXre", "AXim", "angX", "magX"], ["bbXre", "bbXim"], "W3", BIG)
        mXb = maskX.unsqueeze(1).to_broadcast([128, 16, 128])
        dve("tensor_tensor", Wssm4[:, :, 0, :], b3(W3re), mXb, ALU.mult, reads=["W3re", "maskX"], writes=["W3re_b"])
        dve("tensor_tensor", Wssm4[:, :, 1, :], b3(W3im), mXb, ALU.mult, reads=["W3im", "maskX"], writes=["W3im_b"])
        tap("W3re", W3re, "W3re")
        checkpoint("ssmX")
        P.barrier()
        pos[0] = persist_end

        Win_lo = alloc(8 * 2048, BF16)
        Wglu = alloc(4 * 512, BF16)
        Wps = alloc(4 * 1024, BF16); Wpc = alloc(4 * 1024, BF16)
        Win3 = r3(Win_lo, 8); Wglu3 = r3(Wglu, 4); Wps3 = r3(Wps, 4); Wpc3 = r3(Wpc, 4)
        ring = [alloc(8 * 512, BF16) for _ in range(3)]
        for q in range(4):
            src = dr["w_in"][:, q * 512:(q + 1) * 512].rearrange("(k p) n -> p k n", p=128)
            dst = Win3[:, :, q * 512:(q + 1) * 512]
            dma("pool", f"ld_win{q}", dst, src, writes=[f"Win_q{q}"])
        load_group("pool", "ld_wA", [
            (Wglu3, dr["w_glu"].rearrange("(k p) n -> p k n", p=128), "Wglu"),
            (Wps3, dr["w_ps"].rearrange("(k p) n -> p k n", p=128), "Wps"),
            (Wpc3, dr["w_pc"].rearrange("(k p) n -> p k n", p=128), "Wpc")])
        checkpoint("wloadA")
        ring_use = [0]

        def ring_load(src_ap):
            slot = ring_use[0] % 3
            ring_use[0] += 1
            dst = r3(ring[slot], 8)
            dma("pool", f"ring{slot}", dst, src_ap, writes=[f"ring{slot}"])
            return dst, f"ring{slot}"

        def gate_src(which, half):
            c0 = 2048 + which * 1024 + half * 512
            return dr["w_in"][:, c0:c0 + 512].rearrange("(k p) n -> p k n", p=128)

        def wout_src(half):
            return dr["w_out"][:, half * 512:(half + 1) * 512].rearrange("(k p) n -> p k n", p=128)

        sh1b3 = r3(sh1b, 8)
        for ch in range(16):
            mmgroup([(psb[0][:, ch * 4:(ch + 1) * 4], Win3[:, k, ch * 128:(ch + 1) * 128], sh1b3[:, k, :],
                      k == 0, k == 7, None) for k in range(8)],
                    reads=[f"Win_q{ch // 4}", "sh1b"], writes=["psb0"])
        for which in range(2):
            for half in range(2):
                gt, gtag = ring_load(gate_src(which, half))
                for cl in range(4):
                    ch = 16 + which * 8 + half * 4 + cl
                    mmgroup([(psb[0][:, ch * 4:(ch + 1) * 4], gt[:, k, cl * 128:(cl + 1) * 128], sh1b3[:, k, :],
                              k == 0, k == 7, None) for k in range(8)],
                            reads=[gtag, "sh1b"], writes=["psb0"])
        dve("tensor_copy", bias_in, psb[0][:, 0:128], reads=["psb0"], writes=["bias_in"])
        tap("bias_in", bias_in, "bias_in")
        checkpoint("bias_in")

        xr = [alloc(1024) for _ in range(2)]
        hn = alloc(1024, BF16); junk = alloc(1024, BF16)
        hT = alloc(8 * 512, BF16); hT3 = r3(hT, 8)
        uT = alloc(4 * 512, BF16); uT3 = r3(uT, 4)
        U4 = alloc(16 * 128, BF16); U43 = r3(U4, 16)
        Ebre = alloc(4 * 129); Ebim = alloc(4 * 129); Xre = alloc(4 * 129); Xim = alloc(4 * 129)
        Ebre3 = r3(Ebre, 4); Ebim3 = r3(Ebim, 4); Xre3 = r3(Xre, 4); Xim3 = r3(Xim, 4)
        tr1 = alloc(512); tr2 = alloc(512)
        tr13 = r3(tr1, 4); tr23 = r3(tr2, 4)
        Sre = alloc(4 * 128, BF16); Sim = alloc(4 * 128, BF16)
        Sre3 = r3(Sre, 4); Sim3 = r3(Sim, 4)
        ys = alloc(4 * 512, BF16); ys3 = r3(ys, 4)
        yglu = alloc(4 * 512, BF16); yglu3 = r3(yglu, 4)
        yc = alloc(4 * 512, BF16); yc3 = r3(yc, 4)
        cbS = alloc(512); ccS = alloc(512); vbuf = alloc(514); cacc = alloc(512)
        sgA = alloc(512, BF16); sgB = alloc(512, BF16)
        mT = alloc(8 * 512, BF16); mT3 = r3(mT, 8)
        g1row = alloc(1024)
        x1o = [alloc(1024) for _ in range(2)]
        diagt = alloc(128)
        print("phase A arena use:", pos[0], "of", ARENA)

        def stats_rstd(src_d, blk, ss, rstd, tagp):
            dve("memset", ss, 0.0, reads=[], writes=[tagp + "ss"])
            for s in range(4):
                slot = s % 2
                r0 = blk * BLK + s * 128
                dma("sp", f"xr{slot}", xr[slot], src_d[r0:r0 + 128, :],
                      writes=[f"xr{slot}"])
                act(junk, xr[slot], AF.Square, reads=[f"xr{slot}"], writes=["junk", tagp + "ss"],
                    accum_out=ss[:, s:s + 1])
            dve("tensor_scalar", rstd, ss, 1.0 / D, EPS, ALU.mult, ALU.add, reads=[tagp + "ss"], writes=[tagp + "rstd"])
            act(rstd, rstd, AF.Sqrt, reads=[tagp + "rstd"], writes=[tagp + "rstd"])
            dve("reciprocal", rstd, rstd, reads=[tagp + "rstd"], writes=[tagp + "rstd"])

        def row_from_col(dst_row, colT3, kidx0, b, tag_col, tag_row):
            for half in range(2):
                bank = psb[4 + half]
                for kk in range(4):
                    k = half * 4 + kk
                    dve("tensor_scalar", diagt, ident32, colT3[:, kidx0 + k, b:b + 1], None, ALU.mult,
                        reads=["ident32", tag_col], writes=["diagt"])
                    mmgroup([(bank[:, kk * 128:(kk + 1) * 128], ones32, diagt, True, True, None)],
                            reads=["ones32", "diagt"], writes=[f"psb{4 + half}"])
                dve("tensor_copy", dst_row[:, half * 512:(half + 1) * 512], bank[:, :],
                    reads=[f"psb{4 + half}"], writes=[tag_row])

        def norm_transpose_sub(src_d, blk, b, s, rstd, gsT3, tag_gs, dstT3, tag_dst, xr, hn):
            slot = s % 2
            r0 = blk * BLK + s * 128
            dma("sp", f"xr{slot}", xr[slot], src_d[r0:r0 + 128, :], writes=[f"xr{slot}"])
            dve("tensor_scalar", hn, xr[slot], rstd[:, s:s + 1], None, ALU.mult,
                reads=[f"xr{slot}", "Arstd", "Brstd"], writes=["hn"])
            bank = s % 2
            ptb = pbf(bank)

            def fn(e, ptb=ptb, hn=hn, ident16=ident16):
                ins = None
                for k in range(8):
                    ins = e.transpose(ptb[:, k * 128:(k + 1) * 128], hn[:, k * 128:(k + 1) * 128], ident16)
                return ins
            P.op("pe", fn, reads=["hn", "ident16"], writes=[f"psb{bank}"])
            for k in range(8):
                act(dstT3[:, k, s * 128:(s + 1) * 128], ptb[:, k * 128:(k + 1) * 128], AF.Copy,
                    reads=[f"psb{bank}", tag_gs], writes=[tag_dst], scale=gsT3[:, k, b:b + 1])

        def norm_transpose(src_d, blk, b, rstd, gsT3, tag_gs, dstT3, tag_dst):
            for s in range(4):
                norm_transpose_sub(src_d, blk, b, s, rstd, gsT3, tag_gs, dstT3, tag_dst, xr, hn)

        if nblk_a > 0:
            stats_rstd(dr["x"], 0, ssA, rstdA, "A")
        for blk in range(nblk_a):
            b = blk // 4
            qpos = blk % 4
            if qpos == 0:
                row_from_col(g1row, modT3, 16, b, "modT", "g1row")
                dve("memset", carry_re, 0.0, reads=[], writes=["carry_re"])
                dve("memset", carry_im, 0.0, reads=[], writes=["carry_im"])
                dve("memset", vcarry, 0.0, reads=[], writes=["vcarry"])
            if blk == 0:
                tap("ssA", ssA, "Ass"); tap("rstdA", rstdA, "Arstd")
            checkpoint("A_pre")
            norm_transpose(dr["x"], blk, b, rstdA, gs1T3, "gs1T", hT3, "hT")
            if blk == 0:
                tap("hn", hn, "hn"); tap("hT", hT, "hT")
            checkpoint("A_norm")
            if blk + 1 < nblk_a:
                stats_rstd(dr["x"], blk + 1, ssA, rstdA, "A")
            for fc in range(4):
                bank = 2 + fc % 2
                mmgroup([(psb[bank][:, :], Win3[:, k, fc * 128:(fc + 1) * 128], hT3[:, k, :], k == 0, k == 7, None)
                         for k in range(8)], reads=["Win_q0", "hT"], writes=[f"psb{bank}"])
                act(uT3[:, fc, :], psb[bank][:, :], AF.Identity, reads=[f"psb{bank}", "bias_in"], writes=["uT"],
                    bias=bias_in3[:, fc, b:b + 1])
            checkpoint("A_u")
            for fc in range(4):
                for gl in range(4):
                    for j4 in range(4):
                        o = U43[32 * j4:32 * j4 + 32, fc * 4 + gl, :]
                        i_ = uT3[32 * gl:32 * gl + 32, fc, j4:512:4]
                        if (gl + j4) % 2 == 0:
                            act(o, i_, AF.Copy, reads=["uT"], writes=["U4a"])
                        else:
                            dve("tensor_copy", o, i_, reads=["uT"], writes=["U4d"])
            if blk == 0:
                tap("U4", U4, "U4a")
            checkpoint("A_rel")
            for fc in range(4):
                gps = range(fc * 4, fc * 4 + 4)
                mmgroup([(psb[6][:, gl * 128:(gl + 1) * 128], Wssm4[:, gp, 0, :], U43[:, gp, :], True, True, None)
                         for gl, gp in enumerate(gps)], reads=["U4a", "U4d", "W3re_b"], writes=["psb6"])
                mmgroup([(psb[7][:, gl * 128:(gl + 1) * 128], Wssm4[:, gp, 1, :], U43[:, gp, :], True, True, None)
                         for gl, gp in enumerate(gps)], reads=["U4a", "U4d", "W3im_b"], writes=["psb7"])
                Er = r3(psb[6][:, :], 4); Ei = r3(psb[7][:, :], 4)
                cs = cosT3[:, fc * 4:fc * 4 + 4, 1:129]; sn = sinT3[:, fc * 4:fc * 4 + 4, 1:129]
                checkpoint("A_E")
                dve("tensor_tensor", tr13, Er, cs, ALU.mult, reads=["psb6", "cosT"], writes=["tr1"])
                dve("tensor_tensor", tr23, Ei, sn, ALU.mult, reads=["psb7", "sinT"], writes=["tr2"])
                dve("tensor_tensor", Ebre3[:, :, 1:129], tr13, tr23, ALU.add, reads=["tr1", "tr2"], writes=["Ebre"])
                dve("tensor_tensor", tr13, Ei, cs, ALU.mult, reads=["psb7", "cosT"], writes=["tr1"])
                dve("tensor_tensor", tr23, Er, sn, ALU.mult, reads=["psb6", "sinT"], writes=["tr2"])
                dve("tensor_tensor", Ebim3[:, :, 1:129], tr13, tr23, ALU.subtract, reads=["tr1", "tr2"], writes=["Ebim"])
                dve("tensor_copy", Ebre3[:, :, 0], carry_re[:, fc * 4:fc * 4 + 4], reads=["carry_re"], writes=["Ebre"])
                dve("tensor_copy", Ebim3[:, :, 0], carry_im[:, fc * 4:fc * 4 + 4], reads=["carry_im"], writes=["Ebim"])
                for gl, gp in enumerate(gps):
                    rb = rT[:, gp:gp + 1].to_broadcast([128, 129])
                    dve("tensor_tensor_scan", Xre3[:, gl, :], rb, Ebre3[:, gl, :], 0.0, ALU.mult, ALU.add,
                        reads=["rT", "Ebre"], writes=["Xre"])
                    dve("tensor_tensor_scan", Xim3[:, gl, :], rb, Ebim3[:, gl, :], 0.0, ALU.mult, ALU.add,
                        reads=["rT", "Ebim"], writes=["Xim"])
                checkpoint("A_scan")
                cs0 = cosT3[:, fc * 4:fc * 4 + 4, 0:128]; sn0 = sinT3[:, fc * 4:fc * 4 + 4, 0:128]
                dve("tensor_tensor", tr13, Xre3[:, :, 0:128], cs0, ALU.mult, reads=["Xre", "cosT"], writes=["tr1"])
                dve("tensor_tensor", tr23, Xim3[:, :, 0:128], sn0, ALU.mult, reads=["Xim", "sinT"], writes=["tr2"])
                dve("tensor_tensor", Sre3, tr13, tr23, ALU.subtract, reads=["tr1", "tr2"], writes=["Sre"])
                dve("tensor_tensor", tr13, Xre3[:, :, 0:128], sn0, ALU.mult, reads=["Xre", "sinT"], writes=["tr1"])
                dve("tensor_tensor", tr23, Xim3[:, :, 0:128], cs0, ALU.mult, reads=["Xim", "cosT"], writes=["tr2"])
                dve("tensor_tensor", Sim3, tr13, tr23, ALU.add, reads=["tr1", "tr2"], writes=["Sim"])
                c9 = cosT3[:, fc * 4:fc * 4 + 4, 129]; s9 = sinT3[:, fc * 4:fc * 4 + 4, 129]
                t4a = tr1[:, 0:4]; t4b = tr2[:, 0:4]
                dve("tensor_tensor", t4a, Xre3[:, :, 128], c9, ALU.mult, reads=["Xre", "cosT", "Sim"], writes=["tr1"])
                dve("tensor_tensor", t4b, Xim3[:, :, 128], s9, ALU.mult, reads=["Xim", "sinT", "Sim"], writes=["tr2"])
                dve("tensor_tensor", carry_re[:, fc * 4:fc * 4 + 4], t4a, t4b, ALU.subtract,
                    reads=["tr1", "tr2"], writes=["carry_re"])
                dve("tensor_tensor", t4a, Xre3[:, :, 128], s9, ALU.mult, reads=["Xre", "sinT"], writes=["tr1"])
                dve("tensor_tensor", t4b, Xim3[:, :, 128], c9, ALU.mult, reads=["Xim", "cosT"], writes=["tr2"])
                dve("tensor_tensor", carry_im[:, fc * 4:fc * 4 + 4], t4a, t4b, ALU.add,
                    reads=["tr1", "tr2"], writes=["carry_im"])
                checkpoint("A_rot")
                items = []
                for j4 in range(4):
                    for gl, gp in enumerate(gps):
                        o = psb[5][32 * gl:32 * gl + 32, j4:512:4]
                        tp = (0, 32 * gl)
                        items.append((o, Wssm4[:, gp, 4, 32 * j4:32 * j4 + 32], U43[:, gp, :], True, False, tp))
                        items.append((o, Wssm4[:, gp, 2, 32 * j4:32 * j4 + 32], Sre3[:, gl, :], False, False, tp))
                        items.append((o, Wssm4[:, gp, 3, 32 * j4:32 * j4 + 32], Sim3[:, gl, :], False, True, tp))
                mmgroup(items, reads=["U4a", "U4d", "Sre", "Sim", "W1", "W2re", "W2im"], writes=["psb5"])
                checkpoint("A_y")
                yp = psb[5][:, :]
                import os as _os
                _gn = int(_os.environ.get("GELU_N", "5"))
                if _gn >= 1:
                    act(tr1, yp, AF.Square, reads=["psb5"], writes=["tr1"])
                if _gn >= 2:
                    dve("tensor_scalar", tr1, tr1, 0.044715, 1.0, ALU.mult, ALU.add, reads=["tr1"], writes=["tr1"])
                if _gn >= 3:
                    dve("tensor_tensor", tr2, yp, tr1, ALU.mult, reads=["tr1", "psb5"], writes=["tr2"])
                if _gn >= 4:
                    act(tr1, tr2, AF.Sigmoid, reads=["tr2"], writes=["tr1"], scale=1.5957691216057308)
                if _gn >= 5:
                    dve("tensor_tensor", ys3[:, fc, :], yp, tr1, ALU.mult, reads=["tr1", "psb5"], writes=["ys"])
                checkpoint("A_gelu")
            checkpoint("A_ssm")
            for oc in range(4):
                bank = 2 + oc % 2
                mmgroup([(psb[bank][:, :], Wglu3[:, k, oc * 128:(oc + 1) * 128], ys3[:, k, :], k == 0, k == 3, None)
                         for k in range(4)], reads=["Wglu", "ys"], writes=[f"psb{bank}"])
                act(sgA, psb[bank][:, :], AF.Sigmoid, reads=[f"psb{bank}", "b_gluT"], writes=["sgA"],
                    bias=b_gluT[:, oc:oc + 1])
                dve("tensor_tensor", yglu3[:, oc, :], sgA, ys3[:, oc, :], ALU.mult, reads=["sgA", "ys"], writes=["yglu"])
            checkpoint("A_glu")
            for fc in range(4):
                def wmm(bank, ch):
                    mmgroup([(psb[bank][:, :], Win3[:, k, ch * 128:(ch + 1) * 128], hT3[:, k, :], k == 0, k == 7, None)
                             for k in range(8)], reads=[f"Win_q{ch // 4}", "hT"], writes=[f"psb{bank}"])
                wmm(2, 4 + fc)
                act(cbS, psb[2][:, :], AF.Identity, reads=["psb2", "bias_in"], writes=["cbS"],
                    bias=bias_in3[:, 4 + fc, b:b + 1])
                wmm(3, 8 + fc)
                act(ccS, psb[3][:, :], AF.Identity, reads=["psb3", "bias_in"], writes=["ccS"],
                    bias=bias_in3[:, 8 + fc, b:b + 1])
                wmm(2, 12 + fc)
                vc = r3(vcarry, 4)
                dve("tensor_copy", vbuf[:, 0:2], vc[:, fc, :], reads=["vcarry"], writes=["vbuf"])
                dve("scalar_tensor_tensor", vbuf[:, 2:514], psb[2][:, :], bias_in3[:, 12 + fc, b:b + 1], ccS,
                    ALU.add, ALU.mult, reads=["psb2", "bias_in", "ccS"], writes=["vbuf"])
                cw = r3(conv_wT, 4)
                dve("tensor_scalar", cacc, vbuf[:, 2:514], cw[:, fc, 2:3], None, ALU.mult,
                    reads=["vbuf", "conv_wT"], writes=["cacc"])
                dve("scalar_tensor_tensor", cacc, vbuf[:, 1:513], cw[:, fc, 1:2], cacc, ALU.mult, ALU.add,
                    reads=["vbuf", "conv_wT", "cacc"], writes=["cacc"])
                dve("scalar_tensor_tensor", cacc, vbuf[:, 0:512], cw[:, fc, 0:1], cacc, ALU.mult, ALU.add,
                    reads=["vbuf", "conv_wT", "cacc"], writes=["cacc"])
                dve("tensor_tensor", yc3[:, fc, :], cacc, cbS, ALU.mult, reads=["cacc", "cbS"], writes=["yc"])
                dve("tensor_copy", vc[:, fc, :], vbuf[:, 512:514], reads=["vbuf"], writes=["vcarry"])
            if blk == 0:
                tap("ys", ys, "ys"); tap("yglu", yglu, "yglu"); tap("yc", yc, "yc")
            checkpoint("A_conv")
            for half in range(2):
                gs_t, gs_tag = ring_load(gate_src(0, half))
                gc_t, gc_tag = ring_load(gate_src(1, half))
                for cl in range(4):
                    oc = half * 4 + cl
                    mmgroup([(psb[2][:, :], gs_t[:, k, cl * 128:(cl + 1) * 128], hT3[:, k, :], k == 0, k == 7, None)
                             for k in range(8)], reads=[gs_tag, "hT"], writes=["psb2"])
                    act(sgA, psb[2][:, :], AF.Sigmoid, reads=["psb2", "bias_in"], writes=["sgA"],
                        bias=bias_in3[:, 16 + oc, b:b + 1])
                    mmgroup([(psb[3][:, :], gc_t[:, k, cl * 128:(cl + 1) * 128], hT3[:, k, :], k == 0, k == 7, None)
                             for k in range(8)], reads=[gc_tag, "hT"], writes=["psb3"])
                    act(sgB, psb[3][:, :], AF.Sigmoid, reads=["psb3", "bias_in"], writes=["sgB"],
                        bias=bias_in3[:, 24 + oc, b:b + 1])
                    mmgroup([(psb[6][:, :], Wps3[:, k, oc * 128:(oc + 1) * 128], yglu3[:, k, :], k == 0, k == 3, None)
                             for k in range(4)], reads=["Wps", "yglu"], writes=["psb6"])
                    mmgroup([(psb[7][:, :], Wpc3[:, k, oc * 128:(oc + 1) * 128], yc3[:, k, :], k == 0, k == 3, None)
                             for k in range(4)], reads=["Wpc", "yc"], writes=["psb7"])
                    dve("tensor_tensor", tr1, psb[6][:, :], sgA, ALU.mult, reads=["psb6", "sgA"], writes=["tr1"])
                    dve("tensor_tensor", tr2, psb[7][:, :], sgB, ALU.mult, reads=["psb7", "sgB"], writes=["tr2"])
                    dve("tensor_tensor", mT3[:, oc, :], tr1, tr2, ALU.add, reads=["tr1", "tr2"], writes=["mT"])
            if blk == 0:
                tap("mT", mT, "mT")
            checkpoint("A_merge")
            wo = [ring_load(wout_src(0)), ring_load(wout_src(1))]
            for s in range(4):
                slot = s % 2
                r0 = blk * BLK + s * 128
                dma("sp", f"xr{slot}", xr[slot], dr["x"][r0:r0 + 128, :],
                      writes=[f"xr{slot}"])
                for oh in range(2):
                    wt, wtag = wo[oh]
                    bank = 2 + oh
                    mmgroup([(psb[bank][:, :], mT3[:, k, s * 128:(s + 1) * 128], wt[:, k, :], k == 0, k == 7, None)
                             for k in range(8)], reads=["mT", wtag], writes=[f"psb{bank}"])
                    dve("tensor_tensor", tr1, psb[bank][:, :], g1row[:, oh * 512:(oh + 1) * 512], ALU.mult,
                        reads=[f"psb{bank}", "g1row"], writes=["tr1"])
                    dve("tensor_tensor", x1o[slot][:, oh * 512:(oh + 1) * 512], tr1, xr[slot][:, oh * 512:(oh + 1) * 512],
                        ALU.add, reads=["tr1", f"xr{slot}"], writes=[f"x1o{slot}"])
                dma("sp", f"x1o{slot}", x1_d[r0:r0 + 128, :], x1o[slot],
                      reads=[f"x1o{slot}"], writes=["x1_dram"])
        P.barrier()
        if "x1" in tap_d:
            dma("sp", "tapx1a", xr[0], x1_d[0:128, :], reads=["x1_dram"], writes=["xr0"])
            dma("sp", "tapx1b", tap_d["x1"], xr[0], reads=["xr0"], writes=["tapout_x1"])
            P.barrier()

        pos[0] = persistB_end
        W1f = alloc(8 * 4096, BF16); W1f3 = r3(W1f, 8)
        W2f = alloc(32 * 1024, BF16); W2f3 = r3(W2f, 32)
        for q in range(8):
            src = dr["w_ff1"][:, q * 512:(q + 1) * 512].rearrange("(k p) n -> p k n", p=128)
            dst = W1f3[:, :, q * 512:(q + 1) * 512]
            dma("pool", f"ld_w1f{q}", dst, src, writes=[f"W1f_q{q}"])
        for q in range(8):
            src = dr["w_ff2"][q * 512:(q + 1) * 512, :].rearrange("(k p) n -> p k n", p=128)
            dst = W2f3[:, q * 4:(q + 1) * 4, :]
            dma("pool", f"ld_w2f{q}", dst, src, writes=[f"W2f_q{q}"])
        xr = [alloc(1024) for _ in range(2)]
        hn = alloc(1024, BF16); junk = alloc(1024, BF16)
        h2T = alloc(8 * 512, BF16); h2T3 = r3(h2T, 8)
        hid = alloc(32 * 512, BF16); hid3 = r3(hid, 32)
        rl = [alloc(512, BF16) for _ in range(2)]
        tr1 = alloc(512)
        x2 = [alloc(1024) for _ in range(2)]
        g2row = alloc(1024); fgrow = alloc(1024)
        diagt = alloc(128)
        ssB = alloc(4); rstdB = alloc(4); ss2 = alloc(1); rstd2 = alloc(1)
        print("phase B arena use:", pos[0], "of", ARENA)
        load_group("sp", "ld_fg", [(fgrow, dr["fg_row"], "fgrow")])
        sh2b3 = r3(sh2b, 8)
        for ch in range(32):
            mmgroup([(psb[0][:, ch * 4:(ch + 1) * 4], W1f3[:, k, ch * 128:(ch + 1) * 128], sh2b3[:, k, :],
                      k == 0, k == 7, None) for k in range(8)], reads=[f"W1f_q{ch // 4}", "sh2b"], writes=["psb0"])
        dve("tensor_copy", bias_ff1, psb[0][:, 0:128], reads=["psb0"], writes=["bias_ff1"])

        if nblk_b > 0:
            stats_rstd(x1_d, 0, ssB, rstdB, "B")
            norm_transpose(x1_d, 0, 0, rstdB, gs2T3, "gs2T", h2T3, "h2T")
        for blk in range(nblk_b):
            b = blk // 4
            if blk % 4 == 0:
                row_from_col(g2row, modT3, 40, b, "modT", "g2row")
            if blk + 1 < nblk_b:
                stats_rstd(x1_d, blk + 1, ssB, rstdB, "B")
            if blk == 0:
                tap("h2T", h2T, "h2T")
            for hc in range(32):
                bank = 2 + hc % 2
                mmgroup([(psb[bank][:, :], W1f3[:, k, hc * 128:(hc + 1) * 128], h2T3[:, k, :], k == 0, k == 7, None)
                         for k in range(8)], reads=[f"W1f_q{hc // 4}", "h2T"], writes=[f"psb{bank}"])
                act(rl[hc % 2], psb[bank][:, :], AF.Relu, reads=[f"psb{bank}", "bias_ff1"], writes=[f"rl{hc % 2}"],
                    bias=bias_ff13[:, hc, b:b + 1])
                dve("tensor_tensor", hid3[:, hc, :], rl[hc % 2], rl[hc % 2], ALU.mult,
                    reads=[f"rl{hc % 2}"], writes=["hid"])
            for s in range(4):
                slot = s % 2
                r0 = blk * BLK + s * 128
                dma("sp", f"xr{slot}", xr[slot], x1_d[r0:r0 + 128, :], writes=[f"xr{slot}"])
                for oh in range(2):
                    bank = 4 + oh
                    mmgroup([(psb[bank][:, :], hid3[:, k, s * 128:(s + 1) * 128], W2f3[:, k, oh * 512:(oh + 1) * 512],
                              k == 0, k == 31, None) for k in range(32)],
                            reads=["hid"] + [f"W2f_q{q}" for q in range(8)], writes=[f"psb{bank}"])
                    dve("tensor_tensor", tr1, psb[bank][:, :], g2row[:, oh * 512:(oh + 1) * 512], ALU.mult,
                        reads=[f"psb{bank}", "g2row"], writes=["tr1"])
                    dve("tensor_tensor", x2[slot][:, oh * 512:(oh + 1) * 512], tr1, xr[slot][:, oh * 512:(oh + 1) * 512],
                        ALU.add, reads=["tr1", f"xr{slot}"], writes=[f"x2{slot}"])
                if blk + 1 < nblk_b:
                    norm_transpose_sub(x1_d, blk + 1, (blk + 1) // 4, s, rstdB, gs2T3, "gs2T", h2T3, "h2T", xr, hn)
                dve("memset", ss2, 0.0, reads=[], writes=["ss2"])
                act(junk, x2[slot], AF.Square, reads=[f"x2{slot}"], writes=["junk", "ss2"], accum_out=ss2[:, 0:1])
                dve("tensor_scalar", rstd2, ss2, 1.0 / D, EPS, ALU.mult, ALU.add, reads=["ss2"], writes=["rstd2"])
                act(rstd2, rstd2, AF.Sqrt, reads=["rstd2"], writes=["rstd2"])
                dve("reciprocal", rstd2, rstd2, reads=["rstd2"], writes=["rstd2"])
                dve("scalar_tensor_tensor", x2[slot], x2[slot], rstd2[:, 0:1], fgrow, ALU.mult, ALU.mult,
                    reads=[f"x2{slot}", "rstd2", "fgrow"], writes=[f"x2{slot}"])
                dma("sp", f"x2{slot}", out_d[r0:r0 + 128, :], x2[slot], reads=[f"x2{slot}"], writes=["out_dram"])
        P.frozen = False
        P.barrier()
        P.emit()
    return nc


_CACHE = {}


def kernel(**inputs):
    inp = {k: np.asarray(v) for k, v in inputs.items()}
    shared = _shared_inputs(inp)
    x = np.ascontiguousarray(inp["x"], dtype=np.float32)
    c = np.asarray(inp["c"], dtype=np.float32)
    in_maps = []
    for i in range(NCORES):
        m = dict(shared)
        m["x"] = x[NB * i:NB * (i + 1)].reshape(NTOK, D)
        cc = c[NB * i:NB * (i + 1)]
        m["cT"] = np.ascontiguousarray(cc.T.reshape(8, 128, NB).transpose(1, 0, 2).reshape(128, 32))
        in_maps.append(m)
    if "nc" not in _CACHE:
        _CACHE["nc"] = build_program()
    res = run_bass_kernel_spmd(_CACHE["nc"], in_maps, core_ids=list(range(NCORES)))
    out = np.stack([np.asarray(r["out"]).reshape(NB, SEQ, D) for r in res.results], axis=0)
    return out.reshape(NCORES * NB, SEQ, D).astype(np.float32)
```

```python
import contextlib
import math
import numpy as np
import concourse.bass as bass
import concourse.mybir as mybir
from concourse.bass_utils import run_bass_kernel_spmd

F32 = mybir.dt.float32
BF16 = mybir.dt.bfloat16
I32 = mybir.dt.int32
AF = mybir.ActivationFunctionType
ALU = mybir.AluOpType

NCORES = 8
D = 1024
SEQ = 2048
NB = 4
NTOK = NB * SEQ
BLK = 512
NBLK = NTOK // BLK
EPS = 1e-6
TWO_PI = 2.0 * math.pi
EPOCH = 24000
DEBUG = False


class Prog:
    ENGS = ("pe", "act", "dve", "pool", "sp")
    COMPUTE = ("pe", "act", "dve", "pool")

    def __init__(self, nc):
        self.nc = nc
        self.ops = {e: [] for e in self.ENGS}
        self.count = {e: 0 for e in self.COMPUTE}
        self.dma_count = {}
        self.last_write = {}
        self.readers = {}
        self.waited = {e: {} for e in self.ENGS}

    def _deps(self, eng, reads, writes, skip_key=None):
        writes = list(writes) + [t for t in reads if t.startswith("psb") and t not in writes]
        deps = []
        for t in reads:
            lw = self.last_write.get(t)
            if lw is not None:
                deps.append(lw)
        for t in writes:
            lw = self.last_write.get(t)
            if lw is not None:
                deps.append(lw)
            deps.extend(self.readers.get(t, ()))
        out = {}
        for key, val in deps:
            if key == eng and eng == "pe":
                continue
            if key == skip_key:
                continue
            if self.waited[eng].get(key, 0) >= val:
                continue
            if out.get(key, 0) < val:
                out[key] = val
        for key, val in out.items():
            self.waited[eng][key] = val
        return list(out.items())

    def _commit(self, sig, reads, writes):
        writes = list(writes) + [t for t in reads if t.startswith("psb") and t not in writes]
        for t in writes:
            self.last_write[t] = sig
            self.readers[t] = []
        for t in reads:
            if t not in writes:
                self.readers.setdefault(t, []).append(sig)

    frozen = False

    def op(self, eng, fn, reads=(), writes=()):
        if self.frozen:
            return
        waits = self._deps(eng, reads, writes)
        self.count[eng] += 1
        sig = (eng, self.count[eng])
        self._commit(sig, reads, writes)
        self.ops[eng].append((fn, waits, sig, 1))

    def dma(self, eng, sem, fn, reads=(), writes=(), final=None):
        if self.frozen:
            return
        waits = self._deps(eng, reads, writes, skip_key=(sem if final is not None else None))
        self.dma_count[sem] = self.dma_count.get(sem, 0) + 16
        sig = (sem, self.dma_count[sem] if final is None else final)
        self._commit(sig, reads, writes)
        self.ops[eng].append((fn, waits, (sem, self.dma_count[sem]), 16))

    def barrier(self):
        if self.frozen:
            return
        sigs = [(e, c) for e, c in self.count.items() if c > 0]
        sigs += [(s, c) for s, c in self.dma_count.items()]
        for e in self.ENGS:
            waits = []
            for key, val in sigs:
                if key == e and e == "pe":
                    continue
                if self.waited[e].get(key, 0) >= val:
                    continue
                self.waited[e][key] = val
                waits.append((key, val))
            if waits:
                self.ops[e].append((None, waits, None, 0))

    def emit(self):
        nc = self.nc
        with contextlib.ExitStack() as st:
            sems = {}

            def get(key, val):
                if key in self.COMPUTE:
                    k = (val - 1) // EPOCH
                    loc = val - k * EPOCH
                    name = f"s_{key}_{k}"
                else:
                    name, loc = f"d_{key}", val
                if name not in sems:
                    sems[name] = st.enter_context(nc.semaphore(name))
                return sems[name], loc

            for e in self.ENGS:
                for fn, waits, sig, inc in self.ops[e]:
                    for key, val in waits:
                        get(key, val)
                    if sig is not None:
                        get(*sig)
            block = st.enter_context(nc.Block())

            def run(e):
                def body(engine):
                    for fn, waits, sig, inc in self.ops[e]:
                        for key, val in waits:
                            s, loc = get(key, val)
                            engine.wait_ge(s, loc)
                        if fn is None:
                            continue
                        ins = fn(engine)
                        s, _ = get(*sig)
                        ins.then_inc(s, inc)
                return body

            block.tensor(run("pe"))
            block.scalar(run("act"))
            block.vector(run("dve"))
            block.gpsimd(run("pool"))
            block.sync(run("sp"))


def _consts():
    r = np.arange(128)
    j4 = r // 32
    g2_c = (r // 16) % 2
    g2_s = r // 64
    maskY = (g2_s[:, None] == g2_c[None, :]).astype(np.float32)
    maskX = np.ascontiguousarray(maskY.T)
    causal = (j4[None, :] >= j4[:, None]).astype(np.float32)
    kY = np.broadcast_to((j4 + 1).astype(np.float32)[None, :], (128, 128)).copy()
    kX = (3 - j4).astype(np.float32).reshape(128, 1)
    iota = np.broadcast_to((np.arange(130) - 1).astype(np.float32)[None, :], (128, 130)).copy()
    ident = np.eye(128, dtype=np.float32)
    ones = np.ones((128, 128), np.float32)
    return dict(maskY=maskY, maskX=maskX, causal=causal, kY=kY, kX=kX, iota=iota,
                ident=ident, ones=ones)


def _shared_inputs(inp):
    f = np.float32
    A = lambda a: np.ascontiguousarray(a, dtype=f)
    d = {}
    d["w_ada"] = A(inp["w_ada"][0])
    d["b_adaT"] = A(inp["b_ada"][0].reshape(48, 128).T)
    d["n1T"] = A(inp["norm1_g"][0].reshape(8, 128).T)
    d["n2T"] = A(inp["norm2_g"][0].reshape(8, 128).T)
    d["fg_row"] = A(np.broadcast_to(inp["final_g"][None, :], (128, 1024)))
    d["w_in"] = A(inp["w_in"][0])
    d["w_glu"] = A(inp["w_glu"][0])
    d["b_gluT"] = A(inp["b_glu"][0].reshape(4, 128).T)
    d["conv_wT"] = A(inp["conv_w"][0].reshape(3, 4, 128).transpose(2, 1, 0).reshape(128, 12))
    dsk = inp["d_skip"][0].reshape(16, 2, 16).transpose(1, 2, 0).reshape(32, 16)
    d["dX"] = A(np.tile(dsk, (4, 1)))
    d["w_ps"] = A(inp["w_proj_ssm"][0])
    d["w_pc"] = A(inp["w_proj_conv"][0])
    d["w_out"] = A(inp["w_out"][0])
    d["w_ff1"] = A(inp["w_ff1"][0])
    d["w_ff2"] = A(inp["w_ff2"][0])
    lre = inp["lam_re"][0].reshape(16, 2, 64)
    lim = inp["lam_im"][0].reshape(16, 2, 64)
    ldt = inp["log_dt"][0].reshape(16, 2)
    d["lamreY"] = A(lre.transpose(1, 2, 0).reshape(128, 16))
    d["lamimY"] = A(lim.transpose(1, 2, 0).reshape(128, 16))
    d["logdtY"] = A(np.repeat(ldt.T[:, None, :], 64, axis=1).reshape(128, 16))

    def expY(a_p_gp_g2_h):
        t = a_p_gp_g2_h[None, :, :, None, :, :]
        t = np.broadcast_to(t, (2, 64, 16, 4, 2, 16))
        return A(t.reshape(128, 16 * 128))
    d["cYre"] = expY(inp["c_re"][0].reshape(16, 2, 16, 64).transpose(3, 0, 1, 2))
    d["cYim"] = expY(inp["c_im"][0].reshape(16, 2, 16, 64).transpose(3, 0, 1, 2))
    d["bYre"] = expY(inp["b_re"][0].reshape(16, 2, 64, 16).transpose(2, 0, 1, 3))
    d["bYim"] = expY(inp["b_im"][0].reshape(16, 2, 64, 16).transpose(2, 0, 1, 3))
    d["lamreX"] = A(np.broadcast_to(inp["lam_re"][0].reshape(1, 2048), (128, 2048)))
    d["lamimX"] = A(np.broadcast_to(inp["lam_im"][0].reshape(1, 2048), (128, 2048)))
    d["logdtX"] = A(np.broadcast_to(np.repeat(ldt, 64, axis=1).reshape(1, 2048), (128, 2048)))

    def expX(a_h_gp_g2_p):
        t = a_h_gp_g2_p[None, None, :, :, :, :]
        t = np.broadcast_to(t, (4, 2, 16, 16, 2, 64))
        return A(t.reshape(128, 2048))
    d["bXre"] = expX(inp["b_re"][0].reshape(16, 2, 64, 16).transpose(3, 0, 1, 2))
    d["bXim"] = expX(inp["b_im"][0].reshape(16, 2, 64, 16).transpose(3, 0, 1, 2))
    d.update(_consts())
    return d


IN_SHAPES = dict(
    x=[NTOK, D], cT=[128, 32], w_ada=[1024, 6144], b_adaT=[128, 48], n1T=[128, 8], n2T=[128, 8],
    fg_row=[128, 1024], w_in=[1024, 4096], w_glu=[512, 512], b_gluT=[128, 4], conv_wT=[128, 12],
    dX=[128, 16], w_ps=[512, 1024], w_pc=[512, 1024], w_out=[1024, 1024], w_ff1=[1024, 4096],
    w_ff2=[4096, 1024], lamreY=[128, 16], lamimY=[128, 16], logdtY=[128, 16],
    cYre=[128, 2048], cYim=[128, 2048], bYre=[128, 2048], bYim=[128, 2048],
    lamreX=[128, 2048], lamimX=[128, 2048], logdtX=[128, 2048], bXre=[128, 2048], bXim=[128, 2048],
    maskY=[128, 128], maskX=[128, 128], causal=[128, 128], kY=[128, 128], kX=[128, 1],
    iota=[128, 130], ident=[128, 128], ones=[128, 128],
)


def build_program(nblk_a=NBLK, nblk_b=NBLK, taps=(), stop_at=None):
    nc = bass.Bass("TRN2", target_bir_lowering=False)
    dr = {k: nc.dram_tensor(k, s, F32, kind="ExternalInput").ap() for k, s in IN_SHAPES.items()}
    out_d = nc.dram_tensor("out", [NTOK, D], F32, kind="ExternalOutput").ap()
    x1_d = nc.dram_tensor("x1s", [NTOK, D], F32, kind="Internal").ap()
    wg16_d = nc.dram_tensor("wg16", [1024, 2048], BF16, kind="Internal").ap()
    wo16_d = nc.dram_tensor("wo16", [1024, 1024], BF16, kind="Internal").ap()
    tap_d = {}
    for name, shape in taps:
        tap_d[name] = nc.dram_tensor("tap_" + name, shape, F32, kind="ExternalOutput").ap()

    with contextlib.ExitStack() as st:
        ARENA = 52800
        arena = st.enter_context(nc.sbuf_tensor("arena", [128, ARENA], F32))
        psb = [st.enter_context(nc.psum_tensor(f"psb{i}", [128, 512], F32)) for i in range(8)]
        P = Prog(nc)
        pos = [0]
        uniq = [0]

        def alloc(n, dtype=F32):
            nf = n if dtype == F32 else (n + 1) // 2
            assert pos[0] + nf <= ARENA, f"SBUF arena overflow {pos[0]}+{nf}"
            v = arena[:, pos[0]:pos[0] + nf]
            pos[0] += nf
            if dtype != F32:
                v = v.bitcast(dtype)
            return v

        def r3(ap, a):
            return ap.rearrange("p (a b) -> p a b", a=a)

        def pbf(i):
            return psb[i][:, :].bitcast(BF16)

        def dve(fn, *a, reads, writes, **kw):
            P.op("dve", lambda e: getattr(e, fn)(*a, **kw), reads=reads, writes=writes)

        def act(out, in_, func, reads, writes, **kw):
            P.op("act", lambda e: e.activation(out=out, in_=in_, func=func, **kw), reads=reads, writes=writes)

        def mmgroup(items, reads, writes):
            def fn(e):
                ins = None
                for (o, l, r, s0, s1, tp) in items:
                    if tp is None:
                        ins = e.matmul(o, l, r, start=s0, stop=s1)
                    else:
                        ins = e.matmul(o, l, r, start=s0, stop=s1, tile_position=tp)
                return ins
            P.op("pe", fn, reads=reads, writes=writes)

        def dma(eng, sem, out, in_, reads=(), writes=(), final=None):
            P.dma(eng, sem, lambda e: e.dma_start(out=out, in_=in_), reads=reads, writes=writes, final=final)

        def load_group(eng, sem, items):
            base = P.dma_count.get(sem, 0)
            final = base + 16 * len(items)
            for (o, i, tag) in items:
                P.dma(eng, sem, lambda e, o=o, i=i: e.dma_start(out=o, in_=i), writes=[tag], final=final)

        def checkpoint(name):
            if stop_at == name:
                P.barrier()
                P.frozen = True

        def tap(name, src, tag):
            if name in tap_d:
                if src.dtype == BF16:
                    src = src.bitcast(F32)
                uniq[0] += 1
                P.dma("sp", f"tap{uniq[0]}", lambda e, src=src, name=name: e.dma_start(out=tap_d[name], in_=src),
                      reads=[tag], writes=["tapout_" + name])

        ident32 = alloc(128); ones32 = alloc(128)
        ident16 = alloc(128, BF16)
        cT = alloc(32); b_adaT = alloc(48); n1T = alloc(8); n2T = alloc(8)
        b_gluT = alloc(4); conv_wT = alloc(12); dX = alloc(16)
        modT = alloc(192)
        gs1T = alloc(32); gs2T = alloc(32)
        bias_in = alloc(128)
        bias_ff1 = alloc(128)
        sgc = alloc(32); scb = alloc(32, BF16)
        sh1b = alloc(32, BF16); sh2b = alloc(32, BF16)
        persistB_end = pos[0]
        rT = alloc(16)
        cosT = alloc(16 * 130); sinT = alloc(16 * 130)
        carry_re = alloc(16); carry_im = alloc(16)
        vcarry = alloc(8)
        ssA = alloc(4); rstdA = alloc(4)
        Wssm = alloc(16 * 5 * 128, BF16)
        Wssm4 = Wssm.rearrange("p (g w m) -> p g w m", g=16, w=5)
        modT3 = r3(modT, 48); gs1T3 = r3(gs1T, 8); gs2T3 = r3(gs2T, 8)
        bias_in3 = r3(bias_in, 32); bias_ff13 = r3(bias_ff1, 32)
        cosT3 = r3(cosT, 16); sinT3 = r3(sinT, 16)
        persist_end = pos[0]

        load_group("sp", "ld_small", [
            (ident32, dr["ident"], "ident32"), (ones32, dr["ones"], "ones32"),
            (cT, dr["cT"], "cT"), (b_adaT, dr["b_adaT"], "b_adaT"), (n1T, dr["n1T"], "n1T"),
            (n2T, dr["n2T"], "n2T"), (b_gluT, dr["b_gluT"], "b_gluT"),
            (conv_wT, dr["conv_wT"], "conv_wT"), (dX, dr["dX"], "dX")])
        dve("tensor_copy", ident16, ident32, reads=["ident32"], writes=["ident16"])
        for q in range(4):
            dma("pool", f"precast{q}", wg16_d[:, q * 512:(q + 1) * 512].rearrange("(k p) n -> p k n", p=128),
                dr["w_in"][:, 2048 + q * 512:2048 + (q + 1) * 512].rearrange("(k p) n -> p k n", p=128),
                writes=[f"wg16_{q}"])
        for q in range(2):
            dma("pool", f"precast{4 + q}", wo16_d[:, q * 512:(q + 1) * 512].rearrange("(k p) n -> p k n", p=128),
                dr["w_out"][:, q * 512:(q + 1) * 512].rearrange("(k p) n -> p k n", p=128),
                writes=[f"wo16_{q}"])

        act(sgc, cT, AF.Sigmoid, reads=["cT"], writes=["sgc"])
        dve("tensor_tensor", scb, cT, sgc, ALU.mult, reads=["cT", "sgc"], writes=["scb"])
        scb3 = r3(scb, 8)
        scf = alloc(32)
        dve("tensor_tensor", scf, cT, sgc, ALU.mult, reads=["cT", "sgc"], writes=["scf"])
        scf3 = r3(scf, 8)
        wada_ring = [alloc(8 * 128) for _ in range(2)]
        for j in range(48):
            slot = j % 2
            wt = r3(wada_ring[slot], 8)
            src = dr["w_ada"][:, j * 128:(j + 1) * 128].rearrange("(k p) n -> p k n", p=128)
            dma("act", f"wada{slot}", wt, src, writes=[f"wada{slot}"])
            mmgroup([(psb[0][:, j * 4:(j + 1) * 4], wt[:, k, :], scf3[:, k, :], k == 0, k == 7, None)
                     for k in range(8)], reads=[f"wada{slot}", "scf"], writes=["psb0"])
        dve("tensor_tensor", modT3, r3(psb[0][:, 0:192], 48),
            b_adaT.unsqueeze(2).to_broadcast([128, 48, 4]), ALU.add,
            reads=["psb0", "b_adaT"], writes=["modT"])
        dve("scalar_tensor_tensor", gs1T3, modT3[:, 8:16, :], 1.0, n1T.unsqueeze(2).to_broadcast([128, 8, 4]),
            ALU.add, ALU.mult, reads=["modT", "n1T"], writes=["gs1T"])
        dve("scalar_tensor_tensor", gs2T3, modT3[:, 32:40, :], 1.0, n2T.unsqueeze(2).to_broadcast([128, 8, 4]),
            ALU.add, ALU.mult, reads=["modT", "n2T"], writes=["gs2T"])
        dve("tensor_copy", r3(sh1b, 8), modT3[:, 0:8, :], reads=["modT"], writes=["sh1b"])
        dve("tensor_copy", r3(sh2b, 8), modT3[:, 24:32, :], reads=["modT"], writes=["sh2b"])
        tap("modT", modT, "modT")
        checkpoint("mod")

        BIG = 2048
        TMPN = 16 * 130
        T_i = alloc(TMPN).bitcast(I32); T_f = alloc(TMPN); T_red = alloc(TMPN)
        T_c1 = alloc(TMPN); T_c2 = alloc(TMPN)
        setup_scratch = pos[0]

        def big():
            return alloc(BIG)

        def b3(ap):
            return r3(ap, 16)

        def range_reduce(dst, src, shift, n3=None, tagd=None, tags=None):
            n = dst.shape[-1]
            ti = T_i[:, 0:n]; tf = T_f[:, 0:n]
            dve("tensor_scalar", tf, src, 1.0 / TWO_PI, shift / TWO_PI, ALU.mult, ALU.add,
                reads=[tags], writes=["rr_f"])
            dve("tensor_copy", ti, tf, reads=["rr_f"], writes=["rr_i"])
            dve("tensor_copy", tf, ti, reads=["rr_i"], writes=["rr_f"])
            dve("scalar_tensor_tensor", tf, tf, -TWO_PI, src, ALU.mult, ALU.add,
                reads=["rr_f", tags], writes=["rr_f"])
            dve("tensor_scalar", dst, tf, shift, None, ALU.add, reads=["rr_f"], writes=[tagd])
            dve("tensor_scalar", dst, dst, -math.pi, math.pi, ALU.max, ALU.min, reads=[tagd], writes=[tagd])

        def sincos(sin_dst, cos_dst, ang, tag_ang, tag_s, tag_c, n):
            red = T_red[:, 0:n]
            range_reduce(red, ang, 0.0, tagd="red", tags=tag_ang)
            act(sin_dst, red, AF.Sin, reads=["red"], writes=[tag_s])
            range_reduce(red, ang, math.pi / 2, tagd="red", tags=tag_ang)
            act(cos_dst, red, AF.Sin, reads=["red"], writes=[tag_c])

        def cmul(ore, oim, are, aim, bre, bim, tags_a, tags_b, tag_o, n, conj_b=False):
            t1 = T_c1[:, 0:n]; t2 = T_c2[:, 0:n]
            shp = list(ore.shape)

            def v(ap):
                return ap if len(shp) == 2 else ap.rearrange("p (a b) -> p a b", a=shp[1])
            dve("tensor_tensor", v(t1), are, bre, ALU.mult, reads=tags_a + tags_b, writes=["cm1"])
            dve("tensor_tensor", v(t2), aim, bim, ALU.mult, reads=tags_a + tags_b, writes=["cm2"])
            dve("tensor_tensor", ore, v(t1), v(t2), ALU.add if conj_b else ALU.subtract,
                reads=["cm1", "cm2"], writes=[tag_o + "re"])
            dve("tensor_tensor", v(t1), are, bim, ALU.mult, reads=tags_a + tags_b, writes=["cm1"])
            dve("tensor_tensor", v(t2), aim, bre, ALU.mult, reads=tags_a + tags_b, writes=["cm2"])
            dve("tensor_tensor", oim, v(t2), v(t1), ALU.subtract if conj_b else ALU.add,
                reads=["cm1", "cm2"], writes=[tag_o + "im"])

        lamreY = alloc(16); lamimY = alloc(16); logdtY = alloc(16)
        cYre = big(); cYim = big(); bYre = big(); bYim = big()
        maskY = alloc(128); causal = alloc(128); kY = alloc(128); iota = alloc(130)
        load_group("sp", "ld_ssmY", [
            (lamreY, dr["lamreY"], "lamreY"), (lamimY, dr["lamimY"], "lamimY"), (logdtY, dr["logdtY"], "logdtY"),
            (cYre, dr["cYre"], "cYre"), (cYim, dr["cYim"], "cYim"), (bYre, dr["bYre"], "bYre"),
            (bYim, dr["bYim"], "bYim"), (maskY, dr["maskY"], "maskY"), (causal, dr["causal"], "causal"),
            (kY, dr["kY"], "kY"), (iota, dr["iota"], "iota")])
        dtY = alloc(16); lrd = alloc(16); lid = alloc(16)
        act(dtY, logdtY, AF.Exp, reads=["logdtY"], writes=["dtY"])
        dve("tensor_tensor", lrd, lamreY, dtY, ALU.mult, reads=["lamreY", "dtY"], writes=["lrd"])
        dve("tensor_tensor", lid, lamimY, dtY, ALU.mult, reads=["lamimY", "dtY"], writes=["lid"])
        z = alloc(16); acc = alloc(16)
        dve("tensor_scalar", z, lrd, 4.0, None, ALU.mult, reads=["lrd"], writes=["z"])
        dve("tensor_scalar", acc, z, 1.0 / 5040.0, 1.0 / 720.0, ALU.mult, ALU.add, reads=["z"], writes=["acc"])
        for coef in (1.0 / 120.0, 1.0 / 24.0, 1.0 / 6.0, 0.5, 1.0, 1.0):
            dve("tensor_tensor", acc, acc, z, ALU.mult, reads=["acc", "z"], writes=["acc"])
            dve("tensor_scalar", acc, acc, coef, None, ALU.add, reads=["acc"], writes=["acc"])
        dve("tensor_copy", rT, acc, reads=["acc"], writes=["rT"])
        m1 = alloc(16); s1 = alloc(16); c1 = alloc(16)
        act(m1, lrd, AF.Exp, reads=["lrd"], writes=["m1"])
        sincos(s1, c1, lid, "lid", "s1", "c1", 16)
        a1re = alloc(16); a1im = alloc(16)
        dve("tensor_tensor", a1re, m1, c1, ALU.mult, reads=["m1", "c1"], writes=["a1re"])
        dve("tensor_tensor", a1im, m1, s1, ALU.mult, reads=["m1", "s1"], writes=["a1im"])
        dve("tensor_scalar", a1re, a1re, -1.0, None, ALU.add, reads=["a1re"], writes=["a1re"])
        den = alloc(16); t16 = alloc(16); qre = alloc(16); qim = alloc(16)
        dve("tensor_tensor", den, lamreY, lamreY, ALU.mult, reads=["lamreY"], writes=["den"])
        dve("tensor_tensor", t16, lamimY, lamimY, ALU.mult, reads=["lamimY"], writes=["t16"])
        dve("tensor_tensor", den, den, t16, ALU.add, reads=["den", "t16"], writes=["den"])
        dve("reciprocal", den, den, reads=["den"], writes=["den"])
        cmul(qre, qim, a1re, a1im, lamreY, lamimY, ["a1re", "a1im"], ["lamreY", "lamimY"], "q", 16, conj_b=True)
        dve("tensor_tensor", qre, qre, den, ALU.mult, reads=["qre", "den"], writes=["qre"])
        dve("tensor_tensor", qim, qim, den, ALU.mult, reads=["qim", "den"], writes=["qim"])
        bbYre = big(); bbYim = big()
        bc16 = lambda ap: ap.unsqueeze(2).to_broadcast([128, 16, 128])
        cmul(b3(bbYre), b3(bbYim), bc16(qre), bc16(qim), b3(bYre), b3(bYim),
             ["qre", "qim"], ["bYre", "bYim"], "bbY", BIG)
        argm = big(); ang = big()
        kYb = kY.unsqueeze(1).to_broadcast([128, 16, 128])
        dve("tensor_tensor", b3(argm), bc16(lrd), kYb, ALU.mult, reads=["lrd", "kY"], writes=["argm"])
        dve("tensor_tensor", b3(ang), bc16(lid), kYb, ALU.mult, reads=["lid", "kY"], writes=["ang"])
        sinA = big(); cosA = big()
        sincos(sinA, cosA, ang, "ang", "sinA", "cosA", BIG)
        mag = big()
        act(mag, argm, AF.Exp, reads=["argm"], writes=["mag"])
        Are = big(); Aim = big()
        dve("tensor_tensor", Are, mag, cosA, ALU.mult, reads=["mag", "cosA"], writes=["Are"])
        dve("tensor_tensor", Aim, mag, sinA, ALU.mult, reads=["mag", "sinA"], writes=["Aim"])
        Rre = bYre; Rim = bYim
        cmul(Rre, Rim, cYre, cYim, Are, Aim, ["cYre", "cYim", "bbYre", "bbYim"], ["Are", "Aim"], "R", BIG)
        mYb = maskY.unsqueeze(1).to_broadcast([128, 16, 128])
        dve("tensor_tensor", b3(Rre), b3(Rre), mYb, ALU.mult, reads=["Rre", "maskY"], writes=["Rre"])
        dve("scalar_tensor_tensor", b3(Rim), b3(Rim), -1.0, mYb, ALU.mult, ALU.mult,
            reads=["Rim", "maskY"], writes=["Rim"])
        dve("tensor_copy", Wssm4[:, :, 2, :], b3(Rre), reads=["Rre"], writes=["W2re"])
        dve("tensor_copy", Wssm4[:, :, 3, :], b3(Rim), reads=["Rim"], writes=["W2im"])
        act(mag, argm, AF.Exp, reads=["argm", "Are", "Aim"], writes=["mag"], scale=-1.0)
        dve("tensor_tensor", Are, mag, cosA, ALU.mult, reads=["mag", "cosA", "Rre", "Rim"], writes=["Are"])
        dve("scalar_tensor_tensor", Aim, mag, -1.0, sinA, ALU.mult, ALU.mult,
            reads=["mag", "sinA", "Rre", "Rim"], writes=["Aim"])
        Lre = cYre; Lim = cYim
        cmul(Lre, Lim, Are, Aim, bbYre, bbYim, ["Are", "Aim", "Rre", "Rim"], ["bbYre", "bbYim"], "L", BIG)
        dve("tensor_tensor", b3(Lre), b3(Lre), mYb, ALU.mult, reads=["Lre", "maskY"], writes=["Lre"])
        dve("tensor_tensor", b3(Lim), b3(Lim), mYb, ALU.mult, reads=["Lim", "maskY"], writes=["Lim"])
        kt = alloc(128)
        for gp in range(16):
            bank = psb[1 + gp % 2]
            mmgroup([(bank[:, 0:128], b3(Lre)[:, gp, :], b3(Rre)[:, gp, :], True, False, None),
                     (bank[:, 0:128], b3(Lim)[:, gp, :], b3(Rim)[:, gp, :], False, True, None)],
                    reads=["Lre", "Lim", "Rre", "Rim"], writes=[f"psb{1 + gp % 2}"])
            dve("tensor_tensor", kt, bank[:, 0:128], causal, ALU.mult,
                reads=[f"psb{1 + gp % 2}", "causal"], writes=["kt"])
            dve("scalar_tensor_tensor", Wssm4[:, gp, 4, :], ident32, dX[:, gp:gp + 1], kt, ALU.mult, ALU.add,
                reads=["ident32", "dX", "kt"], writes=["W1"])
        th4 = alloc(16); th4r = alloc(16)
        dve("tensor_scalar", th4, lid, 4.0, None, ALU.mult, reads=["lid"], writes=["th4"])
        range_reduce(th4r, th4, 0.0, tagd="th4r", tags="th4")
        angT = alloc(16 * 130)
        dve("tensor_tensor", r3(angT, 16), th4r.unsqueeze(2).to_broadcast([128, 16, 130]),
            iota.unsqueeze(1).to_broadcast([128, 16, 130]), ALU.mult, reads=["th4r", "iota"], writes=["angT"])
        def sincos_tab(dst, shift, tagd):
            red = T_red; tmp_i = T_i; tf = T_f
            dve("tensor_scalar", tf, angT, 1.0 / TWO_PI, shift / TWO_PI, ALU.mult, ALU.add,
                reads=["angT"], writes=["rr_f"])
            dve("tensor_copy", tmp_i, tf, reads=["rr_f"], writes=["rr_i"])
            dve("tensor_copy", tf, tmp_i, reads=["rr_i"], writes=["rr_f"])
            dve("scalar_tensor_tensor", tf, tf, -TWO_PI, angT, ALU.mult, ALU.add,
                reads=["rr_f", "angT"], writes=["rr_f"])
            dve("tensor_scalar", red, tf, shift, None, ALU.add, reads=["rr_f"], writes=["red"])
            dve("tensor_scalar", red, red, -math.pi, math.pi, ALU.max, ALU.min, reads=["red"], writes=["red"])
            act(dst, red, AF.Sin, reads=["red"], writes=[tagd])
        sincos_tab(sinT, 0.0, "sinT")
        sincos_tab(cosT, math.pi / 2, "cosT")
        tap("W2re", Rre, "Rre"); tap("cosT", cosT, "cosT"); tap("sinT", sinT, "sinT"); tap("rT", rT, "rT")
        checkpoint("ssmY")
        P.barrier()
        pos[0] = setup_scratch

        lamreX = big(); lamimX = big(); logdtX = big(); bXre = big(); bXim = big()
        maskX = alloc(128); kX = alloc(1)
        load_group("sp", "ld_ssmX", [
            (lamreX, dr["lamreX"], "lamreX"), (lamimX, dr["lamimX"], "lamimX"), (logdtX, dr["logdtX"], "logdtX"),
            (bXre, dr["bXre"], "bXre"), (bXim, dr["bXim"], "bXim"), (maskX, dr["maskX"], "maskX"),
            (kX, dr["kX"], "kX")])
        dtX = big(); lrdX = big(); lidX = big()
        act(dtX, logdtX, AF.Exp, reads=["logdtX"], writes=["dtX"])
        dve("tensor_tensor", lrdX, lamreX, dtX, ALU.mult, reads=["lamreX", "dtX"], writes=["lrdX"])
        dve("tensor_tensor", lidX, lamimX, dtX, ALU.mult, reads=["lamimX", "dtX"], writes=["lidX"])
        m1X = dtX
        act(m1X, lrdX, AF.Exp, reads=["lrdX", "lidX"], writes=["m1X"])
        s1X = big(); c1X = big()
        sincos(s1X, c1X, lidX, "lidX", "s1X", "c1X", BIG)
        a1reX = big(); a1imX = big()
        dve("tensor_tensor", a1reX, m1X, c1X, ALU.mult, reads=["m1X", "c1X"], writes=["a1reX"])
        dve("tensor_tensor", a1imX, m1X, s1X, ALU.mult, reads=["m1X", "s1X"], writes=["a1imX"])
        dve("tensor_scalar", a1reX, a1reX, -1.0, None, ALU.add, reads=["a1reX"], writes=["a1reX"])
        denX = s1X; tX = c1X
        dve("tensor_tensor", denX, lamreX, lamreX, ALU.mult, reads=["lamreX", "a1imX", "a1reX"], writes=["denX"])
        dve("tensor_tensor", tX, lamimX, lamimX, ALU.mult, reads=["lamimX", "a1imX", "a1reX"], writes=["tX"])
        dve("tensor_tensor", denX, denX, tX, ALU.add, reads=["denX", "tX"], writes=["denX"])
        dve("reciprocal", denX, denX, reads=["denX"], writes=["denX"])
        qreX = big(); qimX = big()
        cmul(qreX, qimX, a1reX, a1imX, lamreX, lamimX, ["a1reX", "a1imX"], ["lamreX", "lamimX"], "qX", BIG, conj_b=True)
        dve("tensor_tensor", qreX, qreX, denX, ALU.mult, reads=["qXre", "denX"], writes=["qXre"])
        dve("tensor_tensor", qimX, qimX, denX, ALU.mult, reads=["qXim", "denX"], writes=["qXim"])
        bbXre = a1reX; bbXim = a1imX
        cmul(bbXre, bbXim, qreX, qimX, bXre, bXim, ["qXre", "qXim", "denX"], ["bXre", "bXim"], "bbX", BIG)
        angX = qreX
        dve("tensor_scalar", angX, lidX, kX[:, 0:1], None, ALU.mult, reads=["lidX", "kX", "bbXre", "bbXim"], writes=["angX"])
        sinX = bXre; cosX = bXim
        sincos(sinX, cosX, angX, "angX", "sinX", "cosX", BIG)
        magX = qimX
        act(magX, lrdX, AF.Exp, reads=["lrdX", "bbXre", "bbXim"], writes=["magX"], scale=kX[:, 0:1])
        AXre = lamreX; AXim = lamimX
        dve("tensor_tensor", AXre, magX, cosX, ALU.mult, reads=["magX", "cosX", "denX", "qXre"], writes=["AXre"])
        dve("tensor_tensor", AXim, magX, sinX, ALU.mult, reads=["magX", "sinX", "denX", "qXre"], writes=["AXim"])
        W3re = lrdX; W3im = lidX
        cmul(W3re, W3im, AXre, AXim, bbXre, bbXim, ["AXre", "AXim", "angX", "magX"], ["bbXre", "bbXim"], "W3", BIG)
        mXb = maskX.unsqueeze(1).to_broadcast([128, 16, 128])
        dve("tensor_tensor", Wssm4[:, :, 0, :], b3(W3re), mXb, ALU.mult, reads=["W3re", "maskX"], writes=["W3re_b"])
        dve("tensor_tensor", Wssm4[:, :, 1, :], b3(W3im), mXb, ALU.mult, reads=["W3im", "maskX"], writes=["W3im_b"])
        tap("W3re", W3re, "W3re")
        checkpoint("ssmX")
        P.barrier()
        pos[0] = persist_end

        Win_lo = alloc(8 * 2048, BF16)
        Wglu = alloc(4 * 512, BF16)
        Wps = alloc(4 * 1024, BF16); Wpc = alloc(4 * 1024, BF16)
        Win3 = r3(Win_lo, 8); Wglu3 = r3(Wglu, 4); Wps3 = r3(Wps, 4); Wpc3 = r3(Wpc, 4)
        ring = [alloc(8 * 512, BF16) for _ in range(3)]
        for q in range(4):
            src = dr["w_in"][:, q * 512:(q + 1) * 512].rearrange("(k p) n -> p k n", p=128)
            dst = Win3[:, :, q * 512:(q + 1) * 512]
            dma("pool", f"ld_win{q}", dst, src, writes=[f"Win_q{q}"])
        load_group("pool", "ld_wA", [
            (Wglu3, dr["w_glu"].rearrange("(k p) n -> p k n", p=128), "Wglu"),
            (Wps3, dr["w_ps"].rearrange("(k p) n -> p k n", p=128), "Wps"),
            (Wpc3, dr["w_pc"].rearrange("(k p) n -> p k n", p=128), "Wpc")])
        checkpoint("wloadA")
        ring_use = [0]

        def ring_load(src):
            src_ap, src_tag = src
            slot = ring_use[0] % 3
            ring_use[0] += 1
            dst = r3(ring[slot], 8)
            dma("sp", f"ring{slot}", dst, src_ap, reads=[src_tag], writes=[f"ring{slot}"])
            return dst, f"ring{slot}"

        def gate_src(which, half):
            q = which * 2 + half
            return wg16_d[:, q * 512:(q + 1) * 512].rearrange("(k p) n -> p k n", p=128), f"wg16_{q}"

        def wout_src(half):
            return wo16_d[:, half * 512:(half + 1) * 512].rearrange("(k p) n -> p k n", p=128), f"wo16_{half}"

        sh1b3 = r3(sh1b, 8)
        for ch in range(16):
            mmgroup([(psb[0][:, ch * 4:(ch + 1) * 4], Win3[:, k, ch * 128:(ch + 1) * 128], sh1b3[:, k, :],
                      k == 0, k == 7, None) for k in range(8)],
                    reads=[f"Win_q{ch // 4}", "sh1b"], writes=["psb0"])
        for which in range(2):
            for half in range(2):
                gt, gtag = ring_load(gate_src(which, half))
                for cl in range(4):
                    ch = 16 + which * 8 + half * 4 + cl
                    mmgroup([(psb[0][:, ch * 4:(ch + 1) * 4], gt[:, k, cl * 128:(cl + 1) * 128], sh1b3[:, k, :],
                              k == 0, k == 7, None) for k in range(8)],
                            reads=[gtag, "sh1b"], writes=["psb0"])
        dve("tensor_copy", bias_in, psb[0][:, 0:128], reads=["psb0"], writes=["bias_in"])
        tap("bias_in", bias_in, "bias_in")
        checkpoint("bias_in")

        xr = [alloc(1024) for _ in range(2)]
        hn = alloc(1024, BF16)
        hT = alloc(8 * 512, BF16); hT3 = r3(hT, 8)
        uT = alloc(4 * 512, BF16); uT3 = r3(uT, 4)
        U4 = alloc(16 * 128, BF16); U43 = r3(U4, 16)
        Ebre = alloc(4 * 129); Ebim = alloc(4 * 129); Xre = alloc(4 * 129); Xim = alloc(4 * 129)
        Ebre3 = r3(Ebre, 4); Ebim3 = r3(Ebim, 4); Xre3 = r3(Xre, 4); Xim3 = r3(Xim, 4)
        tr1 = alloc(512); tr2 = alloc(512)
        tr13 = r3(tr1, 4); tr23 = r3(tr2, 4)
        Sre3b = [r3(alloc(4 * 128, BF16), 4) for _ in range(2)]
        Sim3b = [r3(alloc(4 * 128, BF16), 4) for _ in range(2)]
        tp1 = alloc(512); tp2 = alloc(512); tp13 = r3(tp1, 4); tp23 = r3(tp2, 4)
        ga = tr1; gb = tr2
        one_col = alloc(1)
        dve("memset", one_col, 1.0, reads=[], writes=["one_col"])
        ys = alloc(4 * 512, BF16); ys3 = r3(ys, 4)
        yglu = alloc(4 * 512, BF16); yglu3 = r3(yglu, 4)
        yc = alloc(4 * 512, BF16); yc3 = r3(yc, 4)
        cbS = alloc(512); ccS = alloc(512); vbuf = alloc(514); cacc = alloc(512)
        sgA = alloc(512, BF16); sgB = alloc(512, BF16)
        mT = alloc(8 * 512, BF16); mT3 = r3(mT, 8)
        g1row = alloc(1024)
        xq = [alloc(1024) for _ in range(2)]
        diagt = alloc(128)
        print("phase A arena use:", pos[0], "of", ARENA)

        def stats_rstd(src_d, blk, ss, rstd, tagp):
            dve("memset", ss, 0.0, reads=[], writes=[tagp + "ss"])
            for s in range(4):
                slot = s % 2
                r0 = blk * BLK + s * 128
                dma("sp", f"xr{slot}", xr[slot], src_d[r0:r0 + 128, :],
                      writes=[f"xr{slot}"])
                act(junk, xr[slot], AF.Square, reads=[f"xr{slot}"], writes=["junk", tagp + "ss"],
                    accum_out=ss[:, s:s + 1])
            dve("tensor_scalar", rstd, ss, 1.0 / D, EPS, ALU.mult, ALU.add, reads=[tagp + "ss"], writes=[tagp + "rstd"])
            act(rstd, rstd, AF.Sqrt, reads=[tagp + "rstd"], writes=[tagp + "rstd"])
            dve("reciprocal", rstd, rstd, reads=[tagp + "rstd"], writes=[tagp + "rstd"])

        def row_from_col(dst_row, colT3, kidx0, b, tag_col, tag_row):
            for half in range(2):
                bank = psb[4 + half]
                for kk in range(4):
                    k = half * 4 + kk
                    dve("tensor_scalar", diagt, ident32, colT3[:, kidx0 + k, b:b + 1], None, ALU.mult,
                        reads=["ident32", tag_col], writes=["diagt"])
                    mmgroup([(bank[:, kk * 128:(kk + 1) * 128], ones32, diagt, True, True, None)],
                            reads=["ones32", "diagt"], writes=[f"psb{4 + half}"])
                dve("tensor_copy", dst_row[:, half * 512:(half + 1) * 512], bank[:, :],
                    reads=[f"psb{4 + half}"], writes=[tag_row])

        def norm_transpose_sub(src_d, blk, b, s, rstd, gsT3, tag_gs, dstT3, tag_dst, xr, hn):
            slot = s % 2
            r0 = blk * BLK + s * 128
            dma("sp", f"xr{slot}", xr[slot], src_d[r0:r0 + 128, :], writes=[f"xr{slot}"])
            dve("tensor_scalar", hn, xr[slot], rstd[:, s:s + 1], None, ALU.mult,
                reads=[f"xr{slot}", "Arstd", "Brstd"], writes=["hn"])
            bank = s % 2
            ptb = pbf(bank)

            def fn(e, ptb=ptb, hn=hn, ident16=ident16):
                ins = None
                for k in range(8):
                    ins = e.transpose(ptb[:, k * 128:(k + 1) * 128], hn[:, k * 128:(k + 1) * 128], ident16)
                return ins
            P.op("pe", fn, reads=["hn", "ident16"], writes=[f"psb{bank}"])
            for k in range(8):
                act(dstT3[:, k, s * 128:(s + 1) * 128], ptb[:, k * 128:(k + 1) * 128], AF.Copy,
                    reads=[f"psb{bank}", tag_gs], writes=[tag_dst], scale=gsT3[:, k, b:b + 1])

        def norm_transpose(src_d, blk, b, rstd, gsT3, tag_gs, dstT3, tag_dst):
            for s in range(4):
                norm_transpose_sub(src_d, blk, b, s, rstd, gsT3, tag_gs, dstT3, tag_dst, xr, hn)

        def pool(fn, *a, reads, writes, **kw):
            P.op("pool", lambda e: getattr(e, fn)(*a, **kw), reads=reads, writes=writes)

        junkA = cacc.bitcast(BF16)

        def stats_rstd_A(blk):
            dve("memset", ssA, 0.0, reads=[], writes=["Ass"])
            for s in range(4):
                slot = s % 2
                r0 = blk * BLK + s * 128
                dma("sp", f"xr{slot}", xr[slot], dr["x"][r0:r0 + 128, :], writes=[f"xr{slot}"])
                act(junkA, xr[slot], AF.Square, reads=[f"xr{slot}"], writes=["cacc", "Ass"], accum_out=ssA[:, s:s + 1])
            dve("tensor_scalar", rstdA, ssA, 1.0 / D, EPS, ALU.mult, ALU.add, reads=["Ass"], writes=["Arstd"])
            act(rstdA, rstdA, AF.Sqrt, reads=["Arstd"], writes=["Arstd"])
            dve("reciprocal", rstdA, rstdA, reads=["Arstd"], writes=["Arstd"])

        def emit_U(b):
            for fc in range(4):
                bank = 2 + fc % 2
                mmgroup([(psb[bank][:, :], Win3[:, k, fc * 128:(fc + 1) * 128], hT3[:, k, :], k == 0, k == 7, None)
                         for k in range(8)], reads=["Win_q0", "hT"], writes=[f"psb{bank}"])
                act(uT3[:, fc, :], psb[bank][:, :], AF.Identity, reads=[f"psb{bank}", "bias_in"], writes=[f"uT{fc}"],
                    bias=bias_in3[:, fc, b:b + 1])

        def emit_relayout(fc):
            for gl in range(4):
                for j4 in range(4):
                    pool("tensor_copy", U43[32 * j4:32 * j4 + 32, fc * 4 + gl, :], uT3[32 * gl:32 * gl + 32, fc, j4:512:4],
                         reads=[f"uT{fc}"], writes=[f"U4_{fc}"])

        def emit_E(fc):
            gps = range(fc * 4, fc * 4 + 4)
            mmgroup([(psb[6][:, gl * 128:(gl + 1) * 128], Wssm4[:, gp, 0, :], U43[:, gp, :], True, True, None)
                     for gl, gp in enumerate(gps)], reads=[f"U4_{fc}", "W3re_b"], writes=["psb6"])
            mmgroup([(psb[7][:, gl * 128:(gl + 1) * 128], Wssm4[:, gp, 1, :], U43[:, gp, :], True, True, None)
                     for gl, gp in enumerate(gps)], reads=[f"U4_{fc}", "W3im_b"], writes=["psb7"])

        def emit_chain(fc):
            gps = range(fc * 4, fc * 4 + 4)
            Er = r3(psb[6][:, :], 4); Ei = r3(psb[7][:, :], 4)
            cs = cosT3[:, fc * 4:fc * 4 + 4, 1:129]; sn = sinT3[:, fc * 4:fc * 4 + 4, 1:129]
            dve("tensor_tensor", tr13, Er, cs, ALU.mult, reads=["psb6", "cosT"], writes=["tr1"])
            dve("tensor_tensor", tr23, Ei, sn, ALU.mult, reads=["psb7", "sinT"], writes=["tr2"])
            dve("tensor_tensor", Ebre3[:, :, 1:129], tr13, tr23, ALU.add, reads=["tr1", "tr2"], writes=["Ebre"])
            dve("tensor_tensor", tr13, Ei, cs, ALU.mult, reads=["psb7", "cosT"], writes=["tr1"])
            dve("tensor_tensor", tr23, Er, sn, ALU.mult, reads=["psb6", "sinT"], writes=["tr2"])
            dve("tensor_tensor", Ebim3[:, :, 1:129], tr13, tr23, ALU.subtract, reads=["tr1", "tr2"], writes=["Ebim"])
            dve("tensor_copy", Ebre3[:, :, 0], carry_re[:, fc * 4:fc * 4 + 4], reads=["carry_re"], writes=["Ebre"])
            dve("tensor_copy", Ebim3[:, :, 0], carry_im[:, fc * 4:fc * 4 + 4], reads=["carry_im"], writes=["Ebim"])
            for gl, gp in enumerate(gps):
                rb = rT[:, gp:gp + 1].to_broadcast([128, 129])
                dve("tensor_tensor_scan", Xre3[:, gl, :], rb, Ebre3[:, gl, :], 0.0, ALU.mult, ALU.add,
                    reads=["rT", "Ebre"], writes=["Xre"])
                dve("tensor_tensor_scan", Xim3[:, gl, :], rb, Ebim3[:, gl, :], 0.0, ALU.mult, ALU.add,
                    reads=["rT", "Ebim"], writes=["Xim"])
            sb = fc % 2
            Sr = Sre3b[sb]; Si = Sim3b[sb]
            cs0 = cosT3[:, fc * 4:fc * 4 + 4, 0:128]; sn0 = sinT3[:, fc * 4:fc * 4 + 4, 0:128]
            pool("tensor_tensor", tp13, Xre3[:, :, 0:128], cs0, ALU.mult, reads=["Xre", "cosT"], writes=["tp1"])
            pool("tensor_tensor", tp23, Xim3[:, :, 0:128], sn0, ALU.mult, reads=["Xim", "sinT"], writes=["tp2"])
            pool("tensor_tensor", Sr, tp13, tp23, ALU.subtract, reads=["tp1", "tp2"], writes=[f"Sre{sb}"])
            pool("tensor_tensor", tp13, Xre3[:, :, 0:128], sn0, ALU.mult, reads=["Xre", "sinT"], writes=["tp1"])
            pool("tensor_tensor", tp23, Xim3[:, :, 0:128], cs0, ALU.mult, reads=["Xim", "cosT"], writes=["tp2"])
            pool("tensor_tensor", Si, tp13, tp23, ALU.add, reads=["tp1", "tp2"], writes=[f"Sim{sb}"])
            c9 = cosT3[:, fc * 4:fc * 4 + 4, 129]; s9 = sinT3[:, fc * 4:fc * 4 + 4, 129]
            t4a = tp1[:, 0:4]; t4b = tp2[:, 0:4]
            pool("tensor_tensor", t4a, Xre3[:, :, 128], c9, ALU.mult, reads=["Xre", "cosT"], writes=["tp1"])
            pool("tensor_tensor", t4b, Xim3[:, :, 128], s9, ALU.mult, reads=["Xim", "sinT"], writes=["tp2"])
            pool("tensor_tensor", carry_re[:, fc * 4:fc * 4 + 4], t4a, t4b, ALU.subtract,
                 reads=["tp1", "tp2"], writes=["carry_re"])
            pool("tensor_tensor", t4a, Xre3[:, :, 128], s9, ALU.mult, reads=["Xre", "sinT"], writes=["tp1"])
            pool("tensor_tensor", t4b, Xim3[:, :, 128], c9, ALU.mult, reads=["Xim", "cosT"], writes=["tp2"])
            pool("tensor_tensor", carry_im[:, fc * 4:fc * 4 + 4], t4a, t4b, ALU.add,
                 reads=["tp1", "tp2"], writes=["carry_im"])

        def emit_Y(fc):
            gps = range(fc * 4, fc * 4 + 4)
            sb = fc % 2
            Sr = Sre3b[sb]; Si = Sim3b[sb]
            items = []
            for j4 in range(4):
                for gl, gp in enumerate(gps):
                    o = psb[5][32 * gl:32 * gl + 32, j4:512:4]
                    tp = (0, 32 * gl)
                    items.append((o, Wssm4[:, gp, 4, 32 * j4:32 * j4 + 32], U43[:, gp, :], True, False, tp))
                    items.append((o, Wssm4[:, gp, 2, 32 * j4:32 * j4 + 32], Sr[:, gl, :], False, False, tp))
                    items.append((o, Wssm4[:, gp, 3, 32 * j4:32 * j4 + 32], Si[:, gl, :], False, True, tp))
            mmgroup(items, reads=[f"U4_{fc}", f"Sre{sb}", f"Sim{sb}", "W1", "W2re", "W2im"], writes=["psb5"])
            yp = psb[5][:, :]
            act(ga, yp, AF.Square, reads=["psb5"], writes=["tr1"])
            act(ga, ga, AF.Identity, reads=["tr1"], writes=["tr1"], scale=0.044715, bias=one_col[:, 0:1])
            dve("tensor_tensor", gb, yp, ga, ALU.mult, reads=["tr1", "psb5"], writes=["tr2"])
            act(ga, gb, AF.Sigmoid, reads=["tr2"], writes=["tr1"], scale=1.5957691216057308)
            dve("tensor_tensor", ys3[:, fc, :], yp, ga, ALU.mult, reads=["tr1", "psb5"], writes=["ys"])

        def emit_CONV(fc, b):
            def wmm(bank, ch):
                mmgroup([(psb[bank][:, :], Win3[:, k, ch * 128:(ch + 1) * 128], hT3[:, k, :], k == 0, k == 7, None)
                         for k in range(8)], reads=[f"Win_q{ch // 4}", "hT"], writes=[f"psb{bank}"])
            wmm(0, 4 + fc)
            act(cbS, psb[0][:, :], AF.Identity, reads=["psb0", "bias_in"], writes=["cbS"],
                bias=bias_in3[:, 4 + fc, b:b + 1])
            wmm(1, 8 + fc)
            act(ccS, psb[1][:, :], AF.Identity, reads=["psb1", "bias_in"], writes=["ccS"],
                bias=bias_in3[:, 8 + fc, b:b + 1])
            wmm(3, 12 + fc)
            vc = r3(vcarry, 4)
            dve("tensor_copy", vbuf[:, 0:2], vc[:, fc, :], reads=["vcarry"], writes=["vbuf"])
            dve("scalar_tensor_tensor", vbuf[:, 2:514], psb[3][:, :], bias_in3[:, 12 + fc, b:b + 1], ccS,
                ALU.add, ALU.mult, reads=["psb3", "bias_in", "ccS"], writes=["vbuf"])
            cw = r3(conv_wT, 4)
            dve("tensor_scalar", cacc, vbuf[:, 2:514], cw[:, fc, 2:3], None, ALU.mult,
                reads=["vbuf", "conv_wT"], writes=["cacc"])
            dve("scalar_tensor_tensor", cacc, vbuf[:, 1:513], cw[:, fc, 1:2], cacc, ALU.mult, ALU.add,
                reads=["vbuf", "conv_wT", "cacc"], writes=["cacc"])
            dve("scalar_tensor_tensor", cacc, vbuf[:, 0:512], cw[:, fc, 0:1], cacc, ALU.mult, ALU.add,
                reads=["vbuf", "conv_wT", "cacc"], writes=["cacc"])
            dve("tensor_tensor", yc3[:, fc, :], cacc, cbS, ALU.mult, reads=["cacc", "cbS"], writes=["yc"])
            dve("tensor_copy", vc[:, fc, :], vbuf[:, 512:514], reads=["vbuf"], writes=["vcarry"])

        def emit_GLU():
            for oc in range(4):
                bank = 2 + oc % 2
                mmgroup([(psb[bank][:, :], Wglu3[:, k, oc * 128:(oc + 1) * 128], ys3[:, k, :], k == 0, k == 3, None)
                         for k in range(4)], reads=["Wglu", "ys"], writes=[f"psb{bank}"])
                act(sgA, psb[bank][:, :], AF.Sigmoid, reads=[f"psb{bank}", "b_gluT"], writes=["sgA"],
                    bias=b_gluT[:, oc:oc + 1])
                dve("tensor_tensor", yglu3[:, oc, :], sgA, ys3[:, oc, :], ALU.mult, reads=["sgA", "ys"], writes=["yglu"])

        def emit_MERGE(b, pre):
            tiles = {0: pre[0], 1: pre[1], 2: pre[2]}
            for half in range(2):
                gs_t, gs_tag = tiles[0] if half == 0 else tiles[2]
                if half == 1:
                    tiles[3] = ring_load(gate_src(1, 1))
                    tiles[4] = ring_load(wout_src(0))
                gc_t, gc_tag = tiles[1] if half == 0 else tiles[3]
                for cl in range(4):
                    oc = half * 4 + cl
                    mmgroup([(psb[2][:, :], gs_t[:, k, cl * 128:(cl + 1) * 128], hT3[:, k, :], k == 0, k == 7, None)
                             for k in range(8)], reads=[gs_tag, "hT"], writes=["psb2"])
                    act(sgA, psb[2][:, :], AF.Sigmoid, reads=["psb2", "bias_in"], writes=["sgA"],
                        bias=bias_in3[:, 16 + oc, b:b + 1])
                    mmgroup([(psb[3][:, :], gc_t[:, k, cl * 128:(cl + 1) * 128], hT3[:, k, :], k == 0, k == 7, None)
                             for k in range(8)], reads=[gc_tag, "hT"], writes=["psb3"])
                    act(sgB, psb[3][:, :], AF.Sigmoid, reads=["psb3", "bias_in"], writes=["sgB"],
                        bias=bias_in3[:, 24 + oc, b:b + 1])
                    mmgroup([(psb[6][:, :], Wps3[:, k, oc * 128:(oc + 1) * 128], yglu3[:, k, :], k == 0, k == 3, None)
                             for k in range(4)], reads=["Wps", "yglu"], writes=["psb6"])
                    mmgroup([(psb[7][:, :], Wpc3[:, k, oc * 128:(oc + 1) * 128], yc3[:, k, :], k == 0, k == 3, None)
                             for k in range(4)], reads=["Wpc", "yc"], writes=["psb7"])
                    dve("tensor_tensor", tr1, psb[6][:, :], sgA, ALU.mult, reads=["psb6", "sgA"], writes=["tr1"])
                    dve("tensor_tensor", tr2, psb[7][:, :], sgB, ALU.mult, reads=["psb7", "sgB"], writes=["tr2"])
                    pool("tensor_tensor", mT3[:, oc, :], tr1, tr2, ALU.add, reads=["tr1", "tr2"], writes=["mT"])
            tiles[5] = ring_load(wout_src(1))
            return [tiles[4], tiles[5]]

        def emit_WOUT(blk, nxt, wo):
            for s in range(4):
                slot = s % 2
                r0 = blk * BLK + s * 128
                dma("sp", f"xq{slot}", xq[slot], dr["x"][r0:r0 + 128, :], writes=[f"xq{slot}"])
                for oh in range(2):
                    wt, wtag = wo[oh]
                    bank = 2 + oh
                    mmgroup([(psb[bank][:, :], mT3[:, k, s * 128:(s + 1) * 128], wt[:, k, :], k == 0, k == 7, None)
                             for k in range(8)], reads=["mT", wtag], writes=[f"psb{bank}"])
                    tt = ga if oh == 0 else gb
                    ttag = "tr1" if oh == 0 else "tr2"
                    dve("tensor_tensor", tt, psb[bank][:, :], g1row[:, oh * 512:(oh + 1) * 512], ALU.mult,
                        reads=[f"psb{bank}", "g1row"], writes=[ttag])
                    pool("tensor_tensor", xq[slot][:, oh * 512:(oh + 1) * 512], tt, xq[slot][:, oh * 512:(oh + 1) * 512],
                         ALU.add, reads=[ttag, f"xq{slot}"], writes=[f"xq{slot}"])
                dma("sp", f"xq{slot}", x1_d[r0:r0 + 128, :], xq[slot], reads=[f"xq{slot}"], writes=["x1_dram"])
                if nxt is not None:
                    norm_transpose_sub(dr["x"], nxt, nxt // 4, s, rstdA, gs1T3, "gs1T", hT3, "hT", xr, hn)

        if nblk_a > 0:
            stats_rstd_A(0)
            norm_transpose(dr["x"], 0, 0, rstdA, gs1T3, "gs1T", hT3, "hT")
        for blk in range(nblk_a):
            b = blk // 4
            qpos = blk % 4
            if qpos == 0:
                row_from_col(g1row, modT3, 16, b, "modT", "g1row")
                dve("memset", carry_re, 0.0, reads=[], writes=["carry_re"])
                dve("memset", carry_im, 0.0, reads=[], writes=["carry_im"])
                dve("memset", vcarry, 0.0, reads=[], writes=["vcarry"])
            if blk == 0:
                tap("hT", hT, "hT")
            if blk + 1 < nblk_a:
                stats_rstd_A(blk + 1)
            pre = [ring_load(gate_src(0, 0)), ring_load(gate_src(1, 0)), ring_load(gate_src(0, 1))]
            emit_U(b)
            emit_relayout(0); emit_relayout(1)
            emit_E(0); emit_chain(0); emit_CONV(0, b)
            emit_relayout(2)
            emit_E(1); emit_chain(1); emit_Y(0); emit_CONV(1, b)
            emit_relayout(3)
            emit_E(2); emit_chain(2); emit_Y(1); emit_CONV(2, b)
            emit_E(3); emit_chain(3); emit_Y(2); emit_CONV(3, b)
            emit_Y(3)
            if blk == 0:
                tap("U4", U4, "U4_3")
            emit_GLU()
            if blk == 0:
                tap("ys", ys, "ys"); tap("yglu", yglu, "yglu"); tap("yc", yc, "yc")
            wo = emit_MERGE(b, pre)
            if blk == 0:
                tap("mT", mT, "mT")
            emit_WOUT(blk, blk + 1 if blk + 1 < nblk_a else None, wo)
        P.barrier()
        if "x1" in tap_d:
            dma("sp", "tapx1a", xr[0], x1_d[0:128, :], reads=["x1_dram"], writes=["xr0"])
            dma("sp", "tapx1b", tap_d["x1"], xr[0], reads=["xr0"], writes=["tapout_x1"])
            P.barrier()

        pos[0] = persistB_end
        W1f = alloc(8 * 4096, BF16); W1f3 = r3(W1f, 8)
        W2f = alloc(32 * 1024, BF16); W2f3 = r3(W2f, 32)
        for q in range(8):
            src = dr["w_ff1"][:, q * 512:(q + 1) * 512].rearrange("(k p) n -> p k n", p=128)
            dst = W1f3[:, :, q * 512:(q + 1) * 512]
            dma("pool", f"ld_w1f{q}", dst, src, writes=[f"W1f_q{q}"])
        for q in range(8):
            src = dr["w_ff2"][q * 512:(q + 1) * 512, :].rearrange("(k p) n -> p k n", p=128)
            dst = W2f3[:, q * 4:(q + 1) * 4, :]
            dma("pool", f"ld_w2f{q}", dst, src, writes=[f"W2f_q{q}"])
        xr = [alloc(1024) for _ in range(2)]
        hn = alloc(1024, BF16); junk = alloc(1024, BF16)
        h2T = alloc(8 * 512, BF16); h2T3 = r3(h2T, 8)
        hid = alloc(32 * 512, BF16); hid3 = r3(hid, 32)
        rl = [alloc(512, BF16) for _ in range(2)]
        tr1 = alloc(512)
        x2 = [alloc(1024) for _ in range(2)]
        g2row = alloc(1024); fgrow = alloc(1024)
        diagt = alloc(128)
        ssB = alloc(4); rstdB = alloc(4); ss2 = alloc(1); rstd2 = alloc(1)
        print("phase B arena use:", pos[0], "of", ARENA)
        load_group("sp", "ld_fg", [(fgrow, dr["fg_row"], "fgrow")])
        sh2b3 = r3(sh2b, 8)
        for ch in range(32):
            mmgroup([(psb[0][:, ch * 4:(ch + 1) * 4], W1f3[:, k, ch * 128:(ch + 1) * 128], sh2b3[:, k, :],
                      k == 0, k == 7, None) for k in range(8)], reads=[f"W1f_q{ch // 4}", "sh2b"], writes=["psb0"])
        dve("tensor_copy", bias_ff1, psb[0][:, 0:128], reads=["psb0"], writes=["bias_ff1"])

        if nblk_b > 0:
            stats_rstd(x1_d, 0, ssB, rstdB, "B")
            norm_transpose(x1_d, 0, 0, rstdB, gs2T3, "gs2T", h2T3, "h2T")
        for blk in range(nblk_b):
            b = blk // 4
            if blk % 4 == 0:
                row_from_col(g2row, modT3, 40, b, "modT", "g2row")
            if blk + 1 < nblk_b:
                stats_rstd(x1_d, blk + 1, ssB, rstdB, "B")
            if blk == 0:
                tap("h2T", h2T, "h2T")
            for hc in range(32):
                bank = 2 + hc % 2
                mmgroup([(psb[bank][:, :], W1f3[:, k, hc * 128:(hc + 1) * 128], h2T3[:, k, :], k == 0, k == 7, None)
                         for k in range(8)], reads=[f"W1f_q{hc // 4}", "h2T"], writes=[f"psb{bank}"])
                act(rl[hc % 2], psb[bank][:, :], AF.Relu, reads=[f"psb{bank}", "bias_ff1"], writes=[f"rl{hc % 2}"],
                    bias=bias_ff13[:, hc, b:b + 1])
                dve("tensor_tensor", hid3[:, hc, :], rl[hc % 2], rl[hc % 2], ALU.mult,
                    reads=[f"rl{hc % 2}"], writes=["hid"])
            for s in range(4):
                slot = s % 2
                r0 = blk * BLK + s * 128
                dma("sp", f"xr{slot}", xr[slot], x1_d[r0:r0 + 128, :], writes=[f"xr{slot}"])
                for oh in range(2):
                    bank = 4 + oh
                    mmgroup([(psb[bank][:, :], hid3[:, k, s * 128:(s + 1) * 128], W2f3[:, k, oh * 512:(oh + 1) * 512],
                              k == 0, k == 31, None) for k in range(32)],
                            reads=["hid"] + [f"W2f_q{q}" for q in range(8)], writes=[f"psb{bank}"])
                    dve("tensor_tensor", tr1, psb[bank][:, :], g2row[:, oh * 512:(oh + 1) * 512], ALU.mult,
                        reads=[f"psb{bank}", "g2row"], writes=["tr1"])
                    dve("tensor_tensor", x2[slot][:, oh * 512:(oh + 1) * 512], tr1, xr[slot][:, oh * 512:(oh + 1) * 512],
                        ALU.add, reads=["tr1", f"xr{slot}"], writes=[f"x2{slot}"])
                if blk + 1 < nblk_b:
                    norm_transpose_sub(x1_d, blk + 1, (blk + 1) // 4, s, rstdB, gs2T3, "gs2T", h2T3, "h2T", xr, hn)
                dve("memset", ss2, 0.0, reads=[], writes=["ss2"])
                act(junk, x2[slot], AF.Square, reads=[f"x2{slot}"], writes=["junk", "ss2"], accum_out=ss2[:, 0:1])
                dve("tensor_scalar", rstd2, ss2, 1.0 / D, EPS, ALU.mult, ALU.add, reads=["ss2"], writes=["rstd2"])
                act(rstd2, rstd2, AF.Sqrt, reads=["rstd2"], writes=["rstd2"])
                dve("reciprocal", rstd2, rstd2, reads=["rstd2"], writes=["rstd2"])
                dve("scalar_tensor_tensor", x2[slot], x2[slot], rstd2[:, 0:1], fgrow, ALU.mult, ALU.mult,
                    reads=[f"x2{slot}", "rstd2", "fgrow"], writes=[f"x2{slot}"])
                dma("sp", f"x2{slot}", out_d[r0:r0 + 128, :], x2[slot], reads=[f"x2{slot}"], writes=["out_dram"])
        P.frozen = False
        P.barrier()
        P.emit()
    return nc


_CACHE = {}


def kernel(**inputs):
    inp = {k: np.asarray(v) for k, v in inputs.items()}
    shared = _shared_inputs(inp)
    x = np.ascontiguousarray(inp["x"], dtype=np.float32)
    c = np.asarray(inp["c"], dtype=np.float32)
    in_maps = []
    for i in range(NCORES):
        m = dict(shared)
        m["x"] = x[NB * i:NB * (i + 1)].reshape(NTOK, D)
        cc = c[NB * i:NB * (i + 1)]
        m["cT"] = np.ascontiguousarray(cc.T.reshape(8, 128, NB).transpose(1, 0, 2).reshape(128, 32))
        in_maps.append(m)
    if "nc" not in _CACHE:
        _CACHE["nc"] = build_program()
    res = run_bass_kernel_spmd(_CACHE["nc"], in_maps, core_ids=list(range(NCORES)))
    out = np.stack([np.asarray(r["out"]).reshape(NB, SEQ, D) for r in res.results], axis=0)
    return out.reshape(NCORES * NB, SEQ, D).astype(np.float32)
```

```python
import contextlib
import math
import numpy as np
import concourse.bass as bass
import concourse.mybir as mybir
from concourse.bass_utils import run_bass_kernel_spmd

F32 = mybir.dt.float32
BF16 = mybir.dt.bfloat16
I32 = mybir.dt.int32
AF = mybir.ActivationFunctionType
ALU = mybir.AluOpType

NCORES = 8
D = 1024
SEQ = 2048
NB = 4
NTOK = NB * SEQ
BLK = 512
NBLK = NTOK // BLK
EPS = 1e-6
TWO_PI = 2.0 * math.pi
EPOCH = 24000
DEBUG = False


class Prog:
    ENGS = ("pe", "act", "dve", "pool", "sp")
    COMPUTE = ("pe", "act", "dve", "pool")

    def __init__(self, nc):
        self.nc = nc
        self.ops = {e: [] for e in self.ENGS}
        self.count = {e: 0 for e in self.COMPUTE}
        self.dma_count = {}
        self.last_write = {}
        self.readers = {}
        self.waited = {e: {} for e in self.ENGS}

    def _deps(self, eng, reads, writes, skip_key=None):
        writes = list(writes) + [t for t in reads if t.startswith("psb") and t not in writes]
        deps = []
        for t in reads:
            lw = self.last_write.get(t)
            if lw is not None:
                deps.append(lw)
        for t in writes:
            lw = self.last_write.get(t)
            if lw is not None:
                deps.append(lw)
            deps.extend(self.readers.get(t, ()))
        out = {}
        for key, val in deps:
            if key == eng and eng == "pe":
                continue
            if key == skip_key:
                continue
            if self.waited[eng].get(key, 0) >= val:
                continue
            if out.get(key, 0) < val:
                out[key] = val
        for key, val in out.items():
            self.waited[eng][key] = val
        return list(out.items())

    def _commit(self, sig, reads, writes):
        writes = list(writes) + [t for t in reads if t.startswith("psb") and t not in writes]
        for t in writes:
            self.last_write[t] = sig
            self.readers[t] = []
        for t in reads:
            if t not in writes:
                self.readers.setdefault(t, []).append(sig)

    frozen = False

    def op(self, eng, fn, reads=(), writes=()):
        if self.frozen:
            return
        waits = self._deps(eng, reads, writes)
        self.count[eng] += 1
        sig = (eng, self.count[eng])
        self._commit(sig, reads, writes)
        self.ops[eng].append((fn, waits, sig, 1))

    def dma(self, eng, sem, fn, reads=(), writes=(), final=None):
        if self.frozen:
            return
        waits = self._deps(eng, reads, writes, skip_key=(sem if final is not None else None))
        self.dma_count[sem] = self.dma_count.get(sem, 0) + 16
        sig = (sem, self.dma_count[sem] if final is None else final)
        self._commit(sig, reads, writes)
        self.ops[eng].append((fn, waits, (sem, self.dma_count[sem]), 16))

    def barrier(self):
        if self.frozen:
            return
        sigs = [(e, c) for e, c in self.count.items() if c > 0]
        sigs += [(s, c) for s, c in self.dma_count.items()]
        for e in self.ENGS:
            waits = []
            for key, val in sigs:
                if key == e and e == "pe":
                    continue
                if self.waited[e].get(key, 0) >= val:
                    continue
                self.waited[e][key] = val
                waits.append((key, val))
            if waits:
                self.ops[e].append((None, waits, None, 0))

    def emit(self):
        nc = self.nc
        with contextlib.ExitStack() as st:
            sems = {}

            def get(key, val):
                if key in self.COMPUTE:
                    k = (val - 1) // EPOCH
                    loc = val - k * EPOCH
                    name = f"s_{key}_{k}"
                else:
                    name, loc = f"d_{key}", val
                if name not in sems:
                    sems[name] = st.enter_context(nc.semaphore(name))
                return sems[name], loc

            for e in self.ENGS:
                for fn, waits, sig, inc in self.ops[e]:
                    for key, val in waits:
                        get(key, val)
                    if sig is not None:
                        get(*sig)
            block = st.enter_context(nc.Block())

            def run(e):
                def body(engine):
                    for fn, waits, sig, inc in self.ops[e]:
                        for key, val in waits:
                            s, loc = get(key, val)
                            engine.wait_ge(s, loc)
                        if fn is None:
                            continue
                        ins = fn(engine)
                        s, _ = get(*sig)
                        ins.then_inc(s, inc)
                return body

            block.tensor(run("pe"))
            block.scalar(run("act"))
            block.vector(run("dve"))
            block.gpsimd(run("pool"))
            block.sync(run("sp"))


def _consts():
    r = np.arange(128)
    j4 = r // 32
    g2_c = (r // 16) % 2
    g2_s = r // 64
    maskY = (g2_s[:, None] == g2_c[None, :]).astype(np.float32)
    maskX = np.ascontiguousarray(maskY.T)
    causal = (j4[None, :] >= j4[:, None]).astype(np.float32)
    kY = np.broadcast_to((j4 + 1).astype(np.float32)[None, :], (128, 128)).copy()
    kX = (3 - j4).astype(np.float32).reshape(128, 1)
    iota = np.broadcast_to((np.arange(130) - 1).astype(np.float32)[None, :], (128, 130)).copy()
    ident = np.eye(128, dtype=np.float32)
    ones = np.ones((128, 128), np.float32)
    return dict(maskY=maskY, maskX=maskX, causal=causal, kY=kY, kX=kX, iota=iota,
                ident=ident, ones=ones)


def _shared_inputs(inp):
    f = np.float32
    A = lambda a: np.ascontiguousarray(a, dtype=f)
    d = {}
    d["w_ada"] = A(inp["w_ada"][0])
    d["b_adaT"] = A(inp["b_ada"][0].reshape(48, 128).T)
    d["n1T"] = A(inp["norm1_g"][0].reshape(8, 128).T)
    d["n2T"] = A(inp["norm2_g"][0].reshape(8, 128).T)
    d["fg_row"] = A(np.broadcast_to(inp["final_g"][None, :], (128, 1024)))
    d["w_in"] = A(inp["w_in"][0])
    d["w_glu"] = A(inp["w_glu"][0])
    d["b_gluT"] = A(inp["b_glu"][0].reshape(4, 128).T)
    d["conv_wT"] = A(inp["conv_w"][0].reshape(3, 4, 128).transpose(2, 1, 0).reshape(128, 12))
    dsk = inp["d_skip"][0].reshape(16, 2, 16).transpose(1, 2, 0).reshape(32, 16)
    d["dX"] = A(np.tile(dsk, (4, 1)))
    d["w_ps"] = A(inp["w_proj_ssm"][0])
    d["w_pc"] = A(inp["w_proj_conv"][0])
    d["w_out"] = A(inp["w_out"][0])
    d["w_ff1"] = A(inp["w_ff1"][0])
    d["w_ff2"] = A(inp["w_ff2"][0])
    lre = inp["lam_re"][0].reshape(16, 2, 64)
    lim = inp["lam_im"][0].reshape(16, 2, 64)
    ldt = inp["log_dt"][0].reshape(16, 2)
    d["lamreY"] = A(lre.transpose(1, 2, 0).reshape(128, 16))
    d["lamimY"] = A(lim.transpose(1, 2, 0).reshape(128, 16))
    d["logdtY"] = A(np.repeat(ldt.T[:, None, :], 64, axis=1).reshape(128, 16))

    def expY(a_p_gp_g2_h):
        t = a_p_gp_g2_h[None, :, :, None, :, :]
        t = np.broadcast_to(t, (2, 64, 16, 4, 2, 16))
        return A(t.reshape(128, 16 * 128))
    d["cYre"] = expY(inp["c_re"][0].reshape(16, 2, 16, 64).transpose(3, 0, 1, 2))
    d["cYim"] = expY(inp["c_im"][0].reshape(16, 2, 16, 64).transpose(3, 0, 1, 2))
    d["bYre"] = expY(inp["b_re"][0].reshape(16, 2, 64, 16).transpose(2, 0, 1, 3))
    d["bYim"] = expY(inp["b_im"][0].reshape(16, 2, 64, 16).transpose(2, 0, 1, 3))
    d["lamreX"] = A(np.broadcast_to(inp["lam_re"][0].reshape(1, 2048), (128, 2048)))
    d["lamimX"] = A(np.broadcast_to(inp["lam_im"][0].reshape(1, 2048), (128, 2048)))
    d["logdtX"] = A(np.broadcast_to(np.repeat(ldt, 64, axis=1).reshape(1, 2048), (128, 2048)))

    def expX(a_h_gp_g2_p):
        t = a_h_gp_g2_p[None, None, :, :, :, :]
        t = np.broadcast_to(t, (4, 2, 16, 16, 2, 64))
        return A(t.reshape(128, 2048))
    d["bXre"] = expX(inp["b_re"][0].reshape(16, 2, 64, 16).transpose(3, 0, 1, 2))
    d["bXim"] = expX(inp["b_im"][0].reshape(16, 2, 64, 16).transpose(3, 0, 1, 2))
    d.update(_consts())
    return d


IN_SHAPES = dict(
    x=[NTOK, D], cT=[128, 32], w_ada=[1024, 6144], b_adaT=[128, 48], n1T=[128, 8], n2T=[128, 8],
    fg_row=[128, 1024], w_in=[1024, 4096], w_glu=[512, 512], b_gluT=[128, 4], conv_wT=[128, 12],
    dX=[128, 16], w_ps=[512, 1024], w_pc=[512, 1024], w_out=[1024, 1024], w_ff1=[1024, 4096],
    w_ff2=[4096, 1024], lamreY=[128, 16], lamimY=[128, 16], logdtY=[128, 16],
    cYre=[128, 2048], cYim=[128, 2048], bYre=[128, 2048], bYim=[128, 2048],
    lamreX=[128, 2048], lamimX=[128, 2048], logdtX=[128, 2048], bXre=[128, 2048], bXim=[128, 2048],
    maskY=[128, 128], maskX=[128, 128], causal=[128, 128], kY=[128, 128], kX=[128, 1],
    iota=[128, 130], ident=[128, 128], ones=[128, 128],
)


def build_program(nblk_a=NBLK, nblk_b=NBLK, taps=(), stop_at=None):
    nc = bass.Bass("TRN2", target_bir_lowering=False)
    dr = {k: nc.dram_tensor(k, s, F32, kind="ExternalInput").ap() for k, s in IN_SHAPES.items()}
    out_d = nc.dram_tensor("out", [NTOK, D], F32, kind="ExternalOutput").ap()
    x1_d = nc.dram_tensor("x1s", [NTOK, D], F32, kind="Internal").ap()
    wg16_d = nc.dram_tensor("wg16", [1024, 2048], BF16, kind="Internal").ap()
    wo16_d = nc.dram_tensor("wo16", [1024, 1024], BF16, kind="Internal").ap()
    tap_d = {}
    for name, shape in taps:
        tap_d[name] = nc.dram_tensor("tap_" + name, shape, F32, kind="ExternalOutput").ap()

    with contextlib.ExitStack() as st:
        ARENA = 52800
        arena = st.enter_context(nc.sbuf_tensor("arena", [128, ARENA], F32))
        psb = [st.enter_context(nc.psum_tensor(f"psb{i}", [128, 512], F32)) for i in range(8)]
        P = Prog(nc)
        pos = [0]
        uniq = [0]

        def alloc(n, dtype=F32):
            nf = n if dtype == F32 else (n + 1) // 2
            assert pos[0] + nf <= ARENA, f"SBUF arena overflow {pos[0]}+{nf}"
            v = arena[:, pos[0]:pos[0] + nf]
            pos[0] += nf
            if dtype != F32:
                v = v.bitcast(dtype)
            return v

        def r3(ap, a):
            return ap.rearrange("p (a b) -> p a b", a=a)

        def pbf(i):
            return psb[i][:, :].bitcast(BF16)

        def dve(fn, *a, reads, writes, **kw):
            P.op("dve", lambda e: getattr(e, fn)(*a, **kw), reads=reads, writes=writes)

        def act(out, in_, func, reads, writes, **kw):
            P.op("act", lambda e: e.activation(out=out, in_=in_, func=func, **kw), reads=reads, writes=writes)

        def mmgroup(items, reads, writes):
            def fn(e):
                ins = None
                for (o, l, r, s0, s1, tp) in items:
                    if tp is None:
                        ins = e.matmul(o, l, r, start=s0, stop=s1)
                    else:
                        ins = e.matmul(o, l, r, start=s0, stop=s1, tile_position=tp)
                return ins
            P.op("pe", fn, reads=reads, writes=writes)

        def dma(eng, sem, out, in_, reads=(), writes=(), final=None):
            P.dma(eng, sem, lambda e: e.dma_start(out=out, in_=in_), reads=reads, writes=writes, final=final)

        def load_group(eng, sem, items):
            base = P.dma_count.get(sem, 0)
            final = base + 16 * len(items)
            for (o, i, tag) in items:
                P.dma(eng, sem, lambda e, o=o, i=i: e.dma_start(out=o, in_=i), writes=[tag], final=final)

        def checkpoint(name):
            if stop_at == name:
                P.barrier()
                P.frozen = True

        def tap(name, src, tag):
            if name in tap_d:
                if src.dtype == BF16:
                    src = src.bitcast(F32)
                uniq[0] += 1
                P.dma("sp", f"tap{uniq[0]}", lambda e, src=src, name=name: e.dma_start(out=tap_d[name], in_=src),
                      reads=[tag], writes=["tapout_" + name])

        ident32 = alloc(128); ones32 = alloc(128)
        ident16 = alloc(128, BF16)
        cT = alloc(32); b_adaT = alloc(48); n1T = alloc(8); n2T = alloc(8)
        b_gluT = alloc(4); conv_wT = alloc(12); dX = alloc(16)
        modT = alloc(192)
        gs1T = alloc(32); gs2T = alloc(32)
        bias_in = alloc(128)
        bias_ff1 = alloc(128)
        sgc = alloc(32); scb = alloc(32, BF16)
        sh1b = alloc(32, BF16); sh2b = alloc(32, BF16)
        persistB_end = pos[0]
        rT = alloc(16)
        cosT = alloc(16 * 130); sinT = alloc(16 * 130)
        carry_re = alloc(16); carry_im = alloc(16)
        vcarry = alloc(8)
        ssA = alloc(4); rstdA = alloc(4)
        Wssm = alloc(16 * 5 * 128, BF16)
        Wssm4 = Wssm.rearrange("p (g w m) -> p g w m", g=16, w=5)
        modT3 = r3(modT, 48); gs1T3 = r3(gs1T, 8); gs2T3 = r3(gs2T, 8)
        bias_in3 = r3(bias_in, 32); bias_ff13 = r3(bias_ff1, 32)
        cosT3 = r3(cosT, 16); sinT3 = r3(sinT, 16)
        persist_end = pos[0]

        load_group("sp", "ld_small", [
            (ident32, dr["ident"], "ident32"), (ones32, dr["ones"], "ones32"),
            (cT, dr["cT"], "cT"), (b_adaT, dr["b_adaT"], "b_adaT"), (n1T, dr["n1T"], "n1T"),
            (n2T, dr["n2T"], "n2T"), (b_gluT, dr["b_gluT"], "b_gluT"),
            (conv_wT, dr["conv_wT"], "conv_wT"), (dX, dr["dX"], "dX")])
        dve("tensor_copy", ident16, ident32, reads=["ident32"], writes=["ident16"])
        for q in range(4):
            dma("pool", f"precast{q}", wg16_d[:, q * 512:(q + 1) * 512].rearrange("(k p) n -> p k n", p=128),
                dr["w_in"][:, 2048 + q * 512:2048 + (q + 1) * 512].rearrange("(k p) n -> p k n", p=128),
                writes=[f"wg16_{q}"])
        for q in range(2):
            dma("pool", f"precast{4 + q}", wo16_d[:, q * 512:(q + 1) * 512].rearrange("(k p) n -> p k n", p=128),
                dr["w_out"][:, q * 512:(q + 1) * 512].rearrange("(k p) n -> p k n", p=128),
                writes=[f"wo16_{q}"])

        act(sgc, cT, AF.Sigmoid, reads=["cT"], writes=["sgc"])
        dve("tensor_tensor", scb, cT, sgc, ALU.mult, reads=["cT", "sgc"], writes=["scb"])
        scb3 = r3(scb, 8)
        scf = alloc(32)
        dve("tensor_tensor", scf, cT, sgc, ALU.mult, reads=["cT", "sgc"], writes=["scf"])
        scf3 = r3(scf, 8)
        wada_ring = [alloc(8 * 128) for _ in range(2)]
        for j in range(48):
            slot = j % 2
            wt = r3(wada_ring[slot], 8)
            src = dr["w_ada"][:, j * 128:(j + 1) * 128].rearrange("(k p) n -> p k n", p=128)
            dma("act", f"wada{slot}", wt, src, writes=[f"wada{slot}"])
            mmgroup([(psb[0][:, j * 4:(j + 1) * 4], wt[:, k, :], scf3[:, k, :], k == 0, k == 7, None)
                     for k in range(8)], reads=[f"wada{slot}", "scf"], writes=["psb0"])
        dve("tensor_tensor", modT3, r3(psb[0][:, 0:192], 48),
            b_adaT.unsqueeze(2).to_broadcast([128, 48, 4]), ALU.add,
            reads=["psb0", "b_adaT"], writes=["modT"])
        dve("scalar_tensor_tensor", gs1T3, modT3[:, 8:16, :], 1.0, n1T.unsqueeze(2).to_broadcast([128, 8, 4]),
            ALU.add, ALU.mult, reads=["modT", "n1T"], writes=["gs1T"])
        dve("scalar_tensor_tensor", gs2T3, modT3[:, 32:40, :], 1.0, n2T.unsqueeze(2).to_broadcast([128, 8, 4]),
            ALU.add, ALU.mult, reads=["modT", "n2T"], writes=["gs2T"])
        dve("tensor_copy", r3(sh1b, 8), modT3[:, 0:8, :], reads=["modT"], writes=["sh1b"])
        dve("tensor_copy", r3(sh2b, 8), modT3[:, 24:32, :], reads=["modT"], writes=["sh2b"])
        tap("modT", modT, "modT")
        checkpoint("mod")

        BIG = 2048
        TMPN = 16 * 130
        T_i = alloc(TMPN).bitcast(I32); T_f = alloc(TMPN); T_red = alloc(TMPN)
        T_c1 = alloc(TMPN); T_c2 = alloc(TMPN)
        setup_scratch = pos[0]

        def big():
            return alloc(BIG)

        def b3(ap):
            return r3(ap, 16)

        def range_reduce(dst, src, shift, n3=None, tagd=None, tags=None):
            n = dst.shape[-1]
            ti = T_i[:, 0:n]; tf = T_f[:, 0:n]
            dve("tensor_scalar", tf, src, 1.0 / TWO_PI, shift / TWO_PI, ALU.mult, ALU.add,
                reads=[tags], writes=["rr_f"])
            dve("tensor_copy", ti, tf, reads=["rr_f"], writes=["rr_i"])
            dve("tensor_copy", tf, ti, reads=["rr_i"], writes=["rr_f"])
            dve("scalar_tensor_tensor", tf, tf, -TWO_PI, src, ALU.mult, ALU.add,
                reads=["rr_f", tags], writes=["rr_f"])
            dve("tensor_scalar", dst, tf, shift, None, ALU.add, reads=["rr_f"], writes=[tagd])
            dve("tensor_scalar", dst, dst, -math.pi, math.pi, ALU.max, ALU.min, reads=[tagd], writes=[tagd])

        def sincos(sin_dst, cos_dst, ang, tag_ang, tag_s, tag_c, n):
            red = T_red[:, 0:n]
            range_reduce(red, ang, 0.0, tagd="red", tags=tag_ang)
            act(sin_dst, red, AF.Sin, reads=["red"], writes=[tag_s])
            range_reduce(red, ang, math.pi / 2, tagd="red", tags=tag_ang)
            act(cos_dst, red, AF.Sin, reads=["red"], writes=[tag_c])

        def cmul(ore, oim, are, aim, bre, bim, tags_a, tags_b, tag_o, n, conj_b=False):
            t1 = T_c1[:, 0:n]; t2 = T_c2[:, 0:n]
            shp = list(ore.shape)

            def v(ap):
                return ap if len(shp) == 2 else ap.rearrange("p (a b) -> p a b", a=shp[1])
            dve("tensor_tensor", v(t1), are, bre, ALU.mult, reads=tags_a + tags_b, writes=["cm1"])
            dve("tensor_tensor", v(t2), aim, bim, ALU.mult, reads=tags_a + tags_b, writes=["cm2"])
            dve("tensor_tensor", ore, v(t1), v(t2), ALU.add if conj_b else ALU.subtract,
                reads=["cm1", "cm2"], writes=[tag_o + "re"])
            dve("tensor_tensor", v(t1), are, bim, ALU.mult, reads=tags_a + tags_b, writes=["cm1"])
            dve("tensor_tensor", v(t2), aim, bre, ALU.mult, reads=tags_a + tags_b, writes=["cm2"])
            dve("tensor_tensor", oim, v(t2), v(t1), ALU.subtract if conj_b else ALU.add,
                reads=["cm1", "cm2"], writes=[tag_o + "im"])

        lamreY = alloc(16); lamimY = alloc(16); logdtY = alloc(16)
        cYre = big(); cYim = big(); bYre = big(); bYim = big()
        maskY = alloc(128); causal = alloc(128); kY = alloc(128); iota = alloc(130)
        load_group("sp", "ld_ssmY", [
            (lamreY, dr["lamreY"], "lamreY"), (lamimY, dr["lamimY"], "lamimY"), (logdtY, dr["logdtY"], "logdtY"),
            (cYre, dr["cYre"], "cYre"), (cYim, dr["cYim"], "cYim"), (bYre, dr["bYre"], "bYre"),
            (bYim, dr["bYim"], "bYim"), (maskY, dr["maskY"], "maskY"), (causal, dr["causal"], "causal"),
            (kY, dr["kY"], "kY"), (iota, dr["iota"], "iota")])
        dtY = alloc(16); lrd = alloc(16); lid = alloc(16)
        act(dtY, logdtY, AF.Exp, reads=["logdtY"], writes=["dtY"])
        dve("tensor_tensor", lrd, lamreY, dtY, ALU.mult, reads=["lamreY", "dtY"], writes=["lrd"])
        dve("tensor_tensor", lid, lamimY, dtY, ALU.mult, reads=["lamimY", "dtY"], writes=["lid"])
        z = alloc(16); acc = alloc(16)
        dve("tensor_scalar", z, lrd, 4.0, None, ALU.mult, reads=["lrd"], writes=["z"])
        dve("tensor_scalar", acc, z, 1.0 / 5040.0, 1.0 / 720.0, ALU.mult, ALU.add, reads=["z"], writes=["acc"])
        for coef in (1.0 / 120.0, 1.0 / 24.0, 1.0 / 6.0, 0.5, 1.0, 1.0):
            dve("tensor_tensor", acc, acc, z, ALU.mult, reads=["acc", "z"], writes=["acc"])
            dve("tensor_scalar", acc, acc, coef, None, ALU.add, reads=["acc"], writes=["acc"])
        dve("tensor_copy", rT, acc, reads=["acc"], writes=["rT"])
        m1 = alloc(16); s1 = alloc(16); c1 = alloc(16)
        act(m1, lrd, AF.Exp, reads=["lrd"], writes=["m1"])
        sincos(s1, c1, lid, "lid", "s1", "c1", 16)
        a1re = alloc(16); a1im = alloc(16)
        dve("tensor_tensor", a1re, m1, c1, ALU.mult, reads=["m1", "c1"], writes=["a1re"])
        dve("tensor_tensor", a1im, m1, s1, ALU.mult, reads=["m1", "s1"], writes=["a1im"])
        dve("tensor_scalar", a1re, a1re, -1.0, None, ALU.add, reads=["a1re"], writes=["a1re"])
        den = alloc(16); t16 = alloc(16); qre = alloc(16); qim = alloc(16)
        dve("tensor_tensor", den, lamreY, lamreY, ALU.mult, reads=["lamreY"], writes=["den"])
        dve("tensor_tensor", t16, lamimY, lamimY, ALU.mult, reads=["lamimY"], writes=["t16"])
        dve("tensor_tensor", den, den, t16, ALU.add, reads=["den", "t16"], writes=["den"])
        dve("reciprocal", den, den, reads=["den"], writes=["den"])
        cmul(qre, qim, a1re, a1im, lamreY, lamimY, ["a1re", "a1im"], ["lamreY", "lamimY"], "q", 16, conj_b=True)
        dve("tensor_tensor", qre, qre, den, ALU.mult, reads=["qre", "den"], writes=["qre"])
        dve("tensor_tensor", qim, qim, den, ALU.mult, reads=["qim", "den"], writes=["qim"])
        bbYre = big(); bbYim = big()
        bc16 = lambda ap: ap.unsqueeze(2).to_broadcast([128, 16, 128])
        cmul(b3(bbYre), b3(bbYim), bc16(qre), bc16(qim), b3(bYre), b3(bYim),
             ["qre", "qim"], ["bYre", "bYim"], "bbY", BIG)
        argm = big(); ang = big()
        kYb = kY.unsqueeze(1).to_broadcast([128, 16, 128])
        dve("tensor_tensor", b3(argm), bc16(lrd), kYb, ALU.mult, reads=["lrd", "kY"], writes=["argm"])
        dve("tensor_tensor", b3(ang), bc16(lid), kYb, ALU.mult, reads=["lid", "kY"], writes=["ang"])
        sinA = big(); cosA = big()
        sincos(sinA, cosA, ang, "ang", "sinA", "cosA", BIG)
        mag = big()
        act(mag, argm, AF.Exp, reads=["argm"], writes=["mag"])
        Are = big(); Aim = big()
        dve("tensor_tensor", Are, mag, cosA, ALU.mult, reads=["mag", "cosA"], writes=["Are"])
        dve("tensor_tensor", Aim, mag, sinA, ALU.mult, reads=["mag", "sinA"], writes=["Aim"])
        Rre = bYre; Rim = bYim
        cmul(Rre, Rim, cYre, cYim, Are, Aim, ["cYre", "cYim", "bbYre", "bbYim"], ["Are", "Aim"], "R", BIG)
        mYb = maskY.unsqueeze(1).to_broadcast([128, 16, 128])
        dve("tensor_tensor", b3(Rre), b3(Rre), mYb, ALU.mult, reads=["Rre", "maskY"], writes=["Rre"])
        dve("scalar_tensor_tensor", b3(Rim), b3(Rim), -1.0, mYb, ALU.mult, ALU.mult,
            reads=["Rim", "maskY"], writes=["Rim"])
        dve("tensor_copy", Wssm4[:, :, 2, :], b3(Rre), reads=["Rre"], writes=["W2re"])
        dve("tensor_copy", Wssm4[:, :, 3, :], b3(Rim), reads=["Rim"], writes=["W2im"])
        act(mag, argm, AF.Exp, reads=["argm", "Are", "Aim"], writes=["mag"], scale=-1.0)
        dve("tensor_tensor", Are, mag, cosA, ALU.mult, reads=["mag", "cosA", "Rre", "Rim"], writes=["Are"])
        dve("scalar_tensor_tensor", Aim, mag, -1.0, sinA, ALU.mult, ALU.mult,
            reads=["mag", "sinA", "Rre", "Rim"], writes=["Aim"])
        Lre = cYre; Lim = cYim
        cmul(Lre, Lim, Are, Aim, bbYre, bbYim, ["Are", "Aim", "Rre", "Rim"], ["bbYre", "bbYim"], "L", BIG)
        dve("tensor_tensor", b3(Lre), b3(Lre), mYb, ALU.mult, reads=["Lre", "maskY"], writes=["Lre"])
        dve("tensor_tensor", b3(Lim), b3(Lim), mYb, ALU.mult, reads=["Lim", "maskY"], writes=["Lim"])
        kt = alloc(128)
        for gp in range(16):
            bank = psb[1 + gp % 2]
            mmgroup([(bank[:, 0:128], b3(Lre)[:, gp, :], b3(Rre)[:, gp, :], True, False, None),
                     (bank[:, 0:128], b3(Lim)[:, gp, :], b3(Rim)[:, gp, :], False, True, None)],
                    reads=["Lre", "Lim", "Rre", "Rim"], writes=[f"psb{1 + gp % 2}"])
            dve("tensor_tensor", kt, bank[:, 0:128], causal, ALU.mult,
                reads=[f"psb{1 + gp % 2}", "causal"], writes=["kt"])
            dve("scalar_tensor_tensor", Wssm4[:, gp, 4, :], ident32, dX[:, gp:gp + 1], kt, ALU.mult, ALU.add,
                reads=["ident32", "dX", "kt"], writes=["W1"])
        th4 = alloc(16); th4r = alloc(16)
        dve("tensor_scalar", th4, lid, 4.0, None, ALU.mult, reads=["lid"], writes=["th4"])
        range_reduce(th4r, th4, 0.0, tagd="th4r", tags="th4")
        angT = alloc(16 * 130)
        dve("tensor_tensor", r3(angT, 16), th4r.unsqueeze(2).to_broadcast([128, 16, 130]),
            iota.unsqueeze(1).to_broadcast([128, 16, 130]), ALU.mult, reads=["th4r", "iota"], writes=["angT"])
        def sincos_tab(dst, shift, tagd):
            red = T_red; tmp_i = T_i; tf = T_f
            dve("tensor_scalar", tf, angT, 1.0 / TWO_PI, shift / TWO_PI, ALU.mult, ALU.add,
                reads=["angT"], writes=["rr_f"])
            dve("tensor_copy", tmp_i, tf, reads=["rr_f"], writes=["rr_i"])
            dve("tensor_copy", tf, tmp_i, reads=["rr_i"], writes=["rr_f"])
            dve("scalar_tensor_tensor", tf, tf, -TWO_PI, angT, ALU.mult, ALU.add,
                reads=["rr_f", "angT"], writes=["rr_f"])
            dve("tensor_scalar", red, tf, shift, None, ALU.add, reads=["rr_f"], writes=["red"])
            dve("tensor_scalar", red, red, -math.pi, math.pi, ALU.max, ALU.min, reads=["red"], writes=["red"])
            act(dst, red, AF.Sin, reads=["red"], writes=[tagd])
        sincos_tab(sinT, 0.0, "sinT")
        sincos_tab(cosT, math.pi / 2, "cosT")
        tap("W2re", Rre, "Rre"); tap("cosT", cosT, "cosT"); tap("sinT", sinT, "sinT"); tap("rT", rT, "rT")
        checkpoint("ssmY")
        P.barrier()
        pos[0] = setup_scratch

        lamreX = big(); lamimX = big(); logdtX = big(); bXre = big(); bXim = big()
        maskX = alloc(128); kX = alloc(1)
        load_group("sp", "ld_ssmX", [
            (lamreX, dr["lamreX"], "lamreX"), (lamimX, dr["lamimX"], "lamimX"), (logdtX, dr["logdtX"], "logdtX"),
            (bXre, dr["bXre"], "bXre"), (bXim, dr["bXim"], "bXim"), (maskX, dr["maskX"], "maskX"),
            (kX, dr["kX"], "kX")])
        dtX = big(); lrdX = big(); lidX = big()
        act(dtX, logdtX, AF.Exp, reads=["logdtX"], writes=["dtX"])
        dve("tensor_tensor", lrdX, lamreX, dtX, ALU.mult, reads=["lamreX", "dtX"], writes=["lrdX"])
        dve("tensor_tensor", lidX, lamimX, dtX, ALU.mult, reads=["lamimX", "dtX"], writes=["lidX"])
        m1X = dtX
        act(m1X, lrdX, AF.Exp, reads=["lrdX", "lidX"], writes=["m1X"])
        s1X = big(); c1X = big()
        sincos(s1X, c1X, lidX, "lidX", "s1X", "c1X", BIG)
        a1reX = big(); a1imX = big()
        dve("tensor_tensor", a1reX, m1X, c1X, ALU.mult, reads=["m1X", "c1X"], writes=["a1reX"])
        dve("tensor_tensor", a1imX, m1X, s1X, ALU.mult, reads=["m1X", "s1X"], writes=["a1imX"])
        dve("tensor_scalar", a1reX, a1reX, -1.0, None, ALU.add, reads=["a1reX"], writes=["a1reX"])
        denX = s1X; tX = c1X
        dve("tensor_tensor", denX, lamreX, lamreX, ALU.mult, reads=["lamreX", "a1imX", "a1reX"], writes=["denX"])
        dve("tensor_tensor", tX, lamimX, lamimX, ALU.mult, reads=["lamimX", "a1imX", "a1reX"], writes=["tX"])
        dve("tensor_tensor", denX, denX, tX, ALU.add, reads=["denX", "tX"], writes=["denX"])
        dve("reciprocal", denX, denX, reads=["denX"], writes=["denX"])
        qreX = big(); qimX = big()
        cmul(qreX, qimX, a1reX, a1imX, lamreX, lamimX, ["a1reX", "a1imX"], ["lamreX", "lamimX"], "qX", BIG, conj_b=True)
        dve("tensor_tensor", qreX, qreX, denX, ALU.mult, reads=["qXre", "denX"], writes=["qXre"])
        dve("tensor_tensor", qimX, qimX, denX, ALU.mult, reads=["qXim", "denX"], writes=["qXim"])
        bbXre = a1reX; bbXim = a1imX
        cmul(bbXre, bbXim, qreX, qimX, bXre, bXim, ["qXre", "qXim", "denX"], ["bXre", "bXim"], "bbX", BIG)
        angX = qreX
        dve("tensor_scalar", angX, lidX, kX[:, 0:1], None, ALU.mult, reads=["lidX", "kX", "bbXre", "bbXim"], writes=["angX"])
        sinX = bXre; cosX = bXim
        sincos(sinX, cosX, angX, "angX", "sinX", "cosX", BIG)
        magX = qimX
        act(magX, lrdX, AF.Exp, reads=["lrdX", "bbXre", "bbXim"], writes=["magX"], scale=kX[:, 0:1])
        AXre = lamreX; AXim = lamimX
        dve("tensor_tensor", AXre, magX, cosX, ALU.mult, reads=["magX", "cosX", "denX", "qXre"], writes=["AXre"])
        dve("tensor_tensor", AXim, magX, sinX, ALU.mult, reads=["magX", "sinX", "denX", "qXre"], writes=["AXim"])
        W3re = lrdX; W3im = lidX
        cmul(W3re, W3im, AXre, AXim, bbXre, bbXim, ["AXre", "AXim", "angX", "magX"], ["bbXre", "bbXim"], "W3", BIG)
        mXb = maskX.unsqueeze(1).to_broadcast([128, 16, 128])
        dve("tensor_tensor", Wssm4[:, :, 0, :], b3(W3re), mXb, ALU.mult, reads=["W3re", "maskX"], writes=["W3re_b"])
        dve("tensor_tensor", Wssm4[:, :, 1, :], b3(W3im), mXb, ALU.mult, reads=["W3im", "maskX"], writes=["W3im_b"])
        tap("W3re", W3re, "W3re")
        checkpoint("ssmX")
        P.barrier()
        pos[0] = persist_end

        Win_lo = alloc(8 * 2048, BF16)
        Wglu = alloc(4 * 512, BF16)
        Wps = alloc(4 * 1024, BF16); Wpc = alloc(4 * 1024, BF16)
        Win3 = r3(Win_lo, 8); Wglu3 = r3(Wglu, 4); Wps3 = r3(Wps, 4); Wpc3 = r3(Wpc, 4)
        ring = [alloc(8 * 512, BF16) for _ in range(3)]
        for q in range(4):
            src = dr["w_in"][:, q * 512:(q + 1) * 512].rearrange("(k p) n -> p k n", p=128)
            dst = Win3[:, :, q * 512:(q + 1) * 512]
            dma("pool", f"ld_win{q}", dst, src, writes=[f"Win_q{q}"])
        load_group("pool", "ld_wA", [
            (Wglu3, dr["w_glu"].rearrange("(k p) n -> p k n", p=128), "Wglu"),
            (Wps3, dr["w_ps"].rearrange("(k p) n -> p k n", p=128), "Wps"),
            (Wpc3, dr["w_pc"].rearrange("(k p) n -> p k n", p=128), "Wpc")])
        checkpoint("wloadA")
        ring_use = [0]

        def ring_load(src):
            src_ap, src_tag = src
            slot = ring_use[0] % 3
            ring_use[0] += 1
            dst = r3(ring[slot], 8)
            dma("sp", f"ring{slot}", dst, src_ap, reads=[src_tag], writes=[f"ring{slot}"])
            return dst, f"ring{slot}"

        def gate_src(which, half):
            q = which * 2 + half
            return wg16_d[:, q * 512:(q + 1) * 512].rearrange("(k p) n -> p k n", p=128), f"wg16_{q}"

        def wout_src(half):
            return wo16_d[:, half * 512:(half + 1) * 512].rearrange("(k p) n -> p k n", p=128), f"wo16_{half}"

        sh1b3 = r3(sh1b, 8)
        for ch in range(16):
            mmgroup([(psb[0][:, ch * 4:(ch + 1) * 4], Win3[:, k, ch * 128:(ch + 1) * 128], sh1b3[:, k, :],
                      k == 0, k == 7, None) for k in range(8)],
                    reads=[f"Win_q{ch // 4}", "sh1b"], writes=["psb0"])
        for which in range(2):
            for half in range(2):
                gt, gtag = ring_load(gate_src(which, half))
                for cl in range(4):
                    ch = 16 + which * 8 + half * 4 + cl
                    mmgroup([(psb[0][:, ch * 4:(ch + 1) * 4], gt[:, k, cl * 128:(cl + 1) * 128], sh1b3[:, k, :],
                              k == 0, k == 7, None) for k in range(8)],
                            reads=[gtag, "sh1b"], writes=["psb0"])
        dve("tensor_copy", bias_in, psb[0][:, 0:128], reads=["psb0"], writes=["bias_in"])
        tap("bias_in", bias_in, "bias_in")
        checkpoint("bias_in")

        xr = [alloc(1024) for _ in range(2)]
        hn = alloc(1024, BF16)
        hT = alloc(8 * 512, BF16); hT3 = r3(hT, 8)
        uT = alloc(4 * 512, BF16); uT3 = r3(uT, 4)
        U4 = alloc(16 * 128, BF16); U43 = r3(U4, 16)
        Ebre = alloc(4 * 129); Ebim = alloc(4 * 129); Xre = alloc(4 * 129); Xim = alloc(4 * 129)
        Ebre3 = r3(Ebre, 4); Ebim3 = r3(Ebim, 4); Xre3 = r3(Xre, 4); Xim3 = r3(Xim, 4)
        tr1 = alloc(512); tr2 = alloc(512)
        tr13 = r3(tr1, 4); tr23 = r3(tr2, 4)
        Sre3b = [r3(alloc(4 * 128, BF16), 4) for _ in range(2)]
        Sim3b = [r3(alloc(4 * 128, BF16), 4) for _ in range(2)]
        tp1 = alloc(512); tp2 = alloc(512); tp13 = r3(tp1, 4); tp23 = r3(tp2, 4)
        ga = tr1; gb = tr2
        one_col = alloc(1)
        dve("memset", one_col, 1.0, reads=[], writes=["one_col"])
        ys = alloc(4 * 512, BF16); ys3 = r3(ys, 4)
        yglu = alloc(4 * 512, BF16); yglu3 = r3(yglu, 4)
        yc = alloc(4 * 512, BF16); yc3 = r3(yc, 4)
        cbS = alloc(512); ccS = alloc(512); vbuf = alloc(514); cacc = alloc(512)
        sgA = alloc(512, BF16); sgB = alloc(512, BF16)
        mT = alloc(8 * 512, BF16); mT3 = r3(mT, 8)
        g1row = alloc(1024)
        xq = [alloc(1024) for _ in range(2)]
        diagt = alloc(128)
        print("phase A arena use:", pos[0], "of", ARENA)

        def stats_rstd(src_d, blk, ss, rstd, tagp):
            dve("memset", ss, 0.0, reads=[], writes=[tagp + "ss"])
            for s in range(4):
                slot = s % 2
                r0 = blk * BLK + s * 128
                dma("sp", f"xr{slot}", xr[slot], src_d[r0:r0 + 128, :],
                      writes=[f"xr{slot}"])
                act(junk, xr[slot], AF.Square, reads=[f"xr{slot}"], writes=["junk", tagp + "ss"],
                    accum_out=ss[:, s:s + 1])
            dve("tensor_scalar", rstd, ss, 1.0 / D, EPS, ALU.mult, ALU.add, reads=[tagp + "ss"], writes=[tagp + "rstd"])
            act(rstd, rstd, AF.Sqrt, reads=[tagp + "rstd"], writes=[tagp + "rstd"])
            dve("reciprocal", rstd, rstd, reads=[tagp + "rstd"], writes=[tagp + "rstd"])

        def row_from_col(dst_row, colT3, kidx0, b, tag_col, tag_row):
            for half in range(2):
                bank = psb[4 + half]
                for kk in range(4):
                    k = half * 4 + kk
                    dve("tensor_scalar", diagt, ident32, colT3[:, kidx0 + k, b:b + 1], None, ALU.mult,
                        reads=["ident32", tag_col], writes=["diagt"])
                    mmgroup([(bank[:, kk * 128:(kk + 1) * 128], ones32, diagt, True, True, None)],
                            reads=["ones32", "diagt"], writes=[f"psb{4 + half}"])
                dve("tensor_copy", dst_row[:, half * 512:(half + 1) * 512], bank[:, :],
                    reads=[f"psb{4 + half}"], writes=[tag_row])

        def norm_front(src_d, blk, s, rstd, xr, hn, slot):
            r0 = blk * BLK + s * 128
            dma("sp", f"xr{slot}", xr[slot], src_d[r0:r0 + 128, :], writes=[f"xr{slot}"])
            dve("tensor_scalar", hn, xr[slot], rstd[:, s:s + 1], None, ALU.mult,
                reads=[f"xr{slot}", "Arstd", "Brstd"], writes=["hn"])

        def norm_back(b, s, gsT3, tag_gs, dstT3, tag_dst, hn):
            bank = s % 2
            ptb = pbf(bank)

            def fn(e, ptb=ptb, hn=hn, ident16=ident16):
                ins = None
                for k in range(8):
                    ins = e.transpose(ptb[:, k * 128:(k + 1) * 128], hn[:, k * 128:(k + 1) * 128], ident16)
                return ins
            P.op("pe", fn, reads=["hn", "ident16"], writes=[f"psb{bank}"])
            for k in range(8):
                act(dstT3[:, k, s * 128:(s + 1) * 128], ptb[:, k * 128:(k + 1) * 128], AF.Copy,
                    reads=[f"psb{bank}", tag_gs], writes=[tag_dst], scale=gsT3[:, k, b:b + 1])

        def norm_transpose_sub(src_d, blk, b, s, rstd, gsT3, tag_gs, dstT3, tag_dst, xr, hn):
            norm_front(src_d, blk, s, rstd, xr, hn, s % 2)
            norm_back(b, s, gsT3, tag_gs, dstT3, tag_dst, hn)

        def norm_transpose(src_d, blk, b, rstd, gsT3, tag_gs, dstT3, tag_dst):
            for s in range(4):
                norm_transpose_sub(src_d, blk, b, s, rstd, gsT3, tag_gs, dstT3, tag_dst, xr, hn)

        def pool(fn, *a, reads, writes, **kw):
            P.op("pool", lambda e: getattr(e, fn)(*a, **kw), reads=reads, writes=writes)

        junkA = cacc.bitcast(BF16)

        def stats_rstd_A(blk):
            dve("memset", ssA, 0.0, reads=[], writes=["Ass"])
            for s in range(4):
                slot = s % 2
                r0 = blk * BLK + s * 128
                dma("sp", f"xr{slot}", xr[slot], dr["x"][r0:r0 + 128, :], writes=[f"xr{slot}"])
                act(junkA, xr[slot], AF.Square, reads=[f"xr{slot}"], writes=["cacc", "Ass"], accum_out=ssA[:, s:s + 1])
            dve("tensor_scalar", rstdA, ssA, 1.0 / D, EPS, ALU.mult, ALU.add, reads=["Ass"], writes=["Arstd"])
            act(rstdA, rstdA, AF.Sqrt, reads=["Arstd"], writes=["Arstd"])
            dve("reciprocal", rstdA, rstdA, reads=["Arstd"], writes=["Arstd"])

        def emit_U(b):
            for fc in range(4):
                bank = 2 + fc % 2
                mmgroup([(psb[bank][:, :], Win3[:, k, fc * 128:(fc + 1) * 128], hT3[:, k, :], k == 0, k == 7, None)
                         for k in range(8)], reads=["Win_q0", "hT"], writes=[f"psb{bank}"])
                act(uT3[:, fc, :], psb[bank][:, :], AF.Identity, reads=[f"psb{bank}", "bias_in"], writes=[f"uT{fc}"],
                    bias=bias_in3[:, fc, b:b + 1])

        def emit_relayout(fc):
            for gl in range(4):
                for j4 in range(4):
                    pool("tensor_copy", U43[32 * j4:32 * j4 + 32, fc * 4 + gl, :], uT3[32 * gl:32 * gl + 32, fc, j4:512:4],
                         reads=[f"uT{fc}"], writes=[f"U4_{fc}"])

        def emit_E(fc):
            gps = range(fc * 4, fc * 4 + 4)
            mmgroup([(psb[6][:, gl * 128:(gl + 1) * 128], Wssm4[:, gp, 0, :], U43[:, gp, :], True, True, None)
                     for gl, gp in enumerate(gps)], reads=[f"U4_{fc}", "W3re_b"], writes=["psb6"])
            mmgroup([(psb[7][:, gl * 128:(gl + 1) * 128], Wssm4[:, gp, 1, :], U43[:, gp, :], True, True, None)
                     for gl, gp in enumerate(gps)], reads=[f"U4_{fc}", "W3im_b"], writes=["psb7"])

        def emit_chain(fc):
            gps = range(fc * 4, fc * 4 + 4)
            Er = r3(psb[6][:, :], 4); Ei = r3(psb[7][:, :], 4)
            cs = cosT3[:, fc * 4:fc * 4 + 4, 1:129]; sn = sinT3[:, fc * 4:fc * 4 + 4, 1:129]
            dve("tensor_tensor", tr13, Er, cs, ALU.mult, reads=["psb6", "cosT"], writes=["tr1"])
            dve("tensor_tensor", tr23, Ei, sn, ALU.mult, reads=["psb7", "sinT"], writes=["tr2"])
            dve("tensor_tensor", Ebre3[:, :, 1:129], tr13, tr23, ALU.add, reads=["tr1", "tr2"], writes=["Ebre"])
            dve("tensor_tensor", tr13, Ei, cs, ALU.mult, reads=["psb7", "cosT"], writes=["tr1"])
            dve("tensor_tensor", tr23, Er, sn, ALU.mult, reads=["psb6", "sinT"], writes=["tr2"])
            dve("tensor_tensor", Ebim3[:, :, 1:129], tr13, tr23, ALU.subtract, reads=["tr1", "tr2"], writes=["Ebim"])
            dve("tensor_copy", Ebre3[:, :, 0], carry_re[:, fc * 4:fc * 4 + 4], reads=["carry_re"], writes=["Ebre"])
            dve("tensor_copy", Ebim3[:, :, 0], carry_im[:, fc * 4:fc * 4 + 4], reads=["carry_im"], writes=["Ebim"])
            for gl, gp in enumerate(gps):
                rb = rT[:, gp:gp + 1].to_broadcast([128, 129])
                dve("tensor_tensor_scan", Xre3[:, gl, :], rb, Ebre3[:, gl, :], 0.0, ALU.mult, ALU.add,
                    reads=["rT", "Ebre"], writes=["Xre"])
                dve("tensor_tensor_scan", Xim3[:, gl, :], rb, Ebim3[:, gl, :], 0.0, ALU.mult, ALU.add,
                    reads=["rT", "Ebim"], writes=["Xim"])
            sb = fc % 2
            Sr = Sre3b[sb]; Si = Sim3b[sb]
            cs0 = cosT3[:, fc * 4:fc * 4 + 4, 0:128]; sn0 = sinT3[:, fc * 4:fc * 4 + 4, 0:128]
            pool("tensor_tensor", tp13, Xre3[:, :, 0:128], cs0, ALU.mult, reads=["Xre", "cosT"], writes=["tp1"])
            pool("tensor_tensor", tp23, Xim3[:, :, 0:128], sn0, ALU.mult, reads=["Xim", "sinT"], writes=["tp2"])
            pool("tensor_tensor", Sr, tp13, tp23, ALU.subtract, reads=["tp1", "tp2"], writes=[f"Sre{sb}"])
            pool("tensor_tensor", tp13, Xre3[:, :, 0:128], sn0, ALU.mult, reads=["Xre", "sinT"], writes=["tp1"])
            pool("tensor_tensor", tp23, Xim3[:, :, 0:128], cs0, ALU.mult, reads=["Xim", "cosT"], writes=["tp2"])
            pool("tensor_tensor", Si, tp13, tp23, ALU.add, reads=["tp1", "tp2"], writes=[f"Sim{sb}"])
            c9 = cosT3[:, fc * 4:fc * 4 + 4, 129]; s9 = sinT3[:, fc * 4:fc * 4 + 4, 129]
            t4a = tp1[:, 0:4]; t4b = tp2[:, 0:4]
            pool("tensor_tensor", t4a, Xre3[:, :, 128], c9, ALU.mult, reads=["Xre", "cosT"], writes=["tp1"])
            pool("tensor_tensor", t4b, Xim3[:, :, 128], s9, ALU.mult, reads=["Xim", "sinT"], writes=["tp2"])
            pool("tensor_tensor", carry_re[:, fc * 4:fc * 4 + 4], t4a, t4b, ALU.subtract,
                 reads=["tp1", "tp2"], writes=["carry_re"])
            pool("tensor_tensor", t4a, Xre3[:, :, 128], s9, ALU.mult, reads=["Xre", "sinT"], writes=["tp1"])
            pool("tensor_tensor", t4b, Xim3[:, :, 128], c9, ALU.mult, reads=["Xim", "cosT"], writes=["tp2"])
            pool("tensor_tensor", carry_im[:, fc * 4:fc * 4 + 4], t4a, t4b, ALU.add,
                 reads=["tp1", "tp2"], writes=["carry_im"])

        def emit_Y(fc):
            gps = range(fc * 4, fc * 4 + 4)
            sb = fc % 2
            Sr = Sre3b[sb]; Si = Sim3b[sb]
            items = []
            for j4 in range(4):
                for gl, gp in enumerate(gps):
                    o = psb[5][32 * gl:32 * gl + 32, j4:512:4]
                    tp = (0, 32 * gl)
                    items.append((o, Wssm4[:, gp, 4, 32 * j4:32 * j4 + 32], U43[:, gp, :], True, False, tp))
                    items.append((o, Wssm4[:, gp, 2, 32 * j4:32 * j4 + 32], Sr[:, gl, :], False, False, tp))
                    items.append((o, Wssm4[:, gp, 3, 32 * j4:32 * j4 + 32], Si[:, gl, :], False, True, tp))
            mmgroup(items, reads=[f"U4_{fc}", f"Sre{sb}", f"Sim{sb}", "W1", "W2re", "W2im"], writes=["psb5"])
            yp = psb[5][:, :]
            act(ga, yp, AF.Square, reads=["psb5"], writes=["tr1"])
            act(ga, ga, AF.Identity, reads=["tr1"], writes=["tr1"], scale=0.044715, bias=one_col[:, 0:1])
            dve("tensor_tensor", gb, yp, ga, ALU.mult, reads=["tr1", "psb5"], writes=["tr2"])
            act(ga, gb, AF.Sigmoid, reads=["tr2"], writes=["tr1"], scale=1.5957691216057308)
            dve("tensor_tensor", ys3[:, fc, :], yp, ga, ALU.mult, reads=["tr1", "psb5"], writes=["ys"])

        def emit_CONV(fc, b):
            def wmm(bank, ch):
                mmgroup([(psb[bank][:, :], Win3[:, k, ch * 128:(ch + 1) * 128], hT3[:, k, :], k == 0, k == 7, None)
                         for k in range(8)], reads=[f"Win_q{ch // 4}", "hT"], writes=[f"psb{bank}"])
            wmm(0, 4 + fc)
            act(cbS, psb[0][:, :], AF.Identity, reads=["psb0", "bias_in"], writes=["cbS"],
                bias=bias_in3[:, 4 + fc, b:b + 1])
            wmm(1, 8 + fc)
            act(ccS, psb[1][:, :], AF.Identity, reads=["psb1", "bias_in"], writes=["ccS"],
                bias=bias_in3[:, 8 + fc, b:b + 1])
            wmm(3, 12 + fc)
            vc = r3(vcarry, 4)
            dve("tensor_copy", vbuf[:, 0:2], vc[:, fc, :], reads=["vcarry"], writes=["vbuf"])
            dve("scalar_tensor_tensor", vbuf[:, 2:514], psb[3][:, :], bias_in3[:, 12 + fc, b:b + 1], ccS,
                ALU.add, ALU.mult, reads=["psb3", "bias_in", "ccS"], writes=["vbuf"])
            cw = r3(conv_wT, 4)
            dve("tensor_scalar", cacc, vbuf[:, 2:514], cw[:, fc, 2:3], None, ALU.mult,
                reads=["vbuf", "conv_wT"], writes=["cacc"])
            dve("scalar_tensor_tensor", cacc, vbuf[:, 1:513], cw[:, fc, 1:2], cacc, ALU.mult, ALU.add,
                reads=["vbuf", "conv_wT", "cacc"], writes=["cacc"])
            dve("scalar_tensor_tensor", cacc, vbuf[:, 0:512], cw[:, fc, 0:1], cacc, ALU.mult, ALU.add,
                reads=["vbuf", "conv_wT", "cacc"], writes=["cacc"])
            dve("tensor_tensor", yc3[:, fc, :], cacc, cbS, ALU.mult, reads=["cacc", "cbS"], writes=["yc"])
            dve("tensor_copy", vc[:, fc, :], vbuf[:, 512:514], reads=["vbuf"], writes=["vcarry"])

        def emit_GLU():
            for oc in range(4):
                bank = 2 + oc % 2
                mmgroup([(psb[bank][:, :], Wglu3[:, k, oc * 128:(oc + 1) * 128], ys3[:, k, :], k == 0, k == 3, None)
                         for k in range(4)], reads=["Wglu", "ys"], writes=[f"psb{bank}"])
                act(sgA, psb[bank][:, :], AF.Sigmoid, reads=[f"psb{bank}", "b_gluT"], writes=["sgA"],
                    bias=b_gluT[:, oc:oc + 1])
                dve("tensor_tensor", yglu3[:, oc, :], sgA, ys3[:, oc, :], ALU.mult, reads=["sgA", "ys"], writes=["yglu"])

        def emit_MERGE(b, pre):
            tiles = {0: pre[0], 1: pre[1], 2: pre[2]}
            for half in range(2):
                gs_t, gs_tag = tiles[0] if half == 0 else tiles[2]
                if half == 1:
                    tiles[3] = ring_load(gate_src(1, 1))
                    tiles[4] = ring_load(wout_src(0))
                gc_t, gc_tag = tiles[1] if half == 0 else tiles[3]
                for cl in range(4):
                    oc = half * 4 + cl
                    mmgroup([(psb[2][:, :], gs_t[:, k, cl * 128:(cl + 1) * 128], hT3[:, k, :], k == 0, k == 7, None)
                             for k in range(8)], reads=[gs_tag, "hT"], writes=["psb2"])
                    act(sgA, psb[2][:, :], AF.Sigmoid, reads=["psb2", "bias_in"], writes=["sgA"],
                        bias=bias_in3[:, 16 + oc, b:b + 1])
                    mmgroup([(psb[3][:, :], gc_t[:, k, cl * 128:(cl + 1) * 128], hT3[:, k, :], k == 0, k == 7, None)
                             for k in range(8)], reads=[gc_tag, "hT"], writes=["psb3"])
                    act(sgB, psb[3][:, :], AF.Sigmoid, reads=["psb3", "bias_in"], writes=["sgB"],
                        bias=bias_in3[:, 24 + oc, b:b + 1])
                    mmgroup([(psb[6][:, :], Wps3[:, k, oc * 128:(oc + 1) * 128], yglu3[:, k, :], k == 0, k == 3, None)
                             for k in range(4)], reads=["Wps", "yglu"], writes=["psb6"])
                    mmgroup([(psb[7][:, :], Wpc3[:, k, oc * 128:(oc + 1) * 128], yc3[:, k, :], k == 0, k == 3, None)
                             for k in range(4)], reads=["Wpc", "yc"], writes=["psb7"])
                    dve("tensor_tensor", tr1, psb[6][:, :], sgA, ALU.mult, reads=["psb6", "sgA"], writes=["tr1"])
                    dve("tensor_tensor", tr2, psb[7][:, :], sgB, ALU.mult, reads=["psb7", "sgB"], writes=["tr2"])
                    pool("tensor_tensor", mT3[:, oc, :], tr1, tr2, ALU.add, reads=["tr1", "tr2"], writes=["mT"])
            tiles[5] = ring_load(wout_src(1))
            return [tiles[4], tiles[5]]

        def emit_WOUT(blk, nxt, wo):
            for s in range(4):
                slot = s % 2
                r0 = blk * BLK + s * 128
                dma("sp", f"xq{slot}", xq[slot], dr["x"][r0:r0 + 128, :], writes=[f"xq{slot}"])
                if nxt is not None:
                    norm_front(dr["x"], nxt, s, rstdA, xr, hn, slot)
                for oh in range(2):
                    wt, wtag = wo[oh]
                    bank = 2 + oh
                    mmgroup([(psb[bank][:, :], mT3[:, k, s * 128:(s + 1) * 128], wt[:, k, :], k == 0, k == 7, None)
                             for k in range(8)], reads=["mT", wtag], writes=[f"psb{bank}"])
                    tt = ga if oh == 0 else gb
                    ttag = "tr1" if oh == 0 else "tr2"
                    dve("tensor_tensor", tt, psb[bank][:, :], g1row[:, oh * 512:(oh + 1) * 512], ALU.mult,
                        reads=[f"psb{bank}", "g1row"], writes=[ttag])
                    pool("tensor_tensor", xq[slot][:, oh * 512:(oh + 1) * 512], tt, xq[slot][:, oh * 512:(oh + 1) * 512],
                         ALU.add, reads=[ttag, f"xq{slot}"], writes=[f"xq{slot}"])
                dma("pool", f"xq{slot}", x1_d[r0:r0 + 128, :], xq[slot], reads=[f"xq{slot}"], writes=["x1_dram"])
                if nxt is not None:
                    norm_back(nxt // 4, s, gs1T3, "gs1T", hT3, "hT", hn)

        if nblk_a > 0:
            stats_rstd_A(0)
            norm_transpose(dr["x"], 0, 0, rstdA, gs1T3, "gs1T", hT3, "hT")
        for blk in range(nblk_a):
            b = blk // 4
            qpos = blk % 4
            if qpos == 0:
                row_from_col(g1row, modT3, 16, b, "modT", "g1row")
                dve("memset", carry_re, 0.0, reads=[], writes=["carry_re"])
                dve("memset", carry_im, 0.0, reads=[], writes=["carry_im"])
                dve("memset", vcarry, 0.0, reads=[], writes=["vcarry"])
            if blk == 0:
                tap("hT", hT, "hT")
            if blk + 1 < nblk_a:
                stats_rstd_A(blk + 1)
            pre = [ring_load(gate_src(0, 0)), ring_load(gate_src(1, 0)), ring_load(gate_src(0, 1))]
            emit_U(b)
            emit_relayout(0); emit_relayout(1)
            emit_E(0); emit_chain(0); emit_CONV(0, b)
            emit_relayout(2)
            emit_E(1); emit_chain(1); emit_Y(0); emit_CONV(1, b)
            emit_relayout(3)
            emit_E(2); emit_chain(2); emit_Y(1); emit_CONV(2, b)
            emit_E(3); emit_chain(3); emit_Y(2); emit_CONV(3, b)
            emit_Y(3)
            if blk == 0:
                tap("U4", U4, "U4_3")
            emit_GLU()
            if blk == 0:
                tap("ys", ys, "ys"); tap("yglu", yglu, "yglu"); tap("yc", yc, "yc")
            wo = emit_MERGE(b, pre)
            if blk == 0:
                tap("mT", mT, "mT")
            emit_WOUT(blk, blk + 1 if blk + 1 < nblk_a else None, wo)
        P.barrier()
        if "x1" in tap_d:
            dma("sp", "tapx1a", xr[0], x1_d[0:128, :], reads=["x1_dram"], writes=["xr0"])
            dma("sp", "tapx1b", tap_d["x1"], xr[0], reads=["xr0"], writes=["tapout_x1"])
            P.barrier()

        pos[0] = persistB_end
        W1f = alloc(8 * 4096, BF16); W1f3 = r3(W1f, 8)
        W2f = alloc(32 * 1024, BF16); W2f3 = r3(W2f, 32)
        for q in range(8):
            src = dr["w_ff1"][:, q * 512:(q + 1) * 512].rearrange("(k p) n -> p k n", p=128)
            dst = W1f3[:, :, q * 512:(q + 1) * 512]
            dma("pool", f"ld_w1f{q}", dst, src, writes=[f"W1f_q{q}"])
        for q in range(8):
            src = dr["w_ff2"][q * 512:(q + 1) * 512, :].rearrange("(k p) n -> p k n", p=128)
            dst = W2f3[:, q * 4:(q + 1) * 4, :]
            dma("pool", f"ld_w2f{q}", dst, src, writes=[f"W2f_q{q}"])
        xr = [alloc(1024) for _ in range(2)]
        hn = alloc(1024, BF16); junk = alloc(1024, BF16)
        h2T = alloc(8 * 512, BF16); h2T3 = r3(h2T, 8)
        hid = alloc(32 * 512, BF16); hid3 = r3(hid, 32)
        rl = [alloc(512, BF16) for _ in range(2)]
        tr1 = alloc(512)
        x2 = [alloc(1024) for _ in range(2)]
        g2row = alloc(1024); fgrow = alloc(1024)
        diagt = alloc(128)
        ssB = alloc(4); rstdB = alloc(4); ss2 = alloc(1); rstd2 = alloc(1)
        print("phase B arena use:", pos[0], "of", ARENA)
        load_group("sp", "ld_fg", [(fgrow, dr["fg_row"], "fgrow")])
        sh2b3 = r3(sh2b, 8)
        for ch in range(32):
            mmgroup([(psb[0][:, ch * 4:(ch + 1) * 4], W1f3[:, k, ch * 128:(ch + 1) * 128], sh2b3[:, k, :],
                      k == 0, k == 7, None) for k in range(8)], reads=[f"W1f_q{ch // 4}", "sh2b"], writes=["psb0"])
        dve("tensor_copy", bias_ff1, psb[0][:, 0:128], reads=["psb0"], writes=["bias_ff1"])

        if nblk_b > 0:
            stats_rstd(x1_d, 0, ssB, rstdB, "B")
            norm_transpose(x1_d, 0, 0, rstdB, gs2T3, "gs2T", h2T3, "h2T")
        for blk in range(nblk_b):
            b = blk // 4
            if blk % 4 == 0:
                row_from_col(g2row, modT3, 40, b, "modT", "g2row")
            if blk + 1 < nblk_b:
                stats_rstd(x1_d, blk + 1, ssB, rstdB, "B")
            if blk == 0:
                tap("h2T", h2T, "h2T")
            for hc in range(32):
                bank = 2 + hc % 2
                mmgroup([(psb[bank][:, :], W1f3[:, k, hc * 128:(hc + 1) * 128], h2T3[:, k, :], k == 0, k == 7, None)
                         for k in range(8)], reads=[f"W1f_q{hc // 4}", "h2T"], writes=[f"psb{bank}"])
                act(rl[hc % 2], psb[bank][:, :], AF.Relu, reads=[f"psb{bank}", "bias_ff1"], writes=[f"rl{hc % 2}"],
                    bias=bias_ff13[:, hc, b:b + 1])
                dve("tensor_tensor", hid3[:, hc, :], rl[hc % 2], rl[hc % 2], ALU.mult,
                    reads=[f"rl{hc % 2}"], writes=["hid"])
            for s in range(4):
                slot = s % 2
                r0 = blk * BLK + s * 128
                dma("sp", f"xr{slot}", xr[slot], x1_d[r0:r0 + 128, :], writes=[f"xr{slot}"])
                if blk + 1 < nblk_b:
                    norm_front(x1_d, blk + 1, s, rstdB, xr, hn, (s + 1) % 2)
                for oh in range(2):
                    bank = 4 + oh
                    mmgroup([(psb[bank][:, :], hid3[:, k, s * 128:(s + 1) * 128], W2f3[:, k, oh * 512:(oh + 1) * 512],
                              k == 0, k == 31, None) for k in range(32)],
                            reads=["hid"] + [f"W2f_q{q}" for q in range(8)], writes=[f"psb{bank}"])
                    dve("tensor_tensor", tr1, psb[bank][:, :], g2row[:, oh * 512:(oh + 1) * 512], ALU.mult,
                        reads=[f"psb{bank}", "g2row"], writes=["tr1"])
                    dve("tensor_tensor", x2[slot][:, oh * 512:(oh + 1) * 512], tr1, xr[slot][:, oh * 512:(oh + 1) * 512],
                        ALU.add, reads=["tr1", f"xr{slot}"], writes=[f"x2{slot}"])
                if blk + 1 < nblk_b:
                    norm_back((blk + 1) // 4, s, gs2T3, "gs2T", h2T3, "h2T", hn)
                dve("memset", ss2, 0.0, reads=[], writes=["ss2"])
                act(junk, x2[slot], AF.Square, reads=[f"x2{slot}"], writes=["junk", "ss2"], accum_out=ss2[:, 0:1])
                dve("tensor_scalar", rstd2, ss2, 1.0 / D, EPS, ALU.mult, ALU.add, reads=["ss2"], writes=["rstd2"])
                act(rstd2, rstd2, AF.Sqrt, reads=["rstd2"], writes=["rstd2"])
                dve("reciprocal", rstd2, rstd2, reads=["rstd2"], writes=["rstd2"])
                dve("scalar_tensor_tensor", x2[slot], x2[slot], rstd2[:, 0:1], fgrow, ALU.mult, ALU.mult,
                    reads=[f"x2{slot}", "rstd2", "fgrow"], writes=[f"x2{slot}"])
                dma("pool", f"x2{slot}", out_d[r0:r0 + 128, :], x2[slot], reads=[f"x2{slot}"], writes=["out_dram"])
        P.frozen = False
        P.barrier()
        P.emit()
    return nc


_CACHE = {}


def kernel(**inputs):
    inp = {k: np.asarray(v) for k, v in inputs.items()}
    shared = _shared_inputs(inp)
    x = np.ascontiguousarray(inp["x"], dtype=np.float32)
    c = np.asarray(inp["c"], dtype=np.float32)
    in_maps = []
    for i in range(NCORES):
        m = dict(shared)
        m["x"] = x[NB * i:NB * (i + 1)].reshape(NTOK, D)
        cc = c[NB * i:NB * (i + 1)]
        m["cT"] = np.ascontiguousarray(cc.T.reshape(8, 128, NB).transpose(1, 0, 2).reshape(128, 32))
        in_maps.append(m)
    if "nc" not in _CACHE:
        _CACHE["nc"] = build_program()
    res = run_bass_kernel_spmd(_CACHE["nc"], in_maps, core_ids=list(range(NCORES)))
    out = np.stack([np.asarray(r["out"]).reshape(NB, SEQ, D) for r in res.results], axis=0)
    return out.reshape(NCORES * NB, SEQ, D).astype(np.float32)
```

```python
import contextlib
import math
import numpy as np
import concourse.bass as bass
import concourse.mybir as mybir
from concourse.bass_utils import run_bass_kernel_spmd

F32 = mybir.dt.float32
BF16 = mybir.dt.bfloat16
I32 = mybir.dt.int32
AF = mybir.ActivationFunctionType
ALU = mybir.AluOpType

NCORES = 8
D = 1024
SEQ = 2048
NB = 4
NTOK = NB * SEQ
BLK = 512
NBLK = NTOK // BLK
EPS = 1e-6
TWO_PI = 2.0 * math.pi
EPOCH = 24000
DEBUG = False


class Prog:
    ENGS = ("pe", "act", "dve", "pool", "sp")
    COMPUTE = ("pe", "act", "dve", "pool")

    def __init__(self, nc):
        self.nc = nc
        self.ops = {e: [] for e in self.ENGS}
        self.count = {e: 0 for e in self.COMPUTE}
        self.dma_count = {}
        self.last_write = {}
        self.readers = {}
        self.waited = {e: {} for e in self.ENGS}

    def _deps(self, eng, reads, writes, skip_key=None):
        writes = list(writes) + [t for t in reads if t.startswith("psb") and t not in writes]
        deps = []
        for t in reads:
            lw = self.last_write.get(t)
            if lw is not None:
                deps.append(lw)
        for t in writes:
            lw = self.last_write.get(t)
            if lw is not None:
                deps.append(lw)
            deps.extend(self.readers.get(t, ()))
        out = {}
        for key, val in deps:
            if key == eng and eng == "pe":
                continue
            if key == skip_key:
                continue
            if self.waited[eng].get(key, 0) >= val:
                continue
            if out.get(key, 0) < val:
                out[key] = val
        for key, val in out.items():
            self.waited[eng][key] = val
        return list(out.items())

    def _commit(self, sig, reads, writes):
        writes = list(writes) + [t for t in reads if t.startswith("psb") and t not in writes]
        for t in writes:
            self.last_write[t] = sig
            self.readers[t] = []
        for t in reads:
            if t not in writes:
                self.readers.setdefault(t, []).append(sig)

    frozen = False

    def op(self, eng, fn, reads=(), writes=()):
        if self.frozen:
            return
        waits = self._deps(eng, reads, writes)
        self.count[eng] += 1
        sig = (eng, self.count[eng])
        self._commit(sig, reads, writes)
        self.ops[eng].append((fn, waits, sig, 1))

    def dma(self, eng, sem, fn, reads=(), writes=(), final=None):
        if self.frozen:
            return
        waits = self._deps(eng, reads, writes, skip_key=(sem if final is not None else None))
        self.dma_count[sem] = self.dma_count.get(sem, 0) + 16
        sig = (sem, self.dma_count[sem] if final is None else final)
        self._commit(sig, reads, writes)
        self.ops[eng].append((fn, waits, (sem, self.dma_count[sem]), 16))

    def barrier(self):
        if self.frozen:
            return
        sigs = [(e, c) for e, c in self.count.items() if c > 0]
        sigs += [(s, c) for s, c in self.dma_count.items()]
        for e in self.ENGS:
            waits = []
            for key, val in sigs:
                if key == e and e == "pe":
                    continue
                if self.waited[e].get(key, 0) >= val:
                    continue
                self.waited[e][key] = val
                waits.append((key, val))
            if waits:
                self.ops[e].append((None, waits, None, 0))

    def emit(self):
        nc = self.nc
        with contextlib.ExitStack() as st:
            sems = {}

            def get(key, val):
                if key in self.COMPUTE:
                    k = (val - 1) // EPOCH
                    loc = val - k * EPOCH
                    name = f"s_{key}_{k}"
                else:
                    name, loc = f"d_{key}", val
                if name not in sems:
                    sems[name] = st.enter_context(nc.semaphore(name))
                return sems[name], loc

            for e in self.ENGS:
                for fn, waits, sig, inc in self.ops[e]:
                    for key, val in waits:
                        get(key, val)
                    if sig is not None:
                        get(*sig)
            block = st.enter_context(nc.Block())

            def run(e):
                def body(engine):
                    for fn, waits, sig, inc in self.ops[e]:
                        for key, val in waits:
                            s, loc = get(key, val)
                            engine.wait_ge(s, loc)
                        if fn is None:
                            continue
                        ins = fn(engine)
                        s, _ = get(*sig)
                        ins.then_inc(s, inc)
                return body

            block.tensor(run("pe"))
            block.scalar(run("act"))
            block.vector(run("dve"))
            block.gpsimd(run("pool"))
            block.sync(run("sp"))


def _consts():
    r = np.arange(128)
    j4 = r // 32
    g2_c = (r // 16) % 2
    g2_s = r // 64
    maskY = (g2_s[:, None] == g2_c[None, :]).astype(np.float32)
    maskX = np.ascontiguousarray(maskY.T)
    causal = (j4[None, :] >= j4[:, None]).astype(np.float32)
    kY = np.broadcast_to((j4 + 1).astype(np.float32)[None, :], (128, 128)).copy()
    kX = (3 - j4).astype(np.float32).reshape(128, 1)
    iota = np.broadcast_to((np.arange(130) - 1).astype(np.float32)[None, :], (128, 130)).copy()
    ident = np.eye(128, dtype=np.float32)
    ones = np.ones((128, 128), np.float32)
    return dict(maskY=maskY, maskX=maskX, causal=causal, kY=kY, kX=kX, iota=iota,
                ident=ident, ones=ones)


def _shared_inputs(inp):
    f = np.float32
    A = lambda a: np.ascontiguousarray(a, dtype=f)
    d = {}
    d["w_ada"] = A(inp["w_ada"][0])
    d["b_adaT"] = A(inp["b_ada"][0].reshape(48, 128).T)
    d["n1T"] = A(inp["norm1_g"][0].reshape(8, 128).T)
    d["n2T"] = A(inp["norm2_g"][0].reshape(8, 128).T)
    d["fg_row"] = A(np.broadcast_to(inp["final_g"][None, :], (128, 1024)))
    d["w_in"] = A(inp["w_in"][0])
    d["w_glu"] = A(inp["w_glu"][0])
    d["b_gluT"] = A(inp["b_glu"][0].reshape(4, 128).T)
    d["conv_wT"] = A(inp["conv_w"][0].reshape(3, 4, 128).transpose(2, 1, 0).reshape(128, 12))
    dsk = inp["d_skip"][0].reshape(16, 2, 16).transpose(1, 2, 0).reshape(32, 16)
    d["dX"] = A(np.tile(dsk, (4, 1)))
    d["w_ps"] = A(inp["w_proj_ssm"][0])
    d["w_pc"] = A(inp["w_proj_conv"][0])
    d["w_out"] = A(inp["w_out"][0])
    d["w_ff1"] = A(inp["w_ff1"][0])
    d["w_ff2"] = A(inp["w_ff2"][0])
    lre = inp["lam_re"][0].reshape(16, 2, 64)
    lim = inp["lam_im"][0].reshape(16, 2, 64)
    ldt = inp["log_dt"][0].reshape(16, 2)
    d["lamreY"] = A(lre.transpose(1, 2, 0).reshape(128, 16))
    d["lamimY"] = A(lim.transpose(1, 2, 0).reshape(128, 16))
    d["logdtY"] = A(np.repeat(ldt.T[:, None, :], 64, axis=1).reshape(128, 16))

    def expY(a_p_gp_g2_h):
        t = a_p_gp_g2_h[None, :, :, None, :, :]
        t = np.broadcast_to(t, (2, 64, 16, 4, 2, 16))
        return A(t.reshape(128, 16 * 128))
    d["cYre"] = expY(inp["c_re"][0].reshape(16, 2, 16, 64).transpose(3, 0, 1, 2))
    d["cYim"] = expY(inp["c_im"][0].reshape(16, 2, 16, 64).transpose(3, 0, 1, 2))
    d["bYre"] = expY(inp["b_re"][0].reshape(16, 2, 64, 16).transpose(2, 0, 1, 3))
    d["bYim"] = expY(inp["b_im"][0].reshape(16, 2, 64, 16).transpose(2, 0, 1, 3))
    d["lamreX"] = A(np.broadcast_to(inp["lam_re"][0].reshape(1, 2048), (128, 2048)))
    d["lamimX"] = A(np.broadcast_to(inp["lam_im"][0].reshape(1, 2048), (128, 2048)))
    d["logdtX"] = A(np.broadcast_to(np.repeat(ldt, 64, axis=1).reshape(1, 2048), (128, 2048)))

    def expX(a_h_gp_g2_p):
        t = a_h_gp_g2_p[None, None, :, :, :, :]
        t = np.broadcast_to(t, (4, 2, 16, 16, 2, 64))
        return A(t.reshape(128, 2048))
    d["bXre"] = expX(inp["b_re"][0].reshape(16, 2, 64, 16).transpose(3, 0, 1, 2))
    d["bXim"] = expX(inp["b_im"][0].reshape(16, 2, 64, 16).transpose(3, 0, 1, 2))
    d.update(_consts())
    return d


IN_SHAPES = dict(
    x=[NTOK, D], cT=[128, 32], w_ada=[1024, 6144], b_adaT=[128, 48], n1T=[128, 8], n2T=[128, 8],
    fg_row=[128, 1024], w_in=[1024, 4096], w_glu=[512, 512], b_gluT=[128, 4], conv_wT=[128, 12],
    dX=[128, 16], w_ps=[512, 1024], w_pc=[512, 1024], w_out=[1024, 1024], w_ff1=[1024, 4096],
    w_ff2=[4096, 1024], lamreY=[128, 16], lamimY=[128, 16], logdtY=[128, 16],
    cYre=[128, 2048], cYim=[128, 2048], bYre=[128, 2048], bYim=[128, 2048],
    lamreX=[128, 2048], lamimX=[128, 2048], logdtX=[128, 2048], bXre=[128, 2048], bXim=[128, 2048],
    maskY=[128, 128], maskX=[128, 128], causal=[128, 128], kY=[128, 128], kX=[128, 1],
    iota=[128, 130], ident=[128, 128], ones=[128, 128],
)


def build_program(nblk_a=NBLK, nblk_b=NBLK, taps=(), stop_at=None):
    nc = bass.Bass("TRN2", target_bir_lowering=False)
    dr = {k: nc.dram_tensor(k, s, F32, kind="ExternalInput").ap() for k, s in IN_SHAPES.items()}
    out_d = nc.dram_tensor("out", [NTOK, D], F32, kind="ExternalOutput").ap()
    x1_d = nc.dram_tensor("x1s", [NTOK, D], F32, kind="Internal").ap()
    wg16_d = nc.dram_tensor("wg16", [1024, 2048], BF16, kind="Internal").ap()
    wo16_d = nc.dram_tensor("wo16", [1024, 1024], BF16, kind="Internal").ap()
    w1f16_d = nc.dram_tensor("w1f16", [1024, 4096], BF16, kind="Internal").ap()
    w2f16_d = nc.dram_tensor("w2f16", [4096, 1024], BF16, kind="Internal").ap()
    tap_d = {}
    for name, shape in taps:
        tap_d[name] = nc.dram_tensor("tap_" + name, shape, F32, kind="ExternalOutput").ap()

    with contextlib.ExitStack() as st:
        ARENA = 52800
        arena = st.enter_context(nc.sbuf_tensor("arena", [128, ARENA], F32))
        psb = [st.enter_context(nc.psum_tensor(f"psb{i}", [128, 512], F32)) for i in range(8)]
        P = Prog(nc)
        pos = [0]
        uniq = [0]

        def alloc(n, dtype=F32):
            nf = n if dtype == F32 else (n + 1) // 2
            assert pos[0] + nf <= ARENA, f"SBUF arena overflow {pos[0]}+{nf}"
            v = arena[:, pos[0]:pos[0] + nf]
            pos[0] += nf
            if dtype != F32:
                v = v.bitcast(dtype)
            return v

        def r3(ap, a):
            return ap.rearrange("p (a b) -> p a b", a=a)

        def pbf(i):
            return psb[i][:, :].bitcast(BF16)

        def dve(fn, *a, reads, writes, **kw):
            P.op("dve", lambda e: getattr(e, fn)(*a, **kw), reads=reads, writes=writes)

        def act(out, in_, func, reads, writes, **kw):
            P.op("act", lambda e: e.activation(out=out, in_=in_, func=func, **kw), reads=reads, writes=writes)

        def mmgroup(items, reads, writes):
            def fn(e):
                ins = None
                for (o, l, r, s0, s1, tp) in items:
                    if tp is None:
                        ins = e.matmul(o, l, r, start=s0, stop=s1)
                    else:
                        ins = e.matmul(o, l, r, start=s0, stop=s1, tile_position=tp)
                return ins
            P.op("pe", fn, reads=reads, writes=writes)

        def dma(eng, sem, out, in_, reads=(), writes=(), final=None):
            P.dma(eng, sem, lambda e: e.dma_start(out=out, in_=in_), reads=reads, writes=writes, final=final)

        def load_group(eng, sem, items):
            base = P.dma_count.get(sem, 0)
            final = base + 16 * len(items)
            for (o, i, tag) in items:
                P.dma(eng, sem, lambda e, o=o, i=i: e.dma_start(out=o, in_=i), writes=[tag], final=final)

        def checkpoint(name):
            if stop_at == name:
                P.barrier()
                P.frozen = True

        def tap(name, src, tag):
            if name in tap_d:
                if src.dtype == BF16:
                    src = src.bitcast(F32)
                uniq[0] += 1
                P.dma("sp", f"tap{uniq[0]}", lambda e, src=src, name=name: e.dma_start(out=tap_d[name], in_=src),
                      reads=[tag], writes=["tapout_" + name])

        ident32 = alloc(128); ones32 = alloc(128)
        ident16 = alloc(128, BF16)
        cT = alloc(32); b_adaT = alloc(48); n1T = alloc(8); n2T = alloc(8)
        b_gluT = alloc(4); conv_wT = alloc(12); dX = alloc(16)
        modT = alloc(192)
        gs1T = alloc(32); gs2T = alloc(32)
        bias_in = alloc(128)
        bias_ff1 = alloc(128)
        sgc = alloc(32); scb = alloc(32, BF16)
        sh1b = alloc(32, BF16); sh2b = alloc(32, BF16)
        persistB_end = pos[0]
        rT = alloc(16)
        cosT = alloc(16 * 130); sinT = alloc(16 * 130)
        carry_re = alloc(16); carry_im = alloc(16)
        vcarry = alloc(8)
        ssA = alloc(4); rstdA = alloc(4)
        Wssm = alloc(16 * 5 * 128, BF16)
        Wssm4 = Wssm.rearrange("p (g w m) -> p g w m", g=16, w=5)
        modT3 = r3(modT, 48); gs1T3 = r3(gs1T, 8); gs2T3 = r3(gs2T, 8)
        bias_in3 = r3(bias_in, 32); bias_ff13 = r3(bias_ff1, 32)
        cosT3 = r3(cosT, 16); sinT3 = r3(sinT, 16)
        persist_end = pos[0]

        load_group("sp", "ld_small", [
            (ident32, dr["ident"], "ident32"), (ones32, dr["ones"], "ones32"),
            (cT, dr["cT"], "cT"), (b_adaT, dr["b_adaT"], "b_adaT"), (n1T, dr["n1T"], "n1T"),
            (n2T, dr["n2T"], "n2T"), (b_gluT, dr["b_gluT"], "b_gluT"),
            (conv_wT, dr["conv_wT"], "conv_wT"), (dX, dr["dX"], "dX")])
        dve("tensor_copy", ident16, ident32, reads=["ident32"], writes=["ident16"])
        for q in range(4):
            dma("pool", f"precast{q}", wg16_d[:, q * 512:(q + 1) * 512].rearrange("(k p) n -> p k n", p=128),
                dr["w_in"][:, 2048 + q * 512:2048 + (q + 1) * 512].rearrange("(k p) n -> p k n", p=128),
                writes=[f"wg16_{q}"])
        for q in range(2):
            dma("pool", f"precast{4 + q}", wo16_d[:, q * 512:(q + 1) * 512].rearrange("(k p) n -> p k n", p=128),
                dr["w_out"][:, q * 512:(q + 1) * 512].rearrange("(k p) n -> p k n", p=128),
                writes=[f"wo16_{q}"])

        act(sgc, cT, AF.Sigmoid, reads=["cT"], writes=["sgc"])
        dve("tensor_tensor", scb, cT, sgc, ALU.mult, reads=["cT", "sgc"], writes=["scb"])
        scb3 = r3(scb, 8)
        scf = alloc(32)
        dve("tensor_tensor", scf, cT, sgc, ALU.mult, reads=["cT", "sgc"], writes=["scf"])
        scf3 = r3(scf, 8)
        wada_ring = [alloc(8 * 128) for _ in range(2)]
        for j in range(48):
            slot = j % 2
            wt = r3(wada_ring[slot], 8)
            src = dr["w_ada"][:, j * 128:(j + 1) * 128].rearrange("(k p) n -> p k n", p=128)
            dma("act", f"wada{slot}", wt, src, writes=[f"wada{slot}"])
            mmgroup([(psb[0][:, j * 4:(j + 1) * 4], wt[:, k, :], scf3[:, k, :], k == 0, k == 7, None)
                     for k in range(8)], reads=[f"wada{slot}", "scf"], writes=["psb0"])
        dve("tensor_tensor", modT3, r3(psb[0][:, 0:192], 48),
            b_adaT.unsqueeze(2).to_broadcast([128, 48, 4]), ALU.add,
            reads=["psb0", "b_adaT"], writes=["modT"])
        dve("scalar_tensor_tensor", gs1T3, modT3[:, 8:16, :], 1.0, n1T.unsqueeze(2).to_broadcast([128, 8, 4]),
            ALU.add, ALU.mult, reads=["modT", "n1T"], writes=["gs1T"])
        dve("scalar_tensor_tensor", gs2T3, modT3[:, 32:40, :], 1.0, n2T.unsqueeze(2).to_broadcast([128, 8, 4]),
            ALU.add, ALU.mult, reads=["modT", "n2T"], writes=["gs2T"])
        dve("tensor_copy", r3(sh1b, 8), modT3[:, 0:8, :], reads=["modT"], writes=["sh1b"])
        dve("tensor_copy", r3(sh2b, 8), modT3[:, 24:32, :], reads=["modT"], writes=["sh2b"])
        tap("modT", modT, "modT")
        checkpoint("mod")

        BIG = 2048
        TMPN = 16 * 130
        T_i = alloc(TMPN).bitcast(I32); T_f = alloc(TMPN); T_red = alloc(TMPN)
        T_c1 = alloc(TMPN); T_c2 = alloc(TMPN)
        setup_scratch = pos[0]

        def big():
            return alloc(BIG)

        def b3(ap):
            return r3(ap, 16)

        def range_reduce(dst, src, shift, n3=None, tagd=None, tags=None):
            n = dst.shape[-1]
            ti = T_i[:, 0:n]; tf = T_f[:, 0:n]
            dve("tensor_scalar", tf, src, 1.0 / TWO_PI, shift / TWO_PI, ALU.mult, ALU.add,
                reads=[tags], writes=["rr_f"])
            dve("tensor_copy", ti, tf, reads=["rr_f"], writes=["rr_i"])
            dve("tensor_copy", tf, ti, reads=["rr_i"], writes=["rr_f"])
            dve("scalar_tensor_tensor", tf, tf, -TWO_PI, src, ALU.mult, ALU.add,
                reads=["rr_f", tags], writes=["rr_f"])
            dve("tensor_scalar", dst, tf, shift, None, ALU.add, reads=["rr_f"], writes=[tagd])
            dve("tensor_scalar", dst, dst, -math.pi, math.pi, ALU.max, ALU.min, reads=[tagd], writes=[tagd])

        def sincos(sin_dst, cos_dst, ang, tag_ang, tag_s, tag_c, n):
            red = T_red[:, 0:n]
            range_reduce(red, ang, 0.0, tagd="red", tags=tag_ang)
            act(sin_dst, red, AF.Sin, reads=["red"], writes=[tag_s])
            range_reduce(red, ang, math.pi / 2, tagd="red", tags=tag_ang)
            act(cos_dst, red, AF.Sin, reads=["red"], writes=[tag_c])

        def cmul(ore, oim, are, aim, bre, bim, tags_a, tags_b, tag_o, n, conj_b=False):
            t1 = T_c1[:, 0:n]; t2 = T_c2[:, 0:n]
            shp = list(ore.shape)

            def v(ap):
                return ap if len(shp) == 2 else ap.rearrange("p (a b) -> p a b", a=shp[1])
            dve("tensor_tensor", v(t1), are, bre, ALU.mult, reads=tags_a + tags_b, writes=["cm1"])
            dve("tensor_tensor", v(t2), aim, bim, ALU.mult, reads=tags_a + tags_b, writes=["cm2"])
            dve("tensor_tensor", ore, v(t1), v(t2), ALU.add if conj_b else ALU.subtract,
                reads=["cm1", "cm2"], writes=[tag_o + "re"])
            dve("tensor_tensor", v(t1), are, bim, ALU.mult, reads=tags_a + tags_b, writes=["cm1"])
            dve("tensor_tensor", v(t2), aim, bre, ALU.mult, reads=tags_a + tags_b, writes=["cm2"])
            dve("tensor_tensor", oim, v(t2), v(t1), ALU.subtract if conj_b else ALU.add,
                reads=["cm1", "cm2"], writes=[tag_o + "im"])

        lamreY = alloc(16); lamimY = alloc(16); logdtY = alloc(16)
        cYre = big(); cYim = big(); bYre = big(); bYim = big()
        maskY = alloc(128); causal = alloc(128); kY = alloc(128); iota = alloc(130)
        load_group("sp", "ld_ssmY", [
            (lamreY, dr["lamreY"], "lamreY"), (lamimY, dr["lamimY"], "lamimY"), (logdtY, dr["logdtY"], "logdtY"),
            (cYre, dr["cYre"], "cYre"), (cYim, dr["cYim"], "cYim"), (bYre, dr["bYre"], "bYre"),
            (bYim, dr["bYim"], "bYim"), (maskY, dr["maskY"], "maskY"), (causal, dr["causal"], "causal"),
            (kY, dr["kY"], "kY"), (iota, dr["iota"], "iota")])
        dtY = alloc(16); lrd = alloc(16); lid = alloc(16)
        act(dtY, logdtY, AF.Exp, reads=["logdtY"], writes=["dtY"])
        dve("tensor_tensor", lrd, lamreY, dtY, ALU.mult, reads=["lamreY", "dtY"], writes=["lrd"])
        dve("tensor_tensor", lid, lamimY, dtY, ALU.mult, reads=["lamimY", "dtY"], writes=["lid"])
        z = alloc(16); acc = alloc(16)
        dve("tensor_scalar", z, lrd, 4.0, None, ALU.mult, reads=["lrd"], writes=["z"])
        dve("tensor_scalar", acc, z, 1.0 / 5040.0, 1.0 / 720.0, ALU.mult, ALU.add, reads=["z"], writes=["acc"])
        for coef in (1.0 / 120.0, 1.0 / 24.0, 1.0 / 6.0, 0.5, 1.0, 1.0):
            dve("tensor_tensor", acc, acc, z, ALU.mult, reads=["acc", "z"], writes=["acc"])
            dve("tensor_scalar", acc, acc, coef, None, ALU.add, reads=["acc"], writes=["acc"])
        dve("tensor_copy", rT, acc, reads=["acc"], writes=["rT"])
        m1 = alloc(16); s1 = alloc(16); c1 = alloc(16)
        act(m1, lrd, AF.Exp, reads=["lrd"], writes=["m1"])
        sincos(s1, c1, lid, "lid", "s1", "c1", 16)
        a1re = alloc(16); a1im = alloc(16)
        dve("tensor_tensor", a1re, m1, c1, ALU.mult, reads=["m1", "c1"], writes=["a1re"])
        dve("tensor_tensor", a1im, m1, s1, ALU.mult, reads=["m1", "s1"], writes=["a1im"])
        dve("tensor_scalar", a1re, a1re, -1.0, None, ALU.add, reads=["a1re"], writes=["a1re"])
        den = alloc(16); t16 = alloc(16); qre = alloc(16); qim = alloc(16)
        dve("tensor_tensor", den, lamreY, lamreY, ALU.mult, reads=["lamreY"], writes=["den"])
        dve("tensor_tensor", t16, lamimY, lamimY, ALU.mult, reads=["lamimY"], writes=["t16"])
        dve("tensor_tensor", den, den, t16, ALU.add, reads=["den", "t16"], writes=["den"])
        dve("reciprocal", den, den, reads=["den"], writes=["den"])
        cmul(qre, qim, a1re, a1im, lamreY, lamimY, ["a1re", "a1im"], ["lamreY", "lamimY"], "q", 16, conj_b=True)
        dve("tensor_tensor", qre, qre, den, ALU.mult, reads=["qre", "den"], writes=["qre"])
        dve("tensor_tensor", qim, qim, den, ALU.mult, reads=["qim", "den"], writes=["qim"])
        bbYre = big(); bbYim = big()
        bc16 = lambda ap: ap.unsqueeze(2).to_broadcast([128, 16, 128])
        cmul(b3(bbYre), b3(bbYim), bc16(qre), bc16(qim), b3(bYre), b3(bYim),
             ["qre", "qim"], ["bYre", "bYim"], "bbY", BIG)
        argm = big(); ang = big()
        kYb = kY.unsqueeze(1).to_broadcast([128, 16, 128])
        dve("tensor_tensor", b3(argm), bc16(lrd), kYb, ALU.mult, reads=["lrd", "kY"], writes=["argm"])
        dve("tensor_tensor", b3(ang), bc16(lid), kYb, ALU.mult, reads=["lid", "kY"], writes=["ang"])
        sinA = big(); cosA = big()
        sincos(sinA, cosA, ang, "ang", "sinA", "cosA", BIG)
        mag = big()
        act(mag, argm, AF.Exp, reads=["argm"], writes=["mag"])
        Are = big(); Aim = big()
        dve("tensor_tensor", Are, mag, cosA, ALU.mult, reads=["mag", "cosA"], writes=["Are"])
        dve("tensor_tensor", Aim, mag, sinA, ALU.mult, reads=["mag", "sinA"], writes=["Aim"])
        Rre = bYre; Rim = bYim
        cmul(Rre, Rim, cYre, cYim, Are, Aim, ["cYre", "cYim", "bbYre", "bbYim"], ["Are", "Aim"], "R", BIG)
        mYb = maskY.unsqueeze(1).to_broadcast([128, 16, 128])
        dve("tensor_tensor", b3(Rre), b3(Rre), mYb, ALU.mult, reads=["Rre", "maskY"], writes=["Rre"])
        dve("scalar_tensor_tensor", b3(Rim), b3(Rim), -1.0, mYb, ALU.mult, ALU.mult,
            reads=["Rim", "maskY"], writes=["Rim"])
        dve("tensor_copy", Wssm4[:, :, 2, :], b3(Rre), reads=["Rre"], writes=["W2re"])
        dve("tensor_copy", Wssm4[:, :, 3, :], b3(Rim), reads=["Rim"], writes=["W2im"])
        act(mag, argm, AF.Exp, reads=["argm", "Are", "Aim"], writes=["mag"], scale=-1.0)
        dve("tensor_tensor", Are, mag, cosA, ALU.mult, reads=["mag", "cosA", "Rre", "Rim"], writes=["Are"])
        dve("scalar_tensor_tensor", Aim, mag, -1.0, sinA, ALU.mult, ALU.mult,
            reads=["mag", "sinA", "Rre", "Rim"], writes=["Aim"])
        Lre = cYre; Lim = cYim
        cmul(Lre, Lim, Are, Aim, bbYre, bbYim, ["Are", "Aim", "Rre", "Rim"], ["bbYre", "bbYim"], "L", BIG)
        dve("tensor_tensor", b3(Lre), b3(Lre), mYb, ALU.mult, reads=["Lre", "maskY"], writes=["Lre"])
        dve("tensor_tensor", b3(Lim), b3(Lim), mYb, ALU.mult, reads=["Lim", "maskY"], writes=["Lim"])
        kt = alloc(128)
        for gp in range(16):
            bank = psb[1 + gp % 2]
            mmgroup([(bank[:, 0:128], b3(Lre)[:, gp, :], b3(Rre)[:, gp, :], True, False, None),
                     (bank[:, 0:128], b3(Lim)[:, gp, :], b3(Rim)[:, gp, :], False, True, None)],
                    reads=["Lre", "Lim", "Rre", "Rim"], writes=[f"psb{1 + gp % 2}"])
            dve("tensor_tensor", kt, bank[:, 0:128], causal, ALU.mult,
                reads=[f"psb{1 + gp % 2}", "causal"], writes=["kt"])
            dve("scalar_tensor_tensor", Wssm4[:, gp, 4, :], ident32, dX[:, gp:gp + 1], kt, ALU.mult, ALU.add,
                reads=["ident32", "dX", "kt"], writes=["W1"])
        th4 = alloc(16); th4r = alloc(16)
        dve("tensor_scalar", th4, lid, 4.0, None, ALU.mult, reads=["lid"], writes=["th4"])
        range_reduce(th4r, th4, 0.0, tagd="th4r", tags="th4")
        angT = alloc(16 * 130)
        dve("tensor_tensor", r3(angT, 16), th4r.unsqueeze(2).to_broadcast([128, 16, 130]),
            iota.unsqueeze(1).to_broadcast([128, 16, 130]), ALU.mult, reads=["th4r", "iota"], writes=["angT"])
        def sincos_tab(dst, shift, tagd):
            red = T_red; tmp_i = T_i; tf = T_f
            dve("tensor_scalar", tf, angT, 1.0 / TWO_PI, shift / TWO_PI, ALU.mult, ALU.add,
                reads=["angT"], writes=["rr_f"])
            dve("tensor_copy", tmp_i, tf, reads=["rr_f"], writes=["rr_i"])
            dve("tensor_copy", tf, tmp_i, reads=["rr_i"], writes=["rr_f"])
            dve("scalar_tensor_tensor", tf, tf, -TWO_PI, angT, ALU.mult, ALU.add,
                reads=["rr_f", "angT"], writes=["rr_f"])
            dve("tensor_scalar", red, tf, shift, None, ALU.add, reads=["rr_f"], writes=["red"])
            dve("tensor_scalar", red, red, -math.pi, math.pi, ALU.max, ALU.min, reads=["red"], writes=["red"])
            act(dst, red, AF.Sin, reads=["red"], writes=[tagd])
        sincos_tab(sinT, 0.0, "sinT")
        sincos_tab(cosT, math.pi / 2, "cosT")
        tap("W2re", Rre, "Rre"); tap("cosT", cosT, "cosT"); tap("sinT", sinT, "sinT"); tap("rT", rT, "rT")
        checkpoint("ssmY")
        P.barrier()
        pos[0] = setup_scratch

        lamreX = big(); lamimX = big(); logdtX = big(); bXre = big(); bXim = big()
        maskX = alloc(128); kX = alloc(1)
        load_group("sp", "ld_ssmX", [
            (lamreX, dr["lamreX"], "lamreX"), (lamimX, dr["lamimX"], "lamimX"), (logdtX, dr["logdtX"], "logdtX"),
            (bXre, dr["bXre"], "bXre"), (bXim, dr["bXim"], "bXim"), (maskX, dr["maskX"], "maskX"),
            (kX, dr["kX"], "kX")])
        dtX = big(); lrdX = big(); lidX = big()
        act(dtX, logdtX, AF.Exp, reads=["logdtX"], writes=["dtX"])
        dve("tensor_tensor", lrdX, lamreX, dtX, ALU.mult, reads=["lamreX", "dtX"], writes=["lrdX"])
        dve("tensor_tensor", lidX, lamimX, dtX, ALU.mult, reads=["lamimX", "dtX"], writes=["lidX"])
        m1X = dtX
        act(m1X, lrdX, AF.Exp, reads=["lrdX", "lidX"], writes=["m1X"])
        s1X = big(); c1X = big()
        sincos(s1X, c1X, lidX, "lidX", "s1X", "c1X", BIG)
        a1reX = big(); a1imX = big()
        dve("tensor_tensor", a1reX, m1X, c1X, ALU.mult, reads=["m1X", "c1X"], writes=["a1reX"])
        dve("tensor_tensor", a1imX, m1X, s1X, ALU.mult, reads=["m1X", "s1X"], writes=["a1imX"])
        dve("tensor_scalar", a1reX, a1reX, -1.0, None, ALU.add, reads=["a1reX"], writes=["a1reX"])
        denX = s1X; tX = c1X
        dve("tensor_tensor", denX, lamreX, lamreX, ALU.mult, reads=["lamreX", "a1imX", "a1reX"], writes=["denX"])
        dve("tensor_tensor", tX, lamimX, lamimX, ALU.mult, reads=["lamimX", "a1imX", "a1reX"], writes=["tX"])
        dve("tensor_tensor", denX, denX, tX, ALU.add, reads=["denX", "tX"], writes=["denX"])
        dve("reciprocal", denX, denX, reads=["denX"], writes=["denX"])
        qreX = big(); qimX = big()
        cmul(qreX, qimX, a1reX, a1imX, lamreX, lamimX, ["a1reX", "a1imX"], ["lamreX", "lamimX"], "qX", BIG, conj_b=True)
        dve("tensor_tensor", qreX, qreX, denX, ALU.mult, reads=["qXre", "denX"], writes=["qXre"])
        dve("tensor_tensor", qimX, qimX, denX, ALU.mult, reads=["qXim", "denX"], writes=["qXim"])
        bbXre = a1reX; bbXim = a1imX
        cmul(bbXre, bbXim, qreX, qimX, bXre, bXim, ["qXre", "qXim", "denX"], ["bXre", "bXim"], "bbX", BIG)
        angX = qreX
        dve("tensor_scalar", angX, lidX, kX[:, 0:1], None, ALU.mult, reads=["lidX", "kX", "bbXre", "bbXim"], writes=["angX"])
        sinX = bXre; cosX = bXim
        sincos(sinX, cosX, angX, "angX", "sinX", "cosX", BIG)
        magX = qimX
        act(magX, lrdX, AF.Exp, reads=["lrdX", "bbXre", "bbXim"], writes=["magX"], scale=kX[:, 0:1])
        AXre = lamreX; AXim = lamimX
        dve("tensor_tensor", AXre, magX, cosX, ALU.mult, reads=["magX", "cosX", "denX", "qXre"], writes=["AXre"])
        dve("tensor_tensor", AXim, magX, sinX, ALU.mult, reads=["magX", "sinX", "denX", "qXre"], writes=["AXim"])
        W3re = lrdX; W3im = lidX
        cmul(W3re, W3im, AXre, AXim, bbXre, bbXim, ["AXre", "AXim", "angX", "magX"], ["bbXre", "bbXim"], "W3", BIG)
        mXb = maskX.unsqueeze(1).to_broadcast([128, 16, 128])
        dve("tensor_tensor", Wssm4[:, :, 0, :], b3(W3re), mXb, ALU.mult, reads=["W3re", "maskX"], writes=["W3re_b"])
        dve("tensor_tensor", Wssm4[:, :, 1, :], b3(W3im), mXb, ALU.mult, reads=["W3im", "maskX"], writes=["W3im_b"])
        tap("W3re", W3re, "W3re")
        checkpoint("ssmX")
        P.barrier()
        pos[0] = persist_end

        Win_lo = alloc(8 * 2048, BF16)
        Wglu = alloc(4 * 512, BF16)
        Wps = alloc(4 * 1024, BF16); Wpc = alloc(4 * 1024, BF16)
        Win3 = r3(Win_lo, 8); Wglu3 = r3(Wglu, 4); Wps3 = r3(Wps, 4); Wpc3 = r3(Wpc, 4)
        ring = [alloc(8 * 512, BF16) for _ in range(3)]
        for q in range(4):
            src = dr["w_in"][:, q * 512:(q + 1) * 512].rearrange("(k p) n -> p k n", p=128)
            dst = Win3[:, :, q * 512:(q + 1) * 512]
            dma("pool", f"ld_win{q}", dst, src, writes=[f"Win_q{q}"])
        load_group("pool", "ld_wA", [
            (Wglu3, dr["w_glu"].rearrange("(k p) n -> p k n", p=128), "Wglu"),
            (Wps3, dr["w_ps"].rearrange("(k p) n -> p k n", p=128), "Wps"),
            (Wpc3, dr["w_pc"].rearrange("(k p) n -> p k n", p=128), "Wpc")])
        checkpoint("wloadA")
        ring_use = [0]

        def ring_load(src):
            src_ap, src_tag = src
            slot = ring_use[0] % 3
            ring_use[0] += 1
            dst = r3(ring[slot], 8)
            dma("sp", f"ring{slot}", dst, src_ap, reads=[src_tag], writes=[f"ring{slot}"])
            return dst, f"ring{slot}"

        def gate_src(which, half):
            q = which * 2 + half
            return wg16_d[:, q * 512:(q + 1) * 512].rearrange("(k p) n -> p k n", p=128), f"wg16_{q}"

        def wout_src(half):
            return wo16_d[:, half * 512:(half + 1) * 512].rearrange("(k p) n -> p k n", p=128), f"wo16_{half}"

        sh1b3 = r3(sh1b, 8)
        for ch in range(16):
            mmgroup([(psb[0][:, ch * 4:(ch + 1) * 4], Win3[:, k, ch * 128:(ch + 1) * 128], sh1b3[:, k, :],
                      k == 0, k == 7, None) for k in range(8)],
                    reads=[f"Win_q{ch // 4}", "sh1b"], writes=["psb0"])
        for which in range(2):
            for half in range(2):
                gt, gtag = ring_load(gate_src(which, half))
                for cl in range(4):
                    ch = 16 + which * 8 + half * 4 + cl
                    mmgroup([(psb[0][:, ch * 4:(ch + 1) * 4], gt[:, k, cl * 128:(cl + 1) * 128], sh1b3[:, k, :],
                              k == 0, k == 7, None) for k in range(8)],
                            reads=[gtag, "sh1b"], writes=["psb0"])
        dve("tensor_copy", bias_in, psb[0][:, 0:128], reads=["psb0"], writes=["bias_in"])
        tap("bias_in", bias_in, "bias_in")
        checkpoint("bias_in")

        xr = [alloc(1024) for _ in range(2)]
        hn = alloc(1024, BF16)
        hT = alloc(8 * 512, BF16); hT3 = r3(hT, 8)
        uT = alloc(4 * 512, BF16); uT3 = r3(uT, 4)
        U4 = alloc(16 * 128, BF16); U43 = r3(U4, 16)
        Ebre = alloc(4 * 129); Ebim = alloc(4 * 129); Xre = alloc(4 * 129); Xim = alloc(4 * 129)
        Ebre3 = r3(Ebre, 4); Ebim3 = r3(Ebim, 4); Xre3 = r3(Xre, 4); Xim3 = r3(Xim, 4)
        tr1 = alloc(512); tr2 = alloc(512)
        tr13 = r3(tr1, 4); tr23 = r3(tr2, 4)
        Sre3b = [r3(alloc(4 * 128, BF16), 4) for _ in range(2)]
        Sim3b = [r3(alloc(4 * 128, BF16), 4) for _ in range(2)]
        tp1 = alloc(512); tp2 = alloc(512); tp13 = r3(tp1, 4); tp23 = r3(tp2, 4)
        ga = tr1; gb = tr2
        one_col = alloc(1)
        dve("memset", one_col, 1.0, reads=[], writes=["one_col"])
        ys = alloc(4 * 512, BF16); ys3 = r3(ys, 4)
        yglu = alloc(4 * 512, BF16); yglu3 = r3(yglu, 4)
        yc = alloc(4 * 512, BF16); yc3 = r3(yc, 4)
        cbS = alloc(512); ccS = alloc(512); vbuf = alloc(514); cacc = alloc(512)
        sgA = alloc(512, BF16); sgB = alloc(512, BF16)
        mT = alloc(8 * 512, BF16); mT3 = r3(mT, 8)
        g1row = alloc(1024)
        xq = [alloc(1024) for _ in range(2)]
        diagt = alloc(128)
        print("phase A arena use:", pos[0], "of", ARENA)

        def stats_rstd(src_d, blk, ss, rstd, tagp):
            dve("memset", ss, 0.0, reads=[], writes=[tagp + "ss"])
            for s in range(4):
                slot = s % 2
                r0 = blk * BLK + s * 128
                dma("sp", f"xr{slot}", xr[slot], src_d[r0:r0 + 128, :],
                      writes=[f"xr{slot}"])
                act(junk, xr[slot], AF.Square, reads=[f"xr{slot}"], writes=["junk", tagp + "ss"],
                    accum_out=ss[:, s:s + 1])
            dve("tensor_scalar", rstd, ss, 1.0 / D, EPS, ALU.mult, ALU.add, reads=[tagp + "ss"], writes=[tagp + "rstd"])
            act(rstd, rstd, AF.Sqrt, reads=[tagp + "rstd"], writes=[tagp + "rstd"])
            dve("reciprocal", rstd, rstd, reads=[tagp + "rstd"], writes=[tagp + "rstd"])

        def row_from_col(dst_row, colT3, kidx0, b, tag_col, tag_row):
            for half in range(2):
                bank = psb[4 + half]
                for kk in range(4):
                    k = half * 4 + kk
                    dve("tensor_scalar", diagt, ident32, colT3[:, kidx0 + k, b:b + 1], None, ALU.mult,
                        reads=["ident32", tag_col], writes=["diagt"])
                    mmgroup([(bank[:, kk * 128:(kk + 1) * 128], ones32, diagt, True, True, None)],
                            reads=["ones32", "diagt"], writes=[f"psb{4 + half}"])
                dve("tensor_copy", dst_row[:, half * 512:(half + 1) * 512], bank[:, :],
                    reads=[f"psb{4 + half}"], writes=[tag_row])

        def norm_front(src_d, blk, s, rstd, xr, hn, slot):
            r0 = blk * BLK + s * 128
            dma("sp", f"xr{slot}", xr[slot], src_d[r0:r0 + 128, :], writes=[f"xr{slot}"])
            dve("tensor_scalar", hn, xr[slot], rstd[:, s:s + 1], None, ALU.mult,
                reads=[f"xr{slot}", "Arstd", "Brstd"], writes=["hn"])

        def norm_back(b, s, gsT3, tag_gs, dstT3, tag_dst, hn):
            bank = s % 2
            ptb = pbf(bank)

            def fn(e, ptb=ptb, hn=hn, ident16=ident16):
                ins = None
                for k in range(8):
                    ins = e.transpose(ptb[:, k * 128:(k + 1) * 128], hn[:, k * 128:(k + 1) * 128], ident16)
                return ins
            P.op("pe", fn, reads=["hn", "ident16"], writes=[f"psb{bank}"])
            dve("tensor_tensor", dstT3[:, :, s * 128:(s + 1) * 128], r3(ptb, 8),
                gsT3[:, :, b:b + 1].to_broadcast([128, 8, 128]), ALU.mult,
                reads=[f"psb{bank}", tag_gs], writes=[tag_dst])

        def norm_transpose_sub(src_d, blk, b, s, rstd, gsT3, tag_gs, dstT3, tag_dst, xr, hn):
            norm_front(src_d, blk, s, rstd, xr, hn, s % 2)
            norm_back(b, s, gsT3, tag_gs, dstT3, tag_dst, hn)

        def norm_transpose(src_d, blk, b, rstd, gsT3, tag_gs, dstT3, tag_dst):
            for s in range(4):
                norm_transpose_sub(src_d, blk, b, s, rstd, gsT3, tag_gs, dstT3, tag_dst, xr, hn)

        def pool(fn, *a, reads, writes, **kw):
            P.op("pool", lambda e: getattr(e, fn)(*a, **kw), reads=reads, writes=writes)

        junkA = cacc.bitcast(BF16)

        def stats_rstd_A(blk):
            dve("memset", ssA, 0.0, reads=[], writes=["Ass"])
            for s in range(4):
                slot = s % 2
                r0 = blk * BLK + s * 128
                dma("sp", f"xr{slot}", xr[slot], dr["x"][r0:r0 + 128, :], writes=[f"xr{slot}"])
                act(junkA, xr[slot], AF.Square, reads=[f"xr{slot}"], writes=["cacc", "Ass"], accum_out=ssA[:, s:s + 1])
            dve("tensor_scalar", rstdA, ssA, 1.0 / D, EPS, ALU.mult, ALU.add, reads=["Ass"], writes=["Arstd"])
            act(rstdA, rstdA, AF.Sqrt, reads=["Arstd"], writes=["Arstd"])
            dve("reciprocal", rstdA, rstdA, reads=["Arstd"], writes=["Arstd"])

        def emit_U(b):
            for fc in range(4):
                bank = 2 + fc % 2
                mmgroup([(psb[bank][:, :], Win3[:, k, fc * 128:(fc + 1) * 128], hT3[:, k, :], k == 0, k == 7, None)
                         for k in range(8)], reads=["Win_q0", "hT"], writes=[f"psb{bank}"])
                act(uT3[:, fc, :], psb[bank][:, :], AF.Identity, reads=[f"psb{bank}", "bias_in"], writes=[f"uT{fc}"],
                    bias=bias_in3[:, fc, b:b + 1])

        def emit_relayout(fc):
            for gl in range(4):
                for j4 in range(4):
                    o_ = U43[32 * j4:32 * j4 + 32, fc * 4 + gl, :]
                    i_ = uT3[32 * gl:32 * gl + 32, fc, j4:512:4]
                    if (gl + j4) % 2 == 0:
                        act(o_, i_, AF.Copy, reads=[f"uT{fc}"], writes=[f"U4_{fc}"])
                    else:
                        pool("tensor_copy", o_, i_, reads=[f"uT{fc}"], writes=[f"U4p_{fc}"])

        def emit_E(fc):
            gps = range(fc * 4, fc * 4 + 4)
            mmgroup([(psb[6][:, gl * 128:(gl + 1) * 128], Wssm4[:, gp, 0, :], U43[:, gp, :], True, True, None)
                     for gl, gp in enumerate(gps)], reads=[f"U4_{fc}", f"U4p_{fc}", "W3re_b"], writes=["psb6"])
            mmgroup([(psb[7][:, gl * 128:(gl + 1) * 128], Wssm4[:, gp, 1, :], U43[:, gp, :], True, True, None)
                     for gl, gp in enumerate(gps)], reads=[f"U4_{fc}", f"U4p_{fc}", "W3im_b"], writes=["psb7"])

        def emit_chain(fc):
            gps = range(fc * 4, fc * 4 + 4)
            Er = r3(psb[6][:, :], 4); Ei = r3(psb[7][:, :], 4)
            cs = cosT3[:, fc * 4:fc * 4 + 4, 1:129]; sn = sinT3[:, fc * 4:fc * 4 + 4, 1:129]
            dve("tensor_tensor", tr13, Er, cs, ALU.mult, reads=["psb6", "cosT"], writes=["tr1"])
            dve("tensor_tensor", tr23, Ei, sn, ALU.mult, reads=["psb7", "sinT"], writes=["tr2"])
            dve("tensor_tensor", Ebre3[:, :, 1:129], tr13, tr23, ALU.add, reads=["tr1", "tr2"], writes=["Ebre"])
            dve("tensor_tensor", tr13, Ei, cs, ALU.mult, reads=["psb7", "cosT"], writes=["tr1"])
            dve("tensor_tensor", tr23, Er, sn, ALU.mult, reads=["psb6", "sinT"], writes=["tr2"])
            dve("tensor_tensor", Ebim3[:, :, 1:129], tr13, tr23, ALU.subtract, reads=["tr1", "tr2"], writes=["Ebim"])
            dve("tensor_copy", Ebre3[:, :, 0], carry_re[:, fc * 4:fc * 4 + 4], reads=["carry_re"], writes=["Ebre"])
            dve("tensor_copy", Ebim3[:, :, 0], carry_im[:, fc * 4:fc * 4 + 4], reads=["carry_im"], writes=["Ebim"])
            for gl, gp in enumerate(gps):
                rb = rT[:, gp:gp + 1].to_broadcast([128, 129])
                dve("tensor_tensor_scan", Xre3[:, gl, :], rb, Ebre3[:, gl, :], 0.0, ALU.mult, ALU.add,
                    reads=["rT", "Ebre"], writes=["Xre"])
                dve("tensor_tensor_scan", Xim3[:, gl, :], rb, Ebim3[:, gl, :], 0.0, ALU.mult, ALU.add,
                    reads=["rT", "Ebim"], writes=["Xim"])
            sb = fc % 2
            Sr = Sre3b[sb]; Si = Sim3b[sb]
            cs0 = cosT3[:, fc * 4:fc * 4 + 4, 0:128]; sn0 = sinT3[:, fc * 4:fc * 4 + 4, 0:128]
            pool("tensor_tensor", tp13, Xre3[:, :, 0:128], cs0, ALU.mult, reads=["Xre", "cosT"], writes=["tp1"])
            pool("tensor_tensor", tp23, Xim3[:, :, 0:128], sn0, ALU.mult, reads=["Xim", "sinT"], writes=["tp2"])
            pool("tensor_tensor", Sr, tp13, tp23, ALU.subtract, reads=["tp1", "tp2"], writes=[f"Sre{sb}"])
            pool("tensor_tensor", tp13, Xre3[:, :, 0:128], sn0, ALU.mult, reads=["Xre", "sinT"], writes=["tp1"])
            pool("tensor_tensor", tp23, Xim3[:, :, 0:128], cs0, ALU.mult, reads=["Xim", "cosT"], writes=["tp2"])
            pool("tensor_tensor", Si, tp13, tp23, ALU.add, reads=["tp1", "tp2"], writes=[f"Sim{sb}"])
            c9 = cosT3[:, fc * 4:fc * 4 + 4, 129]; s9 = sinT3[:, fc * 4:fc * 4 + 4, 129]
            t4a = tp1[:, 0:4]; t4b = tp2[:, 0:4]
            pool("tensor_tensor", t4a, Xre3[:, :, 128], c9, ALU.mult, reads=["Xre", "cosT"], writes=["tp1"])
            pool("tensor_tensor", t4b, Xim3[:, :, 128], s9, ALU.mult, reads=["Xim", "sinT"], writes=["tp2"])
            pool("tensor_tensor", carry_re[:, fc * 4:fc * 4 + 4], t4a, t4b, ALU.subtract,
                 reads=["tp1", "tp2"], writes=["carry_re"])
            pool("tensor_tensor", t4a, Xre3[:, :, 128], s9, ALU.mult, reads=["Xre", "sinT"], writes=["tp1"])
            pool("tensor_tensor", t4b, Xim3[:, :, 128], c9, ALU.mult, reads=["Xim", "cosT"], writes=["tp2"])
            pool("tensor_tensor", carry_im[:, fc * 4:fc * 4 + 4], t4a, t4b, ALU.add,
                 reads=["tp1", "tp2"], writes=["carry_im"])

        def emit_Y(fc):
            gps = range(fc * 4, fc * 4 + 4)
            sb = fc % 2
            Sr = Sre3b[sb]; Si = Sim3b[sb]
            items = []
            for j4 in range(4):
                for gl, gp in enumerate(gps):
                    o = psb[5][32 * gl:32 * gl + 32, j4:512:4]
                    tp = (0, 32 * gl)
                    items.append((o, Wssm4[:, gp, 4, 32 * j4:32 * j4 + 32], U43[:, gp, :], True, False, tp))
                    items.append((o, Wssm4[:, gp, 2, 32 * j4:32 * j4 + 32], Sr[:, gl, :], False, False, tp))
                    items.append((o, Wssm4[:, gp, 3, 32 * j4:32 * j4 + 32], Si[:, gl, :], False, True, tp))
            mmgroup(items, reads=[f"U4_{fc}", f"U4p_{fc}", f"Sre{sb}", f"Sim{sb}", "W1", "W2re", "W2im"], writes=["psb5"])
            yp = psb[5][:, :]
            act(ga, yp, AF.Square, reads=["psb5"], writes=["tr1"])
            act(ga, ga, AF.Identity, reads=["tr1"], writes=["tr1"], scale=0.044715, bias=one_col[:, 0:1])
            dve("tensor_tensor", gb, yp, ga, ALU.mult, reads=["tr1", "psb5"], writes=["tr2"])
            act(ga, gb, AF.Sigmoid, reads=["tr2"], writes=["tr1"], scale=1.5957691216057308)
            dve("tensor_tensor", ys3[:, fc, :], yp, ga, ALU.mult, reads=["tr1", "psb5"], writes=["ys"])

        def emit_CONV(fc, b):
            def wmm(bank, ch):
                mmgroup([(psb[bank][:, :], Win3[:, k, ch * 128:(ch + 1) * 128], hT3[:, k, :], k == 0, k == 7, None)
                         for k in range(8)], reads=[f"Win_q{ch // 4}", "hT"], writes=[f"psb{bank}"])
            wmm(0, 4 + fc)
            act(cbS, psb[0][:, :], AF.Identity, reads=["psb0", "bias_in"], writes=["cbS"],
                bias=bias_in3[:, 4 + fc, b:b + 1])
            wmm(1, 8 + fc)
            act(ccS, psb[1][:, :], AF.Identity, reads=["psb1", "bias_in"], writes=["ccS"],
                bias=bias_in3[:, 8 + fc, b:b + 1])
            wmm(3, 12 + fc)
            vc = r3(vcarry, 4)
            dve("tensor_copy", vbuf[:, 0:2], vc[:, fc, :], reads=["vcarry"], writes=["vbuf"])
            dve("scalar_tensor_tensor", vbuf[:, 2:514], psb[3][:, :], bias_in3[:, 12 + fc, b:b + 1], ccS,
                ALU.add, ALU.mult, reads=["psb3", "bias_in", "ccS"], writes=["vbuf"])
            cw = r3(conv_wT, 4)
            dve("tensor_scalar", cacc, vbuf[:, 2:514], cw[:, fc, 2:3], None, ALU.mult,
                reads=["vbuf", "conv_wT"], writes=["cacc"])
            dve("scalar_tensor_tensor", cacc, vbuf[:, 1:513], cw[:, fc, 1:2], cacc, ALU.mult, ALU.add,
                reads=["vbuf", "conv_wT", "cacc"], writes=["cacc"])
            dve("scalar_tensor_tensor", cacc, vbuf[:, 0:512], cw[:, fc, 0:1], cacc, ALU.mult, ALU.add,
                reads=["vbuf", "conv_wT", "cacc"], writes=["cacc"])
            dve("tensor_tensor", yc3[:, fc, :], cacc, cbS, ALU.mult, reads=["cacc", "cbS"], writes=["yc"])
            dve("tensor_copy", vc[:, fc, :], vbuf[:, 512:514], reads=["vbuf"], writes=["vcarry"])

        def emit_GLU():
            for oc in range(4):
                bank = 2 + oc % 2
                mmgroup([(psb[bank][:, :], Wglu3[:, k, oc * 128:(oc + 1) * 128], ys3[:, k, :], k == 0, k == 3, None)
                         for k in range(4)], reads=["Wglu", "ys"], writes=[f"psb{bank}"])
                act(sgA, psb[bank][:, :], AF.Sigmoid, reads=[f"psb{bank}", "b_gluT"], writes=["sgA"],
                    bias=b_gluT[:, oc:oc + 1])
                dve("tensor_tensor", yglu3[:, oc, :], sgA, ys3[:, oc, :], ALU.mult, reads=["sgA", "ys"], writes=["yglu"])

        def emit_MERGE(b, pre):
            tiles = {0: pre[0], 1: pre[1], 2: pre[2]}
            for half in range(2):
                gs_t, gs_tag = tiles[0] if half == 0 else tiles[2]
                if half == 1:
                    tiles[3] = ring_load(gate_src(1, 1))
                    tiles[4] = ring_load(wout_src(0))
                gc_t, gc_tag = tiles[1] if half == 0 else tiles[3]
                for cl in range(4):
                    oc = half * 4 + cl
                    mmgroup([(psb[2][:, :], gs_t[:, k, cl * 128:(cl + 1) * 128], hT3[:, k, :], k == 0, k == 7, None)
                             for k in range(8)], reads=[gs_tag, "hT"], writes=["psb2"])
                    act(sgA, psb[2][:, :], AF.Sigmoid, reads=["psb2", "bias_in"], writes=["sgA"],
                        bias=bias_in3[:, 16 + oc, b:b + 1])
                    mmgroup([(psb[3][:, :], gc_t[:, k, cl * 128:(cl + 1) * 128], hT3[:, k, :], k == 0, k == 7, None)
                             for k in range(8)], reads=[gc_tag, "hT"], writes=["psb3"])
                    act(sgB, psb[3][:, :], AF.Sigmoid, reads=["psb3", "bias_in"], writes=["sgB"],
                        bias=bias_in3[:, 24 + oc, b:b + 1])
                    mmgroup([(psb[6][:, :], Wps3[:, k, oc * 128:(oc + 1) * 128], yglu3[:, k, :], k == 0, k == 3, None)
                             for k in range(4)], reads=["Wps", "yglu"], writes=["psb6"])
                    mmgroup([(psb[7][:, :], Wpc3[:, k, oc * 128:(oc + 1) * 128], yc3[:, k, :], k == 0, k == 3, None)
                             for k in range(4)], reads=["Wpc", "yc"], writes=["psb7"])
                    dve("tensor_tensor", tr1, psb[6][:, :], sgA, ALU.mult, reads=["psb6", "sgA"], writes=["tr1"])
                    dve("tensor_tensor", tr2, psb[7][:, :], sgB, ALU.mult, reads=["psb7", "sgB"], writes=["tr2"])
                    pool("tensor_tensor", mT3[:, oc, :], tr1, tr2, ALU.add, reads=["tr1", "tr2"], writes=["mT"])
            tiles[5] = ring_load(wout_src(1))
            return [tiles[4], tiles[5]]

        def emit_WOUT(blk, nxt, wo):
            for s in range(4):
                slot = s % 2
                r0 = blk * BLK + s * 128
                dma("sp", f"xq{slot}", xq[slot], dr["x"][r0:r0 + 128, :], writes=[f"xq{slot}"])
                if nxt is not None:
                    norm_front(dr["x"], nxt, s, rstdA, xr, hn, slot)
                for oh in range(2):
                    wt, wtag = wo[oh]
                    bank = 2 + oh
                    mmgroup([(psb[bank][:, :], mT3[:, k, s * 128:(s + 1) * 128], wt[:, k, :], k == 0, k == 7, None)
                             for k in range(8)], reads=["mT", wtag], writes=[f"psb{bank}"])
                    tt = ga if oh == 0 else gb
                    ttag = "tr1" if oh == 0 else "tr2"
                    dve("tensor_tensor", tt, psb[bank][:, :], g1row[:, oh * 512:(oh + 1) * 512], ALU.mult,
                        reads=[f"psb{bank}", "g1row"], writes=[ttag])
                    pool("tensor_tensor", xq[slot][:, oh * 512:(oh + 1) * 512], tt, xq[slot][:, oh * 512:(oh + 1) * 512],
                         ALU.add, reads=[ttag, f"xq{slot}"], writes=[f"xq{slot}"])
                dma("pool", f"xq{slot}", x1_d[r0:r0 + 128, :], xq[slot], reads=[f"xq{slot}"], writes=["x1_dram"])
                if nxt is not None:
                    norm_back(nxt // 4, s, gs1T3, "gs1T", hT3, "hT", hn)

        def precast_ffn(i):
            if i < 8:
                dma("pool", f"pcf{i}", w1f16_d[:, i * 512:(i + 1) * 512].rearrange("(k p) n -> p k n", p=128),
                    dr["w_ff1"][:, i * 512:(i + 1) * 512].rearrange("(k p) n -> p k n", p=128), writes=[f"w1f16_{i}"])
            else:
                q = i - 8
                dma("pool", f"pcf{i}", w2f16_d[q * 512:(q + 1) * 512, :].rearrange("(k p) n -> p k n", p=128),
                    dr["w_ff2"][q * 512:(q + 1) * 512, :].rearrange("(k p) n -> p k n", p=128), writes=[f"w2f16_{q}"])

        if nblk_a > 0:
            stats_rstd_A(0)
            norm_transpose(dr["x"], 0, 0, rstdA, gs1T3, "gs1T", hT3, "hT")
        for blk in range(nblk_a):
            b = blk // 4
            qpos = blk % 4
            if qpos == 0:
                row_from_col(g1row, modT3, 16, b, "modT", "g1row")
                dve("memset", carry_re, 0.0, reads=[], writes=["carry_re"])
                dve("memset", carry_im, 0.0, reads=[], writes=["carry_im"])
                dve("memset", vcarry, 0.0, reads=[], writes=["vcarry"])
            if blk == 0:
                tap("hT", hT, "hT")
            if blk + 1 < nblk_a:
                stats_rstd_A(blk + 1)
            pre = [ring_load(gate_src(0, 0)), ring_load(gate_src(1, 0)), ring_load(gate_src(0, 1))]
            precast_ffn(blk)
            emit_U(b)
            emit_relayout(0); emit_relayout(1)
            emit_E(0); emit_chain(0); emit_CONV(0, b)
            emit_relayout(2)
            emit_E(1); emit_chain(1); emit_Y(0); emit_CONV(1, b)
            emit_relayout(3)
            emit_E(2); emit_chain(2); emit_Y(1); emit_CONV(2, b)
            emit_E(3); emit_chain(3); emit_Y(2); emit_CONV(3, b)
            emit_Y(3)
            if blk == 0:
                tap("U4", U4, "U4_3")
            emit_GLU()
            if blk == 0:
                tap("ys", ys, "ys"); tap("yglu", yglu, "yglu"); tap("yc", yc, "yc")
            wo = emit_MERGE(b, pre)
            if blk == 0:
                tap("mT", mT, "mT")
            emit_WOUT(blk, blk + 1 if blk + 1 < nblk_a else None, wo)
        for i in range(nblk_a, 16):
            precast_ffn(i)
        P.barrier()
        if "x1" in tap_d:
            dma("sp", "tapx1a", xr[0], x1_d[0:128, :], reads=["x1_dram"], writes=["xr0"])
            dma("sp", "tapx1b", tap_d["x1"], xr[0], reads=["xr0"], writes=["tapout_x1"])
            P.barrier()

        pos[0] = persistB_end
        W1f = alloc(8 * 4096, BF16); W1f3 = r3(W1f, 8)
        W2f = alloc(32 * 1024, BF16); W2f3 = r3(W2f, 32)
        for q in range(8):
            src = w1f16_d[:, q * 512:(q + 1) * 512].rearrange("(k p) n -> p k n", p=128)
            dst = W1f3[:, :, q * 512:(q + 1) * 512]
            dma("sp" if q % 2 == 0 else "act", f"ld_w1f{q}", dst, src, reads=[f"w1f16_{q}"], writes=[f"W1f_q{q}"])
        for q in range(8):
            src = w2f16_d[q * 512:(q + 1) * 512, :].rearrange("(k p) n -> p k n", p=128)
            dst = W2f3[:, q * 4:(q + 1) * 4, :]
            dma("sp" if q % 2 == 0 else "act", f"ld_w2f{q}", dst, src, reads=[f"w2f16_{q}"], writes=[f"W2f_q{q}"])
        xr = [alloc(1024) for _ in range(2)]
        hn = alloc(1024, BF16); junk = alloc(1024, BF16)
        h2T = alloc(8 * 512, BF16); h2T3 = r3(h2T, 8)
        hid = alloc(32 * 512, BF16); hid3 = r3(hid, 32)
        rl = [alloc(512, BF16) for _ in range(2)]
        tr1 = alloc(512)
        x2 = [alloc(1024) for _ in range(2)]
        g2row = alloc(1024); fgrow = alloc(1024)
        diagt = alloc(128)
        ssB = alloc(4); rstdB = alloc(4); ss2 = alloc(1); rstd2 = alloc(1)
        print("phase B arena use:", pos[0], "of", ARENA)
        load_group("sp", "ld_fg", [(fgrow, dr["fg_row"], "fgrow")])
        sh2b3 = r3(sh2b, 8)
        for ch in range(32):
            mmgroup([(psb[0][:, ch * 4:(ch + 1) * 4], W1f3[:, k, ch * 128:(ch + 1) * 128], sh2b3[:, k, :],
                      k == 0, k == 7, None) for k in range(8)], reads=[f"W1f_q{ch // 4}", "sh2b"], writes=["psb0"])
        dve("tensor_copy", bias_ff1, psb[0][:, 0:128], reads=["psb0"], writes=["bias_ff1"])

        if nblk_b > 0:
            stats_rstd(x1_d, 0, ssB, rstdB, "B")
            norm_transpose(x1_d, 0, 0, rstdB, gs2T3, "gs2T", h2T3, "h2T")
        for blk in range(nblk_b):
            b = blk // 4
            if blk % 4 == 0:
                row_from_col(g2row, modT3, 40, b, "modT", "g2row")
            if blk + 1 < nblk_b:
                stats_rstd(x1_d, blk + 1, ssB, rstdB, "B")
            if blk == 0:
                tap("h2T", h2T, "h2T")
            for hc in range(32):
                bank = 2 + hc % 2
                mmgroup([(psb[bank][:, :], W1f3[:, k, hc * 128:(hc + 1) * 128], h2T3[:, k, :], k == 0, k == 7, None)
                         for k in range(8)], reads=[f"W1f_q{hc // 4}", "h2T"], writes=[f"psb{bank}"])
                act(rl[hc % 2], psb[bank][:, :], AF.Relu, reads=[f"psb{bank}", "bias_ff1"], writes=[f"rl{hc % 2}"],
                    bias=bias_ff13[:, hc, b:b + 1])
                dve("tensor_tensor", hid3[:, hc, :], rl[hc % 2], rl[hc % 2], ALU.mult,
                    reads=[f"rl{hc % 2}"], writes=["hid"])
            for s in range(4):
                slot = s % 2
                r0 = blk * BLK + s * 128
                dma("sp", f"xr{slot}", xr[slot], x1_d[r0:r0 + 128, :], writes=[f"xr{slot}"])
                if blk + 1 < nblk_b:
                    norm_front(x1_d, blk + 1, s, rstdB, xr, hn, (s + 1) % 2)
                for oh in range(2):
                    bank = 4 + oh
                    mmgroup([(psb[bank][:, :], hid3[:, k, s * 128:(s + 1) * 128], W2f3[:, k, oh * 512:(oh + 1) * 512],
                              k == 0, k == 31, None) for k in range(32)],
                            reads=["hid"] + [f"W2f_q{q}" for q in range(8)], writes=[f"psb{bank}"])
                    dve("tensor_tensor", tr1, psb[bank][:, :], g2row[:, oh * 512:(oh + 1) * 512], ALU.mult,
                        reads=[f"psb{bank}", "g2row"], writes=["tr1"])
                    dve("tensor_tensor", x2[slot][:, oh * 512:(oh + 1) * 512], tr1, xr[slot][:, oh * 512:(oh + 1) * 512],
                        ALU.add, reads=["tr1", f"xr{slot}"], writes=[f"x2{slot}"])
                if blk + 1 < nblk_b:
                    norm_back((blk + 1) // 4, s, gs2T3, "gs2T", h2T3, "h2T", hn)
                dve("memset", ss2, 0.0, reads=[], writes=["ss2"])
                act(junk, x2[slot], AF.Square, reads=[f"x2{slot}"], writes=["junk", "ss2"], accum_out=ss2[:, 0:1])
                dve("tensor_scalar", rstd2, ss2, 1.0 / D, EPS, ALU.mult, ALU.add, reads=["ss2"], writes=["rstd2"])
                act(rstd2, rstd2, AF.Sqrt, reads=["rstd2"], writes=["rstd2"])
                dve("reciprocal", rstd2, rstd2, reads=["rstd2"], writes=["rstd2"])
                dve("scalar_tensor_tensor", x2[slot], x2[slot], rstd2[:, 0:1], fgrow, ALU.mult, ALU.mult,
                    reads=[f"x2{slot}", "rstd2", "fgrow"], writes=[f"x2{slot}"])
                dma("pool", f"x2{slot}", out_d[r0:r0 + 128, :], x2[slot], reads=[f"x2{slot}"], writes=["out_dram"])
        P.frozen = False
        P.barrier()
        P.emit()
    return nc


_CACHE = {}


def kernel(**inputs):
    inp = {k: np.asarray(v) for k, v in inputs.items()}
    shared = _shared_inputs(inp)
    x = np.ascontiguousarray(inp["x"], dtype=np.float32)
    c = np.asarray(inp["c"], dtype=np.float32)
    in_maps = []
    for i in range(NCORES):
        m = dict(shared)
        m["x"] = x[NB * i:NB * (i + 1)].reshape(NTOK, D)
        cc = c[NB * i:NB * (i + 1)]
        m["cT"] = np.ascontiguousarray(cc.T.reshape(8, 128, NB).transpose(1, 0, 2).reshape(128, 32))
        in_maps.append(m)
    if "nc" not in _CACHE:
        _CACHE["nc"] = build_program()
    res = run_bass_kernel_spmd(_CACHE["nc"], in_maps, core_ids=list(range(NCORES)))
    out = np.stack([np.asarray(r["out"]).reshape(NB, SEQ, D) for r in res.results], axis=0)
    return out.reshape(NCORES * NB, SEQ, D).astype(np.float32)
```

```python
import contextlib
import math
import numpy as np
import concourse.bass as bass
import concourse.mybir as mybir
from concourse.bass_utils import run_bass_kernel_spmd

F32 = mybir.dt.float32
BF16 = mybir.dt.bfloat16
I32 = mybir.dt.int32
AF = mybir.ActivationFunctionType
ALU = mybir.AluOpType

NCORES = 8
D = 1024
SEQ = 2048
NB = 4
NTOK = NB * SEQ
BLK = 512
NBLK = NTOK // BLK
EPS = 1e-6
TWO_PI = 2.0 * math.pi
EPOCH = 24000
DEBUG = False


class Prog:
    ENGS = ("pe", "act", "dve", "pool", "sp")
    COMPUTE = ("pe", "act", "dve", "pool")

    def __init__(self, nc):
        self.nc = nc
        self.ops = {e: [] for e in self.ENGS}
        self.count = {e: 0 for e in self.COMPUTE}
        self.dma_count = {}
        self.last_write = {}
        self.readers = {}
        self.waited = {e: {} for e in self.ENGS}

    def _deps(self, eng, reads, writes, skip_key=None):
        writes = list(writes) + [t for t in reads if t.startswith("psb") and t not in writes]
        deps = []
        for t in reads:
            lw = self.last_write.get(t)
            if lw is not None:
                deps.append(lw)
        for t in writes:
            lw = self.last_write.get(t)
            if lw is not None:
                deps.append(lw)
            deps.extend(self.readers.get(t, ()))
        out = {}
        for key, val in deps:
            if key == eng and eng == "pe":
                continue
            if key == skip_key:
                continue
            if self.waited[eng].get(key, 0) >= val:
                continue
            if out.get(key, 0) < val:
                out[key] = val
        for key, val in out.items():
            self.waited[eng][key] = val
        return list(out.items())

    def _commit(self, sig, reads, writes):
        writes = list(writes) + [t for t in reads if t.startswith("psb") and t not in writes]
        for t in writes:
            self.last_write[t] = sig
            self.readers[t] = []
        for t in reads:
            if t not in writes:
                self.readers.setdefault(t, []).append(sig)

    frozen = False

    def op(self, eng, fn, reads=(), writes=()):
        if self.frozen:
            return
        waits = self._deps(eng, reads, writes)
        self.count[eng] += 1
        sig = (eng, self.count[eng])
        self._commit(sig, reads, writes)
        self.ops[eng].append((fn, waits, sig, 1))

    def dma(self, eng, sem, fn, reads=(), writes=(), final=None):
        if self.frozen:
            return
        waits = self._deps(eng, reads, writes, skip_key=(sem if final is not None else None))
        self.dma_count[sem] = self.dma_count.get(sem, 0) + 16
        sig = (sem, self.dma_count[sem] if final is None else final)
        self._commit(sig, reads, writes)
        self.ops[eng].append((fn, waits, (sem, self.dma_count[sem]), 16))

    def barrier(self):
        if self.frozen:
            return
        sigs = [(e, c) for e, c in self.count.items() if c > 0]
        sigs += [(s, c) for s, c in self.dma_count.items()]
        for e in self.ENGS:
            waits = []
            for key, val in sigs:
                if key == e and e == "pe":
                    continue
                if self.waited[e].get(key, 0) >= val:
                    continue
                self.waited[e][key] = val
                waits.append((key, val))
            if waits:
                self.ops[e].append((None, waits, None, 0))

    def emit(self):
        nc = self.nc
        with contextlib.ExitStack() as st:
            sems = {}

            def get(key, val):
                if key in self.COMPUTE:
                    k = (val - 1) // EPOCH
                    loc = val - k * EPOCH
                    name = f"s_{key}_{k}"
                else:
                    name, loc = f"d_{key}", val
                if name not in sems:
                    sems[name] = st.enter_context(nc.semaphore(name))
                return sems[name], loc

            for e in self.ENGS:
                for fn, waits, sig, inc in self.ops[e]:
                    for key, val in waits:
                        get(key, val)
                    if sig is not None:
                        get(*sig)
            block = st.enter_context(nc.Block())

            def run(e):
                def body(engine):
                    for fn, waits, sig, inc in self.ops[e]:
                        for key, val in waits:
                            s, loc = get(key, val)
                            engine.wait_ge(s, loc)
                        if fn is None:
                            continue
                        ins = fn(engine)
                        s, _ = get(*sig)
                        ins.then_inc(s, inc)
                return body

            block.tensor(run("pe"))
            block.scalar(run("act"))
            block.vector(run("dve"))
            block.gpsimd(run("pool"))
            block.sync(run("sp"))


def _consts():
    r = np.arange(128)
    j4 = r // 32
    g2_c = (r // 16) % 2
    g2_s = r // 64
    maskY = (g2_s[:, None] == g2_c[None, :]).astype(np.float32)
    maskX = np.ascontiguousarray(maskY.T)
    causal = (j4[None, :] >= j4[:, None]).astype(np.float32)
    kY = np.broadcast_to((j4 + 1).astype(np.float32)[None, :], (128, 128)).copy()
    kX = (3 - j4).astype(np.float32).reshape(128, 1)
    iota = np.broadcast_to((np.arange(130) - 1).astype(np.float32)[None, :], (128, 130)).copy()
    ident = np.eye(128, dtype=np.float32)
    ones = np.ones((128, 128), np.float32)
    return dict(maskY=maskY, maskX=maskX, causal=causal, kY=kY, kX=kX, iota=iota,
                ident=ident, ones=ones)


def _shared_inputs(inp):
    f = np.float32
    A = lambda a: np.ascontiguousarray(a, dtype=f)
    d = {}
    d["w_ada"] = A(inp["w_ada"][0])
    d["b_adaT"] = A(inp["b_ada"][0].reshape(48, 128).T)
    d["n1T"] = A(inp["norm1_g"][0].reshape(8, 128).T)
    d["n2T"] = A(inp["norm2_g"][0].reshape(8, 128).T)
    d["fg_row"] = A(np.broadcast_to(inp["final_g"][None, :], (128, 1024)))
    d["w_in"] = A(inp["w_in"][0])
    d["w_glu"] = A(inp["w_glu"][0])
    d["b_gluT"] = A(inp["b_glu"][0].reshape(4, 128).T)
    d["conv_wT"] = A(inp["conv_w"][0].reshape(3, 4, 128).transpose(2, 1, 0).reshape(128, 12))
    dsk = inp["d_skip"][0].reshape(16, 2, 16).transpose(1, 2, 0).reshape(32, 16)
    d["dX"] = A(np.tile(dsk, (4, 1)))
    d["w_ps"] = A(inp["w_proj_ssm"][0])
    d["w_pc"] = A(inp["w_proj_conv"][0])
    d["w_out"] = A(inp["w_out"][0])
    d["w_ff1"] = A(inp["w_ff1"][0])
    d["w_ff2"] = A(inp["w_ff2"][0])
    lre = inp["lam_re"][0].reshape(16, 2, 64)
    lim = inp["lam_im"][0].reshape(16, 2, 64)
    ldt = inp["log_dt"][0].reshape(16, 2)
    d["lamreY"] = A(lre.transpose(1, 2, 0).reshape(128, 16))
    d["lamimY"] = A(lim.transpose(1, 2, 0).reshape(128, 16))
    d["logdtY"] = A(np.repeat(ldt.T[:, None, :], 64, axis=1).reshape(128, 16))

    def expY(a_p_gp_g2_h):
        t = a_p_gp_g2_h[None, :, :, None, :, :]
        t = np.broadcast_to(t, (2, 64, 16, 4, 2, 16))
        return A(t.reshape(128, 16 * 128))
    d["cYre"] = expY(inp["c_re"][0].reshape(16, 2, 16, 64).transpose(3, 0, 1, 2))
    d["cYim"] = expY(inp["c_im"][0].reshape(16, 2, 16, 64).transpose(3, 0, 1, 2))
    d["bYre"] = expY(inp["b_re"][0].reshape(16, 2, 64, 16).transpose(2, 0, 1, 3))
    d["bYim"] = expY(inp["b_im"][0].reshape(16, 2, 64, 16).transpose(2, 0, 1, 3))
    d["lamreX"] = A(np.broadcast_to(inp["lam_re"][0].reshape(1, 2048), (128, 2048)))
    d["lamimX"] = A(np.broadcast_to(inp["lam_im"][0].reshape(1, 2048), (128, 2048)))
    d["logdtX"] = A(np.broadcast_to(np.repeat(ldt, 64, axis=1).reshape(1, 2048), (128, 2048)))

    def expX(a_h_gp_g2_p):
        t = a_h_gp_g2_p[None, None, :, :, :, :]
        t = np.broadcast_to(t, (4, 2, 16, 16, 2, 64))
        return A(t.reshape(128, 2048))
    d["bXre"] = expX(inp["b_re"][0].reshape(16, 2, 64, 16).transpose(3, 0, 1, 2))
    d["bXim"] = expX(inp["b_im"][0].reshape(16, 2, 64, 16).transpose(3, 0, 1, 2))
    d.update(_consts())
    return d


IN_SHAPES = dict(
    x=[NTOK, D], cT=[128, 32], w_ada=[1024, 6144], b_adaT=[128, 48], n1T=[128, 8], n2T=[128, 8],
    fg_row=[128, 1024], w_in=[1024, 4096], w_glu=[512, 512], b_gluT=[128, 4], conv_wT=[128, 12],
    dX=[128, 16], w_ps=[512, 1024], w_pc=[512, 1024], w_out=[1024, 1024], w_ff1=[1024, 4096],
    w_ff2=[4096, 1024], lamreY=[128, 16], lamimY=[128, 16], logdtY=[128, 16],
    cYre=[128, 2048], cYim=[128, 2048], bYre=[128, 2048], bYim=[128, 2048],
    lamreX=[128, 2048], lamimX=[128, 2048], logdtX=[128, 2048], bXre=[128, 2048], bXim=[128, 2048],
    maskY=[128, 128], maskX=[128, 128], causal=[128, 128], kY=[128, 128], kX=[128, 1],
    iota=[128, 130], ident=[128, 128], ones=[128, 128],
)


def build_program(nblk_a=NBLK, nblk_b=NBLK, taps=(), stop_at=None):
    nc = bass.Bass("TRN2", target_bir_lowering=False)
    dr = {k: nc.dram_tensor(k, s, F32, kind="ExternalInput").ap() for k, s in IN_SHAPES.items()}
    out_d = nc.dram_tensor("out", [NTOK, D], F32, kind="ExternalOutput").ap()
    x1_d = nc.dram_tensor("x1s", [NTOK, D], F32, kind="Internal").ap()
    wg16_d = nc.dram_tensor("wg16", [1024, 2048], BF16, kind="Internal").ap()
    wo16_d = nc.dram_tensor("wo16", [1024, 1024], BF16, kind="Internal").ap()
    w1f16_d = nc.dram_tensor("w1f16", [1024, 4096], BF16, kind="Internal").ap()
    w2f16_d = nc.dram_tensor("w2f16", [4096, 1024], BF16, kind="Internal").ap()
    tap_d = {}
    for name, shape in taps:
        tap_d[name] = nc.dram_tensor("tap_" + name, shape, F32, kind="ExternalOutput").ap()

    with contextlib.ExitStack() as st:
        ARENA = 52800
        arena = st.enter_context(nc.sbuf_tensor("arena", [128, ARENA], F32))
        psb = [st.enter_context(nc.psum_tensor(f"psb{i}", [128, 512], F32)) for i in range(8)]
        P = Prog(nc)
        pos = [0]
        uniq = [0]

        def alloc(n, dtype=F32):
            nf = n if dtype == F32 else (n + 1) // 2
            assert pos[0] + nf <= ARENA, f"SBUF arena overflow {pos[0]}+{nf}"
            v = arena[:, pos[0]:pos[0] + nf]
            pos[0] += nf
            if dtype != F32:
                v = v.bitcast(dtype)
            return v

        def r3(ap, a):
            return ap.rearrange("p (a b) -> p a b", a=a)

        def pbf(i):
            return psb[i][:, :].bitcast(BF16)

        def dve(fn, *a, reads, writes, **kw):
            P.op("dve", lambda e: getattr(e, fn)(*a, **kw), reads=reads, writes=writes)

        def act(out, in_, func, reads, writes, **kw):
            P.op("act", lambda e: e.activation(out=out, in_=in_, func=func, **kw), reads=reads, writes=writes)

        def mmgroup(items, reads, writes):
            def fn(e):
                ins = None
                for (o, l, r, s0, s1, tp) in items:
                    if tp is None:
                        ins = e.matmul(o, l, r, start=s0, stop=s1)
                    else:
                        ins = e.matmul(o, l, r, start=s0, stop=s1, tile_position=tp)
                return ins
            P.op("pe", fn, reads=reads, writes=writes)

        def dma(eng, sem, out, in_, reads=(), writes=(), final=None):
            P.dma(eng, sem, lambda e: e.dma_start(out=out, in_=in_), reads=reads, writes=writes, final=final)

        def load_group(eng, sem, items):
            base = P.dma_count.get(sem, 0)
            final = base + 16 * len(items)
            for (o, i, tag) in items:
                P.dma(eng, sem, lambda e, o=o, i=i: e.dma_start(out=o, in_=i), writes=[tag], final=final)

        def checkpoint(name):
            if stop_at == name:
                P.barrier()
                P.frozen = True

        def tap(name, src, tag):
            if name in tap_d:
                if src.dtype == BF16:
                    src = src.bitcast(F32)
                uniq[0] += 1
                P.dma("sp", f"tap{uniq[0]}", lambda e, src=src, name=name: e.dma_start(out=tap_d[name], in_=src),
                      reads=[tag], writes=["tapout_" + name])

        ident32 = alloc(128); ones32 = alloc(128)
        ident16 = alloc(128, BF16)
        cT = alloc(32); b_adaT = alloc(48); n1T = alloc(8); n2T = alloc(8)
        b_gluT = alloc(4); conv_wT = alloc(12); dX = alloc(16)
        modT = alloc(192)
        gs1T = alloc(32); gs2T = alloc(32)
        bias_in = alloc(128)
        bias_ff1 = alloc(128)
        sgc = alloc(32); scb = alloc(32, BF16)
        sh1b = alloc(32, BF16); sh2b = alloc(32, BF16)
        persistB_end = pos[0]
        rT = alloc(16)
        cosT = alloc(16 * 130); sinT = alloc(16 * 130)
        carry_re = alloc(16); carry_im = alloc(16)
        vcarry = alloc(8)
        ssA = alloc(4); rstdA = alloc(4)
        Wssm = alloc(16 * 5 * 128, BF16)
        Wssm4 = Wssm.rearrange("p (g w m) -> p g w m", g=16, w=5)
        modT3 = r3(modT, 48); gs1T3 = r3(gs1T, 8); gs2T3 = r3(gs2T, 8)
        bias_in3 = r3(bias_in, 32); bias_ff13 = r3(bias_ff1, 32)
        cosT3 = r3(cosT, 16); sinT3 = r3(sinT, 16)
        persist_end = pos[0]

        load_group("sp", "ld_small", [
            (ident32, dr["ident"], "ident32"), (ones32, dr["ones"], "ones32"),
            (cT, dr["cT"], "cT"), (b_adaT, dr["b_adaT"], "b_adaT"), (n1T, dr["n1T"], "n1T"),
            (n2T, dr["n2T"], "n2T"), (b_gluT, dr["b_gluT"], "b_gluT"),
            (conv_wT, dr["conv_wT"], "conv_wT"), (dX, dr["dX"], "dX")])
        dve("tensor_copy", ident16, ident32, reads=["ident32"], writes=["ident16"])
        for q in range(4):
            dma("pool", f"precast{q}", wg16_d[:, q * 512:(q + 1) * 512].rearrange("(k p) n -> p k n", p=128),
                dr["w_in"][:, 2048 + q * 512:2048 + (q + 1) * 512].rearrange("(k p) n -> p k n", p=128),
                writes=[f"wg16_{q}"])
        for q in range(2):
            dma("pool", f"precast{4 + q}", wo16_d[:, q * 512:(q + 1) * 512].rearrange("(k p) n -> p k n", p=128),
                dr["w_out"][:, q * 512:(q + 1) * 512].rearrange("(k p) n -> p k n", p=128),
                writes=[f"wo16_{q}"])

        act(sgc, cT, AF.Sigmoid, reads=["cT"], writes=["sgc"])
        dve("tensor_tensor", scb, cT, sgc, ALU.mult, reads=["cT", "sgc"], writes=["scb"])
        scb3 = r3(scb, 8)
        scf = alloc(32)
        dve("tensor_tensor", scf, cT, sgc, ALU.mult, reads=["cT", "sgc"], writes=["scf"])
        scf3 = r3(scf, 8)
        wada_ring = [alloc(8 * 128) for _ in range(2)]
        for j in range(48):
            slot = j % 2
            wt = r3(wada_ring[slot], 8)
            src = dr["w_ada"][:, j * 128:(j + 1) * 128].rearrange("(k p) n -> p k n", p=128)
            dma("act", f"wada{slot}", wt, src, writes=[f"wada{slot}"])
            mmgroup([(psb[0][:, j * 4:(j + 1) * 4], wt[:, k, :], scf3[:, k, :], k == 0, k == 7, None)
                     for k in range(8)], reads=[f"wada{slot}", "scf"], writes=["psb0"])
        checkpoint("mod")

        BIG = 2048
        TMPN = 16 * 130
        T_i = alloc(TMPN).bitcast(I32); T_f = alloc(TMPN); T_red = alloc(TMPN)
        T_c1 = alloc(TMPN); T_c2 = alloc(TMPN)
        setup_scratch = pos[0]

        def big():
            return alloc(BIG)

        def b3(ap):
            return r3(ap, 16)

        def range_reduce(dst, src, shift, n3=None, tagd=None, tags=None):
            n = dst.shape[-1]
            ti = T_i[:, 0:n]; tf = T_f[:, 0:n]
            dve("tensor_scalar", tf, src, 1.0 / TWO_PI, shift / TWO_PI, ALU.mult, ALU.add,
                reads=[tags], writes=["rr_f"])
            dve("tensor_copy", ti, tf, reads=["rr_f"], writes=["rr_i"])
            dve("tensor_copy", tf, ti, reads=["rr_i"], writes=["rr_f"])
            dve("scalar_tensor_tensor", tf, tf, -TWO_PI, src, ALU.mult, ALU.add,
                reads=["rr_f", tags], writes=["rr_f"])
            dve("tensor_scalar", dst, tf, shift, None, ALU.add, reads=["rr_f"], writes=[tagd])
            dve("tensor_scalar", dst, dst, -math.pi, math.pi, ALU.max, ALU.min, reads=[tagd], writes=[tagd])

        def sincos(sin_dst, cos_dst, ang, tag_ang, tag_s, tag_c, n):
            red = T_red[:, 0:n]
            range_reduce(red, ang, 0.0, tagd="red", tags=tag_ang)
            act(sin_dst, red, AF.Sin, reads=["red"], writes=[tag_s])
            range_reduce(red, ang, math.pi / 2, tagd="red", tags=tag_ang)
            act(cos_dst, red, AF.Sin, reads=["red"], writes=[tag_c])

        def cmul(ore, oim, are, aim, bre, bim, tags_a, tags_b, tag_o, n, conj_b=False):
            t1 = T_c1[:, 0:n]; t2 = T_c2[:, 0:n]
            shp = list(ore.shape)

            def v(ap):
                return ap if len(shp) == 2 else ap.rearrange("p (a b) -> p a b", a=shp[1])
            dve("tensor_tensor", v(t1), are, bre, ALU.mult, reads=tags_a + tags_b, writes=["cm1"])
            dve("tensor_tensor", v(t2), aim, bim, ALU.mult, reads=tags_a + tags_b, writes=["cm2"])
            dve("tensor_tensor", ore, v(t1), v(t2), ALU.add if conj_b else ALU.subtract,
                reads=["cm1", "cm2"], writes=[tag_o + "re"])
            dve("tensor_tensor", v(t1), are, bim, ALU.mult, reads=tags_a + tags_b, writes=["cm1"])
            dve("tensor_tensor", v(t2), aim, bre, ALU.mult, reads=tags_a + tags_b, writes=["cm2"])
            dve("tensor_tensor", oim, v(t2), v(t1), ALU.subtract if conj_b else ALU.add,
                reads=["cm1", "cm2"], writes=[tag_o + "im"])

        lamreY = alloc(16); lamimY = alloc(16); logdtY = alloc(16)
        cYre = big(); cYim = big(); bYre = big(); bYim = big()
        maskY = alloc(128); causal = alloc(128); kY = alloc(128); iota = alloc(130)
        load_group("sp", "ld_ssmY", [
            (lamreY, dr["lamreY"], "lamreY"), (lamimY, dr["lamimY"], "lamimY"), (logdtY, dr["logdtY"], "logdtY"),
            (cYre, dr["cYre"], "cYre"), (cYim, dr["cYim"], "cYim"), (bYre, dr["bYre"], "bYre"),
            (bYim, dr["bYim"], "bYim"), (maskY, dr["maskY"], "maskY"), (causal, dr["causal"], "causal"),
            (kY, dr["kY"], "kY"), (iota, dr["iota"], "iota")])
        dtY = alloc(16); lrd = alloc(16); lid = alloc(16)
        act(dtY, logdtY, AF.Exp, reads=["logdtY"], writes=["dtY"])
        dve("tensor_tensor", lrd, lamreY, dtY, ALU.mult, reads=["lamreY", "dtY"], writes=["lrd"])
        dve("tensor_tensor", lid, lamimY, dtY, ALU.mult, reads=["lamimY", "dtY"], writes=["lid"])
        z = alloc(16); acc = alloc(16)
        dve("tensor_scalar", z, lrd, 4.0, None, ALU.mult, reads=["lrd"], writes=["z"])
        dve("tensor_scalar", acc, z, 1.0 / 5040.0, 1.0 / 720.0, ALU.mult, ALU.add, reads=["z"], writes=["acc"])
        for coef in (1.0 / 120.0, 1.0 / 24.0, 1.0 / 6.0, 0.5, 1.0, 1.0):
            dve("tensor_tensor", acc, acc, z, ALU.mult, reads=["acc", "z"], writes=["acc"])
            dve("tensor_scalar", acc, acc, coef, None, ALU.add, reads=["acc"], writes=["acc"])
        dve("tensor_copy", rT, acc, reads=["acc"], writes=["rT"])
        m1 = alloc(16); s1 = alloc(16); c1 = alloc(16)
        act(m1, lrd, AF.Exp, reads=["lrd"], writes=["m1"])
        sincos(s1, c1, lid, "lid", "s1", "c1", 16)
        a1re = alloc(16); a1im = alloc(16)
        dve("tensor_tensor", a1re, m1, c1, ALU.mult, reads=["m1", "c1"], writes=["a1re"])
        dve("tensor_tensor", a1im, m1, s1, ALU.mult, reads=["m1", "s1"], writes=["a1im"])
        dve("tensor_scalar", a1re, a1re, -1.0, None, ALU.add, reads=["a1re"], writes=["a1re"])
        den = alloc(16); t16 = alloc(16); qre = alloc(16); qim = alloc(16)
        dve("tensor_tensor", den, lamreY, lamreY, ALU.mult, reads=["lamreY"], writes=["den"])
        dve("tensor_tensor", t16, lamimY, lamimY, ALU.mult, reads=["lamimY"], writes=["t16"])
        dve("tensor_tensor", den, den, t16, ALU.add, reads=["den", "t16"], writes=["den"])
        dve("reciprocal", den, den, reads=["den"], writes=["den"])
        cmul(qre, qim, a1re, a1im, lamreY, lamimY, ["a1re", "a1im"], ["lamreY", "lamimY"], "q", 16, conj_b=True)
        dve("tensor_tensor", qre, qre, den, ALU.mult, reads=["qre", "den"], writes=["qre"])
        dve("tensor_tensor", qim, qim, den, ALU.mult, reads=["qim", "den"], writes=["qim"])
        bbYre = big(); bbYim = big()
        bc16 = lambda ap: ap.unsqueeze(2).to_broadcast([128, 16, 128])
        cmul(b3(bbYre), b3(bbYim), bc16(qre), bc16(qim), b3(bYre), b3(bYim),
             ["qre", "qim"], ["bYre", "bYim"], "bbY", BIG)
        argm = big(); ang = big()
        kYb = kY.unsqueeze(1).to_broadcast([128, 16, 128])
        dve("tensor_tensor", b3(argm), bc16(lrd), kYb, ALU.mult, reads=["lrd", "kY"], writes=["argm"])
        dve("tensor_tensor", b3(ang), bc16(lid), kYb, ALU.mult, reads=["lid", "kY"], writes=["ang"])
        sinA = big(); cosA = big()
        sincos(sinA, cosA, ang, "ang", "sinA", "cosA", BIG)
        mag = big()
        act(mag, argm, AF.Exp, reads=["argm"], writes=["mag"])
        Are = big(); Aim = big()
        dve("tensor_tensor", Are, mag, cosA, ALU.mult, reads=["mag", "cosA"], writes=["Are"])
        dve("tensor_tensor", Aim, mag, sinA, ALU.mult, reads=["mag", "sinA"], writes=["Aim"])
        Rre = bYre; Rim = bYim
        cmul(Rre, Rim, cYre, cYim, Are, Aim, ["cYre", "cYim", "bbYre", "bbYim"], ["Are", "Aim"], "R", BIG)
        mYb = maskY.unsqueeze(1).to_broadcast([128, 16, 128])
        dve("tensor_tensor", b3(Rre), b3(Rre), mYb, ALU.mult, reads=["Rre", "maskY"], writes=["Rre"])
        dve("scalar_tensor_tensor", b3(Rim), b3(Rim), -1.0, mYb, ALU.mult, ALU.mult,
            reads=["Rim", "maskY"], writes=["Rim"])
        dve("tensor_copy", Wssm4[:, :, 2, :], b3(Rre), reads=["Rre"], writes=["W2re"])
        dve("tensor_copy", Wssm4[:, :, 3, :], b3(Rim), reads=["Rim"], writes=["W2im"])
        act(mag, argm, AF.Exp, reads=["argm", "Are", "Aim"], writes=["mag"], scale=-1.0)
        dve("tensor_tensor", Are, mag, cosA, ALU.mult, reads=["mag", "cosA", "Rre", "Rim"], writes=["Are"])
        dve("scalar_tensor_tensor", Aim, mag, -1.0, sinA, ALU.mult, ALU.mult,
            reads=["mag", "sinA", "Rre", "Rim"], writes=["Aim"])
        Lre = cYre; Lim = cYim
        cmul(Lre, Lim, Are, Aim, bbYre, bbYim, ["Are", "Aim", "Rre", "Rim"], ["bbYre", "bbYim"], "L", BIG)
        dve("tensor_tensor", b3(Lre), b3(Lre), mYb, ALU.mult, reads=["Lre", "maskY"], writes=["Lre"])
        dve("tensor_tensor", b3(Lim), b3(Lim), mYb, ALU.mult, reads=["Lim", "maskY"], writes=["Lim"])
        kt = alloc(128)
        for gp in range(16):
            bank = psb[1 + gp % 2]
            mmgroup([(bank[:, 0:128], b3(Lre)[:, gp, :], b3(Rre)[:, gp, :], True, False, None),
                     (bank[:, 0:128], b3(Lim)[:, gp, :], b3(Rim)[:, gp, :], False, True, None)],
                    reads=["Lre", "Lim", "Rre", "Rim"], writes=[f"psb{1 + gp % 2}"])
            dve("tensor_tensor", kt, bank[:, 0:128], causal, ALU.mult,
                reads=[f"psb{1 + gp % 2}", "causal"], writes=["kt"])
            dve("scalar_tensor_tensor", Wssm4[:, gp, 4, :], ident32, dX[:, gp:gp + 1], kt, ALU.mult, ALU.add,
                reads=["ident32", "dX", "kt"], writes=["W1"])
        th4 = alloc(16); th4r = alloc(16)
        dve("tensor_scalar", th4, lid, 4.0, None, ALU.mult, reads=["lid"], writes=["th4"])
        range_reduce(th4r, th4, 0.0, tagd="th4r", tags="th4")
        angT = alloc(16 * 130)
        dve("tensor_tensor", r3(angT, 16), th4r.unsqueeze(2).to_broadcast([128, 16, 130]),
            iota.unsqueeze(1).to_broadcast([128, 16, 130]), ALU.mult, reads=["th4r", "iota"], writes=["angT"])
        def sincos_tab(dst, shift, tagd):
            red = T_red; tmp_i = T_i; tf = T_f
            dve("tensor_scalar", tf, angT, 1.0 / TWO_PI, shift / TWO_PI, ALU.mult, ALU.add,
                reads=["angT"], writes=["rr_f"])
            dve("tensor_copy", tmp_i, tf, reads=["rr_f"], writes=["rr_i"])
            dve("tensor_copy", tf, tmp_i, reads=["rr_i"], writes=["rr_f"])
            dve("scalar_tensor_tensor", tf, tf, -TWO_PI, angT, ALU.mult, ALU.add,
                reads=["rr_f", "angT"], writes=["rr_f"])
            dve("tensor_scalar", red, tf, shift, None, ALU.add, reads=["rr_f"], writes=["red"])
            dve("tensor_scalar", red, red, -math.pi, math.pi, ALU.max, ALU.min, reads=["red"], writes=["red"])
            act(dst, red, AF.Sin, reads=["red"], writes=[tagd])
        sincos_tab(sinT, 0.0, "sinT")
        sincos_tab(cosT, math.pi / 2, "cosT")
        tap("W2re", Rre, "Rre"); tap("cosT", cosT, "cosT"); tap("sinT", sinT, "sinT"); tap("rT", rT, "rT")
        checkpoint("ssmY")
        P.barrier()
        pos[0] = setup_scratch

        lamreX = big(); lamimX = big(); logdtX = big(); bXre = big(); bXim = big()
        maskX = alloc(128); kX = alloc(1)
        load_group("sp", "ld_ssmX", [
            (lamreX, dr["lamreX"], "lamreX"), (lamimX, dr["lamimX"], "lamimX"), (logdtX, dr["logdtX"], "logdtX"),
            (bXre, dr["bXre"], "bXre"), (bXim, dr["bXim"], "bXim"), (maskX, dr["maskX"], "maskX"),
            (kX, dr["kX"], "kX")])
        dtX = big(); lrdX = big(); lidX = big()
        act(dtX, logdtX, AF.Exp, reads=["logdtX"], writes=["dtX"])
        dve("tensor_tensor", lrdX, lamreX, dtX, ALU.mult, reads=["lamreX", "dtX"], writes=["lrdX"])
        dve("tensor_tensor", lidX, lamimX, dtX, ALU.mult, reads=["lamimX", "dtX"], writes=["lidX"])
        m1X = dtX
        act(m1X, lrdX, AF.Exp, reads=["lrdX", "lidX"], writes=["m1X"])
        s1X = big(); c1X = big()
        sincos(s1X, c1X, lidX, "lidX", "s1X", "c1X", BIG)
        a1reX = big(); a1imX = big()
        dve("tensor_tensor", a1reX, m1X, c1X, ALU.mult, reads=["m1X", "c1X"], writes=["a1reX"])
        dve("tensor_tensor", a1imX, m1X, s1X, ALU.mult, reads=["m1X", "s1X"], writes=["a1imX"])
        dve("tensor_scalar", a1reX, a1reX, -1.0, None, ALU.add, reads=["a1reX"], writes=["a1reX"])
        denX = s1X; tX = c1X
        dve("tensor_tensor", denX, lamreX, lamreX, ALU.mult, reads=["lamreX", "a1imX", "a1reX"], writes=["denX"])
        dve("tensor_tensor", tX, lamimX, lamimX, ALU.mult, reads=["lamimX", "a1imX", "a1reX"], writes=["tX"])
        dve("tensor_tensor", denX, denX, tX, ALU.add, reads=["denX", "tX"], writes=["denX"])
        dve("reciprocal", denX, denX, reads=["denX"], writes=["denX"])
        qreX = big(); qimX = big()
        cmul(qreX, qimX, a1reX, a1imX, lamreX, lamimX, ["a1reX", "a1imX"], ["lamreX", "lamimX"], "qX", BIG, conj_b=True)
        dve("tensor_tensor", qreX, qreX, denX, ALU.mult, reads=["qXre", "denX"], writes=["qXre"])
        dve("tensor_tensor", qimX, qimX, denX, ALU.mult, reads=["qXim", "denX"], writes=["qXim"])
        bbXre = a1reX; bbXim = a1imX
        cmul(bbXre, bbXim, qreX, qimX, bXre, bXim, ["qXre", "qXim", "denX"], ["bXre", "bXim"], "bbX", BIG)
        angX = qreX
        dve("tensor_scalar", angX, lidX, kX[:, 0:1], None, ALU.mult, reads=["lidX", "kX", "bbXre", "bbXim"], writes=["angX"])
        sinX = bXre; cosX = bXim
        sincos(sinX, cosX, angX, "angX", "sinX", "cosX", BIG)
        magX = qimX
        act(magX, lrdX, AF.Exp, reads=["lrdX", "bbXre", "bbXim"], writes=["magX"], scale=kX[:, 0:1])
        AXre = lamreX; AXim = lamimX
        dve("tensor_tensor", AXre, magX, cosX, ALU.mult, reads=["magX", "cosX", "denX", "qXre"], writes=["AXre"])
        dve("tensor_tensor", AXim, magX, sinX, ALU.mult, reads=["magX", "sinX", "denX", "qXre"], writes=["AXim"])
        W3re = lrdX; W3im = lidX
        cmul(W3re, W3im, AXre, AXim, bbXre, bbXim, ["AXre", "AXim", "angX", "magX"], ["bbXre", "bbXim"], "W3", BIG)
        mXb = maskX.unsqueeze(1).to_broadcast([128, 16, 128])
        dve("tensor_tensor", Wssm4[:, :, 0, :], b3(W3re), mXb, ALU.mult, reads=["W3re", "maskX"], writes=["W3re_b"])
        dve("tensor_tensor", Wssm4[:, :, 1, :], b3(W3im), mXb, ALU.mult, reads=["W3im", "maskX"], writes=["W3im_b"])
        tap("W3re", W3re, "W3re")
        checkpoint("ssmX")
        P.barrier()
        dve("tensor_tensor", modT3, r3(psb[0][:, 0:192], 48),
            b_adaT.unsqueeze(2).to_broadcast([128, 48, 4]), ALU.add,
            reads=["psb0", "b_adaT"], writes=["modT"])
        dve("scalar_tensor_tensor", gs1T3, modT3[:, 8:16, :], 1.0, n1T.unsqueeze(2).to_broadcast([128, 8, 4]),
            ALU.add, ALU.mult, reads=["modT", "n1T"], writes=["gs1T"])
        dve("scalar_tensor_tensor", gs2T3, modT3[:, 32:40, :], 1.0, n2T.unsqueeze(2).to_broadcast([128, 8, 4]),
            ALU.add, ALU.mult, reads=["modT", "n2T"], writes=["gs2T"])
        dve("tensor_copy", r3(sh1b, 8), modT3[:, 0:8, :], reads=["modT"], writes=["sh1b"])
        dve("tensor_copy", r3(sh2b, 8), modT3[:, 24:32, :], reads=["modT"], writes=["sh2b"])
        tap("modT", modT, "modT")
        pos[0] = persist_end

        Win_lo = alloc(8 * 2048, BF16)
        Wglu = alloc(4 * 512, BF16)
        Wps = alloc(4 * 1024, BF16); Wpc = alloc(4 * 1024, BF16)
        Win3 = r3(Win_lo, 8); Wglu3 = r3(Wglu, 4); Wps3 = r3(Wps, 4); Wpc3 = r3(Wpc, 4)
        ring = [alloc(8 * 512, BF16) for _ in range(3)]
        for q in range(4):
            src = dr["w_in"][:, q * 512:(q + 1) * 512].rearrange("(k p) n -> p k n", p=128)
            dst = Win3[:, :, q * 512:(q + 1) * 512]
            dma("pool", f"ld_win{q}", dst, src, writes=[f"Win_q{q}"])
        load_group("pool", "ld_wA", [
            (Wglu3, dr["w_glu"].rearrange("(k p) n -> p k n", p=128), "Wglu"),
            (Wps3, dr["w_ps"].rearrange("(k p) n -> p k n", p=128), "Wps"),
            (Wpc3, dr["w_pc"].rearrange("(k p) n -> p k n", p=128), "Wpc")])
        checkpoint("wloadA")
        ring_use = [0]

        def ring_load(src):
            src_ap, src_tag = src
            slot = ring_use[0] % 3
            ring_use[0] += 1
            dst = r3(ring[slot], 8)
            dma("sp", f"ring{slot}", dst, src_ap, reads=[src_tag], writes=[f"ring{slot}"])
            return dst, f"ring{slot}"

        def gate_src(which, half):
            q = which * 2 + half
            return wg16_d[:, q * 512:(q + 1) * 512].rearrange("(k p) n -> p k n", p=128), f"wg16_{q}"

        def wout_src(half):
            return wo16_d[:, half * 512:(half + 1) * 512].rearrange("(k p) n -> p k n", p=128), f"wo16_{half}"

        sh1b3 = r3(sh1b, 8)
        for ch in range(16):
            mmgroup([(psb[0][:, ch * 4:(ch + 1) * 4], Win3[:, k, ch * 128:(ch + 1) * 128], sh1b3[:, k, :],
                      k == 0, k == 7, None) for k in range(8)],
                    reads=[f"Win_q{ch // 4}", "sh1b"], writes=["psb0"])
        for which in range(2):
            for half in range(2):
                gt, gtag = ring_load(gate_src(which, half))
                for cl in range(4):
                    ch = 16 + which * 8 + half * 4 + cl
                    mmgroup([(psb[0][:, ch * 4:(ch + 1) * 4], gt[:, k, cl * 128:(cl + 1) * 128], sh1b3[:, k, :],
                              k == 0, k == 7, None) for k in range(8)],
                            reads=[gtag, "sh1b"], writes=["psb0"])
        dve("tensor_copy", bias_in, psb[0][:, 0:128], reads=["psb0"], writes=["bias_in"])
        tap("bias_in", bias_in, "bias_in")
        checkpoint("bias_in")

        xr = [alloc(1024) for _ in range(2)]
        hn = alloc(1024, BF16)
        hT = alloc(8 * 512, BF16); hT3 = r3(hT, 8)
        uT = alloc(4 * 512, BF16); uT3 = r3(uT, 4)
        U4 = alloc(16 * 128, BF16); U43 = r3(U4, 16)
        Ebre = alloc(4 * 129); Ebim = alloc(4 * 129); Xre = alloc(4 * 129); Xim = alloc(4 * 129)
        Ebre3 = r3(Ebre, 4); Ebim3 = r3(Ebim, 4); Xre3 = r3(Xre, 4); Xim3 = r3(Xim, 4)
        tr1 = alloc(512); tr2 = alloc(512)
        tr13 = r3(tr1, 4); tr23 = r3(tr2, 4)
        Sre3b = [r3(alloc(4 * 128, BF16), 4) for _ in range(2)]
        Sim3b = [r3(alloc(4 * 128, BF16), 4) for _ in range(2)]
        tp1 = alloc(512); tp2 = alloc(512); tp13 = r3(tp1, 4); tp23 = r3(tp2, 4)
        ga = tr1; gb = tr2
        one_col = alloc(1)
        dve("memset", one_col, 1.0, reads=[], writes=["one_col"])
        ys = alloc(4 * 512, BF16); ys3 = r3(ys, 4)
        yglu = alloc(4 * 512, BF16); yglu3 = r3(yglu, 4)
        yc = alloc(4 * 512, BF16); yc3 = r3(yc, 4)
        cbS = alloc(512); ccS = alloc(512); vbuf = alloc(514); cacc = alloc(512)
        sgA = alloc(512, BF16); sgB = alloc(512, BF16)
        mT = alloc(8 * 512, BF16); mT3 = r3(mT, 8)
        g1row = alloc(1024)
        xq = [alloc(1024) for _ in range(2)]
        diagt = alloc(128)
        print("phase A arena use:", pos[0], "of", ARENA)

        def stats_rstd(src_d, blk, ss, rstd, tagp):
            dve("memset", ss, 0.0, reads=[], writes=[tagp + "ss"])
            for s in range(4):
                slot = s % 2
                r0 = blk * BLK + s * 128
                dma("sp", f"xr{slot}", xr[slot], src_d[r0:r0 + 128, :],
                      writes=[f"xr{slot}"])
                act(junk, xr[slot], AF.Square, reads=[f"xr{slot}"], writes=["junk", tagp + "ss"],
                    accum_out=ss[:, s:s + 1])
            dve("tensor_scalar", rstd, ss, 1.0 / D, EPS, ALU.mult, ALU.add, reads=[tagp + "ss"], writes=[tagp + "rstd"])
            act(rstd, rstd, AF.Sqrt, reads=[tagp + "rstd"], writes=[tagp + "rstd"])
            dve("reciprocal", rstd, rstd, reads=[tagp + "rstd"], writes=[tagp + "rstd"])

        def row_from_col(dst_row, colT3, kidx0, b, tag_col, tag_row):
            for half in range(2):
                bank = psb[4 + half]
                for kk in range(4):
                    k = half * 4 + kk
                    dve("tensor_scalar", diagt, ident32, colT3[:, kidx0 + k, b:b + 1], None, ALU.mult,
                        reads=["ident32", tag_col], writes=["diagt"])
                    mmgroup([(bank[:, kk * 128:(kk + 1) * 128], ones32, diagt, True, True, None)],
                            reads=["ones32", "diagt"], writes=[f"psb{4 + half}"])
                dve("tensor_copy", dst_row[:, half * 512:(half + 1) * 512], bank[:, :],
                    reads=[f"psb{4 + half}"], writes=[tag_row])

        def norm_front(src_d, blk, s, rstd, xr, hn, slot):
            r0 = blk * BLK + s * 128
            dma("sp", f"xr{slot}", xr[slot], src_d[r0:r0 + 128, :], writes=[f"xr{slot}"])
            dve("tensor_scalar", hn, xr[slot], rstd[:, s:s + 1], None, ALU.mult,
                reads=[f"xr{slot}", "Arstd", "Brstd"], writes=["hn"])

        def norm_back(b, s, gsT3, tag_gs, dstT3, tag_dst, hn):
            bank = s % 2
            ptb = pbf(bank)

            def fn(e, ptb=ptb, hn=hn, ident16=ident16):
                ins = None
                for k in range(8):
                    ins = e.transpose(ptb[:, k * 128:(k + 1) * 128], hn[:, k * 128:(k + 1) * 128], ident16)
                return ins
            P.op("pe", fn, reads=["hn", "ident16"], writes=[f"psb{bank}"])
            dve("tensor_tensor", dstT3[:, :, s * 128:(s + 1) * 128], r3(ptb, 8),
                gsT3[:, :, b:b + 1].to_broadcast([128, 8, 128]), ALU.mult,
                reads=[f"psb{bank}", tag_gs], writes=[tag_dst])

        def norm_transpose_sub(src_d, blk, b, s, rstd, gsT3, tag_gs, dstT3, tag_dst, xr, hn):
            norm_front(src_d, blk, s, rstd, xr, hn, s % 2)
            norm_back(b, s, gsT3, tag_gs, dstT3, tag_dst, hn)

        def norm_transpose(src_d, blk, b, rstd, gsT3, tag_gs, dstT3, tag_dst):
            for s in range(4):
                norm_transpose_sub(src_d, blk, b, s, rstd, gsT3, tag_gs, dstT3, tag_dst, xr, hn)

        def pool(fn, *a, reads, writes, **kw):
            P.op("pool", lambda e: getattr(e, fn)(*a, **kw), reads=reads, writes=writes)

        junkA = cacc.bitcast(BF16)

        def stats_rstd_A(blk):
            dve("memset", ssA, 0.0, reads=[], writes=["Ass"])
            for s in range(4):
                slot = s % 2
                r0 = blk * BLK + s * 128
                dma("sp", f"xr{slot}", xr[slot], dr["x"][r0:r0 + 128, :], writes=[f"xr{slot}"])
                act(junkA, xr[slot], AF.Square, reads=[f"xr{slot}"], writes=["cacc", "Ass"], accum_out=ssA[:, s:s + 1])
            dve("tensor_scalar", rstdA, ssA, 1.0 / D, EPS, ALU.mult, ALU.add, reads=["Ass"], writes=["Arstd"])
            act(rstdA, rstdA, AF.Sqrt, reads=["Arstd"], writes=["Arstd"])
            dve("reciprocal", rstdA, rstdA, reads=["Arstd"], writes=["Arstd"])

        def emit_U(b):
            for fc in range(4):
                bank = 2 + fc % 2
                mmgroup([(psb[bank][:, :], Win3[:, k, fc * 128:(fc + 1) * 128], hT3[:, k, :], k == 0, k == 7, None)
                         for k in range(8)], reads=["Win_q0", "hT"], writes=[f"psb{bank}"])
                act(uT3[:, fc, :], psb[bank][:, :], AF.Identity, reads=[f"psb{bank}", "bias_in"], writes=[f"uT{fc}"],
                    bias=bias_in3[:, fc, b:b + 1])

        def emit_relayout(fc):
            for gl in range(4):
                for j4 in range(4):
                    o_ = U43[32 * j4:32 * j4 + 32, fc * 4 + gl, :]
                    i_ = uT3[32 * gl:32 * gl + 32, fc, j4:512:4]
                    if (gl + j4) % 2 == 0:
                        act(o_, i_, AF.Copy, reads=[f"uT{fc}"], writes=[f"U4_{fc}"])
                    else:
                        pool("tensor_copy", o_, i_, reads=[f"uT{fc}"], writes=[f"U4p_{fc}"])

        def emit_E(fc):
            gps = range(fc * 4, fc * 4 + 4)
            mmgroup([(psb[6][:, gl * 128:(gl + 1) * 128], Wssm4[:, gp, 0, :], U43[:, gp, :], True, True, None)
                     for gl, gp in enumerate(gps)], reads=[f"U4_{fc}", f"U4p_{fc}", "W3re_b"], writes=["psb6"])
            mmgroup([(psb[7][:, gl * 128:(gl + 1) * 128], Wssm4[:, gp, 1, :], U43[:, gp, :], True, True, None)
                     for gl, gp in enumerate(gps)], reads=[f"U4_{fc}", f"U4p_{fc}", "W3im_b"], writes=["psb7"])

        def emit_chain(fc):
            gps = range(fc * 4, fc * 4 + 4)
            Er = r3(psb[6][:, :], 4); Ei = r3(psb[7][:, :], 4)
            cs = cosT3[:, fc * 4:fc * 4 + 4, 1:129]; sn = sinT3[:, fc * 4:fc * 4 + 4, 1:129]
            dve("tensor_tensor", tr13, Er, cs, ALU.mult, reads=["psb6", "cosT"], writes=["tr1"])
            dve("tensor_tensor", tr23, Ei, sn, ALU.mult, reads=["psb7", "sinT"], writes=["tr2"])
            dve("tensor_tensor", Ebre3[:, :, 1:129], tr13, tr23, ALU.add, reads=["tr1", "tr2"], writes=["Ebre"])
            dve("tensor_tensor", tr13, Ei, cs, ALU.mult, reads=["psb7", "cosT"], writes=["tr1"])
            dve("tensor_tensor", tr23, Er, sn, ALU.mult, reads=["psb6", "sinT"], writes=["tr2"])
            dve("tensor_tensor", Ebim3[:, :, 1:129], tr13, tr23, ALU.subtract, reads=["tr1", "tr2"], writes=["Ebim"])
            dve("tensor_copy", Ebre3[:, :, 0], carry_re[:, fc * 4:fc * 4 + 4], reads=["carry_re"], writes=["Ebre"])
            dve("tensor_copy", Ebim3[:, :, 0], carry_im[:, fc * 4:fc * 4 + 4], reads=["carry_im"], writes=["Ebim"])
            for gl, gp in enumerate(gps):
                rb = rT[:, gp:gp + 1].to_broadcast([128, 129])
                dve("tensor_tensor_scan", Xre3[:, gl, :], rb, Ebre3[:, gl, :], 0.0, ALU.mult, ALU.add,
                    reads=["rT", "Ebre"], writes=["Xre"])
                dve("tensor_tensor_scan", Xim3[:, gl, :], rb, Ebim3[:, gl, :], 0.0, ALU.mult, ALU.add,
                    reads=["rT", "Ebim"], writes=["Xim"])
            sb = fc % 2
            Sr = Sre3b[sb]; Si = Sim3b[sb]
            cs0 = cosT3[:, fc * 4:fc * 4 + 4, 0:128]; sn0 = sinT3[:, fc * 4:fc * 4 + 4, 0:128]
            pool("tensor_tensor", tp13, Xre3[:, :, 0:128], cs0, ALU.mult, reads=["Xre", "cosT"], writes=["tp1"])
            pool("tensor_tensor", tp23, Xim3[:, :, 0:128], sn0, ALU.mult, reads=["Xim", "sinT"], writes=["tp2"])
            pool("tensor_tensor", Sr, tp13, tp23, ALU.subtract, reads=["tp1", "tp2"], writes=[f"Sre{sb}"])
            pool("tensor_tensor", tp13, Xre3[:, :, 0:128], sn0, ALU.mult, reads=["Xre", "sinT"], writes=["tp1"])
            pool("tensor_tensor", tp23, Xim3[:, :, 0:128], cs0, ALU.mult, reads=["Xim", "cosT"], writes=["tp2"])
            pool("tensor_tensor", Si, tp13, tp23, ALU.add, reads=["tp1", "tp2"], writes=[f"Sim{sb}"])
            c9 = cosT3[:, fc * 4:fc * 4 + 4, 129]; s9 = sinT3[:, fc * 4:fc * 4 + 4, 129]
            t4a = tp1[:, 0:4]; t4b = tp2[:, 0:4]
            pool("tensor_tensor", t4a, Xre3[:, :, 128], c9, ALU.mult, reads=["Xre", "cosT"], writes=["tp1"])
            pool("tensor_tensor", t4b, Xim3[:, :, 128], s9, ALU.mult, reads=["Xim", "sinT"], writes=["tp2"])
            pool("tensor_tensor", carry_re[:, fc * 4:fc * 4 + 4], t4a, t4b, ALU.subtract,
                 reads=["tp1", "tp2"], writes=["carry_re"])
            pool("tensor_tensor", t4a, Xre3[:, :, 128], s9, ALU.mult, reads=["Xre", "sinT"], writes=["tp1"])
            pool("tensor_tensor", t4b, Xim3[:, :, 128], c9, ALU.mult, reads=["Xim", "cosT"], writes=["tp2"])
            pool("tensor_tensor", carry_im[:, fc * 4:fc * 4 + 4], t4a, t4b, ALU.add,
                 reads=["tp1", "tp2"], writes=["carry_im"])

        def emit_Y(fc):
            gps = range(fc * 4, fc * 4 + 4)
            sb = fc % 2
            Sr = Sre3b[sb]; Si = Sim3b[sb]
            items = []
            for j4 in range(4):
                for gl, gp in enumerate(gps):
                    o = psb[5][32 * gl:32 * gl + 32, j4:512:4]
                    tp = (0, 32 * gl)
                    items.append((o, Wssm4[:, gp, 4, 32 * j4:32 * j4 + 32], U43[:, gp, :], True, False, tp))
                    items.append((o, Wssm4[:, gp, 2, 32 * j4:32 * j4 + 32], Sr[:, gl, :], False, False, tp))
                    items.append((o, Wssm4[:, gp, 3, 32 * j4:32 * j4 + 32], Si[:, gl, :], False, True, tp))
            mmgroup(items, reads=[f"U4_{fc}", f"U4p_{fc}", f"Sre{sb}", f"Sim{sb}", "W1", "W2re", "W2im"], writes=["psb5"])
            yp = psb[5][:, :]
            act(ga, yp, AF.Square, reads=["psb5"], writes=["tr1"])
            act(ga, ga, AF.Identity, reads=["tr1"], writes=["tr1"], scale=0.044715, bias=one_col[:, 0:1])
            dve("tensor_tensor", gb, yp, ga, ALU.mult, reads=["tr1", "psb5"], writes=["tr2"])
            act(ga, gb, AF.Sigmoid, reads=["tr2"], writes=["tr1"], scale=1.5957691216057308)
            dve("tensor_tensor", ys3[:, fc, :], yp, ga, ALU.mult, reads=["tr1", "psb5"], writes=["ys"])

        def emit_CONV(fc, b):
            def wmm(bank, ch):
                mmgroup([(psb[bank][:, :], Win3[:, k, ch * 128:(ch + 1) * 128], hT3[:, k, :], k == 0, k == 7, None)
                         for k in range(8)], reads=[f"Win_q{ch // 4}", "hT"], writes=[f"psb{bank}"])
            wmm(0, 4 + fc)
            act(cbS, psb[0][:, :], AF.Identity, reads=["psb0", "bias_in"], writes=["cbS"],
                bias=bias_in3[:, 4 + fc, b:b + 1])
            wmm(1, 8 + fc)
            act(ccS, psb[1][:, :], AF.Identity, reads=["psb1", "bias_in"], writes=["ccS"],
                bias=bias_in3[:, 8 + fc, b:b + 1])
            wmm(3, 12 + fc)
            vc = r3(vcarry, 4)
            dve("tensor_copy", vbuf[:, 0:2], vc[:, fc, :], reads=["vcarry"], writes=["vbuf"])
            dve("scalar_tensor_tensor", vbuf[:, 2:514], psb[3][:, :], bias_in3[:, 12 + fc, b:b + 1], ccS,
                ALU.add, ALU.mult, reads=["psb3", "bias_in", "ccS"], writes=["vbuf"])
            cw = r3(conv_wT, 4)
            dve("tensor_scalar", cacc, vbuf[:, 2:514], cw[:, fc, 2:3], None, ALU.mult,
                reads=["vbuf", "conv_wT"], writes=["cacc"])
            dve("scalar_tensor_tensor", cacc, vbuf[:, 1:513], cw[:, fc, 1:2], cacc, ALU.mult, ALU.add,
                reads=["vbuf", "conv_wT", "cacc"], writes=["cacc"])
            dve("scalar_tensor_tensor", cacc, vbuf[:, 0:512], cw[:, fc, 0:1], cacc, ALU.mult, ALU.add,
                reads=["vbuf", "conv_wT", "cacc"], writes=["cacc"])
            dve("tensor_tensor", yc3[:, fc, :], cacc, cbS, ALU.mult, reads=["cacc", "cbS"], writes=["yc"])
            dve("tensor_copy", vc[:, fc, :], vbuf[:, 512:514], reads=["vbuf"], writes=["vcarry"])

        def emit_GLU():
            for oc in range(4):
                bank = 2 + oc % 2
                mmgroup([(psb[bank][:, :], Wglu3[:, k, oc * 128:(oc + 1) * 128], ys3[:, k, :], k == 0, k == 3, None)
                         for k in range(4)], reads=["Wglu", "ys"], writes=[f"psb{bank}"])
                act(sgA, psb[bank][:, :], AF.Sigmoid, reads=[f"psb{bank}", "b_gluT"], writes=["sgA"],
                    bias=b_gluT[:, oc:oc + 1])
                dve("tensor_tensor", yglu3[:, oc, :], sgA, ys3[:, oc, :], ALU.mult, reads=["sgA", "ys"], writes=["yglu"])

        def emit_MERGE(b, pre):
            tiles = {0: pre[0], 1: pre[1], 2: pre[2]}
            for half in range(2):
                gs_t, gs_tag = tiles[0] if half == 0 else tiles[2]
                if half == 1:
                    tiles[3] = ring_load(gate_src(1, 1))
                    tiles[4] = ring_load(wout_src(0))
                gc_t, gc_tag = tiles[1] if half == 0 else tiles[3]
                for cl in range(4):
                    oc = half * 4 + cl
                    mmgroup([(psb[2][:, :], gs_t[:, k, cl * 128:(cl + 1) * 128], hT3[:, k, :], k == 0, k == 7, None)
                             for k in range(8)], reads=[gs_tag, "hT"], writes=["psb2"])
                    act(sgA, psb[2][:, :], AF.Sigmoid, reads=["psb2", "bias_in"], writes=["sgA"],
                        bias=bias_in3[:, 16 + oc, b:b + 1])
                    mmgroup([(psb[3][:, :], gc_t[:, k, cl * 128:(cl + 1) * 128], hT3[:, k, :], k == 0, k == 7, None)
                             for k in range(8)], reads=[gc_tag, "hT"], writes=["psb3"])
                    act(sgB, psb[3][:, :], AF.Sigmoid, reads=["psb3", "bias_in"], writes=["sgB"],
                        bias=bias_in3[:, 24 + oc, b:b + 1])
                    mmgroup([(psb[6][:, :], Wps3[:, k, oc * 128:(oc + 1) * 128], yglu3[:, k, :], k == 0, k == 3, None)
                             for k in range(4)], reads=["Wps", "yglu"], writes=["psb6"])
                    mmgroup([(psb[7][:, :], Wpc3[:, k, oc * 128:(oc + 1) * 128], yc3[:, k, :], k == 0, k == 3, None)
                             for k in range(4)], reads=["Wpc", "yc"], writes=["psb7"])
                    dve("tensor_tensor", tr1, psb[6][:, :], sgA, ALU.mult, reads=["psb6", "sgA"], writes=["tr1"])
                    dve("tensor_tensor", tr2, psb[7][:, :], sgB, ALU.mult, reads=["psb7", "sgB"], writes=["tr2"])
                    pool("tensor_tensor", mT3[:, oc, :], tr1, tr2, ALU.add, reads=["tr1", "tr2"], writes=["mT"])
            tiles[5] = ring_load(wout_src(1))
            return [tiles[4], tiles[5]]

        def emit_WOUT(blk, nxt, wo):
            for s in range(4):
                slot = s % 2
                r0 = blk * BLK + s * 128
                dma("sp", f"xq{slot}", xq[slot], dr["x"][r0:r0 + 128, :], writes=[f"xq{slot}"])
                if nxt is not None:
                    norm_front(dr["x"], nxt, s, rstdA, xr, hn, slot)
                for oh in range(2):
                    wt, wtag = wo[oh]
                    bank = 2 + oh
                    mmgroup([(psb[bank][:, :], mT3[:, k, s * 128:(s + 1) * 128], wt[:, k, :], k == 0, k == 7, None)
                             for k in range(8)], reads=["mT", wtag], writes=[f"psb{bank}"])
                    tt = ga if oh == 0 else gb
                    ttag = "tr1" if oh == 0 else "tr2"
                    dve("tensor_tensor", tt, psb[bank][:, :], g1row[:, oh * 512:(oh + 1) * 512], ALU.mult,
                        reads=[f"psb{bank}", "g1row"], writes=[ttag])
                    pool("tensor_tensor", xq[slot][:, oh * 512:(oh + 1) * 512], tt, xq[slot][:, oh * 512:(oh + 1) * 512],
                         ALU.add, reads=[ttag, f"xq{slot}"], writes=[f"xq{slot}"])
                dma("pool", f"xq{slot}", x1_d[r0:r0 + 128, :], xq[slot], reads=[f"xq{slot}"], writes=["x1_dram"])
                if nxt is not None:
                    norm_back(nxt // 4, s, gs1T3, "gs1T", hT3, "hT", hn)

        def precast_ffn(i):
            if i < 8:
                dma("pool", f"pcf{i}", w1f16_d[:, i * 512:(i + 1) * 512].rearrange("(k p) n -> p k n", p=128),
                    dr["w_ff1"][:, i * 512:(i + 1) * 512].rearrange("(k p) n -> p k n", p=128), writes=[f"w1f16_{i}"])
            else:
                q = i - 8
                dma("pool", f"pcf{i}", w2f16_d[q * 512:(q + 1) * 512, :].rearrange("(k p) n -> p k n", p=128),
                    dr["w_ff2"][q * 512:(q + 1) * 512, :].rearrange("(k p) n -> p k n", p=128), writes=[f"w2f16_{q}"])

        if nblk_a > 0:
            stats_rstd_A(0)
            norm_transpose(dr["x"], 0, 0, rstdA, gs1T3, "gs1T", hT3, "hT")
        for blk in range(nblk_a):
            b = blk // 4
            qpos = blk % 4
            if qpos == 0:
                row_from_col(g1row, modT3, 16, b, "modT", "g1row")
                dve("memset", carry_re, 0.0, reads=[], writes=["carry_re"])
                dve("memset", carry_im, 0.0, reads=[], writes=["carry_im"])
                dve("memset", vcarry, 0.0, reads=[], writes=["vcarry"])
            if blk == 0:
                tap("hT", hT, "hT")
            if blk + 1 < nblk_a:
                stats_rstd_A(blk + 1)
            pre = [ring_load(gate_src(0, 0)), ring_load(gate_src(1, 0)), ring_load(gate_src(0, 1))]
            precast_ffn(blk)
            emit_U(b)
            emit_relayout(0); emit_relayout(1)
            emit_E(0); emit_chain(0); emit_CONV(0, b)
            emit_relayout(2)
            emit_E(1); emit_chain(1); emit_Y(0); emit_CONV(1, b)
            emit_relayout(3)
            emit_E(2); emit_chain(2); emit_Y(1); emit_CONV(2, b)
            emit_E(3); emit_chain(3); emit_Y(2); emit_CONV(3, b)
            emit_Y(3)
            if blk == 0:
                tap("U4", U4, "U4_3")
            emit_GLU()
            if blk == 0:
                tap("ys", ys, "ys"); tap("yglu", yglu, "yglu"); tap("yc", yc, "yc")
            wo = emit_MERGE(b, pre)
            if blk == 0:
                tap("mT", mT, "mT")
            emit_WOUT(blk, blk + 1 if blk + 1 < nblk_a else None, wo)
        for i in range(nblk_a, 16):
            precast_ffn(i)
        P.barrier()
        if "x1" in tap_d:
            dma("sp", "tapx1a", xr[0], x1_d[0:128, :], reads=["x1_dram"], writes=["xr0"])
            dma("sp", "tapx1b", tap_d["x1"], xr[0], reads=["xr0"], writes=["tapout_x1"])
            P.barrier()

        pos[0] = persistB_end
        W1f = alloc(8 * 4096, BF16); W1f3 = r3(W1f, 8)
        W2f = alloc(32 * 1024, BF16); W2f3 = r3(W2f, 32)
        for q in range(8):
            src = w1f16_d[:, q * 512:(q + 1) * 512].rearrange("(k p) n -> p k n", p=128)
            dst = W1f3[:, :, q * 512:(q + 1) * 512]
            dma("sp" if q % 2 == 0 else "act", f"ld_w1f{q}", dst, src, reads=[f"w1f16_{q}"], writes=[f"W1f_q{q}"])
        for q in range(8):
            src = w2f16_d[q * 512:(q + 1) * 512, :].rearrange("(k p) n -> p k n", p=128)
            dst = W2f3[:, q * 4:(q + 1) * 4, :]
            dma("sp" if q % 2 == 0 else "act", f"ld_w2f{q}", dst, src, reads=[f"w2f16_{q}"], writes=[f"W2f_q{q}"])
        xr = [alloc(1024) for _ in range(2)]
        hn = alloc(1024, BF16); junk = alloc(1024, BF16)
        h2T = alloc(8 * 512, BF16); h2T3 = r3(h2T, 8)
        hid = alloc(32 * 512, BF16); hid3 = r3(hid, 32)
        rl = [alloc(512, BF16) for _ in range(2)]
        tr1 = alloc(512)
        x2 = [alloc(1024) for _ in range(2)]
        g2row = alloc(1024); fgrow = alloc(1024)
        diagt = alloc(128)
        ssB = alloc(4); rstdB = alloc(4); ss2 = alloc(1); rstd2 = alloc(1)
        print("phase B arena use:", pos[0], "of", ARENA)
        load_group("sp", "ld_fg", [(fgrow, dr["fg_row"], "fgrow")])
        sh2b3 = r3(sh2b, 8)
        for ch in range(32):
            mmgroup([(psb[0][:, ch * 4:(ch + 1) * 4], W1f3[:, k, ch * 128:(ch + 1) * 128], sh2b3[:, k, :],
                      k == 0, k == 7, None) for k in range(8)], reads=[f"W1f_q{ch // 4}", "sh2b"], writes=["psb0"])
        dve("tensor_copy", bias_ff1, psb[0][:, 0:128], reads=["psb0"], writes=["bias_ff1"])

        if nblk_b > 0:
            stats_rstd(x1_d, 0, ssB, rstdB, "B")
            norm_transpose(x1_d, 0, 0, rstdB, gs2T3, "gs2T", h2T3, "h2T")
        for blk in range(nblk_b):
            b = blk // 4
            if blk % 4 == 0:
                row_from_col(g2row, modT3, 40, b, "modT", "g2row")
            if blk + 1 < nblk_b:
                stats_rstd(x1_d, blk + 1, ssB, rstdB, "B")
            if blk == 0:
                tap("h2T", h2T, "h2T")
            for hc in range(32):
                bank = 2 + hc % 2
                mmgroup([(psb[bank][:, :], W1f3[:, k, hc * 128:(hc + 1) * 128], h2T3[:, k, :], k == 0, k == 7, None)
                         for k in range(8)], reads=[f"W1f_q{hc // 4}", "h2T"], writes=[f"psb{bank}"])
                act(rl[hc % 2], psb[bank][:, :], AF.Relu, reads=[f"psb{bank}", "bias_ff1"], writes=[f"rl{hc % 2}"],
                    bias=bias_ff13[:, hc, b:b + 1])
                dve("tensor_tensor", hid3[:, hc, :], rl[hc % 2], rl[hc % 2], ALU.mult,
                    reads=[f"rl{hc % 2}"], writes=["hid"])
            for s in range(4):
                slot = s % 2
                r0 = blk * BLK + s * 128
                dma("sp", f"xr{slot}", xr[slot], x1_d[r0:r0 + 128, :], writes=[f"xr{slot}"])
                if blk + 1 < nblk_b:
                    norm_front(x1_d, blk + 1, s, rstdB, xr, hn, (s + 1) % 2)
                for oh in range(2):
                    bank = 4 + oh
                    mmgroup([(psb[bank][:, :], hid3[:, k, s * 128:(s + 1) * 128], W2f3[:, k, oh * 512:(oh + 1) * 512],
                              k == 0, k == 31, None) for k in range(32)],
                            reads=["hid"] + [f"W2f_q{q}" for q in range(8)], writes=[f"psb{bank}"])
                    dve("tensor_tensor", tr1, psb[bank][:, :], g2row[:, oh * 512:(oh + 1) * 512], ALU.mult,
                        reads=[f"psb{bank}", "g2row"], writes=["tr1"])
                    dve("tensor_tensor", x2[slot][:, oh * 512:(oh + 1) * 512], tr1, xr[slot][:, oh * 512:(oh + 1) * 512],
                        ALU.add, reads=["tr1", f"xr{slot}"], writes=[f"x2{slot}"])
                if blk + 1 < nblk_b:
                    norm_back((blk + 1) // 4, s, gs2T3, "gs2T", h2T3, "h2T", hn)
                dve("memset", ss2, 0.0, reads=[], writes=["ss2"])
                act(junk, x2[slot], AF.Square, reads=[f"x2{slot}"], writes=["junk", "ss2"], accum_out=ss2[:, 0:1])
                dve("tensor_scalar", rstd2, ss2, 1.0 / D, EPS, ALU.mult, ALU.add, reads=["ss2"], writes=["rstd2"])
                act(rstd2, rstd2, AF.Sqrt, reads=["rstd2"], writes=["rstd2"])
                dve("reciprocal", rstd2, rstd2, reads=["rstd2"], writes=["rstd2"])
                dve("scalar_tensor_tensor", x2[slot], x2[slot], rstd2[:, 0:1], fgrow, ALU.mult, ALU.mult,
                    reads=[f"x2{slot}", "rstd2", "fgrow"], writes=[f"x2{slot}"])
                dma("pool", f"x2{slot}", out_d[r0:r0 + 128, :], x2[slot], reads=[f"x2{slot}"], writes=["out_dram"])
        P.frozen = False
        P.barrier()
        P.emit()
    return nc


_CACHE = {}


def kernel(**inputs):
    inp = {k: np.asarray(v) for k, v in inputs.items()}
    shared = _shared_inputs(inp)
    x = np.ascontiguousarray(inp["x"], dtype=np.float32)
    c = np.asarray(inp["c"], dtype=np.float32)
    in_maps = []
    for i in range(NCORES):
        m = dict(shared)
        m["x"] = x[NB * i:NB * (i + 1)].reshape(NTOK, D)
        cc = c[NB * i:NB * (i + 1)]
        m["cT"] = np.ascontiguousarray(cc.T.reshape(8, 128, NB).transpose(1, 0, 2).reshape(128, 32))
        in_maps.append(m)
    if "nc" not in _CACHE:
        _CACHE["nc"] = build_program()
    res = run_bass_kernel_spmd(_CACHE["nc"], in_maps, core_ids=list(range(NCORES)))
    out = np.stack([np.asarray(r["out"]).reshape(NB, SEQ, D) for r in res.results], axis=0)
    return out.reshape(NCORES * NB, SEQ, D).astype(np.float32)
```

```python
import contextlib
import math
import numpy as np
import concourse.bass as bass
import concourse.mybir as mybir
from concourse.bass_utils import run_bass_kernel_spmd

F32 = mybir.dt.float32
BF16 = mybir.dt.bfloat16
I32 = mybir.dt.int32
AF = mybir.ActivationFunctionType
ALU = mybir.AluOpType

NCORES = 8
D = 1024
SEQ = 2048
NB = 4
NTOK = NB * SEQ
BLK = 512
NBLK = NTOK // BLK
EPS = 1e-6
TWO_PI = 2.0 * math.pi
EPOCH = 24000
DEBUG = False


class Prog:
    ENGS = ("pe", "act", "dve", "pool", "sp")
    COMPUTE = ("pe", "act", "dve", "pool")

    def __init__(self, nc):
        self.nc = nc
        self.ops = {e: [] for e in self.ENGS}
        self.count = {e: 0 for e in self.COMPUTE}
        self.dma_count = {}
        self.last_write = {}
        self.readers = {}
        self.waited = {e: {} for e in self.ENGS}

    def _deps(self, eng, reads, writes, skip_key=None):
        writes = list(writes) + [t for t in reads if t.startswith("psb") and t not in writes]
        deps = []
        for t in reads:
            lw = self.last_write.get(t)
            if lw is not None:
                deps.append(lw)
        for t in writes:
            lw = self.last_write.get(t)
            if lw is not None:
                deps.append(lw)
            deps.extend(self.readers.get(t, ()))
        out = {}
        for key, val in deps:
            if key == eng and eng == "pe":
                continue
            if key == skip_key:
                continue
            if self.waited[eng].get(key, 0) >= val:
                continue
            if out.get(key, 0) < val:
                out[key] = val
        for key, val in out.items():
            self.waited[eng][key] = val
        return list(out.items())

    def _commit(self, sig, reads, writes):
        writes = list(writes) + [t for t in reads if t.startswith("psb") and t not in writes]
        for t in writes:
            self.last_write[t] = sig
            self.readers[t] = []
        for t in reads:
            if t not in writes:
                self.readers.setdefault(t, []).append(sig)

    frozen = False

    def op(self, eng, fn, reads=(), writes=()):
        if self.frozen:
            return
        waits = self._deps(eng, reads, writes)
        self.count[eng] += 1
        sig = (eng, self.count[eng])
        self._commit(sig, reads, writes)
        self.ops[eng].append((fn, waits, sig, 1))

    def dma(self, eng, sem, fn, reads=(), writes=(), final=None):
        if self.frozen:
            return
        waits = self._deps(eng, reads, writes, skip_key=(sem if final is not None else None))
        self.dma_count[sem] = self.dma_count.get(sem, 0) + 16
        sig = (sem, self.dma_count[sem] if final is None else final)
        self._commit(sig, reads, writes)
        self.ops[eng].append((fn, waits, (sem, self.dma_count[sem]), 16))

    def barrier(self):
        if self.frozen:
            return
        sigs = [(e, c) for e, c in self.count.items() if c > 0]
        sigs += [(s, c) for s, c in self.dma_count.items()]
        for e in self.ENGS:
            waits = []
            for key, val in sigs:
                if key == e and e == "pe":
                    continue
                if self.waited[e].get(key, 0) >= val:
                    continue
                self.waited[e][key] = val
                waits.append((key, val))
            if waits:
                self.ops[e].append((None, waits, None, 0))

    def emit(self):
        nc = self.nc
        with contextlib.ExitStack() as st:
            sems = {}

            def get(key, val):
                if key in self.COMPUTE:
                    k = (val - 1) // EPOCH
                    loc = val - k * EPOCH
                    name = f"s_{key}_{k}"
                else:
                    name, loc = f"d_{key}", val
                if name not in sems:
                    sems[name] = st.enter_context(nc.semaphore(name))
                return sems[name], loc

            for e in self.ENGS:
                for fn, waits, sig, inc in self.ops[e]:
                    for key, val in waits:
                        get(key, val)
                    if sig is not None:
                        get(*sig)
            block = st.enter_context(nc.Block())

            def run(e):
                def body(engine):
                    for fn, waits, sig, inc in self.ops[e]:
                        for key, val in waits:
                            s, loc = get(key, val)
                            engine.wait_ge(s, loc)
                        if fn is None:
                            continue
                        ins = fn(engine)
                        s, _ = get(*sig)
                        ins.then_inc(s, inc)
                return body

            block.tensor(run("pe"))
            block.scalar(run("act"))
            block.vector(run("dve"))
            block.gpsimd(run("pool"))
            block.sync(run("sp"))


def _consts():
    r = np.arange(128)
    j4 = r // 32
    g2_c = (r // 16) % 2
    g2_s = r // 64
    maskY = (g2_s[:, None] == g2_c[None, :]).astype(np.float32)
    maskX = np.ascontiguousarray(maskY.T)
    causal = (j4[None, :] >= j4[:, None]).astype(np.float32)
    kY = np.broadcast_to((j4 + 1).astype(np.float32)[None, :], (128, 128)).copy()
    kX = (3 - j4).astype(np.float32).reshape(128, 1)
    iota = np.broadcast_to((np.arange(130) - 1).astype(np.float32)[None, :], (128, 130)).copy()
    ident = np.eye(128, dtype=np.float32)
    ones = np.ones((128, 128), np.float32)
    return dict(maskY=maskY, maskX=maskX, causal=causal, kY=kY, kX=kX, iota=iota,
                ident=ident, ones=ones)


def _shared_inputs(inp):
    f = np.float32
    A = lambda a: np.ascontiguousarray(a, dtype=f)
    d = {}
    d["w_ada"] = A(inp["w_ada"][0])
    d["b_adaT"] = A(inp["b_ada"][0].reshape(48, 128).T)
    d["n1T"] = A(inp["norm1_g"][0].reshape(8, 128).T)
    d["n2T"] = A(inp["norm2_g"][0].reshape(8, 128).T)
    d["fg_row"] = A(np.broadcast_to(inp["final_g"][None, :], (128, 1024)))
    d["w_in"] = A(inp["w_in"][0])
    d["w_glu"] = A(inp["w_glu"][0])
    d["b_gluT"] = A(inp["b_glu"][0].reshape(4, 128).T)
    d["conv_wT"] = A(inp["conv_w"][0].reshape(3, 4, 128).transpose(2, 1, 0).reshape(128, 12))
    dsk = inp["d_skip"][0].reshape(16, 2, 16).transpose(1, 2, 0).reshape(32, 16)
    d["dX"] = A(np.tile(dsk, (4, 1)))
    d["w_ps"] = A(inp["w_proj_ssm"][0])
    d["w_pc"] = A(inp["w_proj_conv"][0])
    d["w_out"] = A(inp["w_out"][0])
    d["w_ff1"] = A(inp["w_ff1"][0])
    d["w_ff2"] = A(inp["w_ff2"][0])
    lre = inp["lam_re"][0].reshape(16, 2, 64)
    lim = inp["lam_im"][0].reshape(16, 2, 64)
    ldt = inp["log_dt"][0].reshape(16, 2)
    d["lamreY"] = A(lre.transpose(1, 2, 0).reshape(128, 16))
    d["lamimY"] = A(lim.transpose(1, 2, 0).reshape(128, 16))
    d["logdtY"] = A(np.repeat(ldt.T[:, None, :], 64, axis=1).reshape(128, 16))

    def expY(a_p_gp_g2_h):
        t = a_p_gp_g2_h[None, :, :, None, :, :]
        t = np.broadcast_to(t, (2, 64, 16, 4, 2, 16))
        return A(t.reshape(128, 16 * 128))
    d["cYre"] = expY(inp["c_re"][0].reshape(16, 2, 16, 64).transpose(3, 0, 1, 2))
    d["cYim"] = expY(inp["c_im"][0].reshape(16, 2, 16, 64).transpose(3, 0, 1, 2))
    d["bYre"] = expY(inp["b_re"][0].reshape(16, 2, 64, 16).transpose(2, 0, 1, 3))
    d["bYim"] = expY(inp["b_im"][0].reshape(16, 2, 64, 16).transpose(2, 0, 1, 3))
    d["lamreX"] = A(np.broadcast_to(inp["lam_re"][0].reshape(1, 2048), (128, 2048)))
    d["lamimX"] = A(np.broadcast_to(inp["lam_im"][0].reshape(1, 2048), (128, 2048)))
    d["logdtX"] = A(np.broadcast_to(np.repeat(ldt, 64, axis=1).reshape(1, 2048), (128, 2048)))

    def expX(a_h_gp_g2_p):
        t = a_h_gp_g2_p[None, None, :, :, :, :]
        t = np.broadcast_to(t, (4, 2, 16, 16, 2, 64))
        return A(t.reshape(128, 2048))
    d["bXre"] = expX(inp["b_re"][0].reshape(16, 2, 64, 16).transpose(3, 0, 1, 2))
    d["bXim"] = expX(inp["b_im"][0].reshape(16, 2, 64, 16).transpose(3, 0, 1, 2))
    d.update(_consts())
    return d


IN_SHAPES = dict(
    x=[NTOK, D], cT=[128, 32], w_ada=[1024, 6144], b_adaT=[128, 48], n1T=[128, 8], n2T=[128, 8],
    fg_row=[128, 1024], w_in=[1024, 4096], w_glu=[512, 512], b_gluT=[128, 4], conv_wT=[128, 12],
    dX=[128, 16], w_ps=[512, 1024], w_pc=[512, 1024], w_out=[1024, 1024], w_ff1=[1024, 4096],
    w_ff2=[4096, 1024], lamreY=[128, 16], lamimY=[128, 16], logdtY=[128, 16],
    cYre=[128, 2048], cYim=[128, 2048], bYre=[128, 2048], bYim=[128, 2048],
    lamreX=[128, 2048], lamimX=[128, 2048], logdtX=[128, 2048], bXre=[128, 2048], bXim=[128, 2048],
    maskY=[128, 128], maskX=[128, 128], causal=[128, 128], kY=[128, 128], kX=[128, 1],
    iota=[128, 130], ident=[128, 128], ones=[128, 128],
)


def build_program(nblk_a=NBLK, nblk_b=NBLK, taps=(), stop_at=None):
    nc = bass.Bass("TRN2", target_bir_lowering=False)
    dr = {k: nc.dram_tensor(k, s, F32, kind="ExternalInput").ap() for k, s in IN_SHAPES.items()}
    out_d = nc.dram_tensor("out", [NTOK, D], F32, kind="ExternalOutput").ap()
    x1_d = nc.dram_tensor("x1s", [NTOK, D], F32, kind="Internal").ap()
    wg16_d = nc.dram_tensor("wg16", [1024, 2048], BF16, kind="Internal").ap()
    wo16_d = nc.dram_tensor("wo16", [1024, 1024], BF16, kind="Internal").ap()
    w1f16_d = nc.dram_tensor("w1f16", [1024, 4096], BF16, kind="Internal").ap()
    w2f16_d = nc.dram_tensor("w2f16", [4096, 1024], BF16, kind="Internal").ap()
    tap_d = {}
    for name, shape in taps:
        tap_d[name] = nc.dram_tensor("tap_" + name, shape, F32, kind="ExternalOutput").ap()

    with contextlib.ExitStack() as st:
        ARENA = 52800
        arena = st.enter_context(nc.sbuf_tensor("arena", [128, ARENA], F32))
        psb = [st.enter_context(nc.psum_tensor(f"psb{i}", [128, 512], F32)) for i in range(8)]
        P = Prog(nc)
        pos = [0]
        uniq = [0]

        def alloc(n, dtype=F32):
            nf = n if dtype == F32 else (n + 1) // 2
            assert pos[0] + nf <= ARENA, f"SBUF arena overflow {pos[0]}+{nf}"
            v = arena[:, pos[0]:pos[0] + nf]
            pos[0] += nf
            if dtype != F32:
                v = v.bitcast(dtype)
            return v

        def r3(ap, a):
            return ap.rearrange("p (a b) -> p a b", a=a)

        def pbf(i):
            return psb[i][:, :].bitcast(BF16)

        def dve(fn, *a, reads, writes, **kw):
            P.op("dve", lambda e: getattr(e, fn)(*a, **kw), reads=reads, writes=writes)

        def act(out, in_, func, reads, writes, **kw):
            P.op("act", lambda e: e.activation(out=out, in_=in_, func=func, **kw), reads=reads, writes=writes)

        def mmgroup(items, reads, writes):
            def fn(e):
                ins = None
                for (o, l, r, s0, s1, tp) in items:
                    if tp is None:
                        ins = e.matmul(o, l, r, start=s0, stop=s1)
                    else:
                        ins = e.matmul(o, l, r, start=s0, stop=s1, tile_position=tp)
                return ins
            P.op("pe", fn, reads=reads, writes=writes)

        def dma(eng, sem, out, in_, reads=(), writes=(), final=None):
            P.dma(eng, sem, lambda e: e.dma_start(out=out, in_=in_), reads=reads, writes=writes, final=final)

        def load_group(eng, sem, items):
            base = P.dma_count.get(sem, 0)
            final = base + 16 * len(items)
            for (o, i, tag) in items:
                P.dma(eng, sem, lambda e, o=o, i=i: e.dma_start(out=o, in_=i), writes=[tag], final=final)

        def checkpoint(name):
            if stop_at == name:
                P.barrier()
                P.frozen = True

        def tap(name, src, tag):
            if name in tap_d:
                if src.dtype == BF16:
                    src = src.bitcast(F32)
                uniq[0] += 1
                P.dma("sp", f"tap{uniq[0]}", lambda e, src=src, name=name: e.dma_start(out=tap_d[name], in_=src),
                      reads=[tag], writes=["tapout_" + name])

        ident32 = alloc(128); ones32 = alloc(128)
        ident16 = alloc(128, BF16)
        cT = alloc(32); b_adaT = alloc(48); n1T = alloc(8); n2T = alloc(8)
        b_gluT = alloc(4); conv_wT = alloc(12); dX = alloc(16)
        modT = alloc(192)
        gs1T = alloc(32); gs2T = alloc(32)
        bias_in = alloc(128)
        bias_ff1 = alloc(128)
        sgc = alloc(32); scb = alloc(32, BF16)
        sh1b = alloc(32, BF16); sh2b = alloc(32, BF16)
        persistB_end = pos[0]
        rT = alloc(16)
        cosT = alloc(16 * 130); sinT = alloc(16 * 130)
        carry_re = alloc(16); carry_im = alloc(16)
        vcarry = alloc(8)
        ssA = alloc(4); rstdA = alloc(4)
        Wssm = alloc(16 * 5 * 128, BF16)
        Wssm4 = Wssm.rearrange("p (g w m) -> p g w m", g=16, w=5)
        modT3 = r3(modT, 48); gs1T3 = r3(gs1T, 8); gs2T3 = r3(gs2T, 8)
        bias_in3 = r3(bias_in, 32); bias_ff13 = r3(bias_ff1, 32)
        cosT3 = r3(cosT, 16); sinT3 = r3(sinT, 16)
        persist_end = pos[0]

        load_group("sp", "ld_small", [
            (ident32, dr["ident"], "ident32"), (ones32, dr["ones"], "ones32"),
            (cT, dr["cT"], "cT"), (b_adaT, dr["b_adaT"], "b_adaT"), (n1T, dr["n1T"], "n1T"),
            (n2T, dr["n2T"], "n2T"), (b_gluT, dr["b_gluT"], "b_gluT"),
            (conv_wT, dr["conv_wT"], "conv_wT"), (dX, dr["dX"], "dX")])
        dve("tensor_copy", ident16, ident32, reads=["ident32"], writes=["ident16"])
        for q in range(4):
            dma("pool", f"precast{q}", wg16_d[:, q * 512:(q + 1) * 512].rearrange("(k p) n -> p k n", p=128),
                dr["w_in"][:, 2048 + q * 512:2048 + (q + 1) * 512].rearrange("(k p) n -> p k n", p=128),
                writes=[f"wg16_{q}"])
        for q in range(2):
            dma("pool", f"precast{4 + q}", wo16_d[:, q * 512:(q + 1) * 512].rearrange("(k p) n -> p k n", p=128),
                dr["w_out"][:, q * 512:(q + 1) * 512].rearrange("(k p) n -> p k n", p=128),
                writes=[f"wo16_{q}"])

        act(sgc, cT, AF.Sigmoid, reads=["cT"], writes=["sgc"])
        dve("tensor_tensor", scb, cT, sgc, ALU.mult, reads=["cT", "sgc"], writes=["scb"])
        scb3 = r3(scb, 8)
        scf = alloc(32)
        dve("tensor_tensor", scf, cT, sgc, ALU.mult, reads=["cT", "sgc"], writes=["scf"])
        scf3 = r3(scf, 8)
        wada_ring = [alloc(8 * 128) for _ in range(2)]
        for j in range(48):
            slot = j % 2
            wt = r3(wada_ring[slot], 8)
            src = dr["w_ada"][:, j * 128:(j + 1) * 128].rearrange("(k p) n -> p k n", p=128)
            dma("act", f"wada{slot}", wt, src, writes=[f"wada{slot}"])
            mmgroup([(psb[0][:, j * 4:(j + 1) * 4], wt[:, k, :], scf3[:, k, :], k == 0, k == 7, None)
                     for k in range(8)], reads=[f"wada{slot}", "scf"], writes=["psb0"])
        checkpoint("mod")

        BIG = 2048
        TMPN = 16 * 130
        T_i = alloc(TMPN).bitcast(I32); T_f = alloc(TMPN); T_red = alloc(TMPN)
        T_c1 = alloc(TMPN); T_c2 = alloc(TMPN)
        setup_scratch = pos[0]

        def big():
            return alloc(BIG)

        def b3(ap):
            return r3(ap, 16)

        def range_reduce(dst, src, shift, n3=None, tagd=None, tags=None):
            n = dst.shape[-1]
            ti = T_i[:, 0:n]; tf = T_f[:, 0:n]
            dve("tensor_scalar", tf, src, 1.0 / TWO_PI, shift / TWO_PI, ALU.mult, ALU.add,
                reads=[tags], writes=["rr_f"])
            dve("tensor_copy", ti, tf, reads=["rr_f"], writes=["rr_i"])
            dve("tensor_copy", tf, ti, reads=["rr_i"], writes=["rr_f"])
            dve("scalar_tensor_tensor", tf, tf, -TWO_PI, src, ALU.mult, ALU.add,
                reads=["rr_f", tags], writes=["rr_f"])
            dve("tensor_scalar", dst, tf, shift, None, ALU.add, reads=["rr_f"], writes=[tagd])
            dve("tensor_scalar", dst, dst, -math.pi, math.pi, ALU.max, ALU.min, reads=[tagd], writes=[tagd])

        def sincos(sin_dst, cos_dst, ang, tag_ang, tag_s, tag_c, n):
            red = T_red[:, 0:n]
            range_reduce(red, ang, 0.0, tagd="red", tags=tag_ang)
            act(sin_dst, red, AF.Sin, reads=["red"], writes=[tag_s])
            range_reduce(red, ang, math.pi / 2, tagd="red", tags=tag_ang)
            act(cos_dst, red, AF.Sin, reads=["red"], writes=[tag_c])

        def cmul(ore, oim, are, aim, bre, bim, tags_a, tags_b, tag_o, n, conj_b=False):
            t1 = T_c1[:, 0:n]; t2 = T_c2[:, 0:n]
            shp = list(ore.shape)

            def v(ap):
                return ap if len(shp) == 2 else ap.rearrange("p (a b) -> p a b", a=shp[1])
            dve("tensor_tensor", v(t1), are, bre, ALU.mult, reads=tags_a + tags_b, writes=["cm1"])
            dve("tensor_tensor", v(t2), aim, bim, ALU.mult, reads=tags_a + tags_b, writes=["cm2"])
            dve("tensor_tensor", ore, v(t1), v(t2), ALU.add if conj_b else ALU.subtract,
                reads=["cm1", "cm2"], writes=[tag_o + "re"])
            dve("tensor_tensor", v(t1), are, bim, ALU.mult, reads=tags_a + tags_b, writes=["cm1"])
            dve("tensor_tensor", v(t2), aim, bre, ALU.mult, reads=tags_a + tags_b, writes=["cm2"])
            dve("tensor_tensor", oim, v(t2), v(t1), ALU.subtract if conj_b else ALU.add,
                reads=["cm1", "cm2"], writes=[tag_o + "im"])

        lamreY = alloc(16); lamimY = alloc(16); logdtY = alloc(16)
        cYre = big(); cYim = big(); bYre = big(); bYim = big()
        maskY = alloc(128); causal = alloc(128); kY = alloc(128); iota = alloc(130)
        load_group("sp", "ld_ssmY", [
            (lamreY, dr["lamreY"], "lamreY"), (lamimY, dr["lamimY"], "lamimY"), (logdtY, dr["logdtY"], "logdtY"),
            (cYre, dr["cYre"], "cYre"), (cYim, dr["cYim"], "cYim"), (bYre, dr["bYre"], "bYre"),
            (bYim, dr["bYim"], "bYim"), (maskY, dr["maskY"], "maskY"), (causal, dr["causal"], "causal"),
            (kY, dr["kY"], "kY"), (iota, dr["iota"], "iota")])
        dtY = alloc(16); lrd = alloc(16); lid = alloc(16)
        act(dtY, logdtY, AF.Exp, reads=["logdtY"], writes=["dtY"])
        dve("tensor_tensor", lrd, lamreY, dtY, ALU.mult, reads=["lamreY", "dtY"], writes=["lrd"])
        dve("tensor_tensor", lid, lamimY, dtY, ALU.mult, reads=["lamimY", "dtY"], writes=["lid"])
        z = alloc(16); acc = alloc(16)
        dve("tensor_scalar", z, lrd, 4.0, None, ALU.mult, reads=["lrd"], writes=["z"])
        dve("tensor_scalar", acc, z, 1.0 / 5040.0, 1.0 / 720.0, ALU.mult, ALU.add, reads=["z"], writes=["acc"])
        for coef in (1.0 / 120.0, 1.0 / 24.0, 1.0 / 6.0, 0.5, 1.0, 1.0):
            dve("tensor_tensor", acc, acc, z, ALU.mult, reads=["acc", "z"], writes=["acc"])
            dve("tensor_scalar", acc, acc, coef, None, ALU.add, reads=["acc"], writes=["acc"])
        dve("tensor_copy", rT, acc, reads=["acc"], writes=["rT"])
        m1 = alloc(16); s1 = alloc(16); c1 = alloc(16)
        act(m1, lrd, AF.Exp, reads=["lrd"], writes=["m1"])
        sincos(s1, c1, lid, "lid", "s1", "c1", 16)
        a1re = alloc(16); a1im = alloc(16)
        dve("tensor_tensor", a1re, m1, c1, ALU.mult, reads=["m1", "c1"], writes=["a1re"])
        dve("tensor_tensor", a1im, m1, s1, ALU.mult, reads=["m1", "s1"], writes=["a1im"])
        dve("tensor_scalar", a1re, a1re, -1.0, None, ALU.add, reads=["a1re"], writes=["a1re"])
        den = alloc(16); t16 = alloc(16); qre = alloc(16); qim = alloc(16)
        dve("tensor_tensor", den, lamreY, lamreY, ALU.mult, reads=["lamreY"], writes=["den"])
        dve("tensor_tensor", t16, lamimY, lamimY, ALU.mult, reads=["lamimY"], writes=["t16"])
        dve("tensor_tensor", den, den, t16, ALU.add, reads=["den", "t16"], writes=["den"])
        dve("reciprocal", den, den, reads=["den"], writes=["den"])
        cmul(qre, qim, a1re, a1im, lamreY, lamimY, ["a1re", "a1im"], ["lamreY", "lamimY"], "q", 16, conj_b=True)
        dve("tensor_tensor", qre, qre, den, ALU.mult, reads=["qre", "den"], writes=["qre"])
        dve("tensor_tensor", qim, qim, den, ALU.mult, reads=["qim", "den"], writes=["qim"])
        bbYre = big(); bbYim = big()
        bc16 = lambda ap: ap.unsqueeze(2).to_broadcast([128, 16, 128])
        cmul(b3(bbYre), b3(bbYim), bc16(qre), bc16(qim), b3(bYre), b3(bYim),
             ["qre", "qim"], ["bYre", "bYim"], "bbY", BIG)
        argm = big(); ang = big()
        kYb = kY.unsqueeze(1).to_broadcast([128, 16, 128])
        dve("tensor_tensor", b3(argm), bc16(lrd), kYb, ALU.mult, reads=["lrd", "kY"], writes=["argm"])
        dve("tensor_tensor", b3(ang), bc16(lid), kYb, ALU.mult, reads=["lid", "kY"], writes=["ang"])
        sinA = big(); cosA = big()
        sincos(sinA, cosA, ang, "ang", "sinA", "cosA", BIG)
        mag = big()
        act(mag, argm, AF.Exp, reads=["argm"], writes=["mag"])
        Are = big(); Aim = big()
        dve("tensor_tensor", Are, mag, cosA, ALU.mult, reads=["mag", "cosA"], writes=["Are"])
        dve("tensor_tensor", Aim, mag, sinA, ALU.mult, reads=["mag", "sinA"], writes=["Aim"])
        Rre = bYre; Rim = bYim
        cmul(Rre, Rim, cYre, cYim, Are, Aim, ["cYre", "cYim", "bbYre", "bbYim"], ["Are", "Aim"], "R", BIG)
        mYb = maskY.unsqueeze(1).to_broadcast([128, 16, 128])
        dve("tensor_tensor", b3(Rre), b3(Rre), mYb, ALU.mult, reads=["Rre", "maskY"], writes=["Rre"])
        dve("scalar_tensor_tensor", b3(Rim), b3(Rim), -1.0, mYb, ALU.mult, ALU.mult,
            reads=["Rim", "maskY"], writes=["Rim"])
        dve("tensor_copy", Wssm4[:, :, 2, :], b3(Rre), reads=["Rre"], writes=["W2re"])
        dve("tensor_copy", Wssm4[:, :, 3, :], b3(Rim), reads=["Rim"], writes=["W2im"])
        act(mag, argm, AF.Exp, reads=["argm", "Are", "Aim"], writes=["mag"], scale=-1.0)
        dve("tensor_tensor", Are, mag, cosA, ALU.mult, reads=["mag", "cosA", "Rre", "Rim"], writes=["Are"])
        dve("scalar_tensor_tensor", Aim, mag, -1.0, sinA, ALU.mult, ALU.mult,
            reads=["mag", "sinA", "Rre", "Rim"], writes=["Aim"])
        Lre = cYre; Lim = cYim
        cmul(Lre, Lim, Are, Aim, bbYre, bbYim, ["Are", "Aim", "Rre", "Rim"], ["bbYre", "bbYim"], "L", BIG)
        dve("tensor_tensor", b3(Lre), b3(Lre), mYb, ALU.mult, reads=["Lre", "maskY"], writes=["Lre"])
        dve("tensor_tensor", b3(Lim), b3(Lim), mYb, ALU.mult, reads=["Lim", "maskY"], writes=["Lim"])
        kt = alloc(128)
        for gp in range(16):
            bank = psb[1 + gp % 2]
            mmgroup([(bank[:, 0:128], b3(Lre)[:, gp, :], b3(Rre)[:, gp, :], True, False, None),
                     (bank[:, 0:128], b3(Lim)[:, gp, :], b3(Rim)[:, gp, :], False, True, None)],
                    reads=["Lre", "Lim", "Rre", "Rim"], writes=[f"psb{1 + gp % 2}"])
            dve("tensor_tensor", kt, bank[:, 0:128], causal, ALU.mult,
                reads=[f"psb{1 + gp % 2}", "causal"], writes=["kt"])
            dve("scalar_tensor_tensor", Wssm4[:, gp, 4, :], ident32, dX[:, gp:gp + 1], kt, ALU.mult, ALU.add,
                reads=["ident32", "dX", "kt"], writes=["W1"])
        th4 = alloc(16); th4r = alloc(16)
        dve("tensor_scalar", th4, lid, 4.0, None, ALU.mult, reads=["lid"], writes=["th4"])
        range_reduce(th4r, th4, 0.0, tagd="th4r", tags="th4")
        angT = alloc(16 * 130)
        dve("tensor_tensor", r3(angT, 16), th4r.unsqueeze(2).to_broadcast([128, 16, 130]),
            iota.unsqueeze(1).to_broadcast([128, 16, 130]), ALU.mult, reads=["th4r", "iota"], writes=["angT"])
        def sincos_tab(dst, shift, tagd):
            red = T_red; tmp_i = T_i; tf = T_f
            dve("tensor_scalar", tf, angT, 1.0 / TWO_PI, shift / TWO_PI, ALU.mult, ALU.add,
                reads=["angT"], writes=["rr_f"])
            dve("tensor_copy", tmp_i, tf, reads=["rr_f"], writes=["rr_i"])
            dve("tensor_copy", tf, tmp_i, reads=["rr_i"], writes=["rr_f"])
            dve("scalar_tensor_tensor", tf, tf, -TWO_PI, angT, ALU.mult, ALU.add,
                reads=["rr_f", "angT"], writes=["rr_f"])
            dve("tensor_scalar", red, tf, shift, None, ALU.add, reads=["rr_f"], writes=["red"])
            dve("tensor_scalar", red, red, -math.pi, math.pi, ALU.max, ALU.min, reads=["red"], writes=["red"])
            act(dst, red, AF.Sin, reads=["red"], writes=[tagd])
        sincos_tab(sinT, 0.0, "sinT")
        sincos_tab(cosT, math.pi / 2, "cosT")
        tap("W2re", Rre, "Rre"); tap("cosT", cosT, "cosT"); tap("sinT", sinT, "sinT"); tap("rT", rT, "rT")
        checkpoint("ssmY")
        P.barrier()
        pos[0] = setup_scratch

        lamreX = big(); lamimX = big(); logdtX = big(); bXre = big(); bXim = big()
        maskX = alloc(128); kX = alloc(1)
        load_group("sp", "ld_ssmX", [
            (lamreX, dr["lamreX"], "lamreX"), (lamimX, dr["lamimX"], "lamimX"), (logdtX, dr["logdtX"], "logdtX"),
            (bXre, dr["bXre"], "bXre"), (bXim, dr["bXim"], "bXim"), (maskX, dr["maskX"], "maskX"),
            (kX, dr["kX"], "kX")])
        dtX = big(); lrdX = big(); lidX = big()
        act(dtX, logdtX, AF.Exp, reads=["logdtX"], writes=["dtX"])
        dve("tensor_tensor", lrdX, lamreX, dtX, ALU.mult, reads=["lamreX", "dtX"], writes=["lrdX"])
        dve("tensor_tensor", lidX, lamimX, dtX, ALU.mult, reads=["lamimX", "dtX"], writes=["lidX"])
        m1X = dtX
        act(m1X, lrdX, AF.Exp, reads=["lrdX", "lidX"], writes=["m1X"])
        s1X = big(); c1X = big()
        sincos(s1X, c1X, lidX, "lidX", "s1X", "c1X", BIG)
        a1reX = big(); a1imX = big()
        dve("tensor_tensor", a1reX, m1X, c1X, ALU.mult, reads=["m1X", "c1X"], writes=["a1reX"])
        dve("tensor_tensor", a1imX, m1X, s1X, ALU.mult, reads=["m1X", "s1X"], writes=["a1imX"])
        dve("tensor_scalar", a1reX, a1reX, -1.0, None, ALU.add, reads=["a1reX"], writes=["a1reX"])
        denX = s1X; tX = c1X
        dve("tensor_tensor", denX, lamreX, lamreX, ALU.mult, reads=["lamreX", "a1imX", "a1reX"], writes=["denX"])
        dve("tensor_tensor", tX, lamimX, lamimX, ALU.mult, reads=["lamimX", "a1imX", "a1reX"], writes=["tX"])
        dve("tensor_tensor", denX, denX, tX, ALU.add, reads=["denX", "tX"], writes=["denX"])
        dve("reciprocal", denX, denX, reads=["denX"], writes=["denX"])
        qreX = big(); qimX = big()
        cmul(qreX, qimX, a1reX, a1imX, lamreX, lamimX, ["a1reX", "a1imX"], ["lamreX", "lamimX"], "qX", BIG, conj_b=True)
        dve("tensor_tensor", qreX, qreX, denX, ALU.mult, reads=["qXre", "denX"], writes=["qXre"])
        dve("tensor_tensor", qimX, qimX, denX, ALU.mult, reads=["qXim", "denX"], writes=["qXim"])
        bbXre = a1reX; bbXim = a1imX
        cmul(bbXre, bbXim, qreX, qimX, bXre, bXim, ["qXre", "qXim", "denX"], ["bXre", "bXim"], "bbX", BIG)
        angX = qreX
        dve("tensor_scalar", angX, lidX, kX[:, 0:1], None, ALU.mult, reads=["lidX", "kX", "bbXre", "bbXim"], writes=["angX"])
        sinX = bXre; cosX = bXim
        sincos(sinX, cosX, angX, "angX", "sinX", "cosX", BIG)
        magX = qimX
        act(magX, lrdX, AF.Exp, reads=["lrdX", "bbXre", "bbXim"], writes=["magX"], scale=kX[:, 0:1])
        AXre = lamreX; AXim = lamimX
        dve("tensor_tensor", AXre, magX, cosX, ALU.mult, reads=["magX", "cosX", "denX", "qXre"], writes=["AXre"])
        dve("tensor_tensor", AXim, magX, sinX, ALU.mult, reads=["magX", "sinX", "denX", "qXre"], writes=["AXim"])
        W3re = lrdX; W3im = lidX
        cmul(W3re, W3im, AXre, AXim, bbXre, bbXim, ["AXre", "AXim", "angX", "magX"], ["bbXre", "bbXim"], "W3", BIG)
        mXb = maskX.unsqueeze(1).to_broadcast([128, 16, 128])
        dve("tensor_tensor", Wssm4[:, :, 0, :], b3(W3re), mXb, ALU.mult, reads=["W3re", "maskX"], writes=["W3re_b"])
        dve("tensor_tensor", Wssm4[:, :, 1, :], b3(W3im), mXb, ALU.mult, reads=["W3im", "maskX"], writes=["W3im_b"])
        tap("W3re", W3re, "W3re")
        checkpoint("ssmX")
        P.barrier()
        dve("tensor_tensor", modT3, r3(psb[0][:, 0:192], 48),
            b_adaT.unsqueeze(2).to_broadcast([128, 48, 4]), ALU.add,
            reads=["psb0", "b_adaT"], writes=["modT"])
        dve("scalar_tensor_tensor", gs1T3, modT3[:, 8:16, :], 1.0, n1T.unsqueeze(2).to_broadcast([128, 8, 4]),
            ALU.add, ALU.mult, reads=["modT", "n1T"], writes=["gs1T"])
        dve("scalar_tensor_tensor", gs2T3, modT3[:, 32:40, :], 1.0, n2T.unsqueeze(2).to_broadcast([128, 8, 4]),
            ALU.add, ALU.mult, reads=["modT", "n2T"], writes=["gs2T"])
        dve("tensor_copy", r3(sh1b, 8), modT3[:, 0:8, :], reads=["modT"], writes=["sh1b"])
        dve("tensor_copy", r3(sh2b, 8), modT3[:, 24:32, :], reads=["modT"], writes=["sh2b"])
        tap("modT", modT, "modT")
        pos[0] = persist_end

        Win_lo = alloc(8 * 2048, BF16)
        Wglu = alloc(4 * 512, BF16)
        Wps = alloc(4 * 1024, BF16); Wpc = alloc(4 * 1024, BF16)
        Win3 = r3(Win_lo, 8); Wglu3 = r3(Wglu, 4); Wps3 = r3(Wps, 4); Wpc3 = r3(Wpc, 4)
        ring = [alloc(8 * 512, BF16) for _ in range(3)]
        for q in range(4):
            src = dr["w_in"][:, q * 512:(q + 1) * 512].rearrange("(k p) n -> p k n", p=128)
            dst = Win3[:, :, q * 512:(q + 1) * 512]
            dma("pool", f"ld_win{q}", dst, src, writes=[f"Win_q{q}"])
        load_group("pool", "ld_wA", [
            (Wglu3, dr["w_glu"].rearrange("(k p) n -> p k n", p=128), "Wglu"),
            (Wps3, dr["w_ps"].rearrange("(k p) n -> p k n", p=128), "Wps"),
            (Wpc3, dr["w_pc"].rearrange("(k p) n -> p k n", p=128), "Wpc")])
        checkpoint("wloadA")
        ring_use = [0]

        def ring_load(src):
            src_ap, src_tag = src
            slot = ring_use[0] % 3
            ring_use[0] += 1
            dst = r3(ring[slot], 8)
            dma("sp", f"ring{slot}", dst, src_ap, reads=[src_tag], writes=[f"ring{slot}"])
            return dst, f"ring{slot}"

        def gate_src(which, half):
            q = which * 2 + half
            return wg16_d[:, q * 512:(q + 1) * 512].rearrange("(k p) n -> p k n", p=128), f"wg16_{q}"

        def wout_src(half):
            return wo16_d[:, half * 512:(half + 1) * 512].rearrange("(k p) n -> p k n", p=128), f"wo16_{half}"

        sh1b3 = r3(sh1b, 8)
        for ch in range(16):
            mmgroup([(psb[0][:, ch * 4:(ch + 1) * 4], Win3[:, k, ch * 128:(ch + 1) * 128], sh1b3[:, k, :],
                      k == 0, k == 7, None) for k in range(8)],
                    reads=[f"Win_q{ch // 4}", "sh1b"], writes=["psb0"])
        for which in range(2):
            for half in range(2):
                gt, gtag = ring_load(gate_src(which, half))
                for cl in range(4):
                    ch = 16 + which * 8 + half * 4 + cl
                    mmgroup([(psb[0][:, ch * 4:(ch + 1) * 4], gt[:, k, cl * 128:(cl + 1) * 128], sh1b3[:, k, :],
                              k == 0, k == 7, None) for k in range(8)],
                            reads=[gtag, "sh1b"], writes=["psb0"])
        dve("tensor_copy", bias_in, psb[0][:, 0:128], reads=["psb0"], writes=["bias_in"])
        tap("bias_in", bias_in, "bias_in")
        checkpoint("bias_in")

        xr = [alloc(1024) for _ in range(2)]
        hn = alloc(1024, BF16)
        hT = alloc(8 * 512, BF16); hT3 = r3(hT, 8)
        uT = alloc(4 * 512, BF16); uT3 = r3(uT, 4)
        U4 = alloc(16 * 128, BF16); U43 = r3(U4, 16)
        Ebre = alloc(4 * 129); Ebim = alloc(4 * 129); Xre = alloc(4 * 129); Xim = alloc(4 * 129)
        Ebre3 = r3(Ebre, 4); Ebim3 = r3(Ebim, 4); Xre3 = r3(Xre, 4); Xim3 = r3(Xim, 4)
        tr1 = alloc(512); tr2 = alloc(512)
        tr13 = r3(tr1, 4); tr23 = r3(tr2, 4)
        Sre3b = [r3(alloc(4 * 128, BF16), 4) for _ in range(2)]
        Sim3b = [r3(alloc(4 * 128, BF16), 4) for _ in range(2)]
        tp1 = alloc(512); tp2 = alloc(512); tp13 = r3(tp1, 4); tp23 = r3(tp2, 4)
        ga = tr1; gb = tr2
        one_col = alloc(1)
        dve("memset", one_col, 1.0, reads=[], writes=["one_col"])
        ys = alloc(4 * 512, BF16); ys3 = r3(ys, 4)
        yglu = alloc(4 * 512, BF16); yglu3 = r3(yglu, 4)
        yc = alloc(4 * 512, BF16); yc3 = r3(yc, 4)
        cbS = alloc(512); ccS = alloc(512); vbuf = alloc(514, BF16)
        diagW = alloc(12 * 128, BF16); diagW3 = r3(diagW, 12)
        for i_ in range(12):
            dve("tensor_scalar", diagW3[:, i_, :], ident32, conv_wT[:, i_:i_ + 1], None, ALU.mult,
                reads=["ident32", "conv_wT"], writes=["diagW"])
        sgA = alloc(512, BF16); sgB = alloc(512, BF16)
        mT = alloc(8 * 512, BF16); mT3 = r3(mT, 8)
        g1row = alloc(1024)
        xq = [alloc(1024) for _ in range(2)]
        diagt = alloc(128)
        print("phase A arena use:", pos[0], "of", ARENA)

        def stats_rstd(src_d, blk, ss, rstd, tagp):
            dve("memset", ss, 0.0, reads=[], writes=[tagp + "ss"])
            for s in range(4):
                slot = s % 2
                r0 = blk * BLK + s * 128
                dma("sp", f"xr{slot}", xr[slot], src_d[r0:r0 + 128, :],
                      writes=[f"xr{slot}"])
                act(junk, xr[slot], AF.Square, reads=[f"xr{slot}"], writes=["junk", tagp + "ss"],
                    accum_out=ss[:, s:s + 1])
            dve("tensor_scalar", rstd, ss, 1.0 / D, EPS, ALU.mult, ALU.add, reads=[tagp + "ss"], writes=[tagp + "rstd"])
            act(rstd, rstd, AF.Sqrt, reads=[tagp + "rstd"], writes=[tagp + "rstd"])
            dve("reciprocal", rstd, rstd, reads=[tagp + "rstd"], writes=[tagp + "rstd"])

        def row_from_col(dst_row, colT3, kidx0, b, tag_col, tag_row):
            for half in range(2):
                bank = psb[4 + half]
                for kk in range(4):
                    k = half * 4 + kk
                    dve("tensor_scalar", diagt, ident32, colT3[:, kidx0 + k, b:b + 1], None, ALU.mult,
                        reads=["ident32", tag_col], writes=["diagt"])
                    mmgroup([(bank[:, kk * 128:(kk + 1) * 128], ones32, diagt, True, True, None)],
                            reads=["ones32", "diagt"], writes=[f"psb{4 + half}"])
                dve("tensor_copy", dst_row[:, half * 512:(half + 1) * 512], bank[:, :],
                    reads=[f"psb{4 + half}"], writes=[tag_row])

        def norm_front(src_d, blk, s, rstd, xr, hn, slot):
            r0 = blk * BLK + s * 128
            dma("sp", f"xr{slot}", xr[slot], src_d[r0:r0 + 128, :], writes=[f"xr{slot}"])
            dve("tensor_scalar", hn, xr[slot], rstd[:, s:s + 1], None, ALU.mult,
                reads=[f"xr{slot}", "Arstd", "Brstd"], writes=["hn"])

        def norm_back(b, s, gsT3, tag_gs, dstT3, tag_dst, hn):
            bank = s % 2
            ptb = pbf(bank)

            def fn(e, ptb=ptb, hn=hn, ident16=ident16):
                ins = None
                for k in range(8):
                    ins = e.transpose(ptb[:, k * 128:(k + 1) * 128], hn[:, k * 128:(k + 1) * 128], ident16)
                return ins
            P.op("pe", fn, reads=["hn", "ident16"], writes=[f"psb{bank}"])
            dve("tensor_tensor", dstT3[:, :, s * 128:(s + 1) * 128], r3(ptb, 8),
                gsT3[:, :, b:b + 1].to_broadcast([128, 8, 128]), ALU.mult,
                reads=[f"psb{bank}", tag_gs], writes=[tag_dst])

        def norm_transpose_sub(src_d, blk, b, s, rstd, gsT3, tag_gs, dstT3, tag_dst, xr, hn):
            norm_front(src_d, blk, s, rstd, xr, hn, s % 2)
            norm_back(b, s, gsT3, tag_gs, dstT3, tag_dst, hn)

        def norm_transpose(src_d, blk, b, rstd, gsT3, tag_gs, dstT3, tag_dst):
            for s in range(4):
                norm_transpose_sub(src_d, blk, b, s, rstd, gsT3, tag_gs, dstT3, tag_dst, xr, hn)

        def pool(fn, *a, reads, writes, **kw):
            P.op("pool", lambda e: getattr(e, fn)(*a, **kw), reads=reads, writes=writes)

        junkA = tp1.bitcast(BF16)

        def stats_rstd_A(blk):
            dve("memset", ssA, 0.0, reads=[], writes=["Ass"])
            for s in range(4):
                slot = s % 2
                r0 = blk * BLK + s * 128
                dma("sp", f"xr{slot}", xr[slot], dr["x"][r0:r0 + 128, :], writes=[f"xr{slot}"])
                act(junkA, xr[slot], AF.Square, reads=[f"xr{slot}"], writes=["tp1", "Ass"], accum_out=ssA[:, s:s + 1])
            dve("tensor_scalar", rstdA, ssA, 1.0 / D, EPS, ALU.mult, ALU.add, reads=["Ass"], writes=["Arstd"])
            act(rstdA, rstdA, AF.Sqrt, reads=["Arstd"], writes=["Arstd"])
            dve("reciprocal", rstdA, rstdA, reads=["Arstd"], writes=["Arstd"])

        def emit_U(b):
            for fc in range(4):
                bank = 2 + fc % 2
                mmgroup([(psb[bank][:, :], Win3[:, k, fc * 128:(fc + 1) * 128], hT3[:, k, :], k == 0, k == 7, None)
                         for k in range(8)], reads=["Win_q0", "hT"], writes=[f"psb{bank}"])
                act(uT3[:, fc, :], psb[bank][:, :], AF.Identity, reads=[f"psb{bank}", "bias_in"], writes=[f"uT{fc}"],
                    bias=bias_in3[:, fc, b:b + 1])

        def emit_relayout(fc):
            for gl in range(4):
                for j4 in range(4):
                    o_ = U43[32 * j4:32 * j4 + 32, fc * 4 + gl, :]
                    i_ = uT3[32 * gl:32 * gl + 32, fc, j4:512:4]
                    if (gl + j4) % 2 == 0:
                        act(o_, i_, AF.Copy, reads=[f"uT{fc}"], writes=[f"U4_{fc}"])
                    else:
                        pool("tensor_copy", o_, i_, reads=[f"uT{fc}"], writes=[f"U4p_{fc}"])

        def emit_E(fc):
            gps = range(fc * 4, fc * 4 + 4)
            mmgroup([(psb[6][:, gl * 128:(gl + 1) * 128], Wssm4[:, gp, 0, :], U43[:, gp, :], True, True, None)
                     for gl, gp in enumerate(gps)], reads=[f"U4_{fc}", f"U4p_{fc}", "W3re_b"], writes=["psb6"])
            mmgroup([(psb[7][:, gl * 128:(gl + 1) * 128], Wssm4[:, gp, 1, :], U43[:, gp, :], True, True, None)
                     for gl, gp in enumerate(gps)], reads=[f"U4_{fc}", f"U4p_{fc}", "W3im_b"], writes=["psb7"])

        def emit_chain(fc):
            gps = range(fc * 4, fc * 4 + 4)
            Er = r3(psb[6][:, :], 4); Ei = r3(psb[7][:, :], 4)
            cs = cosT3[:, fc * 4:fc * 4 + 4, 1:129]; sn = sinT3[:, fc * 4:fc * 4 + 4, 1:129]
            dve("tensor_tensor", tr13, Er, cs, ALU.mult, reads=["psb6", "cosT"], writes=["tr1"])
            dve("tensor_tensor", tr23, Ei, sn, ALU.mult, reads=["psb7", "sinT"], writes=["tr2"])
            dve("tensor_tensor", Ebre3[:, :, 1:129], tr13, tr23, ALU.add, reads=["tr1", "tr2"], writes=["Ebre"])
            dve("tensor_tensor", tr13, Ei, cs, ALU.mult, reads=["psb7", "cosT"], writes=["tr1"])
            dve("tensor_tensor", tr23, Er, sn, ALU.mult, reads=["psb6", "sinT"], writes=["tr2"])
            dve("tensor_tensor", Ebim3[:, :, 1:129], tr13, tr23, ALU.subtract, reads=["tr1", "tr2"], writes=["Ebim"])
            dve("tensor_copy", Ebre3[:, :, 0], carry_re[:, fc * 4:fc * 4 + 4], reads=["carry_re"], writes=["Ebre"])
            dve("tensor_copy", Ebim3[:, :, 0], carry_im[:, fc * 4:fc * 4 + 4], reads=["carry_im"], writes=["Ebim"])
            for gl, gp in enumerate(gps):
                rb = rT[:, gp:gp + 1].to_broadcast([128, 129])
                dve("tensor_tensor_scan", Xre3[:, gl, :], rb, Ebre3[:, gl, :], 0.0, ALU.mult, ALU.add,
                    reads=["rT", "Ebre"], writes=["Xre"])
                dve("tensor_tensor_scan", Xim3[:, gl, :], rb, Ebim3[:, gl, :], 0.0, ALU.mult, ALU.add,
                    reads=["rT", "Ebim"], writes=["Xim"])
            sb = fc % 2
            Sr = Sre3b[sb]; Si = Sim3b[sb]
            cs0 = cosT3[:, fc * 4:fc * 4 + 4, 0:128]; sn0 = sinT3[:, fc * 4:fc * 4 + 4, 0:128]
            pool("tensor_tensor", tp13, Xre3[:, :, 0:128], cs0, ALU.mult, reads=["Xre", "cosT"], writes=["tp1"])
            pool("tensor_tensor", tp23, Xim3[:, :, 0:128], sn0, ALU.mult, reads=["Xim", "sinT"], writes=["tp2"])
            pool("tensor_tensor", Sr, tp13, tp23, ALU.subtract, reads=["tp1", "tp2"], writes=[f"Sre{sb}"])
            pool("tensor_tensor", tp13, Xre3[:, :, 0:128], sn0, ALU.mult, reads=["Xre", "sinT"], writes=["tp1"])
            pool("tensor_tensor", tp23, Xim3[:, :, 0:128], cs0, ALU.mult, reads=["Xim", "cosT"], writes=["tp2"])
            pool("tensor_tensor", Si, tp13, tp23, ALU.add, reads=["tp1", "tp2"], writes=[f"Sim{sb}"])
            c9 = cosT3[:, fc * 4:fc * 4 + 4, 129]; s9 = sinT3[:, fc * 4:fc * 4 + 4, 129]
            t4a = tp1[:, 0:4]; t4b = tp2[:, 0:4]
            pool("tensor_tensor", t4a, Xre3[:, :, 128], c9, ALU.mult, reads=["Xre", "cosT"], writes=["tp1"])
            pool("tensor_tensor", t4b, Xim3[:, :, 128], s9, ALU.mult, reads=["Xim", "sinT"], writes=["tp2"])
            pool("tensor_tensor", carry_re[:, fc * 4:fc * 4 + 4], t4a, t4b, ALU.subtract,
                 reads=["tp1", "tp2"], writes=["carry_re"])
            pool("tensor_tensor", t4a, Xre3[:, :, 128], s9, ALU.mult, reads=["Xre", "sinT"], writes=["tp1"])
            pool("tensor_tensor", t4b, Xim3[:, :, 128], c9, ALU.mult, reads=["Xim", "cosT"], writes=["tp2"])
            pool("tensor_tensor", carry_im[:, fc * 4:fc * 4 + 4], t4a, t4b, ALU.add,
                 reads=["tp1", "tp2"], writes=["carry_im"])

        def emit_Y(fc):
            gps = range(fc * 4, fc * 4 + 4)
            sb = fc % 2
            Sr = Sre3b[sb]; Si = Sim3b[sb]
            items = []
            for j4 in range(4):
                for gl, gp in enumerate(gps):
                    o = psb[5][32 * gl:32 * gl + 32, j4:512:4]
                    tp = (0, 32 * gl)
                    items.append((o, Wssm4[:, gp, 4, 32 * j4:32 * j4 + 32], U43[:, gp, :], True, False, tp))
                    items.append((o, Wssm4[:, gp, 2, 32 * j4:32 * j4 + 32], Sr[:, gl, :], False, False, tp))
                    items.append((o, Wssm4[:, gp, 3, 32 * j4:32 * j4 + 32], Si[:, gl, :], False, True, tp))
            mmgroup(items, reads=[f"U4_{fc}", f"U4p_{fc}", f"Sre{sb}", f"Sim{sb}", "W1", "W2re", "W2im"], writes=["psb5"])
            yp = psb[5][:, :]
            act(ga, yp, AF.Square, reads=["psb5"], writes=["tr1"])
            act(ga, ga, AF.Identity, reads=["tr1"], writes=["tr1"], scale=0.044715, bias=one_col[:, 0:1])
            dve("tensor_tensor", gb, yp, ga, ALU.mult, reads=["tr1", "psb5"], writes=["tr2"])
            act(ga, gb, AF.Sigmoid, reads=["tr2"], writes=["tr1"], scale=1.5957691216057308)
            dve("tensor_tensor", ys3[:, fc, :], yp, ga, ALU.mult, reads=["tr1", "psb5"], writes=["ys"])

        def emit_CONV(fc, b):
            def wmm(bank, ch):
                mmgroup([(psb[bank][:, :], Win3[:, k, ch * 128:(ch + 1) * 128], hT3[:, k, :], k == 0, k == 7, None)
                         for k in range(8)], reads=[f"Win_q{ch // 4}", "hT"], writes=[f"psb{bank}"])
            wmm(0, 4 + fc)
            act(cbS, psb[0][:, :], AF.Identity, reads=["psb0", "bias_in"], writes=["cbS"],
                bias=bias_in3[:, 4 + fc, b:b + 1])
            wmm(1, 8 + fc)
            act(ccS, psb[1][:, :], AF.Identity, reads=["psb1", "bias_in"], writes=["ccS"],
                bias=bias_in3[:, 8 + fc, b:b + 1])
            wmm(3, 12 + fc)
            vc = r3(vcarry, 4)
            dve("tensor_copy", vbuf[:, 0:2], vc[:, fc, :], reads=["vcarry"], writes=["vbuf"])
            dve("scalar_tensor_tensor", vbuf[:, 2:514], psb[3][:, :], bias_in3[:, 12 + fc, b:b + 1], ccS,
                ALU.add, ALU.mult, reads=["psb3", "bias_in", "ccS"], writes=["vbuf"])
            mmgroup([(psb[4][:, :], diagW3[:, fc * 3 + k, :], vbuf[:, k:k + 512], k == 0, k == 2, None)
                     for k in range(3)], reads=["diagW", "vbuf"], writes=["psb4"])
            dve("tensor_tensor", yc3[:, fc, :], psb[4][:, :], cbS, ALU.mult, reads=["psb4", "cbS"], writes=["yc"])
            dve("tensor_copy", vc[:, fc, :], vbuf[:, 512:514], reads=["vbuf"], writes=["vcarry"])

        def emit_GLU():
            for oc in range(4):
                bank = 2 + oc % 2
                mmgroup([(psb[bank][:, :], Wglu3[:, k, oc * 128:(oc + 1) * 128], ys3[:, k, :], k == 0, k == 3, None)
                         for k in range(4)], reads=["Wglu", "ys"], writes=[f"psb{bank}"])
                act(sgA, psb[bank][:, :], AF.Sigmoid, reads=[f"psb{bank}", "b_gluT"], writes=["sgA"],
                    bias=b_gluT[:, oc:oc + 1])
                dve("tensor_tensor", yglu3[:, oc, :], sgA, ys3[:, oc, :], ALU.mult, reads=["sgA", "ys"], writes=["yglu"])

        def emit_MERGE(b, pre):
            tiles = {0: pre[0], 1: pre[1], 2: pre[2]}
            for half in range(2):
                gs_t, gs_tag = tiles[0] if half == 0 else tiles[2]
                if half == 1:
                    tiles[3] = ring_load(gate_src(1, 1))
                    tiles[4] = ring_load(wout_src(0))
                gc_t, gc_tag = tiles[1] if half == 0 else tiles[3]
                for cl in range(4):
                    oc = half * 4 + cl
                    mmgroup([(psb[2][:, :], gs_t[:, k, cl * 128:(cl + 1) * 128], hT3[:, k, :], k == 0, k == 7, None)
                             for k in range(8)], reads=[gs_tag, "hT"], writes=["psb2"])
                    act(sgA, psb[2][:, :], AF.Sigmoid, reads=["psb2", "bias_in"], writes=["sgA"],
                        bias=bias_in3[:, 16 + oc, b:b + 1])
                    mmgroup([(psb[3][:, :], gc_t[:, k, cl * 128:(cl + 1) * 128], hT3[:, k, :], k == 0, k == 7, None)
                             for k in range(8)], reads=[gc_tag, "hT"], writes=["psb3"])
                    act(sgB, psb[3][:, :], AF.Sigmoid, reads=["psb3", "bias_in"], writes=["sgB"],
                        bias=bias_in3[:, 24 + oc, b:b + 1])
                    mmgroup([(psb[6][:, :], Wps3[:, k, oc * 128:(oc + 1) * 128], yglu3[:, k, :], k == 0, k == 3, None)
                             for k in range(4)], reads=["Wps", "yglu"], writes=["psb6"])
                    mmgroup([(psb[7][:, :], Wpc3[:, k, oc * 128:(oc + 1) * 128], yc3[:, k, :], k == 0, k == 3, None)
                             for k in range(4)], reads=["Wpc", "yc"], writes=["psb7"])
                    dve("tensor_tensor", tr1, psb[6][:, :], sgA, ALU.mult, reads=["psb6", "sgA"], writes=["tr1"])
                    dve("tensor_tensor", tr2, psb[7][:, :], sgB, ALU.mult, reads=["psb7", "sgB"], writes=["tr2"])
                    pool("tensor_tensor", mT3[:, oc, :], tr1, tr2, ALU.add, reads=["tr1", "tr2"], writes=["mT"])
            tiles[5] = ring_load(wout_src(1))
            return [tiles[4], tiles[5]]

        def emit_WOUT(blk, nxt, wo):
            for s in range(4):
                slot = s % 2
                r0 = blk * BLK + s * 128
                dma("sp", f"xq{slot}", xq[slot], dr["x"][r0:r0 + 128, :], writes=[f"xq{slot}"])
                if nxt is not None:
                    norm_front(dr["x"], nxt, s, rstdA, xr, hn, slot)
                for oh in range(2):
                    wt, wtag = wo[oh]
                    bank = 2 + oh
                    mmgroup([(psb[bank][:, :], mT3[:, k, s * 128:(s + 1) * 128], wt[:, k, :], k == 0, k == 7, None)
                             for k in range(8)], reads=["mT", wtag], writes=[f"psb{bank}"])
                    tt = ga if oh == 0 else gb
                    ttag = "tr1" if oh == 0 else "tr2"
                    dve("tensor_tensor", tt, psb[bank][:, :], g1row[:, oh * 512:(oh + 1) * 512], ALU.mult,
                        reads=[f"psb{bank}", "g1row"], writes=[ttag])
                    pool("tensor_tensor", xq[slot][:, oh * 512:(oh + 1) * 512], tt, xq[slot][:, oh * 512:(oh + 1) * 512],
                         ALU.add, reads=[ttag, f"xq{slot}"], writes=[f"xq{slot}"])
                dma("pool", f"xq{slot}", x1_d[r0:r0 + 128, :], xq[slot], reads=[f"xq{slot}"], writes=["x1_dram"])
                if nxt is not None:
                    norm_back(nxt // 4, s, gs1T3, "gs1T", hT3, "hT", hn)

        def precast_ffn(i):
            if i < 8:
                dma("pool", f"pcf{i}", w1f16_d[:, i * 512:(i + 1) * 512].rearrange("(k p) n -> p k n", p=128),
                    dr["w_ff1"][:, i * 512:(i + 1) * 512].rearrange("(k p) n -> p k n", p=128), writes=[f"w1f16_{i}"])
            else:
                q = i - 8
                dma("pool", f"pcf{i}", w2f16_d[q * 512:(q + 1) * 512, :].rearrange("(k p) n -> p k n", p=128),
                    dr["w_ff2"][q * 512:(q + 1) * 512, :].rearrange("(k p) n -> p k n", p=128), writes=[f"w2f16_{q}"])

        if nblk_a > 0:
            stats_rstd_A(0)
            norm_transpose(dr["x"], 0, 0, rstdA, gs1T3, "gs1T", hT3, "hT")
        for blk in range(nblk_a):
            b = blk // 4
            qpos = blk % 4
            if qpos == 0:
                row_from_col(g1row, modT3, 16, b, "modT", "g1row")
                dve("memset", carry_re, 0.0, reads=[], writes=["carry_re"])
                dve("memset", carry_im, 0.0, reads=[], writes=["carry_im"])
                dve("memset", vcarry, 0.0, reads=[], writes=["vcarry"])
            if blk == 0:
                tap("hT", hT, "hT")
            if blk + 1 < nblk_a:
                stats_rstd_A(blk + 1)
            pre = [ring_load(gate_src(0, 0)), ring_load(gate_src(1, 0)), ring_load(gate_src(0, 1))]
            precast_ffn(blk)
            emit_U(b)
            emit_relayout(0); emit_relayout(1)
            emit_E(0); emit_chain(0); emit_CONV(0, b)
            emit_relayout(2)
            emit_E(1); emit_chain(1); emit_Y(0); emit_CONV(1, b)
            emit_relayout(3)
            emit_E(2); emit_chain(2); emit_Y(1); emit_CONV(2, b)
            emit_E(3); emit_chain(3); emit_Y(2); emit_CONV(3, b)
            emit_Y(3)
            if blk == 0:
                tap("U4", U4, "U4_3")
            emit_GLU()
            if blk == 0:
                tap("ys", ys, "ys"); tap("yglu", yglu, "yglu"); tap("yc", yc, "yc")
            wo = emit_MERGE(b, pre)
            if blk == 0:
                tap("mT", mT, "mT")
            emit_WOUT(blk, blk + 1 if blk + 1 < nblk_a else None, wo)
        for i in range(nblk_a, 16):
            precast_ffn(i)
        P.barrier()
        if "x1" in tap_d:
            dma("sp", "tapx1a", xr[0], x1_d[0:128, :], reads=["x1_dram"], writes=["xr0"])
            dma("sp", "tapx1b", tap_d["x1"], xr[0], reads=["xr0"], writes=["tapout_x1"])
            P.barrier()

        pos[0] = persistB_end
        W1f = alloc(8 * 4096, BF16); W1f3 = r3(W1f, 8)
        W2f = alloc(32 * 1024, BF16); W2f3 = r3(W2f, 32)
        for q in range(8):
            src = w1f16_d[:, q * 512:(q + 1) * 512].rearrange("(k p) n -> p k n", p=128)
            dst = W1f3[:, :, q * 512:(q + 1) * 512]
            dma("sp" if q % 2 == 0 else "act", f"ld_w1f{q}", dst, src, reads=[f"w1f16_{q}"], writes=[f"W1f_q{q}"])
        for q in range(8):
            src = w2f16_d[q * 512:(q + 1) * 512, :].rearrange("(k p) n -> p k n", p=128)
            dst = W2f3[:, q * 4:(q + 1) * 4, :]
            dma("sp" if q % 2 == 0 else "act", f"ld_w2f{q}", dst, src, reads=[f"w2f16_{q}"], writes=[f"W2f_q{q}"])
        xr = [alloc(1024) for _ in range(2)]
        hn = alloc(1024, BF16); junk = alloc(1024, BF16)
        h2T = alloc(8 * 512, BF16); h2T3 = r3(h2T, 8)
        hid = alloc(32 * 512, BF16); hid3 = r3(hid, 32)
        rl = [alloc(512, BF16) for _ in range(2)]
        tr1 = alloc(512)
        x2 = [alloc(1024) for _ in range(2)]
        g2row = alloc(1024); fgrow = alloc(1024)
        diagt = alloc(128)
        ssB = alloc(4); rstdB = alloc(4); ss2 = alloc(1); rstd2 = alloc(1)
        print("phase B arena use:", pos[0], "of", ARENA)
        load_group("sp", "ld_fg", [(fgrow, dr["fg_row"], "fgrow")])
        sh2b3 = r3(sh2b, 8)
        for ch in range(32):
            mmgroup([(psb[0][:, ch * 4:(ch + 1) * 4], W1f3[:, k, ch * 128:(ch + 1) * 128], sh2b3[:, k, :],
                      k == 0, k == 7, None) for k in range(8)], reads=[f"W1f_q{ch // 4}", "sh2b"], writes=["psb0"])
        dve("tensor_copy", bias_ff1, psb[0][:, 0:128], reads=["psb0"], writes=["bias_ff1"])

        if nblk_b > 0:
            stats_rstd(x1_d, 0, ssB, rstdB, "B")
            norm_transpose(x1_d, 0, 0, rstdB, gs2T3, "gs2T", h2T3, "h2T")
        for blk in range(nblk_b):
            b = blk // 4
            if blk % 4 == 0:
                row_from_col(g2row, modT3, 40, b, "modT", "g2row")
            if blk + 1 < nblk_b:
                stats_rstd(x1_d, blk + 1, ssB, rstdB, "B")
            if blk == 0:
                tap("h2T", h2T, "h2T")
            for hc in range(32):
                bank = 2 + hc % 2
                mmgroup([(psb[bank][:, :], W1f3[:, k, hc * 128:(hc + 1) * 128], h2T3[:, k, :], k == 0, k == 7, None)
                         for k in range(8)], reads=[f"W1f_q{hc // 4}", "h2T"], writes=[f"psb{bank}"])
                act(rl[hc % 2], psb[bank][:, :], AF.Relu, reads=[f"psb{bank}", "bias_ff1"], writes=[f"rl{hc % 2}"],
                    bias=bias_ff13[:, hc, b:b + 1])
                dve("tensor_tensor", hid3[:, hc, :], rl[hc % 2], rl[hc % 2], ALU.mult,
                    reads=[f"rl{hc % 2}"], writes=["hid"])
            for s in range(4):
                slot = s % 2
                r0 = blk * BLK + s * 128
                dma("sp", f"xr{slot}", xr[slot], x1_d[r0:r0 + 128, :], writes=[f"xr{slot}"])
                if blk + 1 < nblk_b:
                    norm_front(x1_d, blk + 1, s, rstdB, xr, hn, (s + 1) % 2)
                for oh in range(2):
                    bank = 4 + oh
                    mmgroup([(psb[bank][:, :], hid3[:, k, s * 128:(s + 1) * 128], W2f3[:, k, oh * 512:(oh + 1) * 512],
                              k == 0, k == 31, None) for k in range(32)],
                            reads=["hid"] + [f"W2f_q{q}" for q in range(8)], writes=[f"psb{bank}"])
                    dve("tensor_tensor", tr1, psb[bank][:, :], g2row[:, oh * 512:(oh + 1) * 512], ALU.mult,
                        reads=[f"psb{bank}", "g2row"], writes=["tr1"])
                    dve("tensor_tensor", x2[slot][:, oh * 512:(oh + 1) * 512], tr1, xr[slot][:, oh * 512:(oh + 1) * 512],
                        ALU.add, reads=["tr1", f"xr{slot}"], writes=[f"x2{slot}"])
                if blk + 1 < nblk_b:
                    norm_back((blk + 1) // 4, s, gs2T3, "gs2T", h2T3, "h2T", hn)
                dve("memset", ss2, 0.0, reads=[], writes=["ss2"])
                act(junk, x2[slot], AF.Square, reads=[f"x2{slot}"], writes=["junk", "ss2"], accum_out=ss2[:, 0:1])
                dve("tensor_scalar", rstd2, ss2, 1.0 / D, EPS, ALU.mult, ALU.add, reads=["ss2"], writes=["rstd2"])
                act(rstd2, rstd2, AF.Sqrt, reads=["rstd2"], writes=["rstd2"])
                dve("reciprocal", rstd2, rstd2, reads=["rstd2"], writes=["rstd2"])
                dve("scalar_tensor_tensor", x2[slot], x2[slot], rstd2[:, 0:1], fgrow, ALU.mult, ALU.mult,
                    reads=[f"x2{slot}", "rstd2", "fgrow"], writes=[f"x2{slot}"])
                dma("pool", f"x2{slot}", out_d[r0:r0 + 128, :], x2[slot], reads=[f"x2{slot}"], writes=["out_dram"])
        P.frozen = False
        P.barrier()
        P.emit()
    return nc


_CACHE = {}


def kernel(**inputs):
    inp = {k: np.asarray(v) for k, v in inputs.items()}
    shared = _shared_inputs(inp)
    x = np.ascontiguousarray(inp["x"], dtype=np.float32)
    c = np.asarray(inp["c"], dtype=np.float32)
    in_maps = []
    for i in range(NCORES):
        m = dict(shared)
        m["x"] = x[NB * i:NB * (i + 1)].reshape(NTOK, D)
        cc = c[NB * i:NB * (i + 1)]
        m["cT"] = np.ascontiguousarray(cc.T.reshape(8, 128, NB).transpose(1, 0, 2).reshape(128, 32))
        in_maps.append(m)
    if "nc" not in _CACHE:
        _CACHE["nc"] = build_program()
    res = run_bass_kernel_spmd(_CACHE["nc"], in_maps, core_ids=list(range(NCORES)))
    out = np.stack([np.asarray(r["out"]).reshape(NB, SEQ, D) for r in res.results], axis=0)
    return out.reshape(NCORES * NB, SEQ, D).astype(np.float32)
```
